# Optimizing a Trainium2 kernel written in Bass

```python
import math
import jax, jax.numpy as jnp
from jax import lax
import numpy as np

D_MODEL = 4096
BATCH = 2
SEQ = 4096
DEPTH = 2

HEAD_DIM = 128
N_SLOTS = D_MODEL // HEAD_DIM
A_HEADS = N_SLOTS // 8
B_HEADS = (N_SLOTS - 2 * A_HEADS) // 2
C_HEADS = N_SLOTS - 2 * A_HEADS - B_HEADS
A_VDIM = 2 * HEAD_DIM
A_WIDTH = A_HEADS * A_VDIM
B_WIDTH = B_HEADS * HEAD_DIM
C_WIDTH = C_HEADS * HEAD_DIM
MIX_WIDTH = A_WIDTH + B_WIDTH + C_WIDTH
IN_SPLITS = [A_WIDTH] * 3 + [B_WIDTH] * 3 + [C_WIDTH] * 3
IN_WIDTH = sum(IN_SPLITS)
D_FF = -(-8 * D_MODEL // (3 * 256)) * 256
GRID_W = 64
NA_ROWS_MAX = 8
NA_COLS = 16
Q_BLOCK = 128
DILATED_BRANCHES = ((128, 1), (512, 4), (2048, 16))
C_QBLOCK = 64
RMS_EPS = 1e-6

kernel_name = "hybrid_diff_natten_dilated_encoder"


def rmsnorm(x, g):
    xf = x.astype(jnp.float32)
    y = xf * lax.rsqrt(jnp.mean(xf * xf, axis=-1, keepdims=True) + RMS_EPS)
    return (y * g.astype(jnp.float32)).astype(x.dtype)


def alibi_slopes(n):
    return jnp.exp2(-8.0 * jnp.arange(1, n + 1, dtype=jnp.float32) / n)


def diff_attention(q, k, v, lam, slopes):
    b_, s_len, h_, _, d = q.shape
    scale = d ** -0.5
    qh = jnp.transpose(q, (0, 2, 3, 1, 4))
    kh = jnp.transpose(k, (0, 2, 3, 1, 4))
    vh = jnp.transpose(v, (0, 2, 1, 3))
    pos = jnp.arange(s_len)

    def block(i):
        start = i * Q_BLOCK
        qb = lax.dynamic_slice_in_dim(qh, start, Q_BLOCK, axis=3)
        s = jnp.einsum('bhmqd,bhmkd->bhmqk', qb, kh, preferred_element_type=jnp.float32) * scale
        qpos = start + jnp.arange(Q_BLOCK)
        dist = jnp.abs(qpos[:, None] - pos[None, :]).astype(jnp.float32)
        s = s - slopes[None, :, None, None, None] * dist
        p = jax.nn.softmax(s, axis=-1)
        pdiff = p[:, :, 0] - lam * p[:, :, 1]
        return jnp.einsum('bhqk,bhkv->bhqv', pdiff.astype(v.dtype), vh)

    out = lax.map(block, jnp.arange(s_len // Q_BLOCK))
    return jnp.transpose(out, (1, 0, 3, 2, 4)).reshape(b_, s_len, h_, v.shape[-1])


def neighborhood_attention(q, k, v, rpb):
    b_, s_len, h_, d = q.shape
    rows = s_len // GRID_W
    kh_ = min(NA_ROWS_MAX, rows)
    kw_ = NA_COLS
    scale = d ** -0.5

    def grid(t):
        return jnp.transpose(t.reshape(b_, rows, GRID_W, h_, d), (0, 3, 1, 2, 4))

    qg, kg, vg = grid(q), grid(k), grid(v)
    cols = jnp.arange(GRID_W)
    c_start = jnp.clip(cols - kw_ // 2, 0, GRID_W - kw_)
    col_mask = (cols[None, :] >= c_start[:, None]) & (cols[None, :] < c_start[:, None] + kw_)
    dc_idx = jnp.clip(cols[None, :] - cols[:, None] + NA_COLS - 1, 0, 2 * NA_COLS - 2)
    rpb_c = rpb[:, :, dc_idx]

    def row(r):
        r_start = jnp.clip(r - kh_ // 2, 0, rows - kh_)
        qr = lax.dynamic_index_in_dim(qg, r, axis=2, keepdims=False)
        kb = lax.dynamic_slice_in_dim(kg, r_start, kh_, axis=2)
        vb = lax.dynamic_slice_in_dim(vg, r_start, kh_, axis=2)
        s = jnp.einsum('bhcd,bhijd->bhcij', qr, kb, preferred_element_type=jnp.float32) * scale
        dr_idx = r_start + jnp.arange(kh_) - r + NA_ROWS_MAX - 1
        bias = jnp.transpose(rpb_c[:, dr_idx], (0, 2, 1, 3)).astype(jnp.float32)
        s = jnp.where(col_mask[:, None, :], s + bias[None], -jnp.inf)
        p = jax.nn.softmax(s.reshape(b_, h_, GRID_W, kh_ * GRID_W), axis=-1)
        return jnp.einsum('bhcn,bhnd->bhcd', p.astype(v.dtype), vb.reshape(b_, h_, kh_ * GRID_W, d))

    out = lax.map(row, jnp.arange(rows))
    return jnp.transpose(out, (1, 0, 3, 2, 4)).reshape(b_, s_len, h_, d)


def dilated_attention(q, k, v, slopes):
    b_, s_len, h_, d = q.shape
    scale = d ** -0.5
    qh, kh, vh = [jnp.transpose(t, (0, 2, 1, 3)) for t in (q, k, v)]
    outs, lses = [], []
    for window, dil in DILATED_BRANCHES:
        radius = window // (2 * dil)
        L = s_len // dil
        qb_size = math.gcd(L, C_QBLOCK)
        n_blk = L // qb_size
        qs = jnp.transpose(qh.reshape(b_, h_, L, dil, d), (0, 1, 3, 2, 4)).reshape(b_, h_, dil, n_blk, qb_size, d)
        ks = jnp.transpose(kh.reshape(b_, h_, L, dil, d), (0, 1, 3, 2, 4))
        vs = jnp.transpose(vh.reshape(b_, h_, L, dil, d), (0, 1, 3, 2, 4))
        u_q = jnp.arange(n_blk)[:, None] * qb_size + jnp.arange(qb_size)[None, :]
        u_k = jnp.arange(n_blk)[:, None] * qb_size - radius + jnp.arange(qb_size + 2 * radius)[None, :]
        valid = (u_k >= 0) & (u_k < L)
        u_kc = jnp.clip(u_k, 0, L - 1)
        kb = jnp.take(ks, u_kc, axis=3)
        vb = jnp.take(vs, u_kc, axis=3)
        s = jnp.einsum('bhrnqd,bhrnkd->bhrnqk', qs, kb, preferred_element_type=jnp.float32) * scale
        du = jnp.abs(u_k[:, None, :] - u_q[:, :, None])
        mask = valid[:, None, :] & (du <= radius)
        s = s - slopes[None, :, None, None, None, None] * (du * dil).astype(jnp.float32)
        s = jnp.where(mask, s, -jnp.inf)
        lse = jax.nn.logsumexp(s, axis=-1)
        p = jnp.exp(s - lse[..., None])
        o = jnp.einsum('bhrnqk,bhrnkd->bhrnqd', p.astype(v.dtype), vb)
        o = jnp.transpose(o.reshape(b_, h_, dil, L, d), (0, 1, 3, 2, 4)).reshape(b_, h_, s_len, d)
        lse = jnp.transpose(lse.reshape(b_, h_, dil, L), (0, 1, 3, 2)).reshape(b_, h_, s_len)
        outs.append(o)
        lses.append(lse)
    w = jax.nn.softmax(jnp.stack(lses, axis=0), axis=0)
    out = jnp.einsum('gbhs,gbhsd->bhsd', w.astype(v.dtype), jnp.stack(outs, axis=0))
    return jnp.transpose(out, (0, 2, 1, 3))


def setup_inputs(seed: int = 0) -> dict:
    key = jax.random.key(seed)
    ks = jax.random.split(key, 24)
    f32 = jnp.float32

    def normal(k, shape, scale):
        return jax.random.normal(k, shape, dtype=f32) * scale

    def gain(k, shape):
        return 1.0 + 0.02 * jax.random.normal(k, shape, dtype=f32)

    n_rel_r = 2 * NA_ROWS_MAX - 1
    n_rel_c = 2 * NA_COLS - 1
    return {
        "x": normal(ks[0], (BATCH, SEQ, D_MODEL), 1.0),
        "norm1_g": gain(ks[1], (DEPTH, D_MODEL)),
        "w_in": normal(ks[2], (DEPTH, D_MODEL, IN_WIDTH), D_MODEL ** -0.5),
        "a_q_g": gain(ks[3], (DEPTH, HEAD_DIM)),
        "a_k_g": gain(ks[4], (DEPTH, HEAD_DIM)),
        "lambda_q1": normal(ks[5], (DEPTH, HEAD_DIM), 0.1),
        "lambda_k1": normal(ks[6], (DEPTH, HEAD_DIM), 0.1),
        "lambda_q2": normal(ks[7], (DEPTH, HEAD_DIM), 0.1),
        "lambda_k2": normal(ks[8], (DEPTH, HEAD_DIM), 0.1),
        "a_out_g": gain(ks[9], (DEPTH, A_VDIM)),
        "b_q_g": gain(ks[10], (DEPTH, HEAD_DIM)),
        "b_k_g": gain(ks[11], (DEPTH, HEAD_DIM)),
        "b_rpb": normal(ks[12], (DEPTH, B_HEADS, n_rel_r, n_rel_c), 0.02),
        "b_out_g": gain(ks[13], (DEPTH, HEAD_DIM)),
        "c_q_g": gain(ks[14], (DEPTH, HEAD_DIM)),
        "c_k_g": gain(ks[15], (DEPTH, HEAD_DIM)),
        "c_out_g": gain(ks[16], (DEPTH, HEAD_DIM)),
        "w_out": normal(ks[17], (DEPTH, MIX_WIDTH, D_MODEL), MIX_WIDTH ** -0.5),
        "norm2_g": gain(ks[18], (DEPTH, D_MODEL)),
        "w_gate": normal(ks[19], (DEPTH, D_MODEL, D_FF), D_MODEL ** -0.5),
        "w_up": normal(ks[20], (DEPTH, D_MODEL, D_FF), D_MODEL ** -0.5),
        "w_down": normal(ks[21], (DEPTH, D_FF, D_MODEL), D_FF ** -0.5),
    }


def reference(x, norm1_g, w_in, a_q_g, a_k_g, lambda_q1, lambda_k1, lambda_q2, lambda_k2, a_out_g,
              b_q_g, b_k_g, b_rpb, b_out_g, c_q_g, c_k_g, c_out_g, w_out, norm2_g, w_gate, w_up, w_down):
    b_, s_len, _ = x.shape
    slopes_a = alibi_slopes(A_HEADS)
    slopes_c = alibi_slopes(C_HEADS)
    split_points = [int(p) for p in np.cumsum(IN_SPLITS)[:-1]]
    for l in range(DEPTH):
        h = rmsnorm(x, norm1_g[l])
        proj = h @ w_in[l]
        qa, ka, va, qb, kb, vb, qc, kc, vc = jnp.split(proj, split_points, axis=-1)

        qa = rmsnorm(qa.reshape(b_, s_len, A_HEADS, 2, HEAD_DIM), a_q_g[l])
        ka = rmsnorm(ka.reshape(b_, s_len, A_HEADS, 2, HEAD_DIM), a_k_g[l])
        va = va.reshape(b_, s_len, A_HEADS, A_VDIM)
        lam_init = 0.8 - 0.6 * math.exp(-0.3 * l)
        lam = (jnp.exp(jnp.sum(lambda_q1[l].astype(jnp.float32) * lambda_k1[l].astype(jnp.float32)))
               - jnp.exp(jnp.sum(lambda_q2[l].astype(jnp.float32) * lambda_k2[l].astype(jnp.float32)))
               + lam_init)
        oa = diff_attention(qa, ka, va, lam, slopes_a)
        oa = rmsnorm(oa, a_out_g[l]) * (1.0 - lam_init)

        qb = rmsnorm(qb.reshape(b_, s_len, B_HEADS, HEAD_DIM), b_q_g[l])
        kb = rmsnorm(kb.reshape(b_, s_len, B_HEADS, HEAD_DIM), b_k_g[l])
        vb = vb.reshape(b_, s_len, B_HEADS, HEAD_DIM)
        ob = rmsnorm(neighborhood_attention(qb, kb, vb, b_rpb[l]), b_out_g[l])

        qc = rmsnorm(qc.reshape(b_, s_len, C_HEADS, HEAD_DIM), c_q_g[l])
        kc = rmsnorm(kc.reshape(b_, s_len, C_HEADS, HEAD_DIM), c_k_g[l])
        vc = vc.reshape(b_, s_len, C_HEADS, HEAD_DIM)
        oc = rmsnorm(dilated_attention(qc, kc, vc, slopes_c), c_out_g[l])

        mix = jnp.concatenate([oa.reshape(b_, s_len, A_WIDTH),
                               ob.reshape(b_, s_len, B_WIDTH),
                               oc.reshape(b_, s_len, C_WIDTH)], axis=-1)
        x = x + mix @ w_out[l]

        h2 = rmsnorm(x, norm2_g[l])
        x = x + (jax.nn.silu(h2 @ w_gate[l]) * (h2 @ w_up[l])) @ w_down[l]
    return x
```

```python
import math
from contextlib import ExitStack

import numpy as np
import concourse.bass as bass
import concourse.mybir as mybir
from concourse.bass_utils import run_bass_kernel_spmd

F32 = mybir.dt.float32
BF16 = mybir.dt.bfloat16
AF = mybir.ActivationFunctionType
ALU = mybir.AluOpType

D = 4096
S = 4096
NB = 2
DEPTH = 2
DFF = 11008
NF = DFF // 128
EPS = 1e-6
NEG = -30000.0
SCALE = 128.0 ** -0.5
FGROUPS = [(0, 22), (22, 22), (44, 21), (65, 21)]
NCORES = 8
ENGS = ["sync", "scalar", "vector", "gpsimd", "tensor"]
GRP4 = [[0, 1, 2, 3], [4, 5, 6, 7]]
GRP8 = [list(range(8))]


def lam_init(l):
    return 0.8 - 0.6 * math.exp(-0.3 * l)


class Prog:
    def __init__(self, nc, es):
        self.nc = nc
        self.es = es
        self.semcnt = {}
        self.semh = {}
        self.ops = {e: [] for e in ENGS}

    def sem(self, name):
        assert name not in self.semcnt, name
        self.semcnt[name] = 0
        self.semh[name] = self.es.enter_context(self.nc.semaphore(name))
        return name

    def op(self, eng, fn, waits=(), inc=None):
        sig = None
        if inc is not None:
            if isinstance(inc, Slot):
                inc = (inc.f, 16 if eng in ("sync", "gpsimd_dma") else 1)
            s, a = inc
            self.semcnt[s] += a
            sig = (s, self.semcnt[s])
        if eng == "gpsimd_dma":
            eng = "gpsimd"
        wm = {}
        for w in waits:
            if w is None:
                continue
            s_, v_ = w
            if v_ > wm.get(s_, 0):
                wm[s_] = v_
        self.ops[eng].append((tuple(wm.items()), fn, inc))
        return sig

    def build(self):
        h = self.semh
        ops = self.ops
        with self.nc.Block() as block:
            def mk(engname):
                def body(eng):
                    for waits, fn, inc in ops[engname]:
                        for (s, v) in waits:
                            eng.wait_ge(h[s], v)
                        if fn is None:
                            continue
                        ins = fn(eng)
                        if inc is not None:
                            ins.then_inc(h[inc[0]], inc[1])
                return body
            for e in ENGS:
                if ops[e]:
                    getattr(block, e)(mk(e))
        self.ops = {e: [] for e in ENGS}
        self.nc.all_engine_barrier()


class Slot:
    def __init__(self, P, name, own=True, store=False):
        self.P = P
        self.f = P.sem(name) if own else None
        self.s = P.sem(name + "S") if store else None
        self.rel = []

    def pw(self):
        w = self.rel
        self.rel = []
        return list(w)

    def release(self, sig):
        assert sig is not None
        self.rel.append(sig)


def build_program(n_layers, first_layer_index=0, debug=False, stop_after=None):
    nc = bass.Bass("TRN2", target_bir_lowering=False)
    L = n_layers

    def din(name, shape, dt=F32):
        return nc.dram_tensor(name, list(shape), dt, kind="ExternalInput").ap()

    def dscr(name, shape, dt):
        return nc.dram_tensor(name, list(shape), dt)

    xT = din("xT", [32, 128, 1024])
    yT = nc.dram_tensor("yT", [32, 128, 1024], F32, kind="ExternalOutput").ap()
    biasA = din("biasA", [128, 8064])
    biasC = din("biasC", [128, 3 * 2944])
    onehot = din("onehot", [128, 4])
    lw = []
    for l in range(L):
        lw.append(dict(
            g1=din(f"g1_{l}", [128, 32]), g2=din(f"g2_{l}", [128, 32]),
            wqk=din(f"wqk_{l}", [16 * 128, 4096]), wv=din(f"wv_{l}", [128, 32768]),
            wout=din(f"wout_{l}", [8 * 128, 4096]), wg=din(f"wg_{l}", [22 * 128, 4096]),
            wu=din(f"wu_{l}", [22 * 128, 4096]), wd=din(f"wd_{l}", [32 * 128, 2816]),
            wqk_b=dscr(f"wqkb_{l}", [16 * 128, 4096], BF16), wv_b=dscr(f"wvb_{l}", [128, 32768], BF16),
            qkg=din(f"qkg_{l}", [128, 16]), og=din(f"og_{l}", [128, 8]),
            lamv=din(f"lamv_{l}", [128, 4, 128]), biasB=din(f"biasB_{l}", [3, 128, 20 * 512]),
            wout_b=dscr(f"woutb_{l}", [8 * 128, 4096], BF16), wout_a=dscr(f"wouta_{l}", [32 * 128, 4096], BF16),
            wg_b=dscr(f"wgb_{l}", [22 * 128, 4096], BF16), wg_a=dscr(f"wga_{l}", [88 * 128, 4096], BF16),
            wu_b=dscr(f"wub_{l}", [22 * 128, 4096], BF16), wu_a=dscr(f"wua_{l}", [88 * 128, 4096], BF16),
            wd_b=dscr(f"wdb_{l}", [32 * 128, 2816], BF16), wd_a=dscr(f"wda_{l}", [128 * 128, 2816], BF16),
        ))
    hb = dscr("hb", [4096, 1024], BF16)
    hall = dscr("hall", [8 * 4 * 512, 1024], BF16)
    qk = dscr("qk", [16, 128, 4096], BF16)
    vv = dscr("vv", [4096, 1024], BF16)
    mixo = dscr("mixo", [1024, 4096], BF16)
    mixall = dscr("mixall", [8 * 4 * 128, 4096], BF16)
    xres = dscr("xres", [32, 128, 1024], F32)
    dbg = {}
    if debug:
        dbg["hb"] = nc.dram_tensor("d_hb", [4096, 1024], BF16, kind="ExternalOutput").ap()
        dbg["qk"] = nc.dram_tensor("d_qk", [16, 128, 4096], BF16, kind="ExternalOutput").ap()
        dbg["vv"] = nc.dram_tensor("d_vv", [4096, 1024], BF16, kind="ExternalOutput").ap()
        dbg["mixo"] = nc.dram_tensor("d_mixo", [1024, 4096], BF16, kind="ExternalOutput").ap()
        dbg["x1"] = nc.dram_tensor("d_x1", [32, 128, 1024], F32, kind="ExternalOutput").ap()

    def stop(name):
        return stop_after is not None and stop_after == name

    with ExitStack() as es:
        P = Prog(nc, es)
        ones = es.enter_context(nc.sbuf_tensor("ones", [128, 128], BF16))
        g1s = [es.enter_context(nc.sbuf_tensor(f"g1s{l}", [128, 32], F32)) for l in range(L)]
        g2s = [es.enter_context(nc.sbuf_tensor(f"g2s{l}", [128, 32], F32)) for l in range(L)]
        qkgs = [es.enter_context(nc.sbuf_tensor(f"qkgs{l}", [128, 16], F32)) for l in range(L)]
        ogs = [es.enter_context(nc.sbuf_tensor(f"ogs{l}", [128, 8], F32)) for l in range(L)]
        nlam = [es.enter_context(nc.sbuf_tensor(f"nlam{l}", [128, 1], F32)) for l in range(L)]
        ohs = es.enter_context(nc.sbuf_tensor("ohs", [128, 4], F32))
        psb = [es.enter_context(nc.psum_tensor(f"psb{i}", [128, 512], F32)) for i in range(8)]

        s_c = P.sem("constld")
        s_cv = P.sem("constv")
        s_wc = [P.sem(f"wcast{l}") for l in range(L)]
        s_wag = [P.sem(f"wag{l}") for l in range(L)]
        s_wcc = P.sem("wcc")
        s_ag = P.sem("ag")
        s_dbg = P.sem("dbg") if debug else None
        dvp = P.sem("dvp")
        pep = P.sem("pep")
        acp = P.sem("acp")

        def cast_win(l, piece):
            if piece < 2:
                src = lw[l]["wqk"][piece * 1024:(piece + 1) * 1024, :]
                dst = lw[l]["wqk_b"].ap()[piece * 1024:(piece + 1) * 1024, :]
            else:
                src = lw[l]["wv"]
                dst = lw[l]["wv_b"].ap()
            v = P.op("gpsimd", lambda e, src=src, dst=dst: e.dma_start(out=dst, in_=src, max_dma_last_dim=8192), inc=(s_wc[l], 16))
            P.op("gpsimd", None, waits=[v])

        wag_calls = []
        wag_idx = []
        wag_issued = [0] * L
        for l in range(L):
            calls = [("wout", j) for j in range(8)]
            done = set()
            for gi, (f0, nf) in enumerate(FGROUPS):
                for j in range(f0 // 4, (f0 + nf - 1) // 4 + 1):
                    if j not in done:
                        done.add(j)
                        calls += [("wg", j), ("wu", j)]
                calls += [("wd", gi * 8 + j) for j in range(8)]
            wag_calls.append(calls)
            wag_idx.append({c: i for i, c in enumerate(calls)})

        def issue_wag(l, n):
            for _ in range(n):
                i = wag_issued[l]
                if i >= len(wag_calls[l]):
                    return
                k, j = wag_calls[l][i]
                wag_issued[l] += 1
                v = P.op("gpsimd", lambda e, l=l, k=k, j=j: e.dma_start(out=lw[l][k + "_b"].ap()[j * 128:(j + 1) * 128, :], in_=lw[l][k][j * 128:(j + 1) * 128, :], max_dma_last_dim=8192), inc=(s_wcc, 16))
                P.op("gpsimd", None, waits=[v])
                P.op("gpsimd", lambda e, l=l, k=k, j=j: e.collective_compute("AllGather", ALU.bypass, replica_groups=GRP4,
                     ins=[lw[l][k + "_b"].ap()[j * 128:(j + 1) * 128, :]], outs=[lw[l][k + "_a"].ap()[j * 512:(j + 1) * 512, :]]), inc=(s_wag[l], 1))

        def wag_sig(l, call):
            return (s_wag[l], wag_idx[l][call] + 1)

        with ExitStack() as ps:
            lamt = ps.enter_context(nc.sbuf_tensor("lamt", [128, 4, 128], F32))
            lamp = ps.enter_context(nc.sbuf_tensor("lamp", [128, 2, 128], F32))
            lams = ps.enter_context(nc.sbuf_tensor("lams", [128, 2], F32))
            lame = ps.enter_context(nc.sbuf_tensor("lame", [128, 2], F32))
            P.op("vector", lambda e: e.memset(ones[:], 1.0), inc=(s_cv, 1))
            for l in range(L):
                for (dst, src) in ((g1s[l], lw[l]["g1"]), (g2s[l], lw[l]["g2"]), (qkgs[l], lw[l]["qkg"]), (ogs[l], lw[l]["og"])):
                    P.op("sync", lambda e, dst=dst, src=src: e.dma_start(out=dst[:], in_=src), inc=(s_c, 16))
            P.op("sync", lambda e: e.dma_start(out=ohs[:], in_=onehot), inc=(s_c, 16))
            cast_win(0, 0)
            prev = None
            for l in range(L):
                v = P.op("sync", lambda e, l=l: e.dma_start(out=lamt[:], in_=lw[l]["lamv"]), waits=[prev], inc=(s_c, 16))
                li = lam_init(l + first_layer_index)
                v1 = P.op("vector", lambda e: e.tensor_tensor(out=lamp[:, 0, :], in0=lamt[:, 0, :], in1=lamt[:, 1, :], op=ALU.mult), waits=[v], inc=(s_cv, 1))
                v2 = P.op("vector", lambda e: e.tensor_tensor(out=lamp[:, 1, :], in0=lamt[:, 2, :], in1=lamt[:, 3, :], op=ALU.mult), inc=(s_cv, 1))
                v3 = P.op("vector", lambda e: e.tensor_reduce(out=lams[:], in_=lamp[:], axis=mybir.AxisListType.X, op=ALU.add), waits=[v2], inc=(s_cv, 1))
                v4 = P.op("scalar", lambda e: e.activation(out=lame[:], in_=lams[:], func=AF.Exp), waits=[v3], inc=(s_cv, 1))
                prev = P.op("vector", lambda e, l=l, li=li: e.scalar_tensor_tensor(out=nlam[l][:], in0=lame[:, 1:2], scalar=-li, in1=lame[:, 0:1], op0=ALU.add, op1=ALU.subtract), waits=[v4], inc=(s_cv, 1))
            P.op("vector", None, waits=[prev])
            P.op("sync", None, waits=[(s_c, P.semcnt[s_c]), prev])
            P.build()

        st = dict(xres_st=[None] * 32)

        for l in range(L):
            W = lw[l]
            labs = l + first_layer_index
            last = (l == L - 1)
            xsrc = xT if l == 0 else xres.ap()
            xdst = yT if last else xres.ap()

            with ExitStack() as ps:
                xs = [ps.enter_context(nc.sbuf_tensor(f"p1x{i}_L{l}", [128, 32, 512], F32)) for i in range(2)]
                hs = [ps.enter_context(nc.sbuf_tensor(f"p1h{i}_L{l}", [128, 32, 512], BF16)) for i in range(2)]
                sq = [ps.enter_context(nc.sbuf_tensor(f"p1sq{i}_L{l}", [128, 4, 512], BF16)) for i in range(2)]
                rt = [ps.enter_context(nc.sbuf_tensor(f"p1rt{i}_L{l}", [128, 512], F32)) for i in range(2)]
                if l == 0:
                    P.p1 = dict(x=[Slot(P, f"p1x{i}") for i in range(2)], sq=[Slot(P, f"p1sq{i}") for i in range(2)],
                                ps=[Slot(P, f"p1ps{i}") for i in range(2)], rt=[Slot(P, f"p1rt{i}") for i in range(2)],
                                h=[Slot(P, f"p1h{i}", store=True) for i in range(2)])
                p1 = P.p1
                hbv = hb.ap().rearrange("(fc p) t -> p fc t", p=128)
                stores = []
                for th in range(2):
                    b = th
                    X, SQ, PS_, RT, H = p1["x"][b], p1["sq"], p1["ps"][b], p1["rt"][b], p1["h"][b]
                    sigX = P.op("sync", lambda e, b=b, th=th: e.dma_start(out=xs[b][:], in_=xsrc[:, :, th * 512:(th + 1) * 512].rearrange("fc p t -> p fc t")), waits=X.pw(), inc=X)
                    sigPS = None
                    for j in range(8):
                        sb = j % 2
                        sigSQ = P.op("scalar", lambda e, b=b, j=j, sb=sb: e.activation(out=sq[sb][:], in_=xs[b][:, 4 * j:4 * j + 4, :], func=AF.Square), waits=[sigX] + SQ[sb].pw(), inc=SQ[sb])
                        for i in range(4):
                            fc = 4 * j + i
                            sig = P.op("tensor", lambda e, b=b, sb=sb, i=i, fc=fc: e.matmul(psb[b][:], lhsT=ones[:], rhs=sq[sb][:, i, :], start=(fc == 0), stop=(fc == 31)),
                                       waits=([sigSQ] if i == 0 else []) + (PS_.pw() if fc == 0 else []), inc=PS_ if i == 3 else None)
                        SQ[sb].release(sig)
                        sigPS = sig
                    sigRT = P.op("scalar", lambda e, b=b: e.activation(out=rt[b][:], in_=psb[b][:], func=AF.Ln, scale=1.0 / D, bias=EPS), waits=[sigPS] + RT.pw(), inc=RT)
                    PS_.release(sigRT)
                    sigRR = P.op("scalar", lambda e, b=b: e.activation(out=rt[b][:], in_=rt[b][:], func=AF.Exp, scale=-0.5), waits=[sigRT], inc=(acp, 1))
                    for fc in range(32):
                        sigH = P.op("vector", lambda e, b=b, fc=fc, l=l: e.scalar_tensor_tensor(out=hs[b][:, fc, :], in0=xs[b][:, fc, :], scalar=g1s[l][:, fc:fc + 1], in1=rt[b][:], op0=ALU.mult, op1=ALU.mult),
                                    waits=([sigRR] + H.pw()) if fc == 0 else [], inc=H if fc == 31 else None)
                    RT.release(sigH)
                    X.release(sigH)
                    sigSt = P.op("sync", lambda e, b=b, th=th: e.dma_start(out=hbv[:, :, th * 512:(th + 1) * 512], in_=hs[b][:]), waits=[sigH], inc=(H.s, 16))
                    H.release(sigSt)
                    stores.append(sigSt)
                P.op("sync", None, waits=stores)
                for j in range(8):
                    vag = P.op("gpsimd", lambda e, j=j: e.collective_compute("AllGather", ALU.bypass, replica_groups=GRP4, ins=[hb.ap()[j * 512:(j + 1) * 512, :]], outs=[hall.ap()[j * 2048:(j + 1) * 2048, :]]), waits=stores if j == 0 else [], inc=(s_ag, 1))
                P.op("gpsimd", None, waits=[vag])
                if debug and l == 0:
                    v = P.op("gpsimd", lambda e: e.dma_start(out=dbg["hb"], in_=hb.ap()), inc=(s_dbg, 16))
                    P.op("gpsimd", None, waits=[v])
                P.build()
            if stop("p1"):
                break

            with ExitStack() as ps:
                wres = ps.enter_context(nc.sbuf_tensor(f"p2w_L{l}", [128, 32768], BF16))
                ht = [ps.enter_context(nc.sbuf_tensor(f"p2h{i}_L{l}", [128, 32, 512], BF16)) for i in range(3)]
                sqb = [ps.enter_context(nc.sbuf_tensor(f"p2sq{i}_L{l}", [128, 512], BF16)) for i in range(2)]
                rtb = [ps.enter_context(nc.sbuf_tensor(f"p2rt{i}_L{l}", [128, 512], F32)) for i in range(2)]
                qo = [ps.enter_context(nc.sbuf_tensor(f"p2qo{i}_L{l}", [128, 512], BF16)) for i in range(3)]
                if l == 0:
                    P.p2 = dict(w=Slot(P, "p2w"), h=[Slot(P, f"p2h{i}") for i in range(3)],
                                pa=[Slot(P, f"p2pa{i}") for i in range(3)], sq=[Slot(P, f"p2sq{i}") for i in range(2)],
                                pb=[Slot(P, f"p2pb{i}") for i in range(2)], rt=[Slot(P, f"p2rt{i}") for i in range(2)],
                                qo=[Slot(P, f"p2qo{i}", store=True) for i in range(3)],
                                cnt=dict(h=0, pa=0, sq=0, qo=0))
                p2 = P.p2
                c2 = p2["cnt"]
                WS = p2["w"]
                hallv = hall.ap().rearrange("(j r a p) t -> r p j a t", j=8, r=4, a=4, p=128)
                if l == 0:
                    cast_win(0, 1)
                    cast_win(0, 2)
                wag_pace = [15, 15, 15] if l == 0 else [7, 7, 6]
                vvv = vv.ap()
                p2stores = []

                def load_h(t):
                    i = c2["h"] % 3
                    c2["h"] += 1
                    r, half = t // 2, t % 2
                    sig = None
                    for j in range(8):
                        sig = P.op("sync", lambda e, i=i, r=r, half=half, j=j: e.dma_start(out=ht[i][:, 4 * j:4 * j + 4, :], in_=hallv[r][:, j, :, half * 512:(half + 1) * 512]), waits=p2["h"][i].pw() if j == 0 else [], inc=p2["h"][i])
                    return i, sig

                for pss in range(3):
                    sigW = None
                    for i in range(8):
                        if pss < 2:
                            src = W["wqk_b"].ap()[(pss * 8 + i) * 128:(pss * 8 + i + 1) * 128, :]
                        else:
                            src = W["wv_b"].ap()[:, i * 4096:(i + 1) * 4096]
                        sigW = P.op("sync", lambda e, i=i, src=src: e.dma_start(out=wres[:, i * 4096:(i + 1) * 4096], in_=src),
                                    waits=(WS.pw() + [(s_wc[l], 16 * (pss + 1))]) if i == 0 else [], inc=WS)
                    issue_wag(l, wag_pace[pss])
                    hq = [load_h(0), load_h(1)]
                    sigPA_last = None
                    for t in range(8):
                        hi, sigH = hq.pop(0)
                        H = p2["h"][hi]
                        if t + 2 < 8:
                            hq.append(load_h(t + 2))
                        if pss < 2:
                            deferred = None
                            for s in range(8):
                                sg_ = pss * 8 + s
                                ia = c2["pa"] % 3
                                c2["pa"] += 1
                                PA = p2["pa"][ia]
                                for kc in range(32):
                                    w_ = []
                                    if kc == 0:
                                        w_ = PA.pw() + ([sigH] if s == 0 else []) + ([sigW] if (s == 0 and t == 0) else [])
                                    sigPA = P.op("tensor", lambda e, ia=ia, s=s, kc=kc, hi=hi: e.matmul(psb[ia][:], lhsT=wres[:, (s * 32 + kc) * 128:(s * 32 + kc + 1) * 128], rhs=ht[hi][:, kc, :], start=(kc == 0), stop=(kc == 31)),
                                                  waits=w_, inc=PA if kc == 31 else None)
                                sigPA_last = sigPA
                                isq = c2["sq"] % 2
                                c2["sq"] += 1
                                SQ, PB, RT = p2["sq"][isq], p2["pb"][isq], p2["rt"][isq]
                                sigSQ = P.op("scalar", lambda e, ia=ia, isq=isq: e.activation(out=sqb[isq][:], in_=psb[ia][:], func=AF.Square), waits=[sigPA] + SQ.pw(), inc=SQ)

                                def post(ia=ia, isq=isq, sg_=sg_, t=t, SQ=SQ, PB=PB, RT=RT, PA=PA, sigSQ=sigSQ):
                                    sigPB = P.op("tensor", lambda e: e.matmul(psb[4 + isq][:], lhsT=ones[:], rhs=sqb[isq][:], start=True, stop=True), waits=[sigSQ] + PB.pw(), inc=PB)
                                    SQ.release(sigPB)
                                    sigRT = P.op("scalar", lambda e: e.activation(out=rtb[isq][:], in_=psb[4 + isq][:], func=AF.Ln, scale=1.0 / 128, bias=EPS), waits=[sigPB] + RT.pw(), inc=RT)
                                    PB.release(sigRT)
                                    sigRR = P.op("scalar", lambda e: e.activation(out=rtb[isq][:], in_=rtb[isq][:], func=AF.Exp, scale=-0.5), waits=[sigRT], inc=(acp, 1))
                                    iq = c2["qo"] % 3
                                    c2["qo"] += 1
                                    QO = p2["qo"][iq]
                                    sigQO = P.op("vector", lambda e: e.scalar_tensor_tensor(out=qo[iq][:], in0=psb[ia][:], scalar=qkgs[l][:, sg_:sg_ + 1], in1=rtb[isq][:], op0=ALU.mult, op1=ALU.mult),
                                                 waits=[sigRR] + QO.pw(), inc=QO)
                                    PA.release(sigQO)
                                    RT.release(sigQO)
                                    sigSt = P.op("sync", lambda e: e.dma_start(out=qk.ap()[sg_][:, t * 512:(t + 1) * 512], in_=qo[iq][:]), waits=[sigQO], inc=(QO.s, 16))
                                    QO.release(sigSt)
                                    p2stores.append(sigSt)

                                if deferred is not None:
                                    deferred()
                                deferred = post
                            deferred()
                            H.release(sigPA_last)
                        else:
                            for tb4 in range(4):
                                for ch in range(2):
                                    ia = c2["pa"] % 3
                                    c2["pa"] += 1
                                    PA = p2["pa"][ia]
                                    for kc in range(32):
                                        w_ = []
                                        if kc == 0:
                                            first = (tb4 == 0 and ch == 0)
                                            w_ = PA.pw() + ([sigH] if first else []) + ([sigW] if (first and t == 0) else [])
                                        sigPA = P.op("tensor", lambda e, ia=ia, kc=kc, hi=hi, tb4=tb4, ch=ch: e.matmul(psb[ia][:], lhsT=ht[hi][:, kc, tb4 * 128:(tb4 + 1) * 128], rhs=wres[:, kc * 1024 + ch * 512:kc * 1024 + (ch + 1) * 512], start=(kc == 0), stop=(kc == 31)),
                                                      waits=w_, inc=PA if kc == 31 else None)
                                    sigPA_last = sigPA
                                    iq = c2["qo"] % 3
                                    c2["qo"] += 1
                                    QO = p2["qo"][iq]
                                    sigQO = P.op("scalar", lambda e, ia=ia, iq=iq: e.activation(out=qo[iq][:], in_=psb[ia][:], func=AF.Copy), waits=[sigPA] + QO.pw(), inc=QO)
                                    PA.release(sigQO)
                                    sigSt = P.op("sync", lambda e, iq=iq, t=t, tb4=tb4, ch=ch: e.dma_start(out=vvv[t * 512 + tb4 * 128:t * 512 + (tb4 + 1) * 128, ch * 512:(ch + 1) * 512], in_=qo[iq][:]), waits=[sigQO], inc=(QO.s, 16))
                                    QO.release(sigSt)
                                    p2stores.append(sigSt)
                            H.release(sigPA_last)
                    WS.release(sigPA_last)
                    P.op("gpsimd", None, waits=[sigPA_last])
                P.op("sync", None, waits=p2stores[-3:])
                P.op("gpsimd", None, waits=p2stores[-3:])
                if debug and l == 0:
                    v = P.op("gpsimd", lambda e: e.dma_start(out=dbg["qk"], in_=qk.ap()), inc=(s_dbg, 16))
                    v = P.op("gpsimd", lambda e: e.dma_start(out=dbg["vv"], in_=vv.ap()), inc=(s_dbg, 16))
                    P.op("gpsimd", None, waits=[v])
                P.build()
            if stop("p2"):
                break

            with ExitStack() as ps:
                qT = [ps.enter_context(nc.sbuf_tensor(f"p3q{i}_L{l}", [128, 2, 4096], BF16)) for i in range(2)]
                kT = [ps.enter_context(nc.sbuf_tensor(f"p3k{i}_L{l}", [128, 2, 4096], BF16)) for i in range(2)]
                vt = [ps.enter_context(nc.sbuf_tensor(f"p3v{i}_L{l}", [128, 32, 256], BF16)) for i in range(2)]
                bias = ps.enter_context(nc.sbuf_tensor(f"p3bias_L{l}", [128, 10240], F32))
                stt = [ps.enter_context(nc.sbuf_tensor(f"p3st{i}_L{l}", [128, 512], F32)) for i in range(4)]
                et = [ps.enter_context(nc.sbuf_tensor(f"p3e{i}_L{l}", [128, 512], BF16)) for i in range(4)]
                rz = ps.enter_context(nc.sbuf_tensor(f"p3rz_L{l}", [128, 512], F32))
                t0 = ps.enter_context(nc.sbuf_tensor(f"p3t0_L{l}", [128, 2, 512], F32))
                ob = ps.enter_context(nc.sbuf_tensor(f"p3o_L{l}", [128, 2, 512], F32))
                osq = ps.enter_context(nc.sbuf_tensor(f"p3osq_L{l}", [128, 2, 512], BF16))
                ort = ps.enter_context(nc.sbuf_tensor(f"p3ort_L{l}", [128, 512], F32))
                mo = [ps.enter_context(nc.sbuf_tensor(f"p3mo{i}_L{l}", [128, 2, 512], BF16)) for i in range(2)]
                if l == 0:
                    P.p3 = dict(hd=[Slot(P, f"p3hd{i}") for i in range(2)], bias=Slot(P, "p3bias"),
                                sp=[Slot(P, f"p3sp{i}") for i in range(4)], st=[Slot(P, f"p3st{i}") for i in range(4)],
                                e=[Slot(P, f"p3e{i}") for i in range(4)], acc=[Slot(P, f"p3acc{i}", own=False) for i in range(2)],
                                z=Slot(P, "p3z", own=False), rz=Slot(P, "p3rz"), t0=Slot(P, "p3t0"), o=Slot(P, "p3o"), osq=Slot(P, "p3osq"),
                                ss=Slot(P, "p3ss"), ort=Slot(P, "p3ort"), mo=[Slot(P, f"p3mo{i}", store=True) for i in range(2)],
                                pv=P.sem("p3pv"), cnt=dict(hd=0, sp=0, e=0, acc=0, mo=0))
                p3 = P.p3
                c3 = p3["cnt"]
                pv = p3["pv"]
                units = [("A", 0)] + [("B", i) for i in range(3)] + [("C", i) for i in range(3)]
                qkv = qk.ap()
                vvh = vv.ap().rearrange("(tb p) c -> p tb c", p=128)
                mixv = mixo.ap().rearrange("(c p) t -> p c t", p=128)
                p3stores = []
                mixag = []

                def load_unit(u):
                    kind, hi_ = units[u]
                    i = c3["hd"] % 2
                    c3["hd"] += 1
                    HD = p3["hd"][i]
                    if kind == "A":
                        srcs = [(qT[i][:, 0, :], qkv[0]), (qT[i][:, 1, :], qkv[1]), (kT[i][:, 0, :], qkv[2]), (kT[i][:, 1, :], qkv[3]),
                                (vt[i][:, :, :], vvh[:, :, 0:256])]
                    elif kind == "B":
                        srcs = [(qT[i][:, 0, :], qkv[4 + hi_]), (kT[i][:, 0, :], qkv[7 + hi_]), (vt[i][:, :, 0:128], vvh[:, :, 256 + 128 * hi_:256 + 128 * (hi_ + 1)])]
                    else:
                        srcs = [(qT[i][:, 0, :], qkv[10 + hi_]), (kT[i][:, 0, :], qkv[13 + hi_]), (vt[i][:, :, 0:128], vvh[:, :, 640 + 128 * hi_:640 + 128 * (hi_ + 1)])]
                    sig = None
                    for j, (dst, src) in enumerate(srcs):
                        sig = P.op("sync", lambda e, dst=dst, src=src: e.dma_start(out=dst, in_=src), waits=HD.pw() if j == 0 else [], inc=HD)
                    return i, sig

                def load_bias(u):
                    kind, hi_ = units[u]
                    B = p3["bias"]
                    if kind == "A":
                        return P.op("sync", lambda e: e.dma_start(out=bias[:, 0:8064], in_=biasA), waits=B.pw(), inc=B)
                    if kind == "B":
                        return P.op("sync", lambda e, hi_=hi_: e.dma_start(out=bias[:, :], in_=W["biasB"][hi_]), waits=B.pw(), inc=B)
                    return P.op("sync", lambda e: e.dma_start(out=bias[:, 0:3 * 2944], in_=biasC), waits=B.pw(), inc=B)

                def blocks_for(kind, qt):
                    if kind == "A":
                        return [(kb, qt * 512 - kb * 128 + 3968) for kb in range(32)]
                    if kind == "C":
                        return [(kb, qt * 512 - kb * 128 + 1408) for kb in range(max(0, 4 * qt - 8), min(31, 4 * qt + 11) + 1)]
                    if qt == 0:
                        return [(kb, (8 + kb) * 512) for kb in range(6)]
                    if qt == 7:
                        return [(kb, (14 + kb - 26) * 512) for kb in range(26, 32)]
                    return [(4 * qt - 2 + r, r * 512) for r in range(8)]

                nxt = load_unit(0)
                sigB = load_bias(0)
                mix_chunk = 0
                tail = []
                for u, (kind, hi_) in enumerate(units):
                    hs_, sigHD = nxt
                    HD = p3["hd"][hs_]
                    if u + 1 < len(units):
                        nxt = load_unit(u + 1)
                    nmaps = 2 if kind == "A" else 1
                    ndv = 2 if kind == "A" else 1
                    cbase = (hi_ * 2944) if kind == "C" else 0
                    first_of_unit = True
                    sigSTlast = None
                    sigPVlast = None
                    sigT0 = None
                    for qt in range(8):
                        blks = blocks_for(kind, qt)
                        for m in range(nmaps):
                            if kind == "A":
                                ai = 0
                                obank = [4, 5]
                            else:
                                ai = c3["acc"] % 2
                                c3["acc"] += 1
                                obank = [4 + ai]
                            ACC, Z = p3["acc"][ai], p3["z"]
                            nb_ = len(blks)
                            pend = []

                            def emit_pv(item, isfirst, islast, obank=obank, hs_=hs_, ACC=ACC, Z=Z):
                                (kb, ei, E, sigE) = item
                                for dvc in range(ndv):
                                    w_ = [sigE] if dvc == 0 else []
                                    if isfirst and dvc == 0:
                                        w_ = w_ + ACC.pw()
                                    P.op("tensor", lambda e, ei=ei, kb=kb, dvc=dvc: e.matmul(psb[obank[dvc]][:], lhsT=vt[hs_][:, kb, dvc * 128:(dvc + 1) * 128], rhs=et[ei][:], start=isfirst, stop=islast), waits=w_)
                                sig = P.op("tensor", lambda e, ei=ei: e.matmul(psb[6][:], lhsT=ones[:], rhs=et[ei][:], start=isfirst, stop=islast),
                                           waits=Z.pw() if isfirst else [], inc=(pv, 1))
                                E.release(sig)
                                return sig

                            npv = 0
                            for bi, (kb, boff) in enumerate(blks):
                                si = c3["sp"] % 4
                                c3["sp"] += 1
                                SP, ST = p3["sp"][si], p3["st"][si]
                                sigSP = P.op("tensor", lambda e, si=si, kb=kb, qt=qt, m=m, hs_=hs_: e.matmul(psb[si][:], lhsT=kT[hs_][:, m, kb * 128:(kb + 1) * 128], rhs=qT[hs_][:, m, qt * 512:(qt + 1) * 512], start=True, stop=True),
                                              waits=SP.pw() + ([sigHD] if first_of_unit else []), inc=SP)
                                sigST = P.op("vector", lambda e, si=si, boff=boff, cbase=cbase: e.scalar_tensor_tensor(out=stt[si][:], in0=psb[si][:], scalar=SCALE, in1=bias[:, cbase + boff:cbase + boff + 512], op0=ALU.mult, op1=ALU.add),
                                              waits=[sigSP] + ST.pw() + ([sigB] if first_of_unit else []), inc=ST)
                                first_of_unit = False
                                SP.release(sigST)
                                sigSTlast = sigST
                                ei = c3["e"] % 4
                                c3["e"] += 1
                                E = p3["e"][ei]
                                sigE = P.op("scalar", lambda e, si=si, ei=ei: e.activation(out=et[ei][:], in_=stt[si][:], func=AF.Exp), waits=[sigST] + E.pw(), inc=E)
                                ST.release(sigE)
                                pend.append((kb, ei, E, sigE))
                                if len(pend) > 3:
                                    emit_pv(pend.pop(0), npv == 0, False)
                                    npv += 1
                                if bi == 2 and tail:
                                    for fn_ in tail:
                                        fn_()
                                    tail = []
                            while pend:
                                sigPVlast = emit_pv(pend.pop(0), npv == 0, len(pend) == 0)
                                npv += 1
                            RZ, T0, O, OSQ, SS, ORT = p3["rz"], p3["t0"], p3["o"], p3["osq"], p3["ss"], p3["ort"]
                            sigLZ = P.op("scalar", lambda e: e.activation(out=rz[:], in_=psb[6][:], func=AF.Ln), waits=[sigPVlast] + RZ.pw(), inc=(acp, 1))
                            Z.release(sigLZ)
                            sigRZ = P.op("scalar", lambda e: e.activation(out=rz[:], in_=rz[:], func=AF.Exp, scale=-1.0), waits=[sigLZ], inc=RZ)
                            if kind == "A" and m == 0:
                                for dvc in range(2):
                                    sigT0 = P.op("vector", lambda e, dvc=dvc, ai=ai: e.tensor_tensor(out=t0[:, dvc, :], in0=psb[4 + dvc][:], in1=rz[:], op=ALU.mult),
                                                 waits=([sigRZ] + T0.pw()) if dvc == 0 else [], inc=T0 if dvc == 1 else None)
                                ACC.release(sigT0)
                                RZ.release(sigT0)
                                continue
                            if kind == "A":
                                for dvc in range(2):
                                    sigX_ = P.op("vector", lambda e, dvc=dvc, ai=ai: e.tensor_tensor(out=ob[:, dvc, :], in0=psb[4 + dvc][:], in1=rz[:], op=ALU.mult),
                                                 waits=([sigRZ] + O.pw()) if dvc == 0 else [], inc=(dvp, 1) if dvc == 1 else None)
                                ACC.release(sigX_)
                                RZ.release(sigX_)
                                for dvc in range(2):
                                    sigO = P.op("vector", lambda e, dvc=dvc, l=l: e.scalar_tensor_tensor(out=ob[:, dvc, :], in0=ob[:, dvc, :], scalar=nlam[l][:, 0:1], in1=t0[:, dvc, :], op0=ALU.mult, op1=ALU.add),
                                                waits=[sigX_, sigT0] if dvc == 0 else [], inc=O if dvc == 1 else None)
                                T0.release(sigO)
                                nfeat = 256
                            else:
                                sigO = P.op("vector", lambda e, ai=ai: e.tensor_tensor(out=ob[:, 0, :], in0=psb[4 + ai][:], in1=rz[:], op=ALU.mult), waits=[sigRZ] + O.pw(), inc=O)
                                ACC.release(sigO)
                                RZ.release(sigO)
                                nfeat = 128
                            sigOSQ = P.op("scalar", lambda e, ndv=ndv: e.activation(out=osq[:, 0:ndv, :], in_=ob[:, 0:ndv, :], func=AF.Square), waits=[sigO] + OSQ.pw(), inc=OSQ)
                            cm = (1.0 - lam_init(labs)) if kind == "A" else 1.0

                            def fin_tail(ndv=ndv, nfeat=nfeat, cm=cm, mc=mix_chunk, qt=qt, sigOSQ=sigOSQ, O=O, OSQ=OSQ, SS=SS, ORT=ORT, u=u):
                                for dvc in range(ndv):
                                    sigSS = P.op("tensor", lambda e, dvc=dvc: e.matmul(psb[7][:], lhsT=ones[:], rhs=osq[:, dvc, :], start=(dvc == 0), stop=(dvc == ndv - 1)),
                                                 waits=([sigOSQ] + SS.pw()) if dvc == 0 else [], inc=SS if dvc == ndv - 1 else None)
                                OSQ.release(sigSS)
                                sigORT = P.op("scalar", lambda e: e.activation(out=ort[:], in_=psb[7][:], func=AF.Ln, scale=1.0 / nfeat, bias=EPS), waits=[sigSS] + ORT.pw(), inc=ORT)
                                SS.release(sigORT)
                                sigORR = P.op("scalar", lambda e: e.activation(out=ort[:], in_=ort[:], func=AF.Exp, scale=-0.5, bias=math.log(cm)), waits=[sigORT], inc=(acp, 1))
                                mi = c3["mo"] % 2
                                c3["mo"] += 1
                                MO = p3["mo"][mi]
                                for dvc in range(ndv):
                                    sigMO = P.op("vector", lambda e, dvc=dvc: e.scalar_tensor_tensor(out=mo[mi][:, dvc, :], in0=ob[:, dvc, :], scalar=ogs[l][:, mc + dvc:mc + dvc + 1], in1=ort[:], op0=ALU.mult, op1=ALU.mult),
                                                  waits=([sigORR] + MO.pw()) if dvc == 0 else [], inc=MO if dvc == ndv - 1 else None)
                                O.release(sigMO)
                                ORT.release(sigMO)
                                sigSt = P.op("sync", lambda e: e.dma_start(out=mixv[:, mc:mc + ndv, qt * 512:(qt + 1) * 512], in_=mo[mi][:, 0:ndv, :]), waits=[sigMO], inc=(MO.s, 16))
                                MO.release(sigSt)
                                p3stores.append(sigSt)
                                if qt == 7:
                                    for cc in range(mc, mc + ndv):
                                        vag_ = P.op("gpsimd", lambda e, cc=cc: e.collective_compute("AllGather", ALU.bypass, replica_groups=GRP4, ins=[mixo.ap()[cc * 128:(cc + 1) * 128, :]], outs=[mixall.ap()[cc * 512:(cc + 1) * 512, :]]),
                                                     waits=p3stores[-2:], inc=(s_ag, 1))
                                        mixag.append(vag_)
                                    issue_wag(l, 2)

                            tail.append(fin_tail)
                    mix_chunk += ndv
                    HD.release(sigPVlast)
                    nxtu = units[u + 1] if u + 1 < len(units) else None
                    if nxtu is not None and (nxtu[0] != "C" or nxtu[1] == 0):
                        p3["bias"].release(sigSTlast)
                        sigB = load_bias(u + 1)
                    elif nxtu is None:
                        p3["bias"].release(sigSTlast)
                for fn_ in tail:
                    fn_()
                tail = []
                vst = p3stores[-2:]
                P.op("sync", None, waits=vst)
                P.op("gpsimd", None, waits=[mixag[-1]])
                if debug and l == 0:
                    v = P.op("gpsimd", lambda e: e.dma_start(out=dbg["mixo"], in_=mixo.ap()), inc=(s_dbg, 16))
                    P.op("gpsimd", None, waits=[v])
                P.build()
            if stop("p3"):
                break

            with ExitStack() as ps:
                acta = ps.enter_context(nc.sbuf_tensor(f"p4a_L{l}", [128, 32, 1024], BF16))
                actb = ps.enter_context(nc.sbuf_tensor(f"p4b_L{l}", [128, 32, 1024], BF16))
                wr = [ps.enter_context(nc.sbuf_tensor(f"p4w{i}_L{l}", [128, 32, 128], BF16)) for i in range(4)]
                stage = [wr[2 + i][:].rearrange("p a b -> p (a b)").rearrange("p (q t) -> p q t", q=4) for i in range(2)]
                xin = [ps.enter_context(nc.sbuf_tensor(f"p4xin{i}_L{l}", [128, 1024], F32)) for i in range(3)]
                x1 = [ps.enter_context(nc.sbuf_tensor(f"p4x1{i}_L{l}", [128, 1024], F32)) for i in range(4)]
                xsq = [ps.enter_context(nc.sbuf_tensor(f"p4xsq{i}_L{l}", [128, 1024], BF16)) for i in range(2)]
                rt2 = ps.enter_context(nc.sbuf_tensor(f"p4rt2_L{l}", [128, 1024], F32))
                sgt = [ps.enter_context(nc.sbuf_tensor(f"p5sg{i}_L{l}", [128, 1024], F32)) for i in range(2)]
                if l == 0:
                    P.p4 = dict(stg=[Slot(P, f"p4stg{i}") for i in range(2)], a=Slot(P, "p4a", own=False),
                                w=[Slot(P, f"p4w{i}") for i in range(4)], xin=[Slot(P, f"p4xin{i}") for i in range(3)],
                                pm=[Slot(P, f"p4pm{i}") for i in range(2)], x1=[Slot(P, f"p4x1{i}", store=True) for i in range(4)],
                                xsq=[Slot(P, f"p4xsq{i}") for i in range(2)], stat=Slot(P, "p4stat", own=False), rt=Slot(P, "p4rt"),
                                b=Slot(P, "p4b", own=False), pg=Slot(P, "p5pg"), pu=Slot(P, "p5pu"), sg=[Slot(P, f"p5sg{i}") for i in range(2)],
                                act=Slot(P, "p5act", own=False), cnt=dict(w=0, xin=0, pm=0, x1=0, xsq=0, stg=0, sg=0))
                p4 = P.p4
                c4 = p4["cnt"]
                xres_st = st["xres_st"]
                wouta = W["wout_a"].ap().rearrange("(m p) c -> m p c", p=128)
                wga = W["wg_a"].ap().rearrange("(f p) c -> f p c", p=128)
                wua = W["wu_a"].ap().rearrange("(f p) c -> f p c", p=128)
                wda = W["wd_a"].ap().rearrange("(g m p) c -> g m p c", g=4, p=128)
                issue_wag(l, 1000)
                mixallv = mixall.ap().rearrange("(kc p) (q t) -> p kc q t", p=128, q=4)
                A, Bs, STAT, ACT_ = p4["a"], p4["b"], p4["stat"], p4["act"]

                sigA = None
                for kc in range(32):
                    si = c4["stg"] % 2
                    c4["stg"] += 1
                    SG = p4["stg"][si]
                    sigSG = P.op("sync", lambda e, si=si, kc=kc: e.dma_start(out=stage[si], in_=mixallv[:, kc, :, :]), waits=SG.pw(), inc=SG)
                    v = P.op("vector", lambda e, si=si, kc=kc: e.tensor_scalar(out=acta[:, kc, :], in0=stage[si][:, 0, :], scalar1=ohs[:, 0:1], scalar2=0.0, op0=ALU.mult, op1=ALU.add),
                             waits=[sigSG] + (A.pw() if kc == 0 else []), inc=(dvp, 1))
                    for q in range(1, 4):
                        v = P.op("vector", lambda e, si=si, kc=kc, q=q: e.scalar_tensor_tensor(out=acta[:, kc, :], in0=stage[si][:, q, :], scalar=ohs[:, q:q + 1], in1=acta[:, kc, :], op0=ALU.mult, op1=ALU.add),
                                 waits=[v], inc=(dvp, 1))
                    SG.release(v)
                    sigA = v
                p4["w"][2].release(sigA)
                p4["w"][3].release(sigA)

                def load_w(src_ap, call, ncols=4096):
                    i = c4["w"] % 4
                    c4["w"] += 1
                    WSl = p4["w"][i]
                    sig = P.op("sync", lambda e, i=i, src_ap=src_ap, ncols=ncols: e.dma_start(out=wr[i][:].rearrange("p a b -> p (a b)")[:, 0:ncols], in_=src_ap), waits=WSl.pw() + [wag_sig(l, call)], inc=WSl)
                    return i, sig

                def load_xin(m, src, wait_store):
                    i = c4["xin"] % 3
                    c4["xin"] += 1
                    XI = p4["xin"][i]
                    sig = P.op("sync", lambda e, i=i, m=m, src=src: e.dma_start(out=xin[i][:], in_=src[m]), waits=XI.pw() + [wait_store], inc=XI)
                    return i, sig

                wq = [load_w(wouta[0], ("wout", 0)), load_w(wouta[1], ("wout", 0))]
                xq = [load_xin(0, xsrc, xres_st[0] if l > 0 else None), load_xin(1, xsrc, xres_st[1] if l > 0 else None)]
                deferred = None
                sigSTAT = None
                for m in range(32):
                    wi, sigW = wq.pop(0)
                    if m + 2 < 32:
                        wq.append(load_w(wouta[m + 2], ("wout", (m + 2) // 4)))
                    xi, sigXI = xq.pop(0)
                    if m + 2 < 32:
                        xq.append(load_xin(m + 2, xsrc, xres_st[m + 2] if l > 0 else None))
                    WSl, XI = p4["w"][wi], p4["xin"][xi]
                    pi_ = c4["pm"] % 2
                    c4["pm"] += 1
                    PM = p4["pm"][pi_]
                    for kc in range(32):
                        for th in range(2):
                            w_ = []
                            if kc == 0 and th == 0:
                                w_ = PM.pw() + [sigW] + ([sigA] if m == 0 else [])
                            sigPM = P.op("tensor", lambda e, wi=wi, kc=kc, th=th, pi_=pi_: e.matmul(psb[2 * pi_ + th][:], lhsT=wr[wi][:, kc, :], rhs=acta[:, kc, th * 512:(th + 1) * 512], start=(kc == 0), stop=(kc == 31)),
                                          waits=w_, inc=PM if (kc == 31 and th == 1) else None)
                    WSl.release(sigPM)
                    if deferred is not None:
                        deferred()
                        deferred = None
                    x1i = c4["x1"] % 4
                    c4["x1"] += 1
                    X1 = p4["x1"][x1i]
                    for th in range(2):
                        sigX1 = P.op("vector", lambda e, th=th, pi_=pi_, xi=xi, x1i=x1i: e.tensor_tensor(out=x1[x1i][:, th * 512:(th + 1) * 512], in0=psb[2 * pi_ + th][:], in1=xin[xi][:, th * 512:(th + 1) * 512], op=ALU.add),
                                      waits=([sigPM, sigXI] + X1.pw()) if th == 0 else [], inc=X1 if th == 1 else None)
                    PM.release(sigX1)
                    XI.release(sigX1)
                    sigSt = P.op("sync", lambda e, x1i=x1i, m=m: e.dma_start(out=xres.ap()[m], in_=x1[x1i][:]), waits=[sigX1], inc=(X1.s, 16))
                    xres_st[m] = sigSt
                    X1.release(sigSt)
                    qi = c4["xsq"] % 2
                    c4["xsq"] += 1
                    XS = p4["xsq"][qi]
                    sigXS = P.op("scalar", lambda e, x1i=x1i, qi=qi: e.activation(out=xsq[qi][:], in_=x1[x1i][:], func=AF.Square), waits=[sigX1] + XS.pw(), inc=XS)
                    X1.release(sigXS)

                    def stat_mm(m=m, qi=qi, XS=XS, sigXS=sigXS):
                        for th in range(2):
                            sig = P.op("tensor", lambda e, th=th: e.matmul(psb[6 + th][:], lhsT=ones[:], rhs=xsq[qi][:, th * 512:(th + 1) * 512], start=(m == 0), stop=(m == 31)),
                                       waits=([sigXS] + (STAT.pw() if m == 0 else [])) if th == 0 else [], inc=(pep, 1) if th == 1 else None)
                        XS.release(sig)
                        return sig
                    deferred = stat_mm
                sigSTAT = deferred()
                RT = p4["rt"]
                for th in range(2):
                    sigRT = P.op("scalar", lambda e, th=th: e.activation(out=rt2[:, th * 512:(th + 1) * 512], in_=psb[6 + th][:], func=AF.Ln, scale=1.0 / D, bias=EPS),
                                 waits=([sigSTAT] + RT.pw()) if th == 0 else [], inc=RT if th == 1 else None)
                STAT.release(sigRT)
                sigRR = P.op("scalar", lambda e: e.activation(out=rt2[:], in_=rt2[:], func=AF.Exp, scale=-0.5), waits=[sigRT], inc=(acp, 1))
                xq = [load_xin(0, xres.ap(), xres_st[0]), load_xin(1, xres.ap(), xres_st[1])]
                sigB2 = None
                for m in range(32):
                    xi, sigXI = xq.pop(0)
                    if m + 2 < 32:
                        xq.append(load_xin(m + 2, xres.ap(), xres_st[m + 2]))
                    XI = p4["xin"][xi]
                    sigB2 = P.op("vector", lambda e, m=m, xi=xi, l=l: e.scalar_tensor_tensor(out=actb[:, m, :], in0=xin[xi][:], scalar=g2s[l][:, m:m + 1], in1=rt2[:], op0=ALU.mult, op1=ALU.mult),
                                 waits=[sigXI] + (([sigRR] + Bs.pw()) if m == 0 else []), inc=(dvp, 1))
                    XI.release(sigB2)
                RT.release(sigB2)
                if l + 1 < L:
                    P.op("gpsimd", None, waits=[sigB2])
                    for piece in range(3):
                        cast_win(l + 1, piece)
                    issue_wag(l + 1, 50)
                if debug and l == 0:
                    v = P.op("gpsimd", lambda e: e.dma_start(out=dbg["x1"], in_=xres.ap()), waits=xres_st[28:32], inc=(s_dbg, 16))
                    P.op("gpsimd", None, waits=[v])
                if stop("p4"):
                    P.op("sync", None, waits=xres_st[28:32])
                    P.op("vector", None, waits=[sigB2])
                    P.build()
                    break

                PG, PU = p4["pg"], p4["pu"]
                sigPU = None
                sigPM = None
                for gi, (f0, nf) in enumerate(FGROUPS):
                    lastg = (gi == len(FGROUPS) - 1)
                    wq = [(load_w(wga[f0], ("wg", f0 // 4)), load_w(wua[f0], ("wu", f0 // 4)))]
                    sigACT = None
                    for fi in range(nf):
                        f = f0 + fi
                        (wgi, sigWg), (wui, sigWu) = wq.pop(0)
                        if fi + 1 < nf:
                            wq.append((load_w(wga[f + 1], ("wg", (f + 1) // 4)), load_w(wua[f + 1], ("wu", (f + 1) // 4))))
                        sigs = {}
                        for (wi, sigW_, PSL, base) in ((wgi, sigWg, PG, 0), (wui, sigWu, PU, 2)):
                            for kc in range(32):
                                for th in range(2):
                                    w_ = []
                                    if kc == 0 and th == 0:
                                        w_ = PSL.pw() + [sigW_] + ([sigB2] if (gi == 0 and fi == 0 and base == 0) else [])
                                    sig = P.op("tensor", lambda e, wi=wi, kc=kc, th=th, base=base: e.matmul(psb[base + th][:], lhsT=wr[wi][:, kc, :], rhs=actb[:, kc, th * 512:(th + 1) * 512], start=(kc == 0), stop=(kc == 31)),
                                               waits=w_, inc=PSL if (kc == 31 and th == 1) else None)
                            p4["w"][wi].release(sig)
                            sigs[base] = sig
                        sigPG, sigPU = sigs[0], sigs[2]
                        si = c4["sg"] % 2
                        c4["sg"] += 1
                        SGS = p4["sg"][si]
                        for th in range(2):
                            sigSG = P.op("scalar", lambda e, th=th, si=si: e.activation(out=sgt[si][:, th * 512:(th + 1) * 512], in_=psb[th][:], func=AF.Silu),
                                         waits=([sigPG] + SGS.pw()) if th == 0 else [], inc=SGS if th == 1 else None)
                        PG.release(sigSG)
                        for th in range(2):
                            sigACT = P.op("vector", lambda e, th=th, si=si, fi=fi: e.tensor_tensor(out=acta[:, fi, th * 512:(th + 1) * 512], in0=psb[2 + th][:], in1=sgt[si][:, th * 512:(th + 1) * 512], op=ALU.mult),
                                          waits=([sigPU, sigSG] + (ACT_.pw() + A.pw() if fi == 0 else [])) if th == 0 else [], inc=(dvp, 1) if th == 1 else None)
                        PU.release(sigACT)
                        SGS.release(sigACT)

                    def load_wd(m, gi=gi, nf=nf):
                        return load_w(wda[gi][m][:, 0:nf * 128], ("wd", gi * 8 + m // 4), ncols=nf * 128)
                    wq = [load_wd(0), load_wd(1)]
                    xq = [load_xin(0, xres.ap(), xres_st[0]), load_xin(1, xres.ap(), xres_st[1])]
                    for m in range(32):
                        wi, sigW = wq.pop(0)
                        if m + 2 < 32:
                            wq.append(load_wd(m + 2))
                        xi, sigXI = xq.pop(0)
                        if m + 2 < 32:
                            xq.append(load_xin(m + 2, xres.ap(), xres_st[m + 2]))
                        WSl, XI = p4["w"][wi], p4["xin"][xi]
                        pi_ = c4["pm"] % 2
                        c4["pm"] += 1
                        PM = p4["pm"][pi_]
                        for fi in range(nf):
                            for th in range(2):
                                w_ = []
                                if fi == 0 and th == 0:
                                    w_ = PM.pw() + [sigW] + ([sigACT] if m == 0 else [])
                                sigPM = P.op("tensor", lambda e, wi=wi, fi=fi, th=th, pi_=pi_, nf=nf: e.matmul(psb[4 + 2 * pi_ + th][:], lhsT=wr[wi][:, fi, :], rhs=acta[:, fi, th * 512:(th + 1) * 512], start=(fi == 0), stop=(fi == nf - 1)),
                                              waits=w_, inc=PM if (fi == nf - 1 and th == 1) else None)
                        WSl.release(sigPM)
                        x1i = c4["x1"] % 4
                        c4["x1"] += 1
                        X1 = p4["x1"][x1i]
                        for th in range(2):
                            sigX1 = P.op("vector", lambda e, th=th, pi_=pi_, xi=xi, x1i=x1i: e.tensor_tensor(out=x1[x1i][:, th * 512:(th + 1) * 512], in0=psb[4 + 2 * pi_ + th][:], in1=xin[xi][:, th * 512:(th + 1) * 512], op=ALU.add),
                                          waits=([sigPM, sigXI] + X1.pw()) if th == 0 else [], inc=X1 if th == 1 else None)
                        PM.release(sigX1)
                        XI.release(sigX1)
                        dst = xdst if lastg else xres.ap()
                        sigSt = P.op("sync", lambda e, x1i=x1i, m=m, dst=dst: e.dma_start(out=dst[m], in_=x1[x1i][:]), waits=[sigX1], inc=(X1.s, 16))
                        xres_st[m] = sigSt
                        X1.release(sigSt)
                    ACT_.release(sigPM)
                A.release(sigPM)
                Bs.release(sigPU)
                P.op("sync", None, waits=xres_st[28:32])
                P.build()
    return nc


def _alibi(n):
    return np.exp2(-8.0 * np.arange(1, n + 1, dtype=np.float64) / n)


def _bias_A(g):
    slope = _alibi(4)[g]
    p = np.arange(128)[:, None]
    c = np.arange(8064)[None, :]
    return (-slope * np.abs(c - 3968 - p)).astype(np.float32)


def _bias_C(g):
    out = np.empty((128, 3, 2944), np.float32)
    p = np.arange(128)[:, None]
    c = np.arange(2944)[None, :]
    d = c - 1408 - p
    ad = np.abs(d)
    cnt = (ad <= 64).astype(np.int64) + ((d % 4 == 0) & (ad <= 256)) + ((d % 16 == 0) & (ad <= 1024))
    with np.errstate(divide="ignore"):
        lc = np.log(cnt.astype(np.float64))
    for i in range(3):
        slope = _alibi(12)[3 * g + i]
        v = -slope * ad + lc
        out[:, i, :] = np.where(cnt > 0, v, NEG).astype(np.float32)
    return out.reshape(128, 3 * 2944)


def _bias_B(rpb_l, g):
    out = np.full((3, 128, 20, 512), NEG, np.float32)
    pidx = np.arange(128)
    krl, kc = pidx // 64, pidx % 64
    fidx = np.arange(512)
    qrl, qc = fidx // 64, fidx % 64
    cstart = np.clip(qc - 8, 0, 48)
    colok = (kc[:, None] >= cstart[None, :]) & (kc[:, None] < cstart[None, :] + 16)
    dc = np.clip(kc[:, None] - qc[None, :] + 15, 0, 30)
    cases = [(1, 4 * 1 - 2 + r, r) for r in range(8)] + [(0, kb, 8 + kb) for kb in range(6)] + [(7, kb, 14 + kb - 26) for kb in range(26, 32)]
    for (qt, kb, ti) in cases:
        kr = 2 * kb + krl
        qr = 8 * qt + qrl
        rstart = np.clip(qr - 4, 0, 56)
        rowok = (kr[:, None] >= rstart[None, :]) & (kr[:, None] < rstart[None, :] + 8)
        dr = np.clip(kr[:, None] - qr[None, :] + 7, 0, 14)
        ok = rowok & colok
        for i in range(3):
            vals = rpb_l[3 * g + i][dr, dc]
            out[i, :, ti, :] = np.where(ok, vals, NEG)
    return out.reshape(3, 128, 20 * 512)


def _block_w(Wm, kcn, nb):
    return np.ascontiguousarray(Wm.reshape(kcn, 128, nb, 128).transpose(2, 1, 0, 3))


def _qk_cols(g):
    cols = []
    for m in range(2):
        cols.append(np.arange(g * 256 + m * 128, g * 256 + (m + 1) * 128))
    for m in range(2):
        cols.append(1024 + np.arange(g * 256 + m * 128, g * 256 + (m + 1) * 128))
    for i in range(3):
        cols.append(3072 + (3 * g + i) * 128 + np.arange(128))
    for i in range(3):
        cols.append(4608 + (3 * g + i) * 128 + np.arange(128))
    for i in range(3):
        cols.append(7680 + (3 * g + i) * 128 + np.arange(128))
    for i in range(3):
        cols.append(9216 + (3 * g + i) * 128 + np.arange(128))
    return cols


def _v_cols(g):
    c = [2048 + g * 256 + np.arange(256)]
    for i in range(3):
        c.append(6144 + (3 * g + i) * 128 + np.arange(128))
    for i in range(3):
        c.append(10752 + (3 * g + i) * 128 + np.arange(128))
    return np.concatenate(c)


def _wout_perm():
    rows = []
    for c in range(8):
        for r in range(4):
            if c < 2:
                rows.append(r * 256 + c * 128 + np.arange(128))
            elif c < 5:
                rows.append(1024 + (3 * r + (c - 2)) * 128 + np.arange(128))
            else:
                rows.append(2560 + (3 * r + (c - 5)) * 128 + np.arange(128))
    return np.concatenate(rows)


def _col128(v):
    return np.ascontiguousarray(v.reshape(-1, 128).T.astype(np.float32))


def prepare_inputs(inp, layers):
    f32 = np.float32
    maps = [dict() for _ in range(NCORES)]
    x = np.asarray(inp["x"], f32)
    for c in range(NCORES):
        b, g = c // 4, c % 4
        maps[c]["xT"] = np.ascontiguousarray(x[b, g * 1024:(g + 1) * 1024, :].T).reshape(32, 128, 1024)
        maps[c]["biasA"] = _bias_A(g)
        maps[c]["biasC"] = _bias_C(g)
        oh = np.zeros((128, 4), f32)
        oh[:, g] = 1.0
        maps[c]["onehot"] = oh
    perm = _wout_perm()
    for li, l in enumerate(layers):
        w_in = np.asarray(inp["w_in"][l], f32)
        def shard_blocks(blk, npad):
            nb, _, X = blk.shape
            out = np.zeros((4, npad // 4, 128, X), f32)
            for r in range(4):
                sel = blk[r::4]
                out[r, :sel.shape[0]] = sel
            return out.reshape(4, npad // 4 * 128, X)
        wout_blk = shard_blocks(_block_w(np.asarray(inp["w_out"][l], f32)[perm, :], 32, 32).reshape(32, 128, 4096), 32)
        wg_blk = shard_blocks(_block_w(np.asarray(inp["w_gate"][l], f32), 32, NF).reshape(NF, 128, 4096), 88)
        wu_blk = shard_blocks(_block_w(np.asarray(inp["w_up"][l], f32), 32, NF).reshape(NF, 128, 4096), 88)
        wd_full = _block_w(np.asarray(inp["w_down"][l], f32), NF, 32).reshape(32, 128, NF, 128)
        wd_units = np.zeros((4, 32, 128, 2816), f32)
        for gi, (f0, nf) in enumerate(FGROUPS):
            wd_units[gi, :, :, :nf * 128] = wd_full[:, :, f0:f0 + nf, :].reshape(32, 128, nf * 128)
        wd_blk = np.stack([wd_units[:, r::4].reshape(32, 128, 2816) for r in range(4)]).reshape(4, 32 * 128, 2816)
        lamv = np.stack([inp["lambda_q1"][l], inp["lambda_k1"][l], inp["lambda_q2"][l], inp["lambda_k2"][l]]).astype(f32)
        lamv = np.ascontiguousarray(np.broadcast_to(lamv[None], (128, 4, 128)))
        g1 = _col128(np.asarray(inp["norm1_g"][l]))
        g2 = _col128(np.asarray(inp["norm2_g"][l]))
        qkg = np.stack([inp["a_q_g"][l]] * 2 + [inp["a_k_g"][l]] * 2 + [inp["b_q_g"][l]] * 3 + [inp["b_k_g"][l]] * 3
                       + [inp["c_q_g"][l]] * 3 + [inp["c_k_g"][l]] * 3, axis=1).astype(f32)
        og = np.stack([inp["a_out_g"][l][0:128], inp["a_out_g"][l][128:256]] + [inp["b_out_g"][l]] * 3 + [inp["c_out_g"][l]] * 3, axis=1).astype(f32)
        for g in range(4):
            cols = _qk_cols(g)
            wqk = np.stack([w_in[:, cc].reshape(32, 128, 128).transpose(1, 0, 2) for cc in cols]).reshape(16 * 128, 4096)
            wv = np.ascontiguousarray(w_in[:, _v_cols(g)].reshape(32, 128, 1024).transpose(1, 0, 2)).reshape(128, 32768)
            bB = _bias_B(np.asarray(inp["b_rpb"][l], f32), g)
            for b in range(2):
                m = maps[b * 4 + g]
                m[f"wqk_{li}"] = wqk
                m[f"wv_{li}"] = wv
                m[f"biasB_{li}"] = bB
        for c in range(NCORES):
            m = maps[c]
            m[f"g1_{li}"] = g1
            m[f"g2_{li}"] = g2
            m[f"qkg_{li}"] = np.ascontiguousarray(qkg)
            m[f"og_{li}"] = np.ascontiguousarray(og)
            m[f"lamv_{li}"] = lamv
            m[f"wout_{li}"] = wout_blk[c % 4]
            m[f"wg_{li}"] = wg_blk[c % 4]
            m[f"wu_{li}"] = wu_blk[c % 4]
            m[f"wd_{li}"] = wd_blk[c % 4]
    return maps


_NC_CACHE = {}


def _get_nc(n_layers, first, debug=False):
    key = (n_layers, first, debug)
    if key not in _NC_CACHE:
        _NC_CACHE[key] = build_program(n_layers, first, debug)
    return _NC_CACHE[key]


def assemble_output(res):
    out = np.empty((NB, S, D), np.float32)
    for c in range(NCORES):
        b, g = c // 4, c % 4
        out[b, g * 1024:(g + 1) * 1024, :] = res[c]["yT"].reshape(D, 1024).T
    return out


def kernel(**inputs):
    nc = _get_nc(DEPTH, 0)
    maps = prepare_inputs(inputs, list(range(DEPTH)))
    res = run_bass_kernel_spmd(nc, maps, core_ids=list(range(NCORES)))
    return assemble_output(res.results)
```

```python
import math
from contextlib import ExitStack

import numpy as np
import concourse.bass as bass
import concourse.mybir as mybir
from concourse.bass_utils import run_bass_kernel_spmd

F32 = mybir.dt.float32
BF16 = mybir.dt.bfloat16
AF = mybir.ActivationFunctionType
ALU = mybir.AluOpType

D = 4096
S = 4096
NB = 2
DEPTH = 2
DFF = 11008
NF = DFF // 128
EPS = 1e-6
NEG = -30000.0
SCALE = 128.0 ** -0.5
FGROUPS = [(0, 22), (22, 22), (44, 21), (65, 21)]
NCORES = 8
ENGS = ["sync", "scalar", "vector", "gpsimd", "tensor"]
GRP4 = [[0, 1, 2, 3], [4, 5, 6, 7]]
GRP8 = [list(range(8))]


def lam_init(l):
    return 0.8 - 0.6 * math.exp(-0.3 * l)


class Prog:
    def __init__(self, nc, es):
        self.nc = nc
        self.es = es
        self.semcnt = {}
        self.semh = {}
        self.ops = {e: [] for e in ENGS}

    def sem(self, name):
        assert name not in self.semcnt, name
        self.semcnt[name] = 0
        self.semh[name] = self.es.enter_context(self.nc.semaphore(name))
        return name

    def op(self, eng, fn, waits=(), inc=None):
        sig = None
        if inc is not None:
            if isinstance(inc, Slot):
                inc = (inc.f, 16 if eng in ("sync", "gpsimd_dma") else 1)
            s, a = inc
            self.semcnt[s] += a
            sig = (s, self.semcnt[s])
        if eng == "gpsimd_dma":
            eng = "gpsimd"
        wm = {}
        for w in waits:
            if w is None:
                continue
            s_, v_ = w
            if v_ > wm.get(s_, 0):
                wm[s_] = v_
        self.ops[eng].append((tuple(wm.items()), fn, inc))
        return sig

    def build(self):
        h = self.semh
        ops = self.ops
        with self.nc.Block() as block:
            def mk(engname):
                def body(eng):
                    for waits, fn, inc in ops[engname]:
                        for (s, v) in waits:
                            eng.wait_ge(h[s], v)
                        if fn is None:
                            continue
                        ins = fn(eng)
                        if inc is not None:
                            ins.then_inc(h[inc[0]], inc[1])
                return body
            for e in ENGS:
                if ops[e]:
                    getattr(block, e)(mk(e))
        self.ops = {e: [] for e in ENGS}
        self.nc.all_engine_barrier()


class Slot:
    def __init__(self, P, name, own=True, store=False):
        self.P = P
        self.f = P.sem(name) if own else None
        self.s = P.sem(name + "S") if store else None
        self.rel = []

    def pw(self):
        w = self.rel
        self.rel = []
        return list(w)

    def release(self, sig):
        assert sig is not None
        self.rel.append(sig)


def build_program(n_layers, first_layer_index=0, debug=False, stop_after=None):
    nc = bass.Bass("TRN2", target_bir_lowering=False)
    L = n_layers

    def din(name, shape, dt=F32):
        return nc.dram_tensor(name, list(shape), dt, kind="ExternalInput").ap()

    def dscr(name, shape, dt):
        return nc.dram_tensor(name, list(shape), dt)

    xT = din("xT", [32, 128, 1024])
    yT = nc.dram_tensor("yT", [32, 128, 1024], F32, kind="ExternalOutput").ap()
    biasA = din("biasA", [128, 8064])
    biasC = din("biasC", [128, 3 * 2944])
    onehot = din("onehot", [128, 4])
    lw = []
    for l in range(L):
        lw.append(dict(
            g1=din(f"g1_{l}", [128, 32]), g2=din(f"g2_{l}", [128, 32]),
            wqk=din(f"wqk_{l}", [16 * 128, 4096]), wv=din(f"wv_{l}", [128, 32768]),
            wout=din(f"wout_{l}", [8 * 128, 4096]), wg=din(f"wg_{l}", [22 * 128, 4096]),
            wu=din(f"wu_{l}", [22 * 128, 4096]), wd=din(f"wd_{l}", [32 * 128, 2816]),
            wqk_b=dscr(f"wqkb_{l}", [16 * 128, 4096], BF16), wv_b=dscr(f"wvb_{l}", [128, 32768], BF16),
            qkg=din(f"qkg_{l}", [128, 16]), og=din(f"og_{l}", [128, 8]),
            lamv=din(f"lamv_{l}", [128, 4, 128]), biasB=din(f"biasB_{l}", [3, 128, 20 * 512]),
            wout_b=dscr(f"woutb_{l}", [8 * 128, 4096], BF16), wout_a=dscr(f"wouta_{l}", [32 * 128, 4096], BF16),
            wg_b=dscr(f"wgb_{l}", [22 * 128, 4096], BF16), wg_a=dscr(f"wga_{l}", [88 * 128, 4096], BF16),
            wu_b=dscr(f"wub_{l}", [22 * 128, 4096], BF16), wu_a=dscr(f"wua_{l}", [88 * 128, 4096], BF16),
            wd_b=dscr(f"wdb_{l}", [32 * 128, 2816], BF16), wd_a=dscr(f"wda_{l}", [128 * 128, 2816], BF16),
        ))
    hb = dscr("hb", [8 * 128 * 2 * 4, 512], BF16)
    hall = dscr("hall", [8 * 4 * 128 * 2 * 4, 512], BF16)
    qk = dscr("qk", [16, 128, 4096], BF16)
    vv = dscr("vv", [4096, 1024], BF16)
    mixo = dscr("mixo", [1024, 4096], BF16)
    mixall = dscr("mixall", [8 * 4 * 128, 4096], BF16)
    xres = dscr("xres", [32, 128, 1024], F32)
    dbg = {}
    if debug:
        dbg["hb"] = nc.dram_tensor("d_hb", [8192, 512], BF16, kind="ExternalOutput").ap()
        dbg["qk"] = nc.dram_tensor("d_qk", [16, 128, 4096], BF16, kind="ExternalOutput").ap()
        dbg["vv"] = nc.dram_tensor("d_vv", [4096, 1024], BF16, kind="ExternalOutput").ap()
        dbg["mixo"] = nc.dram_tensor("d_mixo", [1024, 4096], BF16, kind="ExternalOutput").ap()
        dbg["x1"] = nc.dram_tensor("d_x1", [32, 128, 1024], F32, kind="ExternalOutput").ap()

    def stop(name):
        return stop_after is not None and stop_after == name

    with ExitStack() as es:
        P = Prog(nc, es)
        ones = es.enter_context(nc.sbuf_tensor("ones", [128, 128], BF16))
        g1s = [es.enter_context(nc.sbuf_tensor(f"g1s{l}", [128, 32], F32)) for l in range(L)]
        g2s = [es.enter_context(nc.sbuf_tensor(f"g2s{l}", [128, 32], F32)) for l in range(L)]
        qkgs = [es.enter_context(nc.sbuf_tensor(f"qkgs{l}", [128, 16], F32)) for l in range(L)]
        ogs = [es.enter_context(nc.sbuf_tensor(f"ogs{l}", [128, 8], F32)) for l in range(L)]
        nlam = [es.enter_context(nc.sbuf_tensor(f"nlam{l}", [128, 1], F32)) for l in range(L)]
        ohs = es.enter_context(nc.sbuf_tensor("ohs", [128, 4], F32))
        psb = [es.enter_context(nc.psum_tensor(f"psb{i}", [128, 512], F32)) for i in range(8)]

        s_c = P.sem("constld")
        s_cv = P.sem("constv")
        s_wc = [P.sem(f"wcast{l}") for l in range(L)]
        s_wag = [P.sem(f"wag{l}") for l in range(L)]
        s_wcc = P.sem("wcc")
        s_ag = P.sem("ag")
        s_dbg = P.sem("dbg") if debug else None
        dvp = P.sem("dvp")
        pep = P.sem("pep")
        acp = P.sem("acp")

        def cast_win(l, piece):
            if piece < 2:
                src = lw[l]["wqk"][piece * 1024:(piece + 1) * 1024, :]
                dst = lw[l]["wqk_b"].ap()[piece * 1024:(piece + 1) * 1024, :]
            else:
                src = lw[l]["wv"]
                dst = lw[l]["wv_b"].ap()
            v = P.op("gpsimd", lambda e, src=src, dst=dst: e.dma_start(out=dst, in_=src, max_dma_last_dim=8192), inc=(s_wc[l], 16))
            P.op("gpsimd", None, waits=[v])

        wag_calls = []
        wag_idx = []
        wag_issued = [0] * L
        for l in range(L):
            calls = [("wout", j) for j in range(8)]
            done = set()
            for gi, (f0, nf) in enumerate(FGROUPS):
                for j in range(f0 // 4, (f0 + nf - 1) // 4 + 1):
                    if j not in done:
                        done.add(j)
                        calls += [("wg", j), ("wu", j)]
                calls += [("wd", gi * 8 + j) for j in range(8)]
            wag_calls.append(calls)
            wag_idx.append({c: i for i, c in enumerate(calls)})

        def issue_wag(l, n):
            for _ in range(n):
                i = wag_issued[l]
                if i >= len(wag_calls[l]):
                    return
                k, j = wag_calls[l][i]
                wag_issued[l] += 1
                v = P.op("gpsimd", lambda e, l=l, k=k, j=j: e.dma_start(out=lw[l][k + "_b"].ap()[j * 128:(j + 1) * 128, :], in_=lw[l][k][j * 128:(j + 1) * 128, :], max_dma_last_dim=8192), inc=(s_wcc, 16))
                P.op("gpsimd", None, waits=[v])
                P.op("gpsimd", lambda e, l=l, k=k, j=j: e.collective_compute("AllGather", ALU.bypass, replica_groups=GRP4,
                     ins=[lw[l][k + "_b"].ap()[j * 128:(j + 1) * 128, :]], outs=[lw[l][k + "_a"].ap()[j * 512:(j + 1) * 512, :]]), inc=(s_wag[l], 1))

        def wag_sig(l, call):
            return (s_wag[l], wag_idx[l][call] + 1)

        with ExitStack() as ps:
            lamt = ps.enter_context(nc.sbuf_tensor("lamt", [128, 4, 128], F32))
            lamp = ps.enter_context(nc.sbuf_tensor("lamp", [128, 2, 128], F32))
            lams = ps.enter_context(nc.sbuf_tensor("lams", [128, 2], F32))
            lame = ps.enter_context(nc.sbuf_tensor("lame", [128, 2], F32))
            P.op("vector", lambda e: e.memset(ones[:], 1.0), inc=(s_cv, 1))
            for l in range(L):
                for (dst, src) in ((g1s[l], lw[l]["g1"]), (g2s[l], lw[l]["g2"]), (qkgs[l], lw[l]["qkg"]), (ogs[l], lw[l]["og"])):
                    P.op("sync", lambda e, dst=dst, src=src: e.dma_start(out=dst[:], in_=src), inc=(s_c, 16))
            P.op("sync", lambda e: e.dma_start(out=ohs[:], in_=onehot), inc=(s_c, 16))
            cast_win(0, 0)
            prev = None
            for l in range(L):
                v = P.op("sync", lambda e, l=l: e.dma_start(out=lamt[:], in_=lw[l]["lamv"]), waits=[prev], inc=(s_c, 16))
                li = lam_init(l + first_layer_index)
                v1 = P.op("vector", lambda e: e.tensor_tensor(out=lamp[:, 0, :], in0=lamt[:, 0, :], in1=lamt[:, 1, :], op=ALU.mult), waits=[v], inc=(s_cv, 1))
                v2 = P.op("vector", lambda e: e.tensor_tensor(out=lamp[:, 1, :], in0=lamt[:, 2, :], in1=lamt[:, 3, :], op=ALU.mult), inc=(s_cv, 1))
                v3 = P.op("vector", lambda e: e.tensor_reduce(out=lams[:], in_=lamp[:], axis=mybir.AxisListType.X, op=ALU.add), waits=[v2], inc=(s_cv, 1))
                v4 = P.op("scalar", lambda e: e.activation(out=lame[:], in_=lams[:], func=AF.Exp), waits=[v3], inc=(s_cv, 1))
                prev = P.op("vector", lambda e, l=l, li=li: e.scalar_tensor_tensor(out=nlam[l][:], in0=lame[:, 1:2], scalar=-li, in1=lame[:, 0:1], op0=ALU.add, op1=ALU.subtract), waits=[v4], inc=(s_cv, 1))
            P.op("vector", None, waits=[prev])
            P.op("sync", None, waits=[(s_c, P.semcnt[s_c]), prev])
            P.build()

        st = dict(xres_st=[None] * 32)

        for l in range(L):
            W = lw[l]
            labs = l + first_layer_index
            last = (l == L - 1)
            xsrc = xT if l == 0 else xres.ap()
            xdst = yT if last else xres.ap()

            with ExitStack() as ps:
                xs = [ps.enter_context(nc.sbuf_tensor(f"p1x{i}_L{l}", [128, 32, 512], F32)) for i in range(2)]
                hs = [ps.enter_context(nc.sbuf_tensor(f"p1h{i}_L{l}", [128, 32, 512], BF16)) for i in range(2)]
                sq = [ps.enter_context(nc.sbuf_tensor(f"p1sq{i}_L{l}", [128, 4, 512], BF16)) for i in range(2)]
                rt = [ps.enter_context(nc.sbuf_tensor(f"p1rt{i}_L{l}", [128, 512], F32)) for i in range(2)]
                if l == 0:
                    P.p1 = dict(x=[Slot(P, f"p1x{i}") for i in range(2)], sq=[Slot(P, f"p1sq{i}") for i in range(2)],
                                ps=[Slot(P, f"p1ps{i}") for i in range(2)], rt=[Slot(P, f"p1rt{i}") for i in range(2)],
                                h=[Slot(P, f"p1h{i}", store=True) for i in range(2)])
                p1 = P.p1
                hbv = hb.ap().rearrange("(j p h a) t -> j p h a t", j=8, p=128, h=2, a=4)
                stores = []
                for th in range(2):
                    b = th
                    X, SQ, PS_, RT, H = p1["x"][b], p1["sq"], p1["ps"][b], p1["rt"][b], p1["h"][b]
                    sigX = P.op("sync", lambda e, b=b, th=th: e.dma_start(out=xs[b][:], in_=xsrc[:, :, th * 512:(th + 1) * 512].rearrange("fc p t -> p fc t")), waits=X.pw(), inc=X)
                    sigPS = None
                    for j in range(8):
                        sb = j % 2
                        sigSQ = P.op("scalar", lambda e, b=b, j=j, sb=sb: e.activation(out=sq[sb][:], in_=xs[b][:, 4 * j:4 * j + 4, :], func=AF.Square), waits=[sigX] + SQ[sb].pw(), inc=SQ[sb])
                        for i in range(4):
                            fc = 4 * j + i
                            sig = P.op("tensor", lambda e, b=b, sb=sb, i=i, fc=fc: e.matmul(psb[b][:], lhsT=ones[:], rhs=sq[sb][:, i, :], start=(fc == 0), stop=(fc == 31)),
                                       waits=([sigSQ] if i == 0 else []) + (PS_.pw() if fc == 0 else []), inc=PS_ if i == 3 else None)
                        SQ[sb].release(sig)
                        sigPS = sig
                    sigRT = P.op("scalar", lambda e, b=b: e.activation(out=rt[b][:], in_=psb[b][:], func=AF.Ln, scale=1.0 / D, bias=EPS), waits=[sigPS] + RT.pw(), inc=RT)
                    PS_.release(sigRT)
                    sigRR = P.op("scalar", lambda e, b=b: e.activation(out=rt[b][:], in_=rt[b][:], func=AF.Exp, scale=-0.5), waits=[sigRT], inc=(acp, 1))
                    for fc in range(32):
                        sigH = P.op("vector", lambda e, b=b, fc=fc, l=l: e.scalar_tensor_tensor(out=hs[b][:, fc, :], in0=xs[b][:, fc, :], scalar=g1s[l][:, fc:fc + 1], in1=rt[b][:], op0=ALU.mult, op1=ALU.mult),
                                    waits=([sigRR] + H.pw()) if fc == 0 else [], inc=H if fc == 31 else None)
                    RT.release(sigH)
                    X.release(sigH)
                    for j in range(8):
                        sigSt = P.op("sync", lambda e, b=b, th=th, j=j: e.dma_start(out=hbv[j][:, th, :, :], in_=hs[b][:, 4 * j:4 * j + 4, :]), waits=[sigH] if j == 0 else [], inc=(H.s, 16))
                    H.release(sigSt)
                    stores.append(sigSt)
                P.op("sync", None, waits=stores)
                for j in range(8):
                    vag = P.op("gpsimd", lambda e, j=j: e.collective_compute("AllGather", ALU.bypass, replica_groups=GRP4, ins=[hb.ap()[j * 1024:(j + 1) * 1024, :]], outs=[hall.ap()[j * 4096:(j + 1) * 4096, :]]), waits=stores if j == 0 else [], inc=(s_ag, 1))
                P.op("gpsimd", None, waits=[vag])
                if debug and l == 0:
                    v = P.op("gpsimd", lambda e: e.dma_start(out=dbg["hb"], in_=hb.ap()), inc=(s_dbg, 16))
                    P.op("gpsimd", None, waits=[v])
                P.build()
            if stop("p1"):
                break

            with ExitStack() as ps:
                wres = ps.enter_context(nc.sbuf_tensor(f"p2w_L{l}", [128, 32768], BF16))
                ht = [ps.enter_context(nc.sbuf_tensor(f"p2h{i}_L{l}", [128, 32, 512], BF16)) for i in range(3)]
                sqb = [ps.enter_context(nc.sbuf_tensor(f"p2sq{i}_L{l}", [128, 512], BF16)) for i in range(2)]
                rtb = [ps.enter_context(nc.sbuf_tensor(f"p2rt{i}_L{l}", [128, 512], F32)) for i in range(2)]
                qo = [ps.enter_context(nc.sbuf_tensor(f"p2qo{i}_L{l}", [128, 512], BF16)) for i in range(3)]
                if l == 0:
                    P.p2 = dict(w=Slot(P, "p2w"), h=[Slot(P, f"p2h{i}") for i in range(3)],
                                pa=[Slot(P, f"p2pa{i}") for i in range(3)], sq=[Slot(P, f"p2sq{i}") for i in range(2)],
                                pb=[Slot(P, f"p2pb{i}") for i in range(2)], rt=[Slot(P, f"p2rt{i}") for i in range(2)],
                                qo=[Slot(P, f"p2qo{i}", store=True) for i in range(3)],
                                cnt=dict(h=0, pa=0, sq=0, qo=0))
                p2 = P.p2
                c2 = p2["cnt"]
                WS = p2["w"]
                hallv = hall.ap().rearrange("(j r p h a) t -> j r p h a t", j=8, r=4, p=128, h=2, a=4)
                if l == 0:
                    cast_win(0, 1)
                    cast_win(0, 2)
                wag_pace = [15, 15, 15] if l == 0 else [7, 7, 6]
                vvv = vv.ap()
                p2stores = []

                def load_h(t):
                    i = c2["h"] % 3
                    c2["h"] += 1
                    r, half = t // 2, t % 2
                    sig = None
                    for j in range(8):
                        sig = P.op("sync", lambda e, i=i, r=r, half=half, j=j: e.dma_start(out=ht[i][:, 4 * j:4 * j + 4, :], in_=hallv[j][r][:, half, :, :]), waits=p2["h"][i].pw() if j == 0 else [], inc=p2["h"][i])
                    return i, sig

                for pss in range(3):
                    sigW = None
                    for i in range(8):
                        if pss < 2:
                            src = W["wqk_b"].ap()[(pss * 8 + i) * 128:(pss * 8 + i + 1) * 128, :]
                        else:
                            src = W["wv_b"].ap()[:, i * 4096:(i + 1) * 4096]
                        sigW = P.op("sync", lambda e, i=i, src=src: e.dma_start(out=wres[:, i * 4096:(i + 1) * 4096], in_=src),
                                    waits=(WS.pw() + [(s_wc[l], 16 * (pss + 1))]) if i == 0 else [], inc=WS)
                    issue_wag(l, wag_pace[pss])
                    hq = [load_h(0), load_h(1)]
                    sigPA_last = None
                    for t in range(8):
                        hi, sigH = hq.pop(0)
                        H = p2["h"][hi]
                        if t + 2 < 8:
                            hq.append(load_h(t + 2))
                        if pss < 2:
                            deferred = None
                            for s in range(8):
                                sg_ = pss * 8 + s
                                ia = c2["pa"] % 3
                                c2["pa"] += 1
                                PA = p2["pa"][ia]
                                for kc in range(32):
                                    w_ = []
                                    if kc == 0:
                                        w_ = PA.pw() + ([sigH] if s == 0 else []) + ([sigW] if (s == 0 and t == 0) else [])
                                    sigPA = P.op("tensor", lambda e, ia=ia, s=s, kc=kc, hi=hi: e.matmul(psb[ia][:], lhsT=wres[:, (s * 32 + kc) * 128:(s * 32 + kc + 1) * 128], rhs=ht[hi][:, kc, :], start=(kc == 0), stop=(kc == 31)),
                                                  waits=w_, inc=PA if kc == 31 else None)
                                sigPA_last = sigPA
                                isq = c2["sq"] % 2
                                c2["sq"] += 1
                                SQ, PB, RT = p2["sq"][isq], p2["pb"][isq], p2["rt"][isq]
                                sigSQ = P.op("scalar", lambda e, ia=ia, isq=isq: e.activation(out=sqb[isq][:], in_=psb[ia][:], func=AF.Square), waits=[sigPA] + SQ.pw(), inc=SQ)

                                def post(ia=ia, isq=isq, sg_=sg_, t=t, SQ=SQ, PB=PB, RT=RT, PA=PA, sigSQ=sigSQ):
                                    sigPB = P.op("tensor", lambda e: e.matmul(psb[4 + isq][:], lhsT=ones[:], rhs=sqb[isq][:], start=True, stop=True), waits=[sigSQ] + PB.pw(), inc=PB)
                                    SQ.release(sigPB)
                                    sigRT = P.op("scalar", lambda e: e.activation(out=rtb[isq][:], in_=psb[4 + isq][:], func=AF.Ln, scale=1.0 / 128, bias=EPS), waits=[sigPB] + RT.pw(), inc=RT)
                                    PB.release(sigRT)
                                    sigRR = P.op("scalar", lambda e: e.activation(out=rtb[isq][:], in_=rtb[isq][:], func=AF.Exp, scale=-0.5), waits=[sigRT], inc=(acp, 1))
                                    iq = c2["qo"] % 3
                                    c2["qo"] += 1
                                    QO = p2["qo"][iq]
                                    sigQO = P.op("vector", lambda e: e.scalar_tensor_tensor(out=qo[iq][:], in0=psb[ia][:], scalar=qkgs[l][:, sg_:sg_ + 1], in1=rtb[isq][:], op0=ALU.mult, op1=ALU.mult),
                                                 waits=[sigRR] + QO.pw(), inc=QO)
                                    PA.release(sigQO)
                                    RT.release(sigQO)
                                    sigSt = P.op("sync", lambda e: e.dma_start(out=qk.ap()[sg_][:, t * 512:(t + 1) * 512], in_=qo[iq][:]), waits=[sigQO], inc=(QO.s, 16))
                                    QO.release(sigSt)
                                    p2stores.append(sigSt)

                                if deferred is not None:
                                    deferred()
                                deferred = post
                            deferred()
                            H.release(sigPA_last)
                        else:
                            for tb4 in range(4):
                                for ch in range(2):
                                    ia = c2["pa"] % 3
                                    c2["pa"] += 1
                                    PA = p2["pa"][ia]
                                    for kc in range(32):
                                        w_ = []
                                        if kc == 0:
                                            first = (tb4 == 0 and ch == 0)
                                            w_ = PA.pw() + ([sigH] if first else []) + ([sigW] if (first and t == 0) else [])
                                        sigPA = P.op("tensor", lambda e, ia=ia, kc=kc, hi=hi, tb4=tb4, ch=ch: e.matmul(psb[ia][:], lhsT=ht[hi][:, kc, tb4 * 128:(tb4 + 1) * 128], rhs=wres[:, kc * 1024 + ch * 512:kc * 1024 + (ch + 1) * 512], start=(kc == 0), stop=(kc == 31)),
                                                      waits=w_, inc=PA if kc == 31 else None)
                                    sigPA_last = sigPA
                                    iq = c2["qo"] % 3
                                    c2["qo"] += 1
                                    QO = p2["qo"][iq]
                                    sigQO = P.op("scalar", lambda e, ia=ia, iq=iq: e.activation(out=qo[iq][:], in_=psb[ia][:], func=AF.Copy), waits=[sigPA] + QO.pw(), inc=QO)
                                    PA.release(sigQO)
                                    sigSt = P.op("sync", lambda e, iq=iq, t=t, tb4=tb4, ch=ch: e.dma_start(out=vvv[t * 512 + tb4 * 128:t * 512 + (tb4 + 1) * 128, ch * 512:(ch + 1) * 512], in_=qo[iq][:]), waits=[sigQO], inc=(QO.s, 16))
                                    QO.release(sigSt)
                                    p2stores.append(sigSt)
                            H.release(sigPA_last)
                    WS.release(sigPA_last)
                    P.op("gpsimd", None, waits=[sigPA_last])
                P.op("sync", None, waits=p2stores[-3:])
                P.op("gpsimd", None, waits=p2stores[-3:])
                if debug and l == 0:
                    v = P.op("gpsimd", lambda e: e.dma_start(out=dbg["qk"], in_=qk.ap()), inc=(s_dbg, 16))
                    v = P.op("gpsimd", lambda e: e.dma_start(out=dbg["vv"], in_=vv.ap()), inc=(s_dbg, 16))
                    P.op("gpsimd", None, waits=[v])
                P.build()
            if stop("p2"):
                break

            with ExitStack() as ps:
                qT = [ps.enter_context(nc.sbuf_tensor(f"p3q{i}_L{l}", [128, 2, 4096], BF16)) for i in range(2)]
                kT = [ps.enter_context(nc.sbuf_tensor(f"p3k{i}_L{l}", [128, 2, 4096], BF16)) for i in range(2)]
                vt = [ps.enter_context(nc.sbuf_tensor(f"p3v{i}_L{l}", [128, 32, 256], BF16)) for i in range(2)]
                bias = ps.enter_context(nc.sbuf_tensor(f"p3bias_L{l}", [128, 10240], F32))
                stt = [ps.enter_context(nc.sbuf_tensor(f"p3st{i}_L{l}", [128, 512], F32)) for i in range(4)]
                et = [ps.enter_context(nc.sbuf_tensor(f"p3e{i}_L{l}", [128, 512], BF16)) for i in range(4)]
                rz = ps.enter_context(nc.sbuf_tensor(f"p3rz_L{l}", [128, 512], F32))
                t0 = ps.enter_context(nc.sbuf_tensor(f"p3t0_L{l}", [128, 2, 512], F32))
                ob = ps.enter_context(nc.sbuf_tensor(f"p3o_L{l}", [128, 2, 512], F32))
                osq = ps.enter_context(nc.sbuf_tensor(f"p3osq_L{l}", [128, 2, 512], BF16))
                ort = ps.enter_context(nc.sbuf_tensor(f"p3ort_L{l}", [128, 512], F32))
                mo = [ps.enter_context(nc.sbuf_tensor(f"p3mo{i}_L{l}", [128, 2, 512], BF16)) for i in range(2)]
                if l == 0:
                    P.p3 = dict(hd=[Slot(P, f"p3hd{i}") for i in range(2)], bias=Slot(P, "p3bias"),
                                sp=[Slot(P, f"p3sp{i}") for i in range(4)], st=[Slot(P, f"p3st{i}") for i in range(4)],
                                e=[Slot(P, f"p3e{i}") for i in range(4)], acc=[Slot(P, f"p3acc{i}", own=False) for i in range(2)],
                                z=Slot(P, "p3z", own=False), rz=Slot(P, "p3rz"), t0=Slot(P, "p3t0"), o=Slot(P, "p3o"), osq=Slot(P, "p3osq"),
                                ss=Slot(P, "p3ss"), ort=Slot(P, "p3ort"), mo=[Slot(P, f"p3mo{i}", store=True) for i in range(2)],
                                pv=P.sem("p3pv"), cnt=dict(hd=0, sp=0, e=0, acc=0, mo=0))
                p3 = P.p3
                c3 = p3["cnt"]
                pv = p3["pv"]
                units = [("A", 0)] + [("B", i) for i in range(3)] + [("C", i) for i in range(3)]
                qkv = qk.ap()
                vvh = vv.ap().rearrange("(tb p) c -> p tb c", p=128)
                mixv = mixo.ap().rearrange("(c p) t -> p c t", p=128)
                p3stores = []
                mixag = []

                def load_unit(u):
                    kind, hi_ = units[u]
                    i = c3["hd"] % 2
                    c3["hd"] += 1
                    HD = p3["hd"][i]
                    if kind == "A":
                        srcs = [(qT[i][:, 0, :], qkv[0]), (qT[i][:, 1, :], qkv[1]), (kT[i][:, 0, :], qkv[2]), (kT[i][:, 1, :], qkv[3]),
                                (vt[i][:, :, :], vvh[:, :, 0:256])]
                    elif kind == "B":
                        srcs = [(qT[i][:, 0, :], qkv[4 + hi_]), (kT[i][:, 0, :], qkv[7 + hi_]), (vt[i][:, :, 0:128], vvh[:, :, 256 + 128 * hi_:256 + 128 * (hi_ + 1)])]
                    else:
                        srcs = [(qT[i][:, 0, :], qkv[10 + hi_]), (kT[i][:, 0, :], qkv[13 + hi_]), (vt[i][:, :, 0:128], vvh[:, :, 640 + 128 * hi_:640 + 128 * (hi_ + 1)])]
                    sig = None
                    for j, (dst, src) in enumerate(srcs):
                        sig = P.op("sync", lambda e, dst=dst, src=src: e.dma_start(out=dst, in_=src), waits=HD.pw() if j == 0 else [], inc=HD)
                    return i, sig

                def load_bias(u):
                    kind, hi_ = units[u]
                    B = p3["bias"]
                    if kind == "A":
                        return P.op("sync", lambda e: e.dma_start(out=bias[:, 0:8064], in_=biasA), waits=B.pw(), inc=B)
                    if kind == "B":
                        return P.op("sync", lambda e, hi_=hi_: e.dma_start(out=bias[:, :], in_=W["biasB"][hi_]), waits=B.pw(), inc=B)
                    return P.op("sync", lambda e: e.dma_start(out=bias[:, 0:3 * 2944], in_=biasC), waits=B.pw(), inc=B)

                def blocks_for(kind, qt):
                    if kind == "A":
                        return [(kb, qt * 512 - kb * 128 + 3968) for kb in range(32)]
                    if kind == "C":
                        return [(kb, qt * 512 - kb * 128 + 1408) for kb in range(max(0, 4 * qt - 8), min(31, 4 * qt + 11) + 1)]
                    if qt == 0:
                        return [(kb, (8 + kb) * 512) for kb in range(6)]
                    if qt == 7:
                        return [(kb, (14 + kb - 26) * 512) for kb in range(26, 32)]
                    return [(4 * qt - 2 + r, r * 512) for r in range(8)]

                nxt = load_unit(0)
                sigB = load_bias(0)
                mix_chunk = 0
                tail = []
                for u, (kind, hi_) in enumerate(units):
                    hs_, sigHD = nxt
                    HD = p3["hd"][hs_]
                    if u + 1 < len(units):
                        nxt = load_unit(u + 1)
                    nmaps = 2 if kind == "A" else 1
                    ndv = 2 if kind == "A" else 1
                    cbase = (hi_ * 2944) if kind == "C" else 0
                    first_of_unit = True
                    sigSTlast = None
                    sigPVlast = None
                    sigT0 = None
                    for qt in range(8):
                        blks = blocks_for(kind, qt)
                        for m in range(nmaps):
                            if kind == "A":
                                ai = 0
                                obank = [4, 5]
                            else:
                                ai = c3["acc"] % 2
                                c3["acc"] += 1
                                obank = [4 + ai]
                            ACC, Z = p3["acc"][ai], p3["z"]
                            nb_ = len(blks)
                            pend = []

                            def emit_pv(item, isfirst, islast, obank=obank, hs_=hs_, ACC=ACC, Z=Z):
                                (kb, ei, E, sigE) = item
                                for dvc in range(ndv):
                                    w_ = [sigE] if dvc == 0 else []
                                    if isfirst and dvc == 0:
                                        w_ = w_ + ACC.pw()
                                    P.op("tensor", lambda e, ei=ei, kb=kb, dvc=dvc: e.matmul(psb[obank[dvc]][:], lhsT=vt[hs_][:, kb, dvc * 128:(dvc + 1) * 128], rhs=et[ei][:], start=isfirst, stop=islast), waits=w_)
                                sig = P.op("tensor", lambda e, ei=ei: e.matmul(psb[6][:], lhsT=ones[:], rhs=et[ei][:], start=isfirst, stop=islast),
                                           waits=Z.pw() if isfirst else [], inc=(pv, 1))
                                E.release(sig)
                                return sig

                            npv = 0
                            for bi, (kb, boff) in enumerate(blks):
                                si = c3["sp"] % 4
                                c3["sp"] += 1
                                SP, ST = p3["sp"][si], p3["st"][si]
                                sigSP = P.op("tensor", lambda e, si=si, kb=kb, qt=qt, m=m, hs_=hs_: e.matmul(psb[si][:], lhsT=kT[hs_][:, m, kb * 128:(kb + 1) * 128], rhs=qT[hs_][:, m, qt * 512:(qt + 1) * 512], start=True, stop=True),
                                              waits=SP.pw() + ([sigHD] if first_of_unit else []), inc=SP)
                                sigST = P.op("vector", lambda e, si=si, boff=boff, cbase=cbase: e.scalar_tensor_tensor(out=stt[si][:], in0=psb[si][:], scalar=SCALE, in1=bias[:, cbase + boff:cbase + boff + 512], op0=ALU.mult, op1=ALU.add),
                                              waits=[sigSP] + ST.pw() + ([sigB] if first_of_unit else []), inc=ST)
                                first_of_unit = False
                                SP.release(sigST)
                                sigSTlast = sigST
                                ei = c3["e"] % 4
                                c3["e"] += 1
                                E = p3["e"][ei]
                                sigE = P.op("scalar", lambda e, si=si, ei=ei: e.activation(out=et[ei][:], in_=stt[si][:], func=AF.Exp), waits=[sigST] + E.pw(), inc=E)
                                ST.release(sigE)
                                pend.append((kb, ei, E, sigE))
                                if len(pend) > 3:
                                    emit_pv(pend.pop(0), npv == 0, False)
                                    npv += 1
                                if bi == 2 and tail:
                                    for fn_ in tail:
                                        fn_()
                                    tail = []
                            while pend:
                                sigPVlast = emit_pv(pend.pop(0), npv == 0, len(pend) == 0)
                                npv += 1
                            RZ, T0, O, OSQ, SS, ORT = p3["rz"], p3["t0"], p3["o"], p3["osq"], p3["ss"], p3["ort"]
                            sigLZ = P.op("scalar", lambda e: e.activation(out=rz[:], in_=psb[6][:], func=AF.Ln), waits=[sigPVlast] + RZ.pw(), inc=(acp, 1))
                            Z.release(sigLZ)
                            sigRZ = P.op("scalar", lambda e: e.activation(out=rz[:], in_=rz[:], func=AF.Exp, scale=-1.0), waits=[sigLZ], inc=RZ)
                            if kind == "A" and m == 0:
                                for dvc in range(2):
                                    sigT0 = P.op("vector", lambda e, dvc=dvc, ai=ai: e.tensor_tensor(out=t0[:, dvc, :], in0=psb[4 + dvc][:], in1=rz[:], op=ALU.mult),
                                                 waits=([sigRZ] + T0.pw()) if dvc == 0 else [], inc=T0 if dvc == 1 else None)
                                ACC.release(sigT0)
                                RZ.release(sigT0)
                                continue
                            if kind == "A":
                                for dvc in range(2):
                                    sigX_ = P.op("vector", lambda e, dvc=dvc, ai=ai: e.tensor_tensor(out=ob[:, dvc, :], in0=psb[4 + dvc][:], in1=rz[:], op=ALU.mult),
                                                 waits=([sigRZ] + O.pw()) if dvc == 0 else [], inc=(dvp, 1) if dvc == 1 else None)
                                ACC.release(sigX_)
                                RZ.release(sigX_)
                                for dvc in range(2):
                                    sigO = P.op("vector", lambda e, dvc=dvc, l=l: e.scalar_tensor_tensor(out=ob[:, dvc, :], in0=ob[:, dvc, :], scalar=nlam[l][:, 0:1], in1=t0[:, dvc, :], op0=ALU.mult, op1=ALU.add),
                                                waits=[sigX_, sigT0] if dvc == 0 else [], inc=O if dvc == 1 else None)
                                T0.release(sigO)
                                nfeat = 256
                            else:
                                sigO = P.op("vector", lambda e, ai=ai: e.tensor_tensor(out=ob[:, 0, :], in0=psb[4 + ai][:], in1=rz[:], op=ALU.mult), waits=[sigRZ] + O.pw(), inc=O)
                                ACC.release(sigO)
                                RZ.release(sigO)
                                nfeat = 128
                            sigOSQ = P.op("scalar", lambda e, ndv=ndv: e.activation(out=osq[:, 0:ndv, :], in_=ob[:, 0:ndv, :], func=AF.Square), waits=[sigO] + OSQ.pw(), inc=OSQ)
                            cm = (1.0 - lam_init(labs)) if kind == "A" else 1.0

                            def fin_tail(ndv=ndv, nfeat=nfeat, cm=cm, mc=mix_chunk, qt=qt, sigOSQ=sigOSQ, O=O, OSQ=OSQ, SS=SS, ORT=ORT, u=u):
                                for dvc in range(ndv):
                                    sigSS = P.op("tensor", lambda e, dvc=dvc: e.matmul(psb[7][:], lhsT=ones[:], rhs=osq[:, dvc, :], start=(dvc == 0), stop=(dvc == ndv - 1)),
                                                 waits=([sigOSQ] + SS.pw()) if dvc == 0 else [], inc=SS if dvc == ndv - 1 else None)
                                OSQ.release(sigSS)
                                sigORT = P.op("scalar", lambda e: e.activation(out=ort[:], in_=psb[7][:], func=AF.Ln, scale=1.0 / nfeat, bias=EPS), waits=[sigSS] + ORT.pw(), inc=ORT)
                                SS.release(sigORT)
                                sigORR = P.op("scalar", lambda e: e.activation(out=ort[:], in_=ort[:], func=AF.Exp, scale=-0.5, bias=math.log(cm)), waits=[sigORT], inc=(acp, 1))
                                mi = c3["mo"] % 2
                                c3["mo"] += 1
                                MO = p3["mo"][mi]
                                for dvc in range(ndv):
                                    sigMO = P.op("vector", lambda e, dvc=dvc: e.scalar_tensor_tensor(out=mo[mi][:, dvc, :], in0=ob[:, dvc, :], scalar=ogs[l][:, mc + dvc:mc + dvc + 1], in1=ort[:], op0=ALU.mult, op1=ALU.mult),
                                                  waits=([sigORR] + MO.pw()) if dvc == 0 else [], inc=MO if dvc == ndv - 1 else None)
                                O.release(sigMO)
                                ORT.release(sigMO)
                                sigSt = P.op("sync", lambda e: e.dma_start(out=mixv[:, mc:mc + ndv, qt * 512:(qt + 1) * 512], in_=mo[mi][:, 0:ndv, :]), waits=[sigMO], inc=(MO.s, 16))
                                MO.release(sigSt)
                                p3stores.append(sigSt)
                                if qt == 7:
                                    for cc in range(mc, mc + ndv):
                                        vag_ = P.op("gpsimd", lambda e, cc=cc: e.collective_compute("AllGather", ALU.bypass, replica_groups=GRP4, ins=[mixo.ap()[cc * 128:(cc + 1) * 128, :]], outs=[mixall.ap()[cc * 512:(cc + 1) * 512, :]]),
                                                     waits=p3stores[-2:], inc=(s_ag, 1))
                                        mixag.append(vag_)
                                    issue_wag(l, 2)

                            tail.append(fin_tail)
                    mix_chunk += ndv
                    HD.release(sigPVlast)
                    nxtu = units[u + 1] if u + 1 < len(units) else None
                    if nxtu is not None and (nxtu[0] != "C" or nxtu[1] == 0):
                        p3["bias"].release(sigSTlast)
                        sigB = load_bias(u + 1)
                    elif nxtu is None:
                        p3["bias"].release(sigSTlast)
                for fn_ in tail:
                    fn_()
                tail = []
                vst = p3stores[-2:]
                P.op("sync", None, waits=vst)
                P.op("gpsimd", None, waits=[mixag[-1]])
                if debug and l == 0:
                    v = P.op("gpsimd", lambda e: e.dma_start(out=dbg["mixo"], in_=mixo.ap()), inc=(s_dbg, 16))
                    P.op("gpsimd", None, waits=[v])
                P.build()
            if stop("p3"):
                break

            with ExitStack() as ps:
                acta = ps.enter_context(nc.sbuf_tensor(f"p4a_L{l}", [128, 32, 1024], BF16))
                actb = ps.enter_context(nc.sbuf_tensor(f"p4b_L{l}", [128, 32, 1024], BF16))
                wr = [ps.enter_context(nc.sbuf_tensor(f"p4w{i}_L{l}", [128, 32, 128], BF16)) for i in range(4)]
                stage = [wr[2 + i][:].rearrange("p a b -> p (a b)").rearrange("p (q t) -> p q t", q=4) for i in range(2)]
                xin = [ps.enter_context(nc.sbuf_tensor(f"p4xin{i}_L{l}", [128, 1024], F32)) for i in range(3)]
                x1 = [ps.enter_context(nc.sbuf_tensor(f"p4x1{i}_L{l}", [128, 1024], F32)) for i in range(4)]
                xsq = [ps.enter_context(nc.sbuf_tensor(f"p4xsq{i}_L{l}", [128, 1024], BF16)) for i in range(2)]
                rt2 = ps.enter_context(nc.sbuf_tensor(f"p4rt2_L{l}", [128, 1024], F32))
                sgt = [ps.enter_context(nc.sbuf_tensor(f"p5sg{i}_L{l}", [128, 1024], F32)) for i in range(2)]
                if l == 0:
                    P.p4 = dict(stg=[Slot(P, f"p4stg{i}") for i in range(2)], a=Slot(P, "p4a", own=False),
                                w=[Slot(P, f"p4w{i}") for i in range(4)], xin=[Slot(P, f"p4xin{i}") for i in range(3)],
                                pm=[Slot(P, f"p4pm{i}") for i in range(2)], x1=[Slot(P, f"p4x1{i}", store=True) for i in range(4)],
                                xsq=[Slot(P, f"p4xsq{i}") for i in range(2)], stat=Slot(P, "p4stat", own=False), rt=Slot(P, "p4rt"),
                                b=Slot(P, "p4b", own=False), pg=Slot(P, "p5pg"), pu=Slot(P, "p5pu"), sg=[Slot(P, f"p5sg{i}") for i in range(2)],
                                act=Slot(P, "p5act", own=False), cnt=dict(w=0, xin=0, pm=0, x1=0, xsq=0, stg=0, sg=0))
                p4 = P.p4
                c4 = p4["cnt"]
                xres_st = st["xres_st"]
                wouta = W["wout_a"].ap().rearrange("(m p) c -> m p c", p=128)
                wga = W["wg_a"].ap().rearrange("(f p) c -> f p c", p=128)
                wua = W["wu_a"].ap().rearrange("(f p) c -> f p c", p=128)
                wda = W["wd_a"].ap().rearrange("(g m p) c -> g m p c", g=4, p=128)
                issue_wag(l, 1000)
                mixallv = mixall.ap().rearrange("(kc p) (q t) -> p kc q t", p=128, q=4)
                A, Bs, STAT, ACT_ = p4["a"], p4["b"], p4["stat"], p4["act"]

                sigA = None
                for kc in range(32):
                    si = c4["stg"] % 2
                    c4["stg"] += 1
                    SG = p4["stg"][si]
                    sigSG = P.op("sync", lambda e, si=si, kc=kc: e.dma_start(out=stage[si], in_=mixallv[:, kc, :, :]), waits=SG.pw(), inc=SG)
                    v = P.op("vector", lambda e, si=si, kc=kc: e.tensor_scalar(out=acta[:, kc, :], in0=stage[si][:, 0, :], scalar1=ohs[:, 0:1], scalar2=0.0, op0=ALU.mult, op1=ALU.add),
                             waits=[sigSG] + (A.pw() if kc == 0 else []), inc=(dvp, 1))
                    for q in range(1, 4):
                        v = P.op("vector", lambda e, si=si, kc=kc, q=q: e.scalar_tensor_tensor(out=acta[:, kc, :], in0=stage[si][:, q, :], scalar=ohs[:, q:q + 1], in1=acta[:, kc, :], op0=ALU.mult, op1=ALU.add),
                                 waits=[v], inc=(dvp, 1))
                    SG.release(v)
                    sigA = v
                p4["w"][2].release(sigA)
                p4["w"][3].release(sigA)

                def load_w(src_ap, call, ncols=4096):
                    i = c4["w"] % 4
                    c4["w"] += 1
                    WSl = p4["w"][i]
                    sig = P.op("sync", lambda e, i=i, src_ap=src_ap, ncols=ncols: e.dma_start(out=wr[i][:].rearrange("p a b -> p (a b)")[:, 0:ncols], in_=src_ap), waits=WSl.pw() + [wag_sig(l, call)], inc=WSl)
                    return i, sig

                def load_xin(m, src, wait_store):
                    i = c4["xin"] % 3
                    c4["xin"] += 1
                    XI = p4["xin"][i]
                    sig = P.op("sync", lambda e, i=i, m=m, src=src: e.dma_start(out=xin[i][:], in_=src[m]), waits=XI.pw() + [wait_store], inc=XI)
                    return i, sig

                wq = [load_w(wouta[0], ("wout", 0)), load_w(wouta[1], ("wout", 0)), load_w(wouta[2], ("wout", 0))]
                xq = [load_xin(0, xsrc, xres_st[0] if l > 0 else None), load_xin(1, xsrc, xres_st[1] if l > 0 else None)]
                deferred = None
                sigSTAT = None
                for m in range(32):
                    wi, sigW = wq.pop(0)
                    if m + 3 < 32:
                        wq.append(load_w(wouta[m + 3], ("wout", (m + 3) // 4)))
                    xi, sigXI = xq.pop(0)
                    if m + 2 < 32:
                        xq.append(load_xin(m + 2, xsrc, xres_st[m + 2] if l > 0 else None))
                    WSl, XI = p4["w"][wi], p4["xin"][xi]
                    pi_ = c4["pm"] % 2
                    c4["pm"] += 1
                    PM = p4["pm"][pi_]
                    for kc in range(32):
                        for th in range(2):
                            w_ = []
                            if kc == 0 and th == 0:
                                w_ = PM.pw() + [sigW] + ([sigA] if m == 0 else [])
                            sigPM = P.op("tensor", lambda e, wi=wi, kc=kc, th=th, pi_=pi_: e.matmul(psb[2 * pi_ + th][:], lhsT=wr[wi][:, kc, :], rhs=acta[:, kc, th * 512:(th + 1) * 512], start=(kc == 0), stop=(kc == 31)),
                                          waits=w_, inc=PM if (kc == 31 and th == 1) else None)
                    WSl.release(sigPM)
                    if deferred is not None:
                        deferred()
                        deferred = None
                    x1i = c4["x1"] % 4
                    c4["x1"] += 1
                    X1 = p4["x1"][x1i]
                    for th in range(2):
                        sigX1 = P.op("vector", lambda e, th=th, pi_=pi_, xi=xi, x1i=x1i: e.tensor_tensor(out=x1[x1i][:, th * 512:(th + 1) * 512], in0=psb[2 * pi_ + th][:], in1=xin[xi][:, th * 512:(th + 1) * 512], op=ALU.add),
                                      waits=([sigPM, sigXI] + X1.pw()) if th == 0 else [], inc=X1 if th == 1 else None)
                    PM.release(sigX1)
                    XI.release(sigX1)
                    sigSt = P.op("sync", lambda e, x1i=x1i, m=m: e.dma_start(out=xres.ap()[m], in_=x1[x1i][:]), waits=[sigX1], inc=(X1.s, 16))
                    xres_st[m] = sigSt
                    X1.release(sigSt)
                    qi = c4["xsq"] % 2
                    c4["xsq"] += 1
                    XS = p4["xsq"][qi]
                    sigXS = P.op("scalar", lambda e, x1i=x1i, qi=qi: e.activation(out=xsq[qi][:], in_=x1[x1i][:], func=AF.Square), waits=[sigX1] + XS.pw(), inc=XS)
                    X1.release(sigXS)

                    def stat_mm(m=m, qi=qi, XS=XS, sigXS=sigXS):
                        for th in range(2):
                            sig = P.op("tensor", lambda e, th=th: e.matmul(psb[6 + th][:], lhsT=ones[:], rhs=xsq[qi][:, th * 512:(th + 1) * 512], start=(m == 0), stop=(m == 31)),
                                       waits=([sigXS] + (STAT.pw() if m == 0 else [])) if th == 0 else [], inc=(pep, 1) if th == 1 else None)
                        XS.release(sig)
                        return sig
                    deferred = stat_mm
                sigSTAT = deferred()
                RT = p4["rt"]
                for th in range(2):
                    sigRT = P.op("scalar", lambda e, th=th: e.activation(out=rt2[:, th * 512:(th + 1) * 512], in_=psb[6 + th][:], func=AF.Ln, scale=1.0 / D, bias=EPS),
                                 waits=([sigSTAT] + RT.pw()) if th == 0 else [], inc=RT if th == 1 else None)
                STAT.release(sigRT)
                sigRR = P.op("scalar", lambda e: e.activation(out=rt2[:], in_=rt2[:], func=AF.Exp, scale=-0.5), waits=[sigRT], inc=(acp, 1))
                xq = [load_xin(0, xres.ap(), xres_st[0]), load_xin(1, xres.ap(), xres_st[1])]
                sigB2 = None
                for m in range(32):
                    xi, sigXI = xq.pop(0)
                    if m + 2 < 32:
                        xq.append(load_xin(m + 2, xres.ap(), xres_st[m + 2]))
                    XI = p4["xin"][xi]
                    sigB2 = P.op("vector", lambda e, m=m, xi=xi, l=l: e.scalar_tensor_tensor(out=actb[:, m, :], in0=xin[xi][:], scalar=g2s[l][:, m:m + 1], in1=rt2[:], op0=ALU.mult, op1=ALU.mult),
                                 waits=[sigXI] + (([sigRR] + Bs.pw()) if m == 0 else []), inc=(dvp, 1))
                    XI.release(sigB2)
                RT.release(sigB2)
                if l + 1 < L:
                    P.op("gpsimd", None, waits=[sigB2])
                    for piece in range(3):
                        cast_win(l + 1, piece)
                    issue_wag(l + 1, 50)
                if debug and l == 0:
                    v = P.op("gpsimd", lambda e: e.dma_start(out=dbg["x1"], in_=xres.ap()), waits=xres_st[28:32], inc=(s_dbg, 16))
                    P.op("gpsimd", None, waits=[v])
                if stop("p4"):
                    P.op("sync", None, waits=xres_st[28:32])
                    P.op("vector", None, waits=[sigB2])
                    P.build()
                    break

                PG, PU = p4["pg"], p4["pu"]
                sigPU = None
                sigPM = None
                for gi, (f0, nf) in enumerate(FGROUPS):
                    lastg = (gi == len(FGROUPS) - 1)
                    wq = [(load_w(wga[f0], ("wg", f0 // 4)), load_w(wua[f0], ("wu", f0 // 4)))]
                    sigACT = None
                    for fi in range(nf):
                        f = f0 + fi
                        (wgi, sigWg), (wui, sigWu) = wq.pop(0)
                        if fi + 1 < nf:
                            wq.append((load_w(wga[f + 1], ("wg", (f + 1) // 4)), load_w(wua[f + 1], ("wu", (f + 1) // 4))))
                        sigs = {}
                        for (wi, sigW_, PSL, base) in ((wgi, sigWg, PG, 0), (wui, sigWu, PU, 2)):
                            for kc in range(32):
                                for th in range(2):
                                    w_ = []
                                    if kc == 0 and th == 0:
                                        w_ = PSL.pw() + [sigW_] + ([sigB2] if (gi == 0 and fi == 0 and base == 0) else [])
                                    sig = P.op("tensor", lambda e, wi=wi, kc=kc, th=th, base=base: e.matmul(psb[base + th][:], lhsT=wr[wi][:, kc, :], rhs=actb[:, kc, th * 512:(th + 1) * 512], start=(kc == 0), stop=(kc == 31)),
                                               waits=w_, inc=PSL if (kc == 31 and th == 1) else None)
                            p4["w"][wi].release(sig)
                            sigs[base] = sig
                        sigPG, sigPU = sigs[0], sigs[2]
                        si = c4["sg"] % 2
                        c4["sg"] += 1
                        SGS = p4["sg"][si]
                        for th in range(2):
                            sigSG = P.op("scalar", lambda e, th=th, si=si: e.activation(out=sgt[si][:, th * 512:(th + 1) * 512], in_=psb[th][:], func=AF.Silu),
                                         waits=([sigPG] + SGS.pw()) if th == 0 else [], inc=SGS if th == 1 else None)
                        PG.release(sigSG)
                        for th in range(2):
                            sigACT = P.op("vector", lambda e, th=th, si=si, fi=fi: e.tensor_tensor(out=acta[:, fi, th * 512:(th + 1) * 512], in0=psb[2 + th][:], in1=sgt[si][:, th * 512:(th + 1) * 512], op=ALU.mult),
                                          waits=([sigPU, sigSG] + (ACT_.pw() + A.pw() if fi == 0 else [])) if th == 0 else [], inc=(dvp, 1) if th == 1 else None)
                        PU.release(sigACT)
                        SGS.release(sigACT)

                    def load_wd(m, gi=gi, nf=nf):
                        return load_w(wda[gi][m][:, 0:nf * 128], ("wd", gi * 8 + m // 4), ncols=nf * 128)
                    wq = [load_wd(0), load_wd(1), load_wd(2)]
                    xq = [load_xin(0, xres.ap(), xres_st[0]), load_xin(1, xres.ap(), xres_st[1])]
                    for m in range(32):
                        wi, sigW = wq.pop(0)
                        if m + 3 < 32:
                            wq.append(load_wd(m + 3))
                        xi, sigXI = xq.pop(0)
                        if m + 2 < 32:
                            xq.append(load_xin(m + 2, xres.ap(), xres_st[m + 2]))
                        WSl, XI = p4["w"][wi], p4["xin"][xi]
                        pi_ = c4["pm"] % 2
                        c4["pm"] += 1
                        PM = p4["pm"][pi_]
                        for fi in range(nf):
                            for th in range(2):
                                w_ = []
                                if fi == 0 and th == 0:
                                    w_ = PM.pw() + [sigW] + ([sigACT] if m == 0 else [])
                                sigPM = P.op("tensor", lambda e, wi=wi, fi=fi, th=th, pi_=pi_, nf=nf: e.matmul(psb[4 + 2 * pi_ + th][:], lhsT=wr[wi][:, fi, :], rhs=acta[:, fi, th * 512:(th + 1) * 512], start=(fi == 0), stop=(fi == nf - 1)),
                                              waits=w_, inc=PM if (fi == nf - 1 and th == 1) else None)
                        WSl.release(sigPM)
                        x1i = c4["x1"] % 4
                        c4["x1"] += 1
                        X1 = p4["x1"][x1i]
                        for th in range(2):
                            sigX1 = P.op("vector", lambda e, th=th, pi_=pi_, xi=xi, x1i=x1i: e.tensor_tensor(out=x1[x1i][:, th * 512:(th + 1) * 512], in0=psb[4 + 2 * pi_ + th][:], in1=xin[xi][:, th * 512:(th + 1) * 512], op=ALU.add),
                                          waits=([sigPM, sigXI] + X1.pw()) if th == 0 else [], inc=X1 if th == 1 else None)
                        PM.release(sigX1)
                        XI.release(sigX1)
                        dst = xdst if lastg else xres.ap()
                        sigSt = P.op("sync", lambda e, x1i=x1i, m=m, dst=dst: e.dma_start(out=dst[m], in_=x1[x1i][:]), waits=[sigX1], inc=(X1.s, 16))
                        xres_st[m] = sigSt
                        X1.release(sigSt)
                    ACT_.release(sigPM)
                A.release(sigPM)
                Bs.release(sigPU)
                P.op("sync", None, waits=xres_st[28:32])
                P.build()
    return nc


def _alibi(n):
    return np.exp2(-8.0 * np.arange(1, n + 1, dtype=np.float64) / n)


def _bias_A(g):
    slope = _alibi(4)[g]
    p = np.arange(128)[:, None]
    c = np.arange(8064)[None, :]
    return (-slope * np.abs(c - 3968 - p)).astype(np.float32)


def _bias_C(g):
    out = np.empty((128, 3, 2944), np.float32)
    p = np.arange(128)[:, None]
    c = np.arange(2944)[None, :]
    d = c - 1408 - p
    ad = np.abs(d)
    cnt = (ad <= 64).astype(np.int64) + ((d % 4 == 0) & (ad <= 256)) + ((d % 16 == 0) & (ad <= 1024))
    with np.errstate(divide="ignore"):
        lc = np.log(cnt.astype(np.float64))
    for i in range(3):
        slope = _alibi(12)[3 * g + i]
        v = -slope * ad + lc
        out[:, i, :] = np.where(cnt > 0, v, NEG).astype(np.float32)
    return out.reshape(128, 3 * 2944)


def _bias_B(rpb_l, g):
    out = np.full((3, 128, 20, 512), NEG, np.float32)
    pidx = np.arange(128)
    krl, kc = pidx // 64, pidx % 64
    fidx = np.arange(512)
    qrl, qc = fidx // 64, fidx % 64
    cstart = np.clip(qc - 8, 0, 48)
    colok = (kc[:, None] >= cstart[None, :]) & (kc[:, None] < cstart[None, :] + 16)
    dc = np.clip(kc[:, None] - qc[None, :] + 15, 0, 30)
    cases = [(1, 4 * 1 - 2 + r, r) for r in range(8)] + [(0, kb, 8 + kb) for kb in range(6)] + [(7, kb, 14 + kb - 26) for kb in range(26, 32)]
    for (qt, kb, ti) in cases:
        kr = 2 * kb + krl
        qr = 8 * qt + qrl
        rstart = np.clip(qr - 4, 0, 56)
        rowok = (kr[:, None] >= rstart[None, :]) & (kr[:, None] < rstart[None, :] + 8)
        dr = np.clip(kr[:, None] - qr[None, :] + 7, 0, 14)
        ok = rowok & colok
        for i in range(3):
            vals = rpb_l[3 * g + i][dr, dc]
            out[i, :, ti, :] = np.where(ok, vals, NEG)
    return out.reshape(3, 128, 20 * 512)


def _block_w(Wm, kcn, nb):
    return np.ascontiguousarray(Wm.reshape(kcn, 128, nb, 128).transpose(2, 1, 0, 3))


def _qk_cols(g):
    cols = []
    for m in range(2):
        cols.append(np.arange(g * 256 + m * 128, g * 256 + (m + 1) * 128))
    for m in range(2):
        cols.append(1024 + np.arange(g * 256 + m * 128, g * 256 + (m + 1) * 128))
    for i in range(3):
        cols.append(3072 + (3 * g + i) * 128 + np.arange(128))
    for i in range(3):
        cols.append(4608 + (3 * g + i) * 128 + np.arange(128))
    for i in range(3):
        cols.append(7680 + (3 * g + i) * 128 + np.arange(128))
    for i in range(3):
        cols.append(9216 + (3 * g + i) * 128 + np.arange(128))
    return cols


def _v_cols(g):
    c = [2048 + g * 256 + np.arange(256)]
    for i in range(3):
        c.append(6144 + (3 * g + i) * 128 + np.arange(128))
    for i in range(3):
        c.append(10752 + (3 * g + i) * 128 + np.arange(128))
    return np.concatenate(c)


def _wout_perm():
    rows = []
    for c in range(8):
        for r in range(4):
            if c < 2:
                rows.append(r * 256 + c * 128 + np.arange(128))
            elif c < 5:
                rows.append(1024 + (3 * r + (c - 2)) * 128 + np.arange(128))
            else:
                rows.append(2560 + (3 * r + (c - 5)) * 128 + np.arange(128))
    return np.concatenate(rows)


def _col128(v):
    return np.ascontiguousarray(v.reshape(-1, 128).T.astype(np.float32))


def prepare_inputs(inp, layers):
    f32 = np.float32
    maps = [dict() for _ in range(NCORES)]
    x = np.asarray(inp["x"], f32)
    for c in range(NCORES):
        b, g = c // 4, c % 4
        maps[c]["xT"] = np.ascontiguousarray(x[b, g * 1024:(g + 1) * 1024, :].T).reshape(32, 128, 1024)
        maps[c]["biasA"] = _bias_A(g)
        maps[c]["biasC"] = _bias_C(g)
        oh = np.zeros((128, 4), f32)
        oh[:, g] = 1.0
        maps[c]["onehot"] = oh
    perm = _wout_perm()
    for li, l in enumerate(layers):
        w_in = np.asarray(inp["w_in"][l], f32)
        def shard_blocks(blk, npad):
            nb, _, X = blk.shape
            out = np.zeros((4, npad // 4, 128, X), f32)
            for r in range(4):
                sel = blk[r::4]
                out[r, :sel.shape[0]] = sel
            return out.reshape(4, npad // 4 * 128, X)
        wout_blk = shard_blocks(_block_w(np.asarray(inp["w_out"][l], f32)[perm, :], 32, 32).reshape(32, 128, 4096), 32)
        wg_blk = shard_blocks(_block_w(np.asarray(inp["w_gate"][l], f32), 32, NF).reshape(NF, 128, 4096), 88)
        wu_blk = shard_blocks(_block_w(np.asarray(inp["w_up"][l], f32), 32, NF).reshape(NF, 128, 4096), 88)
        wd_full = _block_w(np.asarray(inp["w_down"][l], f32), NF, 32).reshape(32, 128, NF, 128)
        wd_units = np.zeros((4, 32, 128, 2816), f32)
        for gi, (f0, nf) in enumerate(FGROUPS):
            wd_units[gi, :, :, :nf * 128] = wd_full[:, :, f0:f0 + nf, :].reshape(32, 128, nf * 128)
        wd_blk = np.stack([wd_units[:, r::4].reshape(32, 128, 2816) for r in range(4)]).reshape(4, 32 * 128, 2816)
        lamv = np.stack([inp["lambda_q1"][l], inp["lambda_k1"][l], inp["lambda_q2"][l], inp["lambda_k2"][l]]).astype(f32)
        lamv = np.ascontiguousarray(np.broadcast_to(lamv[None], (128, 4, 128)))
        g1 = _col128(np.asarray(inp["norm1_g"][l]))
        g2 = _col128(np.asarray(inp["norm2_g"][l]))
        qkg = np.stack([inp["a_q_g"][l]] * 2 + [inp["a_k_g"][l]] * 2 + [inp["b_q_g"][l]] * 3 + [inp["b_k_g"][l]] * 3
                       + [inp["c_q_g"][l]] * 3 + [inp["c_k_g"][l]] * 3, axis=1).astype(f32)
        og = np.stack([inp["a_out_g"][l][0:128], inp["a_out_g"][l][128:256]] + [inp["b_out_g"][l]] * 3 + [inp["c_out_g"][l]] * 3, axis=1).astype(f32)
        for g in range(4):
            cols = _qk_cols(g)
            wqk = np.stack([w_in[:, cc].reshape(32, 128, 128).transpose(1, 0, 2) for cc in cols]).reshape(16 * 128, 4096)
            wv = np.ascontiguousarray(w_in[:, _v_cols(g)].reshape(32, 128, 1024).transpose(1, 0, 2)).reshape(128, 32768)
            bB = _bias_B(np.asarray(inp["b_rpb"][l], f32), g)
            for b in range(2):
                m = maps[b * 4 + g]
                m[f"wqk_{li}"] = wqk
                m[f"wv_{li}"] = wv
                m[f"biasB_{li}"] = bB
        for c in range(NCORES):
            m = maps[c]
            m[f"g1_{li}"] = g1
            m[f"g2_{li}"] = g2
            m[f"qkg_{li}"] = np.ascontiguousarray(qkg)
            m[f"og_{li}"] = np.ascontiguousarray(og)
            m[f"lamv_{li}"] = lamv
            m[f"wout_{li}"] = wout_blk[c % 4]
            m[f"wg_{li}"] = wg_blk[c % 4]
            m[f"wu_{li}"] = wu_blk[c % 4]
            m[f"wd_{li}"] = wd_blk[c % 4]
    return maps


_NC_CACHE = {}


def _get_nc(n_layers, first, debug=False):
    key = (n_layers, first, debug)
    if key not in _NC_CACHE:
        _NC_CACHE[key] = build_program(n_layers, first, debug)
    return _NC_CACHE[key]


def assemble_output(res):
    out = np.empty((NB, S, D), np.float32)
    for c in range(NCORES):
        b, g = c // 4, c % 4
        out[b, g * 1024:(g + 1) * 1024, :] = res[c]["yT"].reshape(D, 1024).T
    return out


def kernel(**inputs):
    nc = _get_nc(DEPTH, 0)
    maps = prepare_inputs(inputs, list(range(DEPTH)))
    res = run_bass_kernel_spmd(nc, maps, core_ids=list(range(NCORES)))
    return assemble_output(res.results)
```

```python
import math
from contextlib import ExitStack

import numpy as np
import concourse.bass as bass
import concourse.mybir as mybir
from concourse.bass_utils import run_bass_kernel_spmd

F32 = mybir.dt.float32
BF16 = mybir.dt.bfloat16
AF = mybir.ActivationFunctionType
ALU = mybir.AluOpType

D = 4096
S = 4096
NB = 2
DEPTH = 2
DFF = 11008
NF = DFF // 128
EPS = 1e-6
NEG = -30000.0
SCALE = 128.0 ** -0.5
FGROUPS = [(0, 22), (22, 22), (44, 21), (65, 21)]
NCORES = 8
ENGS = ["sync", "scalar", "vector", "gpsimd", "tensor"]
GRP4 = [[0, 1, 2, 3], [4, 5, 6, 7]]
GRP8 = [list(range(8))]


def lam_init(l):
    return 0.8 - 0.6 * math.exp(-0.3 * l)


class Prog:
    def __init__(self, nc, es):
        self.nc = nc
        self.es = es
        self.semcnt = {}
        self.semh = {}
        self.ops = {e: [] for e in ENGS}

    def sem(self, name):
        assert name not in self.semcnt, name
        self.semcnt[name] = 0
        self.semh[name] = self.es.enter_context(self.nc.semaphore(name))
        return name

    def op(self, eng, fn, waits=(), inc=None):
        sig = None
        if inc is not None:
            if isinstance(inc, Slot):
                inc = (inc.f, 16 if eng in ("sync", "gpsimd_dma") else 1)
            s, a = inc
            self.semcnt[s] += a
            sig = (s, self.semcnt[s])
        if eng == "gpsimd_dma":
            eng = "gpsimd"
        wm = {}
        for w in waits:
            if w is None:
                continue
            s_, v_ = w
            if v_ > wm.get(s_, 0):
                wm[s_] = v_
        self.ops[eng].append((tuple(wm.items()), fn, inc))
        return sig

    def build(self):
        h = self.semh
        ops = self.ops
        with self.nc.Block() as block:
            def mk(engname):
                def body(eng):
                    for waits, fn, inc in ops[engname]:
                        for (s, v) in waits:
                            eng.wait_ge(h[s], v)
                        if fn is None:
                            continue
                        ins = fn(eng)
                        if inc is not None:
                            ins.then_inc(h[inc[0]], inc[1])
                return body
            for e in ENGS:
                if ops[e]:
                    getattr(block, e)(mk(e))
        self.ops = {e: [] for e in ENGS}
        self.nc.all_engine_barrier()


class Slot:
    def __init__(self, P, name, own=True, store=False):
        self.P = P
        self.f = P.sem(name) if own else None
        self.s = P.sem(name + "S") if store else None
        self.rel = []

    def pw(self):
        w = self.rel
        self.rel = []
        return list(w)

    def release(self, sig):
        assert sig is not None
        self.rel.append(sig)


def build_program(n_layers, first_layer_index=0, debug=False, stop_after=None):
    nc = bass.Bass("TRN2", target_bir_lowering=False)
    L = n_layers

    def din(name, shape, dt=F32):
        return nc.dram_tensor(name, list(shape), dt, kind="ExternalInput").ap()

    def dscr(name, shape, dt):
        return nc.dram_tensor(name, list(shape), dt)

    xT = din("xT", [32, 128, 1024])
    yT = nc.dram_tensor("yT", [32, 128, 1024], F32, kind="ExternalOutput").ap()
    biasA = din("biasA", [128, 8064])
    biasC = din("biasC", [128, 3 * 2944])
    onehot = din("onehot", [128, 4])
    lw = []
    for l in range(L):
        lw.append(dict(
            g1=din(f"g1_{l}", [128, 32]), g2=din(f"g2_{l}", [128, 32]),
            wqk=din(f"wqk_{l}", [16 * 128, 4096]), wv=din(f"wv_{l}", [128, 32768]),
            wout=din(f"wout_{l}", [8 * 128, 4096]), wg=din(f"wg_{l}", [22 * 128, 4096]),
            wu=din(f"wu_{l}", [22 * 128, 4096]), wd=din(f"wd_{l}", [32 * 128, 2816]),
            wqk_b=dscr(f"wqkb_{l}", [16 * 128, 4096], BF16), wv_b=dscr(f"wvb_{l}", [128, 32768], BF16),
            qkg=din(f"qkg_{l}", [128, 16]), og=din(f"og_{l}", [128, 8]),
            lamv=din(f"lamv_{l}", [128, 4, 128]), biasB=din(f"biasB_{l}", [3, 128, 20 * 512]),
            wout_b=dscr(f"woutb_{l}", [8 * 128, 4096], BF16), wout_a=dscr(f"wouta_{l}", [32 * 128, 4096], BF16),
            wg_b=dscr(f"wgb_{l}", [22 * 128, 4096], BF16), wg_a=dscr(f"wga_{l}", [88 * 128, 4096], BF16),
            wu_b=dscr(f"wub_{l}", [22 * 128, 4096], BF16), wu_a=dscr(f"wua_{l}", [88 * 128, 4096], BF16),
            wd_b=dscr(f"wdb_{l}", [32 * 128, 2816], BF16), wd_a=dscr(f"wda_{l}", [128 * 128, 2816], BF16),
        ))
    hb = dscr("hb", [8 * 128 * 2 * 4, 512], BF16)
    hall = dscr("hall", [8 * 4 * 128 * 2 * 4, 512], BF16)
    qk = dscr("qk", [16, 128, 4096], BF16)
    vv = dscr("vv", [4096, 1024], BF16)
    mixo = dscr("mixo", [1024, 4096], BF16)
    mixall = dscr("mixall", [8 * 4 * 128, 4096], BF16)
    xres = dscr("xres", [32, 128, 1024], F32)
    dbg = {}
    if debug:
        dbg["hb"] = nc.dram_tensor("d_hb", [8192, 512], BF16, kind="ExternalOutput").ap()
        dbg["qk"] = nc.dram_tensor("d_qk", [16, 128, 4096], BF16, kind="ExternalOutput").ap()
        dbg["vv"] = nc.dram_tensor("d_vv", [4096, 1024], BF16, kind="ExternalOutput").ap()
        dbg["mixo"] = nc.dram_tensor("d_mixo", [1024, 4096], BF16, kind="ExternalOutput").ap()
        dbg["x1"] = nc.dram_tensor("d_x1", [32, 128, 1024], F32, kind="ExternalOutput").ap()

    def stop(name):
        return stop_after is not None and stop_after == name

    with ExitStack() as es:
        P = Prog(nc, es)
        ones = es.enter_context(nc.sbuf_tensor("ones", [128, 128], BF16))
        g1s = [es.enter_context(nc.sbuf_tensor(f"g1s{l}", [128, 32], F32)) for l in range(L)]
        g2s = [es.enter_context(nc.sbuf_tensor(f"g2s{l}", [128, 32], F32)) for l in range(L)]
        qkgs = [es.enter_context(nc.sbuf_tensor(f"qkgs{l}", [128, 16], F32)) for l in range(L)]
        ogs = [es.enter_context(nc.sbuf_tensor(f"ogs{l}", [128, 8], F32)) for l in range(L)]
        nlam = [es.enter_context(nc.sbuf_tensor(f"nlam{l}", [128, 1], F32)) for l in range(L)]
        ohs = es.enter_context(nc.sbuf_tensor("ohs", [128, 4], F32))
        psb = [es.enter_context(nc.psum_tensor(f"psb{i}", [128, 512], F32)) for i in range(8)]

        s_c = P.sem("constld")
        s_cv = P.sem("constv")
        s_wc = [P.sem(f"wcast{l}") for l in range(L)]
        s_wag = [P.sem(f"wag{l}") for l in range(L)]
        s_wcc = P.sem("wcc")
        s_ag = P.sem("ag")
        s_dbg = P.sem("dbg") if debug else None
        dvp = P.sem("dvp")
        pep = P.sem("pep")
        acp = P.sem("acp")

        def cast_win(l, piece):
            if piece < 2:
                src = lw[l]["wqk"][piece * 1024:(piece + 1) * 1024, :]
                dst = lw[l]["wqk_b"].ap()[piece * 1024:(piece + 1) * 1024, :]
            else:
                src = lw[l]["wv"]
                dst = lw[l]["wv_b"].ap()
            v = P.op("gpsimd", lambda e, src=src, dst=dst: e.dma_start(out=dst, in_=src, max_dma_last_dim=8192), inc=(s_wc[l], 16))
            P.op("gpsimd", None, waits=[v])

        wag_calls = []
        wag_idx = []
        wag_issued = [0] * L
        for l in range(L):
            calls = [("wout", j) for j in range(8)]
            done = set()
            for gi, (f0, nf) in enumerate(FGROUPS):
                for j in range(f0 // 4, (f0 + nf - 1) // 4 + 1):
                    if j not in done:
                        done.add(j)
                        calls += [("wg", j), ("wu", j)]
                calls += [("wd", gi * 8 + j) for j in range(8)]
            wag_calls.append(calls)
            wag_idx.append({c: i for i, c in enumerate(calls)})

        def issue_wag(l, n):
            for _ in range(n):
                i = wag_issued[l]
                if i >= len(wag_calls[l]):
                    return
                k, j = wag_calls[l][i]
                wag_issued[l] += 1
                v = P.op("gpsimd", lambda e, l=l, k=k, j=j: e.dma_start(out=lw[l][k + "_b"].ap()[j * 128:(j + 1) * 128, :], in_=lw[l][k][j * 128:(j + 1) * 128, :], max_dma_last_dim=8192), inc=(s_wcc, 16))
                P.op("gpsimd", None, waits=[v])
                P.op("gpsimd", lambda e, l=l, k=k, j=j: e.collective_compute("AllGather", ALU.bypass, replica_groups=GRP4,
                     ins=[lw[l][k + "_b"].ap()[j * 128:(j + 1) * 128, :]], outs=[lw[l][k + "_a"].ap()[j * 512:(j + 1) * 512, :]]), inc=(s_wag[l], 1))

        def wag_sig(l, call):
            return (s_wag[l], wag_idx[l][call] + 1)

        with ExitStack() as ps:
            lamt = ps.enter_context(nc.sbuf_tensor("lamt", [128, 4, 128], F32))
            lamp = ps.enter_context(nc.sbuf_tensor("lamp", [128, 2, 128], F32))
            lams = ps.enter_context(nc.sbuf_tensor("lams", [128, 2], F32))
            lame = ps.enter_context(nc.sbuf_tensor("lame", [128, 2], F32))
            P.op("vector", lambda e: e.memset(ones[:], 1.0), inc=(s_cv, 1))
            for l in range(L):
                for (dst, src) in ((g1s[l], lw[l]["g1"]), (g2s[l], lw[l]["g2"]), (qkgs[l], lw[l]["qkg"]), (ogs[l], lw[l]["og"])):
                    P.op("sync", lambda e, dst=dst, src=src: e.dma_start(out=dst[:], in_=src), inc=(s_c, 16))
            P.op("sync", lambda e: e.dma_start(out=ohs[:], in_=onehot), inc=(s_c, 16))
            cast_win(0, 0)
            prev = None
            for l in range(L):
                v = P.op("sync", lambda e, l=l: e.dma_start(out=lamt[:], in_=lw[l]["lamv"]), waits=[prev], inc=(s_c, 16))
                li = lam_init(l + first_layer_index)
                v1 = P.op("vector", lambda e: e.tensor_tensor(out=lamp[:, 0, :], in0=lamt[:, 0, :], in1=lamt[:, 1, :], op=ALU.mult), waits=[v], inc=(s_cv, 1))
                v2 = P.op("vector", lambda e: e.tensor_tensor(out=lamp[:, 1, :], in0=lamt[:, 2, :], in1=lamt[:, 3, :], op=ALU.mult), inc=(s_cv, 1))
                v3 = P.op("vector", lambda e: e.tensor_reduce(out=lams[:], in_=lamp[:], axis=mybir.AxisListType.X, op=ALU.add), waits=[v2], inc=(s_cv, 1))
                v4 = P.op("scalar", lambda e: e.activation(out=lame[:], in_=lams[:], func=AF.Exp), waits=[v3], inc=(s_cv, 1))
                prev = P.op("vector", lambda e, l=l, li=li: e.scalar_tensor_tensor(out=nlam[l][:], in0=lame[:, 1:2], scalar=-li, in1=lame[:, 0:1], op0=ALU.add, op1=ALU.subtract), waits=[v4], inc=(s_cv, 1))
            P.op("vector", None, waits=[prev])
            P.op("sync", None, waits=[(s_c, P.semcnt[s_c]), prev])
            P.build()

        st = dict(xres_st=[None] * 32)

        for l in range(L):
            W = lw[l]
            labs = l + first_layer_index
            last = (l == L - 1)
            xsrc = xT if l == 0 else xres.ap()
            xdst = yT if last else xres.ap()

            with ExitStack() as ps:
                xs = [ps.enter_context(nc.sbuf_tensor(f"p1x{i}_L{l}", [128, 32, 512], F32)) for i in range(2)]
                hs = [ps.enter_context(nc.sbuf_tensor(f"p1h{i}_L{l}", [128, 32, 512], BF16)) for i in range(2)]
                sq = [ps.enter_context(nc.sbuf_tensor(f"p1sq{i}_L{l}", [128, 4, 512], BF16)) for i in range(2)]
                rt = [ps.enter_context(nc.sbuf_tensor(f"p1rt{i}_L{l}", [128, 512], F32)) for i in range(2)]
                if l == 0:
                    P.p1 = dict(x=[Slot(P, f"p1x{i}") for i in range(2)], sq=[Slot(P, f"p1sq{i}") for i in range(2)],
                                ps=[Slot(P, f"p1ps{i}") for i in range(2)], rt=[Slot(P, f"p1rt{i}") for i in range(2)],
                                h=[Slot(P, f"p1h{i}", store=True) for i in range(2)])
                p1 = P.p1
                hbv = hb.ap().rearrange("(j p h a) t -> j p h a t", j=8, p=128, h=2, a=4)
                stores = []
                for th in range(2):
                    b = th
                    X, SQ, PS_, RT, H = p1["x"][b], p1["sq"], p1["ps"][b], p1["rt"][b], p1["h"][b]
                    sigX = P.op("sync", lambda e, b=b, th=th: e.dma_start(out=xs[b][:], in_=xsrc[:, :, th * 512:(th + 1) * 512].rearrange("fc p t -> p fc t")), waits=X.pw(), inc=X)
                    sigPS = None
                    for j in range(8):
                        sb = j % 2
                        sigSQ = P.op("scalar", lambda e, b=b, j=j, sb=sb: e.activation(out=sq[sb][:], in_=xs[b][:, 4 * j:4 * j + 4, :], func=AF.Square), waits=[sigX] + SQ[sb].pw(), inc=SQ[sb])
                        for i in range(4):
                            fc = 4 * j + i
                            sig = P.op("tensor", lambda e, b=b, sb=sb, i=i, fc=fc: e.matmul(psb[b][:], lhsT=ones[:], rhs=sq[sb][:, i, :], start=(fc == 0), stop=(fc == 31)),
                                       waits=([sigSQ] if i == 0 else []) + (PS_.pw() if fc == 0 else []), inc=PS_ if i == 3 else None)
                        SQ[sb].release(sig)
                        sigPS = sig
                    sigRT = P.op("scalar", lambda e, b=b: e.activation(out=rt[b][:], in_=psb[b][:], func=AF.Ln, scale=1.0 / D, bias=EPS), waits=[sigPS] + RT.pw(), inc=RT)
                    PS_.release(sigRT)
                    sigRR = P.op("scalar", lambda e, b=b: e.activation(out=rt[b][:], in_=rt[b][:], func=AF.Exp, scale=-0.5), waits=[sigRT], inc=(acp, 1))
                    for fc in range(32):
                        sigH = P.op("vector", lambda e, b=b, fc=fc, l=l: e.scalar_tensor_tensor(out=hs[b][:, fc, :], in0=xs[b][:, fc, :], scalar=g1s[l][:, fc:fc + 1], in1=rt[b][:], op0=ALU.mult, op1=ALU.mult),
                                    waits=([sigRR] + H.pw()) if fc == 0 else [], inc=H if fc == 31 else None)
                    RT.release(sigH)
                    X.release(sigH)
                    for j in range(8):
                        sigSt = P.op("sync", lambda e, b=b, th=th, j=j: e.dma_start(out=hbv[j][:, th, :, :], in_=hs[b][:, 4 * j:4 * j + 4, :]), waits=[sigH] if j == 0 else [], inc=(H.s, 16))
                    H.release(sigSt)
                    stores.append(sigSt)
                P.op("sync", None, waits=stores)
                for j in range(8):
                    vag = P.op("gpsimd", lambda e, j=j: e.collective_compute("AllGather", ALU.bypass, replica_groups=GRP4, ins=[hb.ap()[j * 1024:(j + 1) * 1024, :]], outs=[hall.ap()[j * 4096:(j + 1) * 4096, :]]), waits=stores if j == 0 else [], inc=(s_ag, 1))
                P.op("gpsimd", None, waits=[vag])
                if debug and l == 0:
                    v = P.op("gpsimd", lambda e: e.dma_start(out=dbg["hb"], in_=hb.ap()), inc=(s_dbg, 16))
                    P.op("gpsimd", None, waits=[v])
                P.build()
            if stop("p1"):
                break

            with ExitStack() as ps:
                wres = ps.enter_context(nc.sbuf_tensor(f"p2w_L{l}", [128, 32768], BF16))
                ht = [ps.enter_context(nc.sbuf_tensor(f"p2h{i}_L{l}", [128, 32, 512], BF16)) for i in range(3)]
                sqb = [ps.enter_context(nc.sbuf_tensor(f"p2sq{i}_L{l}", [128, 512], BF16)) for i in range(2)]
                rtb = [ps.enter_context(nc.sbuf_tensor(f"p2rt{i}_L{l}", [128, 512], F32)) for i in range(2)]
                qo = [ps.enter_context(nc.sbuf_tensor(f"p2qo{i}_L{l}", [128, 512], BF16)) for i in range(3)]
                if l == 0:
                    P.p2 = dict(w=Slot(P, "p2w"), h=[Slot(P, f"p2h{i}") for i in range(3)],
                                pa=[Slot(P, f"p2pa{i}") for i in range(3)], sq=[Slot(P, f"p2sq{i}") for i in range(2)],
                                pb=[Slot(P, f"p2pb{i}") for i in range(2)], rt=[Slot(P, f"p2rt{i}") for i in range(2)],
                                qo=[Slot(P, f"p2qo{i}", store=True) for i in range(3)],
                                cnt=dict(h=0, pa=0, sq=0, qo=0))
                p2 = P.p2
                c2 = p2["cnt"]
                WS = p2["w"]
                hallv = hall.ap().rearrange("(j r p h a) t -> j r p h a t", j=8, r=4, p=128, h=2, a=4)
                if l == 0:
                    cast_win(0, 1)
                    cast_win(0, 2)
                wag_pace = [15, 15, 15] if l == 0 else [7, 7, 6]
                vvv = vv.ap()
                p2stores = []

                def load_h(t):
                    i = c2["h"] % 3
                    c2["h"] += 1
                    r, half = t // 2, t % 2
                    sig = None
                    for j in range(8):
                        sig = P.op("sync", lambda e, i=i, r=r, half=half, j=j: e.dma_start(out=ht[i][:, 4 * j:4 * j + 4, :], in_=hallv[j][r][:, half, :, :]), waits=p2["h"][i].pw() if j == 0 else [], inc=p2["h"][i])
                    return i, sig

                def load_wreg(pss_, i, waits):
                    if pss_ < 2:
                        src = W["wqk_b"].ap()[(pss_ * 8 + i) * 128:(pss_ * 8 + i + 1) * 128, :]
                    else:
                        src = W["wv_b"].ap()[:, i * 4096:(i + 1) * 4096]
                    return P.op("sync", lambda e, i=i, src=src: e.dma_start(out=wres[:, i * 4096:(i + 1) * 4096], in_=src),
                                waits=waits + [(s_wc[l], 16 * (pss_ + 1))], inc=WS)

                sigW_next = None
                for pss in range(3):
                    if pss == 0:
                        sigW = None
                        for i in range(8):
                            sigW = load_wreg(0, i, WS.pw() if i == 0 else [])
                    else:
                        sigW = sigW_next
                    issue_wag(l, wag_pace[pss])
                    hq = [load_h(0), load_h(1)]
                    sigPA_last = None
                    for t in range(8):
                        hi, sigH = hq.pop(0)
                        H = p2["h"][hi]
                        if t + 2 < 8:
                            hq.append(load_h(t + 2))
                        if pss < 2:
                            deferred = None
                            for s in range(8):
                                sg_ = pss * 8 + s
                                ia = c2["pa"] % 3
                                c2["pa"] += 1
                                PA = p2["pa"][ia]
                                for kc in range(32):
                                    w_ = []
                                    if kc == 0:
                                        w_ = PA.pw() + ([sigH] if s == 0 else []) + ([sigW] if (s == 0 and t == 0) else [])
                                    sigPA = P.op("tensor", lambda e, ia=ia, s=s, kc=kc, hi=hi: e.matmul(psb[ia][:], lhsT=wres[:, (s * 32 + kc) * 128:(s * 32 + kc + 1) * 128], rhs=ht[hi][:, kc, :], start=(kc == 0), stop=(kc == 31)),
                                                  waits=w_, inc=PA if kc == 31 else None)
                                sigPA_last = sigPA
                                if t == 7:
                                    sigW_next = load_wreg(pss + 1, s, [sigPA])
                                isq = c2["sq"] % 2
                                c2["sq"] += 1
                                SQ, PB, RT = p2["sq"][isq], p2["pb"][isq], p2["rt"][isq]
                                sigSQ = P.op("scalar", lambda e, ia=ia, isq=isq: e.activation(out=sqb[isq][:], in_=psb[ia][:], func=AF.Square), waits=[sigPA] + SQ.pw(), inc=SQ)

                                def post(ia=ia, isq=isq, sg_=sg_, t=t, SQ=SQ, PB=PB, RT=RT, PA=PA, sigSQ=sigSQ):
                                    sigPB = P.op("tensor", lambda e: e.matmul(psb[4 + isq][:], lhsT=ones[:], rhs=sqb[isq][:], start=True, stop=True), waits=[sigSQ] + PB.pw(), inc=PB)
                                    SQ.release(sigPB)
                                    sigRT = P.op("scalar", lambda e: e.activation(out=rtb[isq][:], in_=psb[4 + isq][:], func=AF.Ln, scale=1.0 / 128, bias=EPS), waits=[sigPB] + RT.pw(), inc=RT)
                                    PB.release(sigRT)
                                    sigRR = P.op("scalar", lambda e: e.activation(out=rtb[isq][:], in_=rtb[isq][:], func=AF.Exp, scale=-0.5), waits=[sigRT], inc=(acp, 1))
                                    iq = c2["qo"] % 3
                                    c2["qo"] += 1
                                    QO = p2["qo"][iq]
                                    sigQO = P.op("vector", lambda e: e.scalar_tensor_tensor(out=qo[iq][:], in0=psb[ia][:], scalar=qkgs[l][:, sg_:sg_ + 1], in1=rtb[isq][:], op0=ALU.mult, op1=ALU.mult),
                                                 waits=[sigRR] + QO.pw(), inc=QO)
                                    PA.release(sigQO)
                                    RT.release(sigQO)
                                    sigSt = P.op("sync", lambda e: e.dma_start(out=qk.ap()[sg_][:, t * 512:(t + 1) * 512], in_=qo[iq][:]), waits=[sigQO], inc=(QO.s, 16))
                                    QO.release(sigSt)
                                    p2stores.append(sigSt)

                                if deferred is not None:
                                    deferred()
                                deferred = post
                            deferred()
                            H.release(sigPA_last)
                        else:
                            for tb4 in range(4):
                                for ch in range(2):
                                    ia = c2["pa"] % 3
                                    c2["pa"] += 1
                                    PA = p2["pa"][ia]
                                    for kc in range(32):
                                        w_ = []
                                        if kc == 0:
                                            first = (tb4 == 0 and ch == 0)
                                            w_ = PA.pw() + ([sigH] if first else []) + ([sigW] if (first and t == 0) else [])
                                        sigPA = P.op("tensor", lambda e, ia=ia, kc=kc, hi=hi, tb4=tb4, ch=ch: e.matmul(psb[ia][:], lhsT=ht[hi][:, kc, tb4 * 128:(tb4 + 1) * 128], rhs=wres[:, kc * 1024 + ch * 512:kc * 1024 + (ch + 1) * 512], start=(kc == 0), stop=(kc == 31)),
                                                      waits=w_, inc=PA if kc == 31 else None)
                                    sigPA_last = sigPA
                                    iq = c2["qo"] % 3
                                    c2["qo"] += 1
                                    QO = p2["qo"][iq]
                                    sigQO = P.op("scalar", lambda e, ia=ia, iq=iq: e.activation(out=qo[iq][:], in_=psb[ia][:], func=AF.Copy), waits=[sigPA] + QO.pw(), inc=QO)
                                    PA.release(sigQO)
                                    sigSt = P.op("sync", lambda e, iq=iq, t=t, tb4=tb4, ch=ch: e.dma_start(out=vvv[t * 512 + tb4 * 128:t * 512 + (tb4 + 1) * 128, ch * 512:(ch + 1) * 512], in_=qo[iq][:]), waits=[sigQO], inc=(QO.s, 16))
                                    QO.release(sigSt)
                                    p2stores.append(sigSt)
                            H.release(sigPA_last)
                    WS.release(sigPA_last)
                    P.op("gpsimd", None, waits=[sigPA_last])
                P.op("sync", None, waits=p2stores[-3:])
                P.op("gpsimd", None, waits=p2stores[-3:])
                if debug and l == 0:
                    v = P.op("gpsimd", lambda e: e.dma_start(out=dbg["qk"], in_=qk.ap()), inc=(s_dbg, 16))
                    v = P.op("gpsimd", lambda e: e.dma_start(out=dbg["vv"], in_=vv.ap()), inc=(s_dbg, 16))
                    P.op("gpsimd", None, waits=[v])
                P.build()
            if stop("p2"):
                break

            with ExitStack() as ps:
                qT = [ps.enter_context(nc.sbuf_tensor(f"p3q{i}_L{l}", [128, 2, 4096], BF16)) for i in range(2)]
                kT = [ps.enter_context(nc.sbuf_tensor(f"p3k{i}_L{l}", [128, 2, 4096], BF16)) for i in range(2)]
                vt = [ps.enter_context(nc.sbuf_tensor(f"p3v{i}_L{l}", [128, 32, 256], BF16)) for i in range(2)]
                bias = ps.enter_context(nc.sbuf_tensor(f"p3bias_L{l}", [128, 10240], F32))
                stt = [ps.enter_context(nc.sbuf_tensor(f"p3st{i}_L{l}", [128, 512], F32)) for i in range(4)]
                et = [ps.enter_context(nc.sbuf_tensor(f"p3e{i}_L{l}", [128, 512], BF16)) for i in range(4)]
                rz = ps.enter_context(nc.sbuf_tensor(f"p3rz_L{l}", [128, 512], F32))
                t0 = ps.enter_context(nc.sbuf_tensor(f"p3t0_L{l}", [128, 2, 512], F32))
                ob = ps.enter_context(nc.sbuf_tensor(f"p3o_L{l}", [128, 2, 512], F32))
                osq = ps.enter_context(nc.sbuf_tensor(f"p3osq_L{l}", [128, 2, 512], BF16))
                ort = ps.enter_context(nc.sbuf_tensor(f"p3ort_L{l}", [128, 512], F32))
                mo = [ps.enter_context(nc.sbuf_tensor(f"p3mo{i}_L{l}", [128, 2, 512], BF16)) for i in range(2)]
                if l == 0:
                    P.p3 = dict(hd=[Slot(P, f"p3hd{i}") for i in range(2)], bias=Slot(P, "p3bias"),
                                sp=[Slot(P, f"p3sp{i}") for i in range(4)], st=[Slot(P, f"p3st{i}") for i in range(4)],
                                e=[Slot(P, f"p3e{i}") for i in range(4)], acc=[Slot(P, f"p3acc{i}", own=False) for i in range(2)],
                                z=Slot(P, "p3z", own=False), rz=Slot(P, "p3rz"), t0=Slot(P, "p3t0"), o=Slot(P, "p3o"), osq=Slot(P, "p3osq"),
                                ss=Slot(P, "p3ss"), ort=Slot(P, "p3ort"), mo=[Slot(P, f"p3mo{i}", store=True) for i in range(2)],
                                pv=P.sem("p3pv"), cnt=dict(hd=0, sp=0, e=0, acc=0, mo=0))
                p3 = P.p3
                c3 = p3["cnt"]
                pv = p3["pv"]
                units = [("A", 0)] + [("B", i) for i in range(3)] + [("C", i) for i in range(3)]
                qkv = qk.ap()
                vvh = vv.ap().rearrange("(tb p) c -> p tb c", p=128)
                mixv = mixo.ap().rearrange("(c p) t -> p c t", p=128)
                p3stores = []
                mixag = []

                def load_unit(u):
                    kind, hi_ = units[u]
                    i = c3["hd"] % 2
                    c3["hd"] += 1
                    HD = p3["hd"][i]
                    if kind == "A":
                        srcs = [(qT[i][:, 0, :], qkv[0]), (qT[i][:, 1, :], qkv[1]), (kT[i][:, 0, :], qkv[2]), (kT[i][:, 1, :], qkv[3]),
                                (vt[i][:, :, :], vvh[:, :, 0:256])]
                    elif kind == "B":
                        srcs = [(qT[i][:, 0, :], qkv[4 + hi_]), (kT[i][:, 0, :], qkv[7 + hi_]), (vt[i][:, :, 0:128], vvh[:, :, 256 + 128 * hi_:256 + 128 * (hi_ + 1)])]
                    else:
                        srcs = [(qT[i][:, 0, :], qkv[10 + hi_]), (kT[i][:, 0, :], qkv[13 + hi_]), (vt[i][:, :, 0:128], vvh[:, :, 640 + 128 * hi_:640 + 128 * (hi_ + 1)])]
                    sig = None
                    for j, (dst, src) in enumerate(srcs):
                        sig = P.op("sync", lambda e, dst=dst, src=src: e.dma_start(out=dst, in_=src), waits=HD.pw() if j == 0 else [], inc=HD)
                    return i, sig

                def load_bias(u):
                    kind, hi_ = units[u]
                    B = p3["bias"]
                    if kind == "A":
                        return P.op("sync", lambda e: e.dma_start(out=bias[:, 0:8064], in_=biasA), waits=B.pw(), inc=B)
                    if kind == "B":
                        return P.op("sync", lambda e, hi_=hi_: e.dma_start(out=bias[:, :], in_=W["biasB"][hi_]), waits=B.pw(), inc=B)
                    return P.op("sync", lambda e: e.dma_start(out=bias[:, 0:3 * 2944], in_=biasC), waits=B.pw(), inc=B)

                def blocks_for(kind, qt):
                    if kind == "A":
                        return [(kb, qt * 512 - kb * 128 + 3968) for kb in range(32)]
                    if kind == "C":
                        return [(kb, qt * 512 - kb * 128 + 1408) for kb in range(max(0, 4 * qt - 8), min(31, 4 * qt + 11) + 1)]
                    if qt == 0:
                        return [(kb, (8 + kb) * 512) for kb in range(6)]
                    if qt == 7:
                        return [(kb, (14 + kb - 26) * 512) for kb in range(26, 32)]
                    return [(4 * qt - 2 + r, r * 512) for r in range(8)]

                nxt = load_unit(0)
                sigB = load_bias(0)
                mix_chunk = 0
                tail = []
                for u, (kind, hi_) in enumerate(units):
                    hs_, sigHD = nxt
                    HD = p3["hd"][hs_]
                    if u + 1 < len(units):
                        nxt = load_unit(u + 1)
                    nmaps = 2 if kind == "A" else 1
                    ndv = 2 if kind == "A" else 1
                    cbase = (hi_ * 2944) if kind == "C" else 0
                    first_of_unit = True
                    sigSTlast = None
                    sigPVlast = None
                    sigT0 = None
                    for qt in range(8):
                        blks = blocks_for(kind, qt)
                        for m in range(nmaps):
                            if kind == "A":
                                ai = 0
                                obank = [4, 5]
                            else:
                                ai = c3["acc"] % 2
                                c3["acc"] += 1
                                obank = [4 + ai]
                            ACC, Z = p3["acc"][ai], p3["z"]
                            nb_ = len(blks)
                            pend = []

                            def emit_pv(item, isfirst, islast, obank=obank, hs_=hs_, ACC=ACC, Z=Z):
                                (kb, ei, E, sigE) = item
                                for dvc in range(ndv):
                                    w_ = [sigE] if dvc == 0 else []
                                    if isfirst and dvc == 0:
                                        w_ = w_ + ACC.pw()
                                    P.op("tensor", lambda e, ei=ei, kb=kb, dvc=dvc: e.matmul(psb[obank[dvc]][:], lhsT=vt[hs_][:, kb, dvc * 128:(dvc + 1) * 128], rhs=et[ei][:], start=isfirst, stop=islast), waits=w_)
                                sig = P.op("tensor", lambda e, ei=ei: e.matmul(psb[6][:], lhsT=ones[:], rhs=et[ei][:], start=isfirst, stop=islast),
                                           waits=Z.pw() if isfirst else [], inc=(pv, 1))
                                E.release(sig)
                                return sig

                            npv = 0
                            for bi, (kb, boff) in enumerate(blks):
                                si = c3["sp"] % 4
                                c3["sp"] += 1
                                SP, ST = p3["sp"][si], p3["st"][si]
                                sigSP = P.op("tensor", lambda e, si=si, kb=kb, qt=qt, m=m, hs_=hs_: e.matmul(psb[si][:], lhsT=kT[hs_][:, m, kb * 128:(kb + 1) * 128], rhs=qT[hs_][:, m, qt * 512:(qt + 1) * 512], start=True, stop=True),
                                              waits=SP.pw() + ([sigHD] if first_of_unit else []), inc=SP)
                                sigST = P.op("vector", lambda e, si=si, boff=boff, cbase=cbase: e.scalar_tensor_tensor(out=stt[si][:], in0=psb[si][:], scalar=SCALE, in1=bias[:, cbase + boff:cbase + boff + 512], op0=ALU.mult, op1=ALU.add),
                                              waits=[sigSP] + ST.pw() + ([sigB] if first_of_unit else []), inc=ST)
                                first_of_unit = False
                                SP.release(sigST)
                                sigSTlast = sigST
                                ei = c3["e"] % 4
                                c3["e"] += 1
                                E = p3["e"][ei]
                                sigE = P.op("scalar", lambda e, si=si, ei=ei: e.activation(out=et[ei][:], in_=stt[si][:], func=AF.Exp), waits=[sigST] + E.pw(), inc=E)
                                ST.release(sigE)
                                pend.append((kb, ei, E, sigE))
                                if len(pend) > 3:
                                    emit_pv(pend.pop(0), npv == 0, False)
                                    npv += 1
                                if bi == 2 and tail:
                                    for fn_ in tail:
                                        fn_()
                                    tail = []
                            while pend:
                                sigPVlast = emit_pv(pend.pop(0), npv == 0, len(pend) == 0)
                                npv += 1
                            RZ, T0, O, OSQ, SS, ORT = p3["rz"], p3["t0"], p3["o"], p3["osq"], p3["ss"], p3["ort"]
                            sigLZ = P.op("scalar", lambda e: e.activation(out=rz[:], in_=psb[6][:], func=AF.Ln), waits=[sigPVlast] + RZ.pw(), inc=(acp, 1))
                            Z.release(sigLZ)
                            sigRZ = P.op("scalar", lambda e: e.activation(out=rz[:], in_=rz[:], func=AF.Exp, scale=-1.0), waits=[sigLZ], inc=RZ)
                            if kind == "A" and m == 0:
                                for dvc in range(2):
                                    sigT0 = P.op("vector", lambda e, dvc=dvc, ai=ai: e.tensor_tensor(out=t0[:, dvc, :], in0=psb[4 + dvc][:], in1=rz[:], op=ALU.mult),
                                                 waits=([sigRZ] + T0.pw()) if dvc == 0 else [], inc=T0 if dvc == 1 else None)
                                ACC.release(sigT0)
                                RZ.release(sigT0)
                                continue
                            if kind == "A":
                                for dvc in range(2):
                                    sigX_ = P.op("vector", lambda e, dvc=dvc, ai=ai: e.tensor_tensor(out=ob[:, dvc, :], in0=psb[4 + dvc][:], in1=rz[:], op=ALU.mult),
                                                 waits=([sigRZ] + O.pw()) if dvc == 0 else [], inc=(dvp, 1) if dvc == 1 else None)
                                ACC.release(sigX_)
                                RZ.release(sigX_)
                                for dvc in range(2):
                                    sigO = P.op("vector", lambda e, dvc=dvc, l=l: e.scalar_tensor_tensor(out=ob[:, dvc, :], in0=ob[:, dvc, :], scalar=nlam[l][:, 0:1], in1=t0[:, dvc, :], op0=ALU.mult, op1=ALU.add),
                                                waits=[sigX_, sigT0] if dvc == 0 else [], inc=O if dvc == 1 else None)
                                T0.release(sigO)
                                nfeat = 256
                            else:
                                sigO = P.op("vector", lambda e, ai=ai: e.tensor_tensor(out=ob[:, 0, :], in0=psb[4 + ai][:], in1=rz[:], op=ALU.mult), waits=[sigRZ] + O.pw(), inc=O)
                                ACC.release(sigO)
                                RZ.release(sigO)
                                nfeat = 128
                            sigOSQ = P.op("scalar", lambda e, ndv=ndv: e.activation(out=osq[:, 0:ndv, :], in_=ob[:, 0:ndv, :], func=AF.Square), waits=[sigO] + OSQ.pw(), inc=OSQ)
                            cm = (1.0 - lam_init(labs)) if kind == "A" else 1.0

                            def fin_tail(ndv=ndv, nfeat=nfeat, cm=cm, mc=mix_chunk, qt=qt, sigOSQ=sigOSQ, O=O, OSQ=OSQ, SS=SS, ORT=ORT, u=u):
                                for dvc in range(ndv):
                                    sigSS = P.op("tensor", lambda e, dvc=dvc: e.matmul(psb[7][:], lhsT=ones[:], rhs=osq[:, dvc, :], start=(dvc == 0), stop=(dvc == ndv - 1)),
                                                 waits=([sigOSQ] + SS.pw()) if dvc == 0 else [], inc=SS if dvc == ndv - 1 else None)
                                OSQ.release(sigSS)
                                sigORT = P.op("scalar", lambda e: e.activation(out=ort[:], in_=psb[7][:], func=AF.Ln, scale=1.0 / nfeat, bias=EPS), waits=[sigSS] + ORT.pw(), inc=ORT)
                                SS.release(sigORT)
                                sigORR = P.op("scalar", lambda e: e.activation(out=ort[:], in_=ort[:], func=AF.Exp, scale=-0.5, bias=math.log(cm)), waits=[sigORT], inc=(acp, 1))
                                mi = c3["mo"] % 2
                                c3["mo"] += 1
                                MO = p3["mo"][mi]
                                for dvc in range(ndv):
                                    sigMO = P.op("vector", lambda e, dvc=dvc: e.scalar_tensor_tensor(out=mo[mi][:, dvc, :], in0=ob[:, dvc, :], scalar=ogs[l][:, mc + dvc:mc + dvc + 1], in1=ort[:], op0=ALU.mult, op1=ALU.mult),
                                                  waits=([sigORR] + MO.pw()) if dvc == 0 else [], inc=MO if dvc == ndv - 1 else None)
                                O.release(sigMO)
                                ORT.release(sigMO)
                                sigSt = P.op("sync", lambda e: e.dma_start(out=mixv[:, mc:mc + ndv, qt * 512:(qt + 1) * 512], in_=mo[mi][:, 0:ndv, :]), waits=[sigMO], inc=(MO.s, 16))
                                MO.release(sigSt)
                                p3stores.append(sigSt)
                                if qt == 7:
                                    for cc in range(mc, mc + ndv):
                                        vag_ = P.op("gpsimd", lambda e, cc=cc: e.collective_compute("AllGather", ALU.bypass, replica_groups=GRP4, ins=[mixo.ap()[cc * 128:(cc + 1) * 128, :]], outs=[mixall.ap()[cc * 512:(cc + 1) * 512, :]]),
                                                     waits=p3stores[-2:], inc=(s_ag, 1))
                                        mixag.append(vag_)
                                    issue_wag(l, 2)

                            tail.append(fin_tail)
                    mix_chunk += ndv
                    HD.release(sigPVlast)
                    nxtu = units[u + 1] if u + 1 < len(units) else None
                    if nxtu is not None and (nxtu[0] != "C" or nxtu[1] == 0):
                        p3["bias"].release(sigSTlast)
                        sigB = load_bias(u + 1)
                    elif nxtu is None:
                        p3["bias"].release(sigSTlast)
                for fn_ in tail:
                    fn_()
                tail = []
                vst = p3stores[-2:]
                P.op("sync", None, waits=vst)
                P.op("gpsimd", None, waits=[mixag[-1]])
                if debug and l == 0:
                    v = P.op("gpsimd", lambda e: e.dma_start(out=dbg["mixo"], in_=mixo.ap()), inc=(s_dbg, 16))
                    P.op("gpsimd", None, waits=[v])
                P.build()
            if stop("p3"):
                break

            with ExitStack() as ps:
                acta = ps.enter_context(nc.sbuf_tensor(f"p4a_L{l}", [128, 32, 1024], BF16))
                actb = ps.enter_context(nc.sbuf_tensor(f"p4b_L{l}", [128, 32, 1024], BF16))
                wr = [ps.enter_context(nc.sbuf_tensor(f"p4w{i}_L{l}", [128, 32, 128], BF16)) for i in range(4)]
                stage = [wr[2 + i][:].rearrange("p a b -> p (a b)").rearrange("p (q t) -> p q t", q=4) for i in range(2)]
                xin = [ps.enter_context(nc.sbuf_tensor(f"p4xin{i}_L{l}", [128, 1024], F32)) for i in range(3)]
                x1 = [ps.enter_context(nc.sbuf_tensor(f"p4x1{i}_L{l}", [128, 1024], F32)) for i in range(4)]
                xsq = [ps.enter_context(nc.sbuf_tensor(f"p4xsq{i}_L{l}", [128, 1024], BF16)) for i in range(2)]
                rt2 = ps.enter_context(nc.sbuf_tensor(f"p4rt2_L{l}", [128, 1024], F32))
                sgt = [ps.enter_context(nc.sbuf_tensor(f"p5sg{i}_L{l}", [128, 1024], F32)) for i in range(2)]
                if l == 0:
                    P.p4 = dict(stg=[Slot(P, f"p4stg{i}") for i in range(2)], a=Slot(P, "p4a", own=False),
                                w=[Slot(P, f"p4w{i}") for i in range(4)], xin=[Slot(P, f"p4xin{i}") for i in range(3)],
                                pm=[Slot(P, f"p4pm{i}") for i in range(2)], x1=[Slot(P, f"p4x1{i}", store=True) for i in range(4)],
                                xsq=[Slot(P, f"p4xsq{i}") for i in range(2)], stat=Slot(P, "p4stat", own=False), rt=Slot(P, "p4rt"),
                                b=Slot(P, "p4b", own=False), pg=Slot(P, "p5pg"), pu=Slot(P, "p5pu"), sg=[Slot(P, f"p5sg{i}") for i in range(2)],
                                act=Slot(P, "p5act", own=False), cnt=dict(w=0, xin=0, pm=0, x1=0, xsq=0, stg=0, sg=0))
                p4 = P.p4
                c4 = p4["cnt"]
                xres_st = st["xres_st"]
                wouta = W["wout_a"].ap().rearrange("(m p) c -> m p c", p=128)
                wga = W["wg_a"].ap().rearrange("(f p) c -> f p c", p=128)
                wua = W["wu_a"].ap().rearrange("(f p) c -> f p c", p=128)
                wda = W["wd_a"].ap().rearrange("(g m p) c -> g m p c", g=4, p=128)
                issue_wag(l, 1000)
                mixallv = mixall.ap().rearrange("(kc p) (q t) -> p kc q t", p=128, q=4)
                A, Bs, STAT, ACT_ = p4["a"], p4["b"], p4["stat"], p4["act"]

                sigA = None
                for kc in range(32):
                    si = c4["stg"] % 2
                    c4["stg"] += 1
                    SG = p4["stg"][si]
                    sigSG = P.op("sync", lambda e, si=si, kc=kc: e.dma_start(out=stage[si], in_=mixallv[:, kc, :, :]), waits=SG.pw(), inc=SG)
                    v = P.op("vector", lambda e, si=si, kc=kc: e.tensor_scalar(out=acta[:, kc, :], in0=stage[si][:, 0, :], scalar1=ohs[:, 0:1], scalar2=0.0, op0=ALU.mult, op1=ALU.add),
                             waits=[sigSG] + (A.pw() if kc == 0 else []), inc=(dvp, 1))
                    for q in range(1, 4):
                        v = P.op("vector", lambda e, si=si, kc=kc, q=q: e.scalar_tensor_tensor(out=acta[:, kc, :], in0=stage[si][:, q, :], scalar=ohs[:, q:q + 1], in1=acta[:, kc, :], op0=ALU.mult, op1=ALU.add),
                                 waits=[v], inc=(dvp, 1))
                    SG.release(v)
                    sigA = v
                p4["w"][2].release(sigA)
                p4["w"][3].release(sigA)

                def load_w(src_ap, call, ncols=4096):
                    i = c4["w"] % 4
                    c4["w"] += 1
                    WSl = p4["w"][i]
                    sig = P.op("sync", lambda e, i=i, src_ap=src_ap, ncols=ncols: e.dma_start(out=wr[i][:].rearrange("p a b -> p (a b)")[:, 0:ncols], in_=src_ap), waits=WSl.pw() + [wag_sig(l, call)], inc=WSl)
                    return i, sig

                def load_xin(m, src, wait_store):
                    i = c4["xin"] % 3
                    c4["xin"] += 1
                    XI = p4["xin"][i]
                    sig = P.op("sync", lambda e, i=i, m=m, src=src: e.dma_start(out=xin[i][:], in_=src[m]), waits=XI.pw() + [wait_store], inc=XI)
                    return i, sig

                wq = [load_w(wouta[0], ("wout", 0)), load_w(wouta[1], ("wout", 0)), load_w(wouta[2], ("wout", 0))]
                xq = [load_xin(0, xsrc, xres_st[0] if l > 0 else None), load_xin(1, xsrc, xres_st[1] if l > 0 else None)]
                deferred = None
                sigSTAT = None
                for m in range(32):
                    wi, sigW = wq.pop(0)
                    if m + 3 < 32:
                        wq.append(load_w(wouta[m + 3], ("wout", (m + 3) // 4)))
                    xi, sigXI = xq.pop(0)
                    if m + 2 < 32:
                        xq.append(load_xin(m + 2, xsrc, xres_st[m + 2] if l > 0 else None))
                    WSl, XI = p4["w"][wi], p4["xin"][xi]
                    pi_ = c4["pm"] % 2
                    c4["pm"] += 1
                    PM = p4["pm"][pi_]
                    for kc in range(32):
                        for th in range(2):
                            w_ = []
                            if kc == 0 and th == 0:
                                w_ = PM.pw() + [sigW] + ([sigA] if m == 0 else [])
                            sigPM = P.op("tensor", lambda e, wi=wi, kc=kc, th=th, pi_=pi_: e.matmul(psb[2 * pi_ + th][:], lhsT=wr[wi][:, kc, :], rhs=acta[:, kc, th * 512:(th + 1) * 512], start=(kc == 0), stop=(kc == 31)),
                                          waits=w_, inc=PM if (kc == 31 and th == 1) else None)
                    WSl.release(sigPM)
                    if deferred is not None:
                        deferred()
                        deferred = None
                    x1i = c4["x1"] % 4
                    c4["x1"] += 1
                    X1 = p4["x1"][x1i]
                    for th in range(2):
                        sigX1 = P.op("vector", lambda e, th=th, pi_=pi_, xi=xi, x1i=x1i: e.tensor_tensor(out=x1[x1i][:, th * 512:(th + 1) * 512], in0=psb[2 * pi_ + th][:], in1=xin[xi][:, th * 512:(th + 1) * 512], op=ALU.add),
                                      waits=([sigPM, sigXI] + X1.pw()) if th == 0 else [], inc=X1 if th == 1 else None)
                    PM.release(sigX1)
                    XI.release(sigX1)
                    sigSt = P.op("sync", lambda e, x1i=x1i, m=m: e.dma_start(out=xres.ap()[m], in_=x1[x1i][:]), waits=[sigX1], inc=(X1.s, 16))
                    xres_st[m] = sigSt
                    X1.release(sigSt)
                    qi = c4["xsq"] % 2
                    c4["xsq"] += 1
                    XS = p4["xsq"][qi]
                    sigXS = P.op("scalar", lambda e, x1i=x1i, qi=qi: e.activation(out=xsq[qi][:], in_=x1[x1i][:], func=AF.Square), waits=[sigX1] + XS.pw(), inc=XS)
                    X1.release(sigXS)

                    def stat_mm(m=m, qi=qi, XS=XS, sigXS=sigXS):
                        for th in range(2):
                            sig = P.op("tensor", lambda e, th=th: e.matmul(psb[6 + th][:], lhsT=ones[:], rhs=xsq[qi][:, th * 512:(th + 1) * 512], start=(m == 0), stop=(m == 31)),
                                       waits=([sigXS] + (STAT.pw() if m == 0 else [])) if th == 0 else [], inc=(pep, 1) if th == 1 else None)
                        XS.release(sig)
                        return sig
                    deferred = stat_mm
                sigSTAT = deferred()
                RT = p4["rt"]
                for th in range(2):
                    sigRT = P.op("scalar", lambda e, th=th: e.activation(out=rt2[:, th * 512:(th + 1) * 512], in_=psb[6 + th][:], func=AF.Ln, scale=1.0 / D, bias=EPS),
                                 waits=([sigSTAT] + RT.pw()) if th == 0 else [], inc=RT if th == 1 else None)
                STAT.release(sigRT)
                sigRR = P.op("scalar", lambda e: e.activation(out=rt2[:], in_=rt2[:], func=AF.Exp, scale=-0.5), waits=[sigRT], inc=(acp, 1))
                xq = [load_xin(0, xres.ap(), xres_st[0]), load_xin(1, xres.ap(), xres_st[1])]
                sigB2 = None
                for m in range(32):
                    xi, sigXI = xq.pop(0)
                    if m + 2 < 32:
                        xq.append(load_xin(m + 2, xres.ap(), xres_st[m + 2]))
                    XI = p4["xin"][xi]
                    sigB2 = P.op("vector", lambda e, m=m, xi=xi, l=l: e.scalar_tensor_tensor(out=actb[:, m, :], in0=xin[xi][:], scalar=g2s[l][:, m:m + 1], in1=rt2[:], op0=ALU.mult, op1=ALU.mult),
                                 waits=[sigXI] + (([sigRR] + Bs.pw()) if m == 0 else []), inc=(dvp, 1))
                    XI.release(sigB2)
                RT.release(sigB2)
                if l + 1 < L:
                    P.op("gpsimd", None, waits=[sigB2])
                    for piece in range(3):
                        cast_win(l + 1, piece)
                    issue_wag(l + 1, 50)
                if debug and l == 0:
                    v = P.op("gpsimd", lambda e: e.dma_start(out=dbg["x1"], in_=xres.ap()), waits=xres_st[28:32], inc=(s_dbg, 16))
                    P.op("gpsimd", None, waits=[v])
                if stop("p4"):
                    P.op("sync", None, waits=xres_st[28:32])
                    P.op("vector", None, waits=[sigB2])
                    P.build()
                    break

                PG, PU = p4["pg"], p4["pu"]
                sigPU = None
                sigPM = None
                for gi, (f0, nf) in enumerate(FGROUPS):
                    lastg = (gi == len(FGROUPS) - 1)
                    wq = [(load_w(wga[f0], ("wg", f0 // 4)), load_w(wua[f0], ("wu", f0 // 4)))]
                    sigACT = None
                    for fi in range(nf):
                        f = f0 + fi
                        (wgi, sigWg), (wui, sigWu) = wq.pop(0)
                        if fi + 1 < nf:
                            wq.append((load_w(wga[f + 1], ("wg", (f + 1) // 4)), load_w(wua[f + 1], ("wu", (f + 1) // 4))))
                        sigs = {}
                        for (wi, sigW_, PSL, base) in ((wgi, sigWg, PG, 0), (wui, sigWu, PU, 2)):
                            for kc in range(32):
                                for th in range(2):
                                    w_ = []
                                    if kc == 0 and th == 0:
                                        w_ = PSL.pw() + [sigW_] + ([sigB2] if (gi == 0 and fi == 0 and base == 0) else [])
                                    sig = P.op("tensor", lambda e, wi=wi, kc=kc, th=th, base=base: e.matmul(psb[base + th][:], lhsT=wr[wi][:, kc, :], rhs=actb[:, kc, th * 512:(th + 1) * 512], start=(kc == 0), stop=(kc == 31)),
                                               waits=w_, inc=PSL if (kc == 31 and th == 1) else None)
                            p4["w"][wi].release(sig)
                            sigs[base] = sig
                        sigPG, sigPU = sigs[0], sigs[2]
                        si = c4["sg"] % 2
                        c4["sg"] += 1
                        SGS = p4["sg"][si]
                        for th in range(2):
                            sigSG = P.op("scalar", lambda e, th=th, si=si: e.activation(out=sgt[si][:, th * 512:(th + 1) * 512], in_=psb[th][:], func=AF.Silu),
                                         waits=([sigPG] + SGS.pw()) if th == 0 else [], inc=SGS if th == 1 else None)
                        PG.release(sigSG)
                        for th in range(2):
                            sigACT = P.op("vector", lambda e, th=th, si=si, fi=fi: e.tensor_tensor(out=acta[:, fi, th * 512:(th + 1) * 512], in0=psb[2 + th][:], in1=sgt[si][:, th * 512:(th + 1) * 512], op=ALU.mult),
                                          waits=([sigPU, sigSG] + (ACT_.pw() + A.pw() if fi == 0 else [])) if th == 0 else [], inc=(dvp, 1) if th == 1 else None)
                        PU.release(sigACT)
                        SGS.release(sigACT)

                    def load_wd(m, gi=gi, nf=nf):
                        return load_w(wda[gi][m][:, 0:nf * 128], ("wd", gi * 8 + m // 4), ncols=nf * 128)
                    wq = [load_wd(0), load_wd(1), load_wd(2)]
                    xq = [load_xin(0, xres.ap(), xres_st[0]), load_xin(1, xres.ap(), xres_st[1])]
                    for m in range(32):
                        wi, sigW = wq.pop(0)
                        if m + 3 < 32:
                            wq.append(load_wd(m + 3))
                        xi, sigXI = xq.pop(0)
                        if m + 2 < 32:
                            xq.append(load_xin(m + 2, xres.ap(), xres_st[m + 2]))
                        WSl, XI = p4["w"][wi], p4["xin"][xi]
                        pi_ = c4["pm"] % 2
                        c4["pm"] += 1
                        PM = p4["pm"][pi_]
                        for fi in range(nf):
                            for th in range(2):
                                w_ = []
                                if fi == 0 and th == 0:
                                    w_ = PM.pw() + [sigW] + ([sigACT] if m == 0 else [])
                                sigPM = P.op("tensor", lambda e, wi=wi, fi=fi, th=th, pi_=pi_, nf=nf: e.matmul(psb[4 + 2 * pi_ + th][:], lhsT=wr[wi][:, fi, :], rhs=acta[:, fi, th * 512:(th + 1) * 512], start=(fi == 0), stop=(fi == nf - 1)),
                                              waits=w_, inc=PM if (fi == nf - 1 and th == 1) else None)
                        WSl.release(sigPM)
                        x1i = c4["x1"] % 4
                        c4["x1"] += 1
                        X1 = p4["x1"][x1i]
                        for th in range(2):
                            sigX1 = P.op("vector", lambda e, th=th, pi_=pi_, xi=xi, x1i=x1i: e.tensor_tensor(out=x1[x1i][:, th * 512:(th + 1) * 512], in0=psb[4 + 2 * pi_ + th][:], in1=xin[xi][:, th * 512:(th + 1) * 512], op=ALU.add),
                                          waits=([sigPM, sigXI] + X1.pw()) if th == 0 else [], inc=X1 if th == 1 else None)
                        PM.release(sigX1)
                        XI.release(sigX1)
                        dst = xdst if lastg else xres.ap()
                        sigSt = P.op("sync", lambda e, x1i=x1i, m=m, dst=dst: e.dma_start(out=dst[m], in_=x1[x1i][:]), waits=[sigX1], inc=(X1.s, 16))
                        xres_st[m] = sigSt
                        X1.release(sigSt)
                    ACT_.release(sigPM)
                A.release(sigPM)
                Bs.release(sigPU)
                P.op("sync", None, waits=xres_st[28:32])
                P.build()
    return nc


def _alibi(n):
    return np.exp2(-8.0 * np.arange(1, n + 1, dtype=np.float64) / n)


def _bias_A(g):
    slope = _alibi(4)[g]
    p = np.arange(128)[:, None]
    c = np.arange(8064)[None, :]
    return (-slope * np.abs(c - 3968 - p)).astype(np.float32)


def _bias_C(g):
    out = np.empty((128, 3, 2944), np.float32)
    p = np.arange(128)[:, None]
    c = np.arange(2944)[None, :]
    d = c - 1408 - p
    ad = np.abs(d)
    cnt = (ad <= 64).astype(np.int64) + ((d % 4 == 0) & (ad <= 256)) + ((d % 16 == 0) & (ad <= 1024))
    with np.errstate(divide="ignore"):
        lc = np.log(cnt.astype(np.float64))
    for i in range(3):
        slope = _alibi(12)[3 * g + i]
        v = -slope * ad + lc
        out[:, i, :] = np.where(cnt > 0, v, NEG).astype(np.float32)
    return out.reshape(128, 3 * 2944)


def _bias_B(rpb_l, g):
    out = np.full((3, 128, 20, 512), NEG, np.float32)
    pidx = np.arange(128)
    krl, kc = pidx // 64, pidx % 64
    fidx = np.arange(512)
    qrl, qc = fidx // 64, fidx % 64
    cstart = np.clip(qc - 8, 0, 48)
    colok = (kc[:, None] >= cstart[None, :]) & (kc[:, None] < cstart[None, :] + 16)
    dc = np.clip(kc[:, None] - qc[None, :] + 15, 0, 30)
    cases = [(1, 4 * 1 - 2 + r, r) for r in range(8)] + [(0, kb, 8 + kb) for kb in range(6)] + [(7, kb, 14 + kb - 26) for kb in range(26, 32)]
    for (qt, kb, ti) in cases:
        kr = 2 * kb + krl
        qr = 8 * qt + qrl
        rstart = np.clip(qr - 4, 0, 56)
        rowok = (kr[:, None] >= rstart[None, :]) & (kr[:, None] < rstart[None, :] + 8)
        dr = np.clip(kr[:, None] - qr[None, :] + 7, 0, 14)
        ok = rowok & colok
        for i in range(3):
            vals = rpb_l[3 * g + i][dr, dc]
            out[i, :, ti, :] = np.where(ok, vals, NEG)
    return out.reshape(3, 128, 20 * 512)


def _block_w(Wm, kcn, nb):
    return np.ascontiguousarray(Wm.reshape(kcn, 128, nb, 128).transpose(2, 1, 0, 3))


def _qk_cols(g):
    cols = []
    for m in range(2):
        cols.append(np.arange(g * 256 + m * 128, g * 256 + (m + 1) * 128))
    for m in range(2):
        cols.append(1024 + np.arange(g * 256 + m * 128, g * 256 + (m + 1) * 128))
    for i in range(3):
        cols.append(3072 + (3 * g + i) * 128 + np.arange(128))
    for i in range(3):
        cols.append(4608 + (3 * g + i) * 128 + np.arange(128))
    for i in range(3):
        cols.append(7680 + (3 * g + i) * 128 + np.arange(128))
    for i in range(3):
        cols.append(9216 + (3 * g + i) * 128 + np.arange(128))
    return cols


def _v_cols(g):
    c = [2048 + g * 256 + np.arange(256)]
    for i in range(3):
        c.append(6144 + (3 * g + i) * 128 + np.arange(128))
    for i in range(3):
        c.append(10752 + (3 * g + i) * 128 + np.arange(128))
    return np.concatenate(c)


def _wout_perm():
    rows = []
    for c in range(8):
        for r in range(4):
            if c < 2:
                rows.append(r * 256 + c * 128 + np.arange(128))
            elif c < 5:
                rows.append(1024 + (3 * r + (c - 2)) * 128 + np.arange(128))
            else:
                rows.append(2560 + (3 * r + (c - 5)) * 128 + np.arange(128))
    return np.concatenate(rows)


def _col128(v):
    return np.ascontiguousarray(v.reshape(-1, 128).T.astype(np.float32))


def prepare_inputs(inp, layers):
    f32 = np.float32
    maps = [dict() for _ in range(NCORES)]
    x = np.asarray(inp["x"], f32)
    for c in range(NCORES):
        b, g = c // 4, c % 4
        maps[c]["xT"] = np.ascontiguousarray(x[b, g * 1024:(g + 1) * 1024, :].T).reshape(32, 128, 1024)
        maps[c]["biasA"] = _bias_A(g)
        maps[c]["biasC"] = _bias_C(g)
        oh = np.zeros((128, 4), f32)
        oh[:, g] = 1.0
        maps[c]["onehot"] = oh
    perm = _wout_perm()
    for li, l in enumerate(layers):
        w_in = np.asarray(inp["w_in"][l], f32)
        def shard_blocks(blk, npad):
            nb, _, X = blk.shape
            out = np.zeros((4, npad // 4, 128, X), f32)
            for r in range(4):
                sel = blk[r::4]
                out[r, :sel.shape[0]] = sel
            return out.reshape(4, npad // 4 * 128, X)
        wout_blk = shard_blocks(_block_w(np.asarray(inp["w_out"][l], f32)[perm, :], 32, 32).reshape(32, 128, 4096), 32)
        wg_blk = shard_blocks(_block_w(np.asarray(inp["w_gate"][l], f32), 32, NF).reshape(NF, 128, 4096), 88)
        wu_blk = shard_blocks(_block_w(np.asarray(inp["w_up"][l], f32), 32, NF).reshape(NF, 128, 4096), 88)
        wd_full = _block_w(np.asarray(inp["w_down"][l], f32), NF, 32).reshape(32, 128, NF, 128)
        wd_units = np.zeros((4, 32, 128, 2816), f32)
        for gi, (f0, nf) in enumerate(FGROUPS):
            wd_units[gi, :, :, :nf * 128] = wd_full[:, :, f0:f0 + nf, :].reshape(32, 128, nf * 128)
        wd_blk = np.stack([wd_units[:, r::4].reshape(32, 128, 2816) for r in range(4)]).reshape(4, 32 * 128, 2816)
        lamv = np.stack([inp["lambda_q1"][l], inp["lambda_k1"][l], inp["lambda_q2"][l], inp["lambda_k2"][l]]).astype(f32)
        lamv = np.ascontiguousarray(np.broadcast_to(lamv[None], (128, 4, 128)))
        g1 = _col128(np.asarray(inp["norm1_g"][l]))
        g2 = _col128(np.asarray(inp["norm2_g"][l]))
        qkg = np.stack([inp["a_q_g"][l]] * 2 + [inp["a_k_g"][l]] * 2 + [inp["b_q_g"][l]] * 3 + [inp["b_k_g"][l]] * 3
                       + [inp["c_q_g"][l]] * 3 + [inp["c_k_g"][l]] * 3, axis=1).astype(f32)
        og = np.stack([inp["a_out_g"][l][0:128], inp["a_out_g"][l][128:256]] + [inp["b_out_g"][l]] * 3 + [inp["c_out_g"][l]] * 3, axis=1).astype(f32)
        for g in range(4):
            cols = _qk_cols(g)
            wqk = np.stack([w_in[:, cc].reshape(32, 128, 128).transpose(1, 0, 2) for cc in cols]).reshape(16 * 128, 4096)
            wv = np.ascontiguousarray(w_in[:, _v_cols(g)].reshape(32, 128, 1024).transpose(1, 0, 2)).reshape(128, 32768)
            bB = _bias_B(np.asarray(inp["b_rpb"][l], f32), g)
            for b in range(2):
                m = maps[b * 4 + g]
                m[f"wqk_{li}"] = wqk
                m[f"wv_{li}"] = wv
                m[f"biasB_{li}"] = bB
        for c in range(NCORES):
            m = maps[c]
            m[f"g1_{li}"] = g1
            m[f"g2_{li}"] = g2
            m[f"qkg_{li}"] = np.ascontiguousarray(qkg)
            m[f"og_{li}"] = np.ascontiguousarray(og)
            m[f"lamv_{li}"] = lamv
            m[f"wout_{li}"] = wout_blk[c % 4]
            m[f"wg_{li}"] = wg_blk[c % 4]
            m[f"wu_{li}"] = wu_blk[c % 4]
            m[f"wd_{li}"] = wd_blk[c % 4]
    return maps


_NC_CACHE = {}


def _get_nc(n_layers, first, debug=False):
    key = (n_layers, first, debug)
    if key not in _NC_CACHE:
        _NC_CACHE[key] = build_program(n_layers, first, debug)
    return _NC_CACHE[key]


def assemble_output(res):
    out = np.empty((NB, S, D), np.float32)
    for c in range(NCORES):
        b, g = c // 4, c % 4
        out[b, g * 1024:(g + 1) * 1024, :] = res[c]["yT"].reshape(D, 1024).T
    return out


def kernel(**inputs):
    nc = _get_nc(DEPTH, 0)
    maps = prepare_inputs(inputs, list(range(DEPTH)))
    res = run_bass_kernel_spmd(nc, maps, core_ids=list(range(NCORES)))
    return assemble_output(res.results)
```

```python
import math
from contextlib import ExitStack

import numpy as np
import concourse.bass as bass
import concourse.mybir as mybir
from concourse.bass_utils import run_bass_kernel_spmd

F32 = mybir.dt.float32
BF16 = mybir.dt.bfloat16
AF = mybir.ActivationFunctionType
ALU = mybir.AluOpType

D = 4096
S = 4096
NB = 2
DEPTH = 2
DFF = 11008
NF = DFF // 128
EPS = 1e-6
NEG = -30000.0
SCALE = 128.0 ** -0.5
FGROUPS = [(0, 22), (22, 22), (44, 21), (65, 21)]
NCORES = 8
ENGS = ["sync", "scalar", "vector", "gpsimd", "tensor"]
GRP4 = [[0, 1, 2, 3], [4, 5, 6, 7]]
GRP8 = [list(range(8))]


def lam_init(l):
    return 0.8 - 0.6 * math.exp(-0.3 * l)


class Prog:
    def __init__(self, nc, es):
        self.nc = nc
        self.es = es
        self.semcnt = {}
        self.semh = {}
        self.ops = {e: [] for e in ENGS}

    def sem(self, name):
        assert name not in self.semcnt, name
        self.semcnt[name] = 0
        self.semh[name] = self.es.enter_context(self.nc.semaphore(name))
        return name

    def op(self, eng, fn, waits=(), inc=None):
        sig = None
        if inc is not None:
            if isinstance(inc, Slot):
                inc = (inc.f, 16 if eng in ("sync", "gpsimd_dma") else 1)
            s, a = inc
            self.semcnt[s] += a
            sig = (s, self.semcnt[s])
        if eng == "gpsimd_dma":
            eng = "gpsimd"
        wm = {}
        for w in waits:
            if w is None:
                continue
            s_, v_ = w
            if v_ > wm.get(s_, 0):
                wm[s_] = v_
        self.ops[eng].append((tuple(wm.items()), fn, inc))
        return sig

    def build(self):
        h = self.semh
        ops = self.ops
        with self.nc.Block() as block:
            def mk(engname):
                def body(eng):
                    for waits, fn, inc in ops[engname]:
                        for (s, v) in waits:
                            eng.wait_ge(h[s], v)
                        if fn is None:
                            continue
                        ins = fn(eng)
                        if inc is not None:
                            ins.then_inc(h[inc[0]], inc[1])
                return body
            for e in ENGS:
                if ops[e]:
                    getattr(block, e)(mk(e))
        self.ops = {e: [] for e in ENGS}
        self.nc.all_engine_barrier()


class Slot:
    def __init__(self, P, name, own=True, store=False):
        self.P = P
        self.f = P.sem(name) if own else None
        self.s = P.sem(name + "S") if store else None
        self.rel = []

    def pw(self):
        w = self.rel
        self.rel = []
        return list(w)

    def release(self, sig):
        assert sig is not None
        self.rel.append(sig)


def build_program(n_layers, first_layer_index=0, debug=False, stop_after=None):
    nc = bass.Bass("TRN2", target_bir_lowering=False)
    L = n_layers

    def din(name, shape, dt=F32):
        return nc.dram_tensor(name, list(shape), dt, kind="ExternalInput").ap()

    def dscr(name, shape, dt):
        return nc.dram_tensor(name, list(shape), dt)

    xT = din("xT", [32, 128, 1024])
    yT = nc.dram_tensor("yT", [32, 128, 1024], F32, kind="ExternalOutput").ap()
    biasA = din("biasA", [128, 8064])
    biasC = din("biasC", [128, 3 * 2944])
    onehot = din("onehot", [128, 4])
    lw = []
    for l in range(L):
        lw.append(dict(
            g1=din(f"g1_{l}", [128, 32]), g2=din(f"g2_{l}", [128, 32]),
            wqk=din(f"wqk_{l}", [16 * 128, 4096]), wv=din(f"wv_{l}", [128, 32768]),
            wout=din(f"wout_{l}", [8 * 128, 4096]), wg=din(f"wg_{l}", [22 * 128, 4096]),
            wu=din(f"wu_{l}", [22 * 128, 4096]), wd=din(f"wd_{l}", [32 * 128, 2816]),
            wqk_b=dscr(f"wqkb_{l}", [16 * 128, 4096], BF16), wv_b=dscr(f"wvb_{l}", [128, 32768], BF16),
            qkg=din(f"qkg_{l}", [128, 16]), og=din(f"og_{l}", [128, 8]),
            lamv=din(f"lamv_{l}", [128, 4, 128]), biasB=din(f"biasB_{l}", [3, 128, 20 * 512]),
            wout_b=dscr(f"woutb_{l}", [8 * 128, 4096], BF16), wout_a=dscr(f"wouta_{l}", [32 * 128, 4096], BF16),
            wg_b=dscr(f"wgb_{l}", [22 * 128, 4096], BF16), wg_a=dscr(f"wga_{l}", [88 * 128, 4096], BF16),
            wu_b=dscr(f"wub_{l}", [22 * 128, 4096], BF16), wu_a=dscr(f"wua_{l}", [88 * 128, 4096], BF16),
            wd_b=dscr(f"wdb_{l}", [32 * 128, 2816], BF16), wd_a=dscr(f"wda_{l}", [128 * 128, 2816], BF16),
        ))
    hb = dscr("hb", [8 * 128 * 2 * 4, 512], BF16)
    hall = dscr("hall", [8 * 4 * 128 * 2 * 4, 512], BF16)
    qk = dscr("qk", [16, 128, 4096], BF16)
    vv = dscr("vv", [4096, 1024], BF16)
    mixo = dscr("mixo", [1024, 4096], BF16)
    mixall = dscr("mixall", [8 * 4 * 128, 4096], BF16)
    xres = dscr("xres", [32, 128, 1024], F32)
    dbg = {}
    if debug:
        dbg["hb"] = nc.dram_tensor("d_hb", [8192, 512], BF16, kind="ExternalOutput").ap()
        dbg["qk"] = nc.dram_tensor("d_qk", [16, 128, 4096], BF16, kind="ExternalOutput").ap()
        dbg["vv"] = nc.dram_tensor("d_vv", [4096, 1024], BF16, kind="ExternalOutput").ap()
        dbg["mixo"] = nc.dram_tensor("d_mixo", [1024, 4096], BF16, kind="ExternalOutput").ap()
        dbg["x1"] = nc.dram_tensor("d_x1", [32, 128, 1024], F32, kind="ExternalOutput").ap()

    def stop(name):
        return stop_after is not None and stop_after == name

    with ExitStack() as es:
        P = Prog(nc, es)
        ones = es.enter_context(nc.sbuf_tensor("ones", [128, 128], BF16))
        g1s = [es.enter_context(nc.sbuf_tensor(f"g1s{l}", [128, 32], F32)) for l in range(L)]
        g2s = [es.enter_context(nc.sbuf_tensor(f"g2s{l}", [128, 32], F32)) for l in range(L)]
        qkgs = [es.enter_context(nc.sbuf_tensor(f"qkgs{l}", [128, 16], F32)) for l in range(L)]
        ogs = [es.enter_context(nc.sbuf_tensor(f"ogs{l}", [128, 8], F32)) for l in range(L)]
        nlam = [es.enter_context(nc.sbuf_tensor(f"nlam{l}", [128, 1], F32)) for l in range(L)]
        ohs = es.enter_context(nc.sbuf_tensor("ohs", [128, 4], F32))
        psb = [es.enter_context(nc.psum_tensor(f"psb{i}", [128, 512], F32)) for i in range(8)]

        s_c = P.sem("constld")
        s_cv = P.sem("constv")
        s_wc = [P.sem(f"wcast{l}") for l in range(L)]
        s_wag = [P.sem(f"wag{l}") for l in range(L)]
        s_wcc = P.sem("wcc")
        s_ag = P.sem("ag")
        s_dbg = P.sem("dbg") if debug else None
        dvp = P.sem("dvp")
        pep = P.sem("pep")
        acp = P.sem("acp")

        def cast_win(l, piece):
            if piece < 2:
                src = lw[l]["wqk"][piece * 1024:(piece + 1) * 1024, :]
                dst = lw[l]["wqk_b"].ap()[piece * 1024:(piece + 1) * 1024, :]
            else:
                src = lw[l]["wv"]
                dst = lw[l]["wv_b"].ap()
            v = P.op("gpsimd", lambda e, src=src, dst=dst: e.dma_start(out=dst, in_=src, max_dma_last_dim=8192), inc=(s_wc[l], 16))
            P.op("gpsimd", None, waits=[v])

        wag_calls = []
        wag_idx = []
        wag_issued = [0] * L
        for l in range(L):
            calls = [("wout", j) for j in range(8)]
            done = set()
            for gi, (f0, nf) in enumerate(FGROUPS):
                for j in range(f0 // 4, (f0 + nf - 1) // 4 + 1):
                    if j not in done:
                        done.add(j)
                        calls += [("wg", j), ("wu", j)]
                calls += [("wd", gi * 8 + j) for j in range(8)]
            wag_calls.append(calls)
            wag_idx.append({c: i for i, c in enumerate(calls)})

        def issue_wag(l, n):
            for _ in range(n):
                i = wag_issued[l]
                if i >= len(wag_calls[l]):
                    return
                k, j = wag_calls[l][i]
                wag_issued[l] += 1
                v = P.op("gpsimd", lambda e, l=l, k=k, j=j: e.dma_start(out=lw[l][k + "_b"].ap()[j * 128:(j + 1) * 128, :], in_=lw[l][k][j * 128:(j + 1) * 128, :], max_dma_last_dim=8192), inc=(s_wcc, 16))
                P.op("gpsimd", None, waits=[v])
                P.op("gpsimd", lambda e, l=l, k=k, j=j: e.collective_compute("AllGather", ALU.bypass, replica_groups=GRP4,
                     ins=[lw[l][k + "_b"].ap()[j * 128:(j + 1) * 128, :]], outs=[lw[l][k + "_a"].ap()[j * 512:(j + 1) * 512, :]]), inc=(s_wag[l], 1))

        def wag_sig(l, call):
            return (s_wag[l], wag_idx[l][call] + 1)

        with ExitStack() as ps:
            lamt = ps.enter_context(nc.sbuf_tensor("lamt", [128, 4, 128], F32))
            lamp = ps.enter_context(nc.sbuf_tensor("lamp", [128, 2, 128], F32))
            lams = ps.enter_context(nc.sbuf_tensor("lams", [128, 2], F32))
            lame = ps.enter_context(nc.sbuf_tensor("lame", [128, 2], F32))
            P.op("vector", lambda e: e.memset(ones[:], 1.0), inc=(s_cv, 1))
            for l in range(L):
                for (dst, src) in ((g1s[l], lw[l]["g1"]), (g2s[l], lw[l]["g2"]), (qkgs[l], lw[l]["qkg"]), (ogs[l], lw[l]["og"])):
                    P.op("sync", lambda e, dst=dst, src=src: e.dma_start(out=dst[:], in_=src), inc=(s_c, 16))
            P.op("sync", lambda e: e.dma_start(out=ohs[:], in_=onehot), inc=(s_c, 16))
            cast_win(0, 0)
            prev = None
            for l in range(L):
                v = P.op("sync", lambda e, l=l: e.dma_start(out=lamt[:], in_=lw[l]["lamv"]), waits=[prev], inc=(s_c, 16))
                li = lam_init(l + first_layer_index)
                v1 = P.op("vector", lambda e: e.tensor_tensor(out=lamp[:, 0, :], in0=lamt[:, 0, :], in1=lamt[:, 1, :], op=ALU.mult), waits=[v], inc=(s_cv, 1))
                v2 = P.op("vector", lambda e: e.tensor_tensor(out=lamp[:, 1, :], in0=lamt[:, 2, :], in1=lamt[:, 3, :], op=ALU.mult), inc=(s_cv, 1))
                v3 = P.op("vector", lambda e: e.tensor_reduce(out=lams[:], in_=lamp[:], axis=mybir.AxisListType.X, op=ALU.add), waits=[v2], inc=(s_cv, 1))
                v4 = P.op("scalar", lambda e: e.activation(out=lame[:], in_=lams[:], func=AF.Exp), waits=[v3], inc=(s_cv, 1))
                prev = P.op("vector", lambda e, l=l, li=li: e.scalar_tensor_tensor(out=nlam[l][:], in0=lame[:, 1:2], scalar=-li, in1=lame[:, 0:1], op0=ALU.add, op1=ALU.subtract), waits=[v4], inc=(s_cv, 1))
            P.op("vector", None, waits=[prev])
            P.op("sync", None, waits=[(s_c, P.semcnt[s_c]), prev])
            P.build()

        st = dict(xres_st=[None] * 32)

        for l in range(L):
            W = lw[l]
            labs = l + first_layer_index
            last = (l == L - 1)
            xsrc = xT if l == 0 else xres.ap()
            xdst = yT if last else xres.ap()

            with ExitStack() as ps:
                xs = [ps.enter_context(nc.sbuf_tensor(f"p1x{i}_L{l}", [128, 32, 512], F32)) for i in range(2)]
                hs = [ps.enter_context(nc.sbuf_tensor(f"p1h{i}_L{l}", [128, 32, 512], BF16)) for i in range(2)]
                sq = [ps.enter_context(nc.sbuf_tensor(f"p1sq{i}_L{l}", [128, 4, 512], BF16)) for i in range(2)]
                rt = [ps.enter_context(nc.sbuf_tensor(f"p1rt{i}_L{l}", [128, 512], F32)) for i in range(2)]
                if l == 0:
                    P.p1 = dict(x=[Slot(P, f"p1x{i}") for i in range(2)], sq=[Slot(P, f"p1sq{i}") for i in range(2)],
                                ps=[Slot(P, f"p1ps{i}") for i in range(2)], rt=[Slot(P, f"p1rt{i}") for i in range(2)],
                                h=[Slot(P, f"p1h{i}", store=True) for i in range(2)])
                p1 = P.p1
                hbv = hb.ap().rearrange("(j p h a) t -> j p h a t", j=8, p=128, h=2, a=4)
                stores = []
                for th in range(2):
                    b = th
                    X, SQ, PS_, RT, H = p1["x"][b], p1["sq"], p1["ps"][b], p1["rt"][b], p1["h"][b]
                    sigX = P.op("sync", lambda e, b=b, th=th: e.dma_start(out=xs[b][:], in_=xsrc[:, :, th * 512:(th + 1) * 512].rearrange("fc p t -> p fc t")), waits=X.pw(), inc=X)
                    sigPS = None
                    for j in range(8):
                        sb = j % 2
                        sigSQ = P.op("scalar", lambda e, b=b, j=j, sb=sb: e.activation(out=sq[sb][:], in_=xs[b][:, 4 * j:4 * j + 4, :], func=AF.Square), waits=[sigX] + SQ[sb].pw(), inc=SQ[sb])
                        for i in range(4):
                            fc = 4 * j + i
                            sig = P.op("tensor", lambda e, b=b, sb=sb, i=i, fc=fc: e.matmul(psb[b][:], lhsT=ones[:], rhs=sq[sb][:, i, :], start=(fc == 0), stop=(fc == 31)),
                                       waits=([sigSQ] if i == 0 else []) + (PS_.pw() if fc == 0 else []), inc=PS_ if i == 3 else None)
                        SQ[sb].release(sig)
                        sigPS = sig
                    sigRT = P.op("scalar", lambda e, b=b: e.activation(out=rt[b][:], in_=psb[b][:], func=AF.Ln, scale=1.0 / D, bias=EPS), waits=[sigPS] + RT.pw(), inc=RT)
                    PS_.release(sigRT)
                    sigRR = P.op("scalar", lambda e, b=b: e.activation(out=rt[b][:], in_=rt[b][:], func=AF.Exp, scale=-0.5), waits=[sigRT], inc=(acp, 1))
                    for fc in range(32):
                        sigH = P.op("vector", lambda e, b=b, fc=fc, l=l: e.scalar_tensor_tensor(out=hs[b][:, fc, :], in0=xs[b][:, fc, :], scalar=g1s[l][:, fc:fc + 1], in1=rt[b][:], op0=ALU.mult, op1=ALU.mult),
                                    waits=([sigRR] + H.pw()) if fc == 0 else [], inc=H if fc == 31 else None)
                    RT.release(sigH)
                    X.release(sigH)
                    for j in range(8):
                        sigSt = P.op("sync", lambda e, b=b, th=th, j=j: e.dma_start(out=hbv[j][:, th, :, :], in_=hs[b][:, 4 * j:4 * j + 4, :]), waits=[sigH] if j == 0 else [], inc=(H.s, 16))
                    H.release(sigSt)
                    stores.append(sigSt)
                P.op("sync", None, waits=stores)
                for j in range(8):
                    vag = P.op("gpsimd", lambda e, j=j: e.collective_compute("AllGather", ALU.bypass, replica_groups=GRP4, ins=[hb.ap()[j * 1024:(j + 1) * 1024, :]], outs=[hall.ap()[j * 4096:(j + 1) * 4096, :]]), waits=stores if j == 0 else [], inc=(s_ag, 1))
                P.op("gpsimd", None, waits=[vag])
                if debug and l == 0:
                    v = P.op("gpsimd", lambda e: e.dma_start(out=dbg["hb"], in_=hb.ap()), inc=(s_dbg, 16))
                    P.op("gpsimd", None, waits=[v])
                P.build()
            if stop("p1"):
                break

            with ExitStack() as ps:
                wres = ps.enter_context(nc.sbuf_tensor(f"p2w_L{l}", [128, 32768], BF16))
                ht = [ps.enter_context(nc.sbuf_tensor(f"p2h{i}_L{l}", [128, 32, 512], BF16)) for i in range(3)]
                sqb = [ps.enter_context(nc.sbuf_tensor(f"p2sq{i}_L{l}", [128, 512], BF16)) for i in range(2)]
                rtb = [ps.enter_context(nc.sbuf_tensor(f"p2rt{i}_L{l}", [128, 512], F32)) for i in range(2)]
                qo = [ps.enter_context(nc.sbuf_tensor(f"p2qo{i}_L{l}", [128, 512], BF16)) for i in range(3)]
                if l == 0:
                    P.p2 = dict(w=Slot(P, "p2w"), h=[Slot(P, f"p2h{i}") for i in range(3)],
                                pa=[Slot(P, f"p2pa{i}") for i in range(3)], sq=[Slot(P, f"p2sq{i}") for i in range(2)],
                                pb=[Slot(P, f"p2pb{i}") for i in range(2)], rt=[Slot(P, f"p2rt{i}") for i in range(2)],
                                qo=[Slot(P, f"p2qo{i}", store=True) for i in range(3)],
                                cnt=dict(h=0, pa=0, sq=0, qo=0))
                p2 = P.p2
                c2 = p2["cnt"]
                WS = p2["w"]
                hallv = hall.ap().rearrange("(j r p h a) t -> j r p h a t", j=8, r=4, p=128, h=2, a=4)
                if l == 0:
                    cast_win(0, 1)
                    cast_win(0, 2)
                wag_pace = [15, 15, 15] if l == 0 else [7, 7, 6]
                vvv = vv.ap()
                p2stores = []

                def load_h(t):
                    i = c2["h"] % 3
                    c2["h"] += 1
                    r, half = t // 2, t % 2
                    sig = None
                    for j in range(8):
                        sig = P.op("sync", lambda e, i=i, r=r, half=half, j=j: e.dma_start(out=ht[i][:, 4 * j:4 * j + 4, :], in_=hallv[j][r][:, half, :, :]), waits=p2["h"][i].pw() if j == 0 else [], inc=p2["h"][i])
                    return i, sig

                def load_wreg(pss_, i, waits):
                    if pss_ < 2:
                        src = W["wqk_b"].ap()[(pss_ * 8 + i) * 128:(pss_ * 8 + i + 1) * 128, :]
                    else:
                        src = W["wv_b"].ap()[:, i * 4096:(i + 1) * 4096]
                    return P.op("sync", lambda e, i=i, src=src: e.dma_start(out=wres[:, i * 4096:(i + 1) * 4096], in_=src),
                                waits=waits + [(s_wc[l], 16 * (pss_ + 1))], inc=WS)

                sigW_next = None
                for pss in range(3):
                    if pss == 0:
                        sigW = None
                        for i in range(8):
                            sigW = load_wreg(0, i, WS.pw() if i == 0 else [])
                    else:
                        sigW = sigW_next
                    issue_wag(l, wag_pace[pss])
                    hq = [load_h(0), load_h(1)]
                    sigPA_last = None
                    for t in range(8):
                        hi, sigH = hq.pop(0)
                        H = p2["h"][hi]
                        if t + 2 < 8:
                            hq.append(load_h(t + 2))
                        if pss < 2:
                            deferred = None
                            for s in range(8):
                                sg_ = pss * 8 + s
                                ia = c2["pa"] % 3
                                c2["pa"] += 1
                                PA = p2["pa"][ia]
                                for kc in range(32):
                                    w_ = []
                                    if kc == 0:
                                        w_ = PA.pw() + ([sigH] if s == 0 else []) + ([sigW] if (s == 0 and t == 0) else [])
                                    sigPA = P.op("tensor", lambda e, ia=ia, s=s, kc=kc, hi=hi: e.matmul(psb[ia][:], lhsT=wres[:, (s * 32 + kc) * 128:(s * 32 + kc + 1) * 128], rhs=ht[hi][:, kc, :], start=(kc == 0), stop=(kc == 31)),
                                                  waits=w_, inc=PA if kc == 31 else None)
                                sigPA_last = sigPA
                                if t == 7:
                                    sigW_next = load_wreg(pss + 1, s, [sigPA])
                                isq = c2["sq"] % 2
                                c2["sq"] += 1
                                SQ, PB, RT = p2["sq"][isq], p2["pb"][isq], p2["rt"][isq]
                                sigSQ = P.op("scalar", lambda e, ia=ia, isq=isq: e.activation(out=sqb[isq][:], in_=psb[ia][:], func=AF.Square), waits=[sigPA] + SQ.pw(), inc=SQ)

                                def post(ia=ia, isq=isq, sg_=sg_, t=t, SQ=SQ, PB=PB, RT=RT, PA=PA, sigSQ=sigSQ):
                                    sigPB = P.op("tensor", lambda e: e.matmul(psb[4 + isq][:], lhsT=ones[:], rhs=sqb[isq][:], start=True, stop=True), waits=[sigSQ] + PB.pw(), inc=PB)
                                    SQ.release(sigPB)
                                    sigRT = P.op("scalar", lambda e: e.activation(out=rtb[isq][:], in_=psb[4 + isq][:], func=AF.Ln, scale=1.0 / 128, bias=EPS), waits=[sigPB] + RT.pw(), inc=RT)
                                    PB.release(sigRT)
                                    sigRR = P.op("scalar", lambda e: e.activation(out=rtb[isq][:], in_=rtb[isq][:], func=AF.Exp, scale=-0.5), waits=[sigRT], inc=(acp, 1))
                                    iq = c2["qo"] % 3
                                    c2["qo"] += 1
                                    QO = p2["qo"][iq]
                                    sigQO = P.op("vector", lambda e: e.scalar_tensor_tensor(out=qo[iq][:], in0=psb[ia][:], scalar=qkgs[l][:, sg_:sg_ + 1], in1=rtb[isq][:], op0=ALU.mult, op1=ALU.mult),
                                                 waits=[sigRR] + QO.pw(), inc=QO)
                                    PA.release(sigQO)
                                    RT.release(sigQO)
                                    sigSt = P.op("sync", lambda e: e.dma_start(out=qk.ap()[sg_][:, t * 512:(t + 1) * 512], in_=qo[iq][:]), waits=[sigQO], inc=(QO.s, 16))
                                    QO.release(sigSt)
                                    p2stores.append(sigSt)

                                if deferred is not None:
                                    deferred()
                                deferred = post
                            deferred()
                            H.release(sigPA_last)
                        else:
                            for tb4 in range(4):
                                for ch in range(2):
                                    ia = c2["pa"] % 3
                                    c2["pa"] += 1
                                    PA = p2["pa"][ia]
                                    for kc in range(32):
                                        w_ = []
                                        if kc == 0:
                                            first = (tb4 == 0 and ch == 0)
                                            w_ = PA.pw() + ([sigH] if first else []) + ([sigW] if (first and t == 0) else [])
                                        sigPA = P.op("tensor", lambda e, ia=ia, kc=kc, hi=hi, tb4=tb4, ch=ch: e.matmul(psb[ia][:], lhsT=ht[hi][:, kc, tb4 * 128:(tb4 + 1) * 128], rhs=wres[:, kc * 1024 + ch * 512:kc * 1024 + (ch + 1) * 512], start=(kc == 0), stop=(kc == 31)),
                                                      waits=w_, inc=PA if kc == 31 else None)
                                    sigPA_last = sigPA
                                    iq = c2["qo"] % 3
                                    c2["qo"] += 1
                                    QO = p2["qo"][iq]
                                    sigQO = P.op("scalar", lambda e, ia=ia, iq=iq: e.activation(out=qo[iq][:], in_=psb[ia][:], func=AF.Copy), waits=[sigPA] + QO.pw(), inc=QO)
                                    PA.release(sigQO)
                                    sigSt = P.op("sync", lambda e, iq=iq, t=t, tb4=tb4, ch=ch: e.dma_start(out=vvv[t * 512 + tb4 * 128:t * 512 + (tb4 + 1) * 128, ch * 512:(ch + 1) * 512], in_=qo[iq][:]), waits=[sigQO], inc=(QO.s, 16))
                                    QO.release(sigSt)
                                    p2stores.append(sigSt)
                            H.release(sigPA_last)
                    WS.release(sigPA_last)
                    P.op("gpsimd", None, waits=[sigPA_last])
                P.op("sync", None, waits=p2stores[-3:])
                P.op("gpsimd", None, waits=p2stores[-3:])
                if debug and l == 0:
                    v = P.op("gpsimd", lambda e: e.dma_start(out=dbg["qk"], in_=qk.ap()), inc=(s_dbg, 16))
                    v = P.op("gpsimd", lambda e: e.dma_start(out=dbg["vv"], in_=vv.ap()), inc=(s_dbg, 16))
                    P.op("gpsimd", None, waits=[v])
                P.build()
            if stop("p2"):
                break

            with ExitStack() as ps:
                qT = [ps.enter_context(nc.sbuf_tensor(f"p3q{i}_L{l}", [128, 2, 4096], BF16)) for i in range(2)]
                kT = [ps.enter_context(nc.sbuf_tensor(f"p3k{i}_L{l}", [128, 2, 4096], BF16)) for i in range(2)]
                vt = [ps.enter_context(nc.sbuf_tensor(f"p3v{i}_L{l}", [128, 32, 256], BF16)) for i in range(2)]
                bias = ps.enter_context(nc.sbuf_tensor(f"p3bias_L{l}", [128, 10240], F32))
                stt = [ps.enter_context(nc.sbuf_tensor(f"p3st{i}_L{l}", [128, 512], F32)) for i in range(4)]
                et = [ps.enter_context(nc.sbuf_tensor(f"p3e{i}_L{l}", [128, 512], BF16)) for i in range(4)]
                rz = ps.enter_context(nc.sbuf_tensor(f"p3rz_L{l}", [128, 512], F32))
                t0 = ps.enter_context(nc.sbuf_tensor(f"p3t0_L{l}", [128, 2, 512], F32))
                ob = ps.enter_context(nc.sbuf_tensor(f"p3o_L{l}", [128, 2, 512], F32))
                osq = ps.enter_context(nc.sbuf_tensor(f"p3osq_L{l}", [128, 2, 512], BF16))
                ort = ps.enter_context(nc.sbuf_tensor(f"p3ort_L{l}", [128, 512], F32))
                mo = [ps.enter_context(nc.sbuf_tensor(f"p3mo{i}_L{l}", [128, 2, 512], BF16)) for i in range(2)]
                if l == 0:
                    P.p3 = dict(hd=[Slot(P, f"p3hd{i}") for i in range(2)], bias=Slot(P, "p3bias"),
                                sp=[Slot(P, f"p3sp{i}") for i in range(4)], st=[Slot(P, f"p3st{i}") for i in range(4)],
                                e=[Slot(P, f"p3e{i}") for i in range(4)], acc=[Slot(P, f"p3acc{i}", own=False) for i in range(2)],
                                z=Slot(P, "p3z", own=False), rz=Slot(P, "p3rz"), t0=Slot(P, "p3t0"), o=Slot(P, "p3o"), osq=Slot(P, "p3osq"),
                                ss=Slot(P, "p3ss"), ort=Slot(P, "p3ort"), mo=[Slot(P, f"p3mo{i}", store=True) for i in range(2)],
                                pv=P.sem("p3pv"), cnt=dict(hd=0, sp=0, e=0, acc=0, mo=0))
                p3 = P.p3
                c3 = p3["cnt"]
                pv = p3["pv"]
                units = [("A", 0)] + [("B", i) for i in range(3)] + [("C", i) for i in range(3)]
                qkv = qk.ap()
                vvh = vv.ap().rearrange("(tb p) c -> p tb c", p=128)
                mixv = mixo.ap().rearrange("(c p) t -> p c t", p=128)
                p3stores = []
                mixag = []

                def load_unit(u):
                    kind, hi_ = units[u]
                    i = c3["hd"] % 2
                    c3["hd"] += 1
                    HD = p3["hd"][i]
                    if kind == "A":
                        srcs = [(qT[i][:, 0, :], qkv[0]), (qT[i][:, 1, :], qkv[1]), (kT[i][:, 0, :], qkv[2]), (kT[i][:, 1, :], qkv[3]),
                                (vt[i][:, :, :], vvh[:, :, 0:256])]
                    elif kind == "B":
                        srcs = [(qT[i][:, 0, :], qkv[4 + hi_]), (kT[i][:, 0, :], qkv[7 + hi_]), (vt[i][:, :, 0:128], vvh[:, :, 256 + 128 * hi_:256 + 128 * (hi_ + 1)])]
                    else:
                        srcs = [(qT[i][:, 0, :], qkv[10 + hi_]), (kT[i][:, 0, :], qkv[13 + hi_]), (vt[i][:, :, 0:128], vvh[:, :, 640 + 128 * hi_:640 + 128 * (hi_ + 1)])]
                    sig = None
                    for j, (dst, src) in enumerate(srcs):
                        sig = P.op("sync", lambda e, dst=dst, src=src: e.dma_start(out=dst, in_=src), waits=HD.pw() if j == 0 else [], inc=HD)
                    return i, sig

                def load_bias(u):
                    kind, hi_ = units[u]
                    B = p3["bias"]
                    if kind == "A":
                        return P.op("sync", lambda e: e.dma_start(out=bias[:, 0:8064], in_=biasA), waits=B.pw(), inc=B)
                    if kind == "B":
                        return P.op("sync", lambda e, hi_=hi_: e.dma_start(out=bias[:, :], in_=W["biasB"][hi_]), waits=B.pw(), inc=B)
                    return P.op("sync", lambda e: e.dma_start(out=bias[:, 0:3 * 2944], in_=biasC), waits=B.pw(), inc=B)

                def blocks_for(kind, qt):
                    if kind == "A":
                        return [(kb, qt * 512 - kb * 128 + 3968) for kb in range(32)]
                    if kind == "C":
                        return [(kb, qt * 512 - kb * 128 + 1408) for kb in range(max(0, 4 * qt - 8), min(31, 4 * qt + 11) + 1)]
                    if qt == 0:
                        return [(kb, (8 + kb) * 512) for kb in range(6)]
                    if qt == 7:
                        return [(kb, (14 + kb - 26) * 512) for kb in range(26, 32)]
                    return [(4 * qt - 2 + r, r * 512) for r in range(8)]

                nxt = load_unit(0)
                sigB = load_bias(0)
                mix_chunk = 0
                tail = []
                for u, (kind, hi_) in enumerate(units):
                    hs_, sigHD = nxt
                    HD = p3["hd"][hs_]
                    if u + 1 < len(units):
                        nxt = load_unit(u + 1)
                    nmaps = 2 if kind == "A" else 1
                    ndv = 2 if kind == "A" else 1
                    cbase = (hi_ * 2944) if kind == "C" else 0
                    first_of_unit = True
                    sigSTlast = None
                    sigPVlast = None
                    sigT0 = None
                    for qt in range(8):
                        blks = blocks_for(kind, qt)
                        for m in range(nmaps):
                            if kind == "A":
                                ai = 0
                                obank = [4, 5]
                            else:
                                ai = c3["acc"] % 2
                                c3["acc"] += 1
                                obank = [4 + ai]
                            ACC, Z = p3["acc"][ai], p3["z"]
                            nb_ = len(blks)
                            pend = []

                            def emit_pv(item, isfirst, islast, obank=obank, hs_=hs_, ACC=ACC, Z=Z):
                                (kb, ei, E, sigE) = item
                                for dvc in range(ndv):
                                    w_ = [sigE] if dvc == 0 else []
                                    if isfirst and dvc == 0:
                                        w_ = w_ + ACC.pw()
                                    P.op("tensor", lambda e, ei=ei, kb=kb, dvc=dvc: e.matmul(psb[obank[dvc]][:], lhsT=vt[hs_][:, kb, dvc * 128:(dvc + 1) * 128], rhs=et[ei][:], start=isfirst, stop=islast), waits=w_)
                                sig = P.op("tensor", lambda e, ei=ei: e.matmul(psb[6][:], lhsT=ones[:], rhs=et[ei][:], start=isfirst, stop=islast),
                                           waits=Z.pw() if isfirst else [], inc=(pv, 1))
                                E.release(sig)
                                return sig

                            npv = 0
                            for bi, (kb, boff) in enumerate(blks):
                                si = c3["sp"] % 4
                                c3["sp"] += 1
                                SP, ST = p3["sp"][si], p3["st"][si]
                                sigSP = P.op("tensor", lambda e, si=si, kb=kb, qt=qt, m=m, hs_=hs_: e.matmul(psb[si][:], lhsT=kT[hs_][:, m, kb * 128:(kb + 1) * 128], rhs=qT[hs_][:, m, qt * 512:(qt + 1) * 512], start=True, stop=True),
                                              waits=SP.pw() + ([sigHD] if first_of_unit else []), inc=SP)
                                sigST = P.op("vector", lambda e, si=si, boff=boff, cbase=cbase: e.scalar_tensor_tensor(out=stt[si][:], in0=psb[si][:], scalar=SCALE, in1=bias[:, cbase + boff:cbase + boff + 512], op0=ALU.mult, op1=ALU.add),
                                              waits=[sigSP] + ST.pw() + ([sigB] if first_of_unit else []), inc=ST)
                                first_of_unit = False
                                SP.release(sigST)
                                sigSTlast = sigST
                                ei = c3["e"] % 4
                                c3["e"] += 1
                                E = p3["e"][ei]
                                sigE = P.op("scalar", lambda e, si=si, ei=ei: e.activation(out=et[ei][:], in_=stt[si][:], func=AF.Exp), waits=[sigST] + E.pw(), inc=E)
                                ST.release(sigE)
                                pend.append((kb, ei, E, sigE))
                                if len(pend) > 3:
                                    emit_pv(pend.pop(0), npv == 0, False)
                                    npv += 1
                                if bi == 2 and tail:
                                    for fn_ in tail:
                                        fn_()
                                    tail = []
                            while pend:
                                sigPVlast = emit_pv(pend.pop(0), npv == 0, len(pend) == 0)
                                npv += 1
                            RZ, T0, O, OSQ, SS, ORT = p3["rz"], p3["t0"], p3["o"], p3["osq"], p3["ss"], p3["ort"]
                            sigLZ = P.op("scalar", lambda e: e.activation(out=rz[:], in_=psb[6][:], func=AF.Ln), waits=[sigPVlast] + RZ.pw(), inc=(acp, 1))
                            Z.release(sigLZ)
                            sigRZ = P.op("scalar", lambda e: e.activation(out=rz[:], in_=rz[:], func=AF.Exp, scale=-1.0), waits=[sigLZ], inc=RZ)
                            if kind == "A" and m == 0:
                                for dvc in range(2):
                                    sigT0 = P.op("vector", lambda e, dvc=dvc, ai=ai: e.tensor_tensor(out=t0[:, dvc, :], in0=psb[4 + dvc][:], in1=rz[:], op=ALU.mult),
                                                 waits=([sigRZ] + T0.pw()) if dvc == 0 else [], inc=T0 if dvc == 1 else None)
                                ACC.release(sigT0)
                                RZ.release(sigT0)
                                continue
                            if kind == "A":
                                for dvc in range(2):
                                    sigX_ = P.op("vector", lambda e, dvc=dvc, ai=ai: e.tensor_tensor(out=ob[:, dvc, :], in0=psb[4 + dvc][:], in1=rz[:], op=ALU.mult),
                                                 waits=([sigRZ] + O.pw()) if dvc == 0 else [], inc=(dvp, 1) if dvc == 1 else None)
                                ACC.release(sigX_)
                                RZ.release(sigX_)
                                for dvc in range(2):
                                    sigO = P.op("vector", lambda e, dvc=dvc, l=l: e.scalar_tensor_tensor(out=ob[:, dvc, :], in0=ob[:, dvc, :], scalar=nlam[l][:, 0:1], in1=t0[:, dvc, :], op0=ALU.mult, op1=ALU.add),
                                                waits=[sigX_, sigT0] if dvc == 0 else [], inc=O if dvc == 1 else None)
                                T0.release(sigO)
                                nfeat = 256
                            else:
                                sigO = P.op("vector", lambda e, ai=ai: e.tensor_tensor(out=ob[:, 0, :], in0=psb[4 + ai][:], in1=rz[:], op=ALU.mult), waits=[sigRZ] + O.pw(), inc=O)
                                ACC.release(sigO)
                                RZ.release(sigO)
                                nfeat = 128
                            sigOSQ = P.op("scalar", lambda e, ndv=ndv: e.activation(out=osq[:, 0:ndv, :], in_=ob[:, 0:ndv, :], func=AF.Square), waits=[sigO] + OSQ.pw(), inc=OSQ)
                            cm = (1.0 - lam_init(labs)) if kind == "A" else 1.0

                            def fin_tail(ndv=ndv, nfeat=nfeat, cm=cm, mc=mix_chunk, qt=qt, sigOSQ=sigOSQ, O=O, OSQ=OSQ, SS=SS, ORT=ORT, u=u):
                                for dvc in range(ndv):
                                    sigSS = P.op("tensor", lambda e, dvc=dvc: e.matmul(psb[7][:], lhsT=ones[:], rhs=osq[:, dvc, :], start=(dvc == 0), stop=(dvc == ndv - 1)),
                                                 waits=([sigOSQ] + SS.pw()) if dvc == 0 else [], inc=SS if dvc == ndv - 1 else None)
                                OSQ.release(sigSS)
                                sigORT = P.op("scalar", lambda e: e.activation(out=ort[:], in_=psb[7][:], func=AF.Ln, scale=1.0 / nfeat, bias=EPS), waits=[sigSS] + ORT.pw(), inc=ORT)
                                SS.release(sigORT)
                                sigORR = P.op("scalar", lambda e: e.activation(out=ort[:], in_=ort[:], func=AF.Exp, scale=-0.5, bias=math.log(cm)), waits=[sigORT], inc=(acp, 1))
                                mi = c3["mo"] % 2
                                c3["mo"] += 1
                                MO = p3["mo"][mi]
                                for dvc in range(ndv):
                                    sigMO = P.op("vector", lambda e, dvc=dvc: e.scalar_tensor_tensor(out=mo[mi][:, dvc, :], in0=ob[:, dvc, :], scalar=ogs[l][:, mc + dvc:mc + dvc + 1], in1=ort[:], op0=ALU.mult, op1=ALU.mult),
                                                  waits=([sigORR] + MO.pw()) if dvc == 0 else [], inc=MO if dvc == ndv - 1 else None)
                                O.release(sigMO)
                                ORT.release(sigMO)
                                sigSt = P.op("sync", lambda e: e.dma_start(out=mixv[:, mc:mc + ndv, qt * 512:(qt + 1) * 512], in_=mo[mi][:, 0:ndv, :]), waits=[sigMO], inc=(MO.s, 16))
                                MO.release(sigSt)
                                p3stores.append(sigSt)
                                if qt == 7:
                                    for cc in range(mc, mc + ndv):
                                        vag_ = P.op("gpsimd", lambda e, cc=cc: e.collective_compute("AllGather", ALU.bypass, replica_groups=GRP4, ins=[mixo.ap()[cc * 128:(cc + 1) * 128, :]], outs=[mixall.ap()[cc * 512:(cc + 1) * 512, :]]),
                                                     waits=p3stores[-2:], inc=(s_ag, 1))
                                        mixag.append(vag_)
                                    issue_wag(l, 2)

                            tail.append(fin_tail)
                    mix_chunk += ndv
                    HD.release(sigPVlast)
                    nxtu = units[u + 1] if u + 1 < len(units) else None
                    if nxtu is not None and (nxtu[0] != "C" or nxtu[1] == 0):
                        p3["bias"].release(sigSTlast)
                        sigB = load_bias(u + 1)
                    elif nxtu is None:
                        p3["bias"].release(sigSTlast)
                for fn_ in tail:
                    fn_()
                tail = []
                vst = p3stores[-2:]
                P.op("sync", None, waits=vst)
                P.op("gpsimd", None, waits=[mixag[-1]])
                if debug and l == 0:
                    v = P.op("gpsimd", lambda e: e.dma_start(out=dbg["mixo"], in_=mixo.ap()), inc=(s_dbg, 16))
                    P.op("gpsimd", None, waits=[v])
                P.build()
            if stop("p3"):
                break

            with ExitStack() as ps:
                acta = ps.enter_context(nc.sbuf_tensor(f"p4a_L{l}", [128, 32, 1024], BF16))
                actb = ps.enter_context(nc.sbuf_tensor(f"p4b_L{l}", [128, 32, 1024], BF16))
                wr = [ps.enter_context(nc.sbuf_tensor(f"p4w{i}_L{l}", [128, 32, 128], BF16)) for i in range(4)]
                stage = [wr[i][:].rearrange("p a b -> p (a b)").rearrange("p (q t) -> p q t", q=4) for i in range(4)]
                xin = [ps.enter_context(nc.sbuf_tensor(f"p4xin{i}_L{l}", [128, 1024], F32)) for i in range(3)]
                x1 = [ps.enter_context(nc.sbuf_tensor(f"p4x1{i}_L{l}", [128, 1024], F32)) for i in range(4)]
                xsq = [ps.enter_context(nc.sbuf_tensor(f"p4xsq{i}_L{l}", [128, 1024], BF16)) for i in range(2)]
                rt2 = ps.enter_context(nc.sbuf_tensor(f"p4rt2_L{l}", [128, 1024], F32))
                sgt = [ps.enter_context(nc.sbuf_tensor(f"p5sg{i}_L{l}", [128, 1024], F32)) for i in range(2)]
                if l == 0:
                    P.p4 = dict(stg=[Slot(P, f"p4stg{i}") for i in range(4)], a=Slot(P, "p4a", own=False),
                                w=[Slot(P, f"p4w{i}") for i in range(4)], xin=[Slot(P, f"p4xin{i}") for i in range(3)],
                                pm=[Slot(P, f"p4pm{i}") for i in range(2)], x1=[Slot(P, f"p4x1{i}", store=True) for i in range(4)],
                                xsq=[Slot(P, f"p4xsq{i}") for i in range(2)], stat=Slot(P, "p4stat", own=False), rt=Slot(P, "p4rt"),
                                b=Slot(P, "p4b", own=False), pg=Slot(P, "p5pg"), pu=Slot(P, "p5pu"), sg=[Slot(P, f"p5sg{i}") for i in range(2)],
                                act=Slot(P, "p5act", own=False), cnt=dict(w=0, xin=0, pm=0, x1=0, xsq=0, stg=0, sg=0))
                p4 = P.p4
                c4 = p4["cnt"]
                xres_st = st["xres_st"]
                wouta = W["wout_a"].ap().rearrange("(m p) c -> m p c", p=128)
                wga = W["wg_a"].ap().rearrange("(f p) c -> f p c", p=128)
                wua = W["wu_a"].ap().rearrange("(f p) c -> f p c", p=128)
                wda = W["wd_a"].ap().rearrange("(g m p) c -> g m p c", g=4, p=128)
                issue_wag(l, 1000)
                mixallv = mixall.ap().rearrange("(kc p) (q t) -> p kc q t", p=128, q=4)
                A, Bs, STAT, ACT_ = p4["a"], p4["b"], p4["stat"], p4["act"]

                sigA = None
                for kc in range(32):
                    si = c4["stg"] % 4
                    c4["stg"] += 1
                    SG = p4["stg"][si]
                    sigSG = P.op("sync", lambda e, si=si, kc=kc: e.dma_start(out=stage[si], in_=mixallv[:, kc, :, :]), waits=SG.pw(), inc=SG)
                    v = P.op("vector", lambda e, si=si, kc=kc: e.tensor_scalar(out=acta[:, kc, :], in0=stage[si][:, 0, :], scalar1=ohs[:, 0:1], scalar2=0.0, op0=ALU.mult, op1=ALU.add),
                             waits=[sigSG] + (A.pw() if kc == 0 else []), inc=(dvp, 1))
                    for q in range(1, 4):
                        v = P.op("vector", lambda e, si=si, kc=kc, q=q: e.scalar_tensor_tensor(out=acta[:, kc, :], in0=stage[si][:, q, :], scalar=ohs[:, q:q + 1], in1=acta[:, kc, :], op0=ALU.mult, op1=ALU.add),
                                 waits=[v], inc=(dvp, 1))
                    SG.release(v)
                    sigA = v
                for i_ in range(4):
                    p4["w"][i_].release(sigA)

                def load_w(src_ap, call, ncols=4096):
                    i = c4["w"] % 4
                    c4["w"] += 1
                    WSl = p4["w"][i]
                    sig = P.op("sync", lambda e, i=i, src_ap=src_ap, ncols=ncols: e.dma_start(out=wr[i][:].rearrange("p a b -> p (a b)")[:, 0:ncols], in_=src_ap), waits=WSl.pw() + [wag_sig(l, call)], inc=WSl)
                    return i, sig

                def load_xin(m, src, wait_store):
                    i = c4["xin"] % 3
                    c4["xin"] += 1
                    XI = p4["xin"][i]
                    sig = P.op("sync", lambda e, i=i, m=m, src=src: e.dma_start(out=xin[i][:], in_=src[m]), waits=XI.pw() + [wait_store], inc=XI)
                    return i, sig

                wq = [load_w(wouta[0], ("wout", 0)), load_w(wouta[1], ("wout", 0)), load_w(wouta[2], ("wout", 0))]
                xq = [load_xin(0, xsrc, xres_st[0] if l > 0 else None), load_xin(1, xsrc, xres_st[1] if l > 0 else None)]
                deferred = None
                sigSTAT = None
                for m in range(32):
                    wi, sigW = wq.pop(0)
                    if m + 3 < 32:
                        wq.append(load_w(wouta[m + 3], ("wout", (m + 3) // 4)))
                    xi, sigXI = xq.pop(0)
                    if m + 2 < 32:
                        xq.append(load_xin(m + 2, xsrc, xres_st[m + 2] if l > 0 else None))
                    WSl, XI = p4["w"][wi], p4["xin"][xi]
                    pi_ = c4["pm"] % 2
                    c4["pm"] += 1
                    PM = p4["pm"][pi_]
                    for kc in range(32):
                        for th in range(2):
                            w_ = []
                            if kc == 0 and th == 0:
                                w_ = PM.pw() + [sigW] + ([sigA] if m == 0 else [])
                            sigPM = P.op("tensor", lambda e, wi=wi, kc=kc, th=th, pi_=pi_: e.matmul(psb[2 * pi_ + th][:], lhsT=wr[wi][:, kc, :], rhs=acta[:, kc, th * 512:(th + 1) * 512], start=(kc == 0), stop=(kc == 31)),
                                          waits=w_, inc=PM if (kc == 31 and th == 1) else None)
                    WSl.release(sigPM)
                    if deferred is not None:
                        deferred()
                        deferred = None
                    x1i = c4["x1"] % 4
                    c4["x1"] += 1
                    X1 = p4["x1"][x1i]
                    for th in range(2):
                        sigX1 = P.op("vector", lambda e, th=th, pi_=pi_, xi=xi, x1i=x1i: e.tensor_tensor(out=x1[x1i][:, th * 512:(th + 1) * 512], in0=psb[2 * pi_ + th][:], in1=xin[xi][:, th * 512:(th + 1) * 512], op=ALU.add),
                                      waits=([sigPM, sigXI] + X1.pw()) if th == 0 else [], inc=X1 if th == 1 else None)
                    PM.release(sigX1)
                    XI.release(sigX1)
                    sigSt = P.op("sync", lambda e, x1i=x1i, m=m: e.dma_start(out=xres.ap()[m], in_=x1[x1i][:]), waits=[sigX1], inc=(X1.s, 16))
                    xres_st[m] = sigSt
                    X1.release(sigSt)
                    qi = c4["xsq"] % 2
                    c4["xsq"] += 1
                    XS = p4["xsq"][qi]
                    sigXS = P.op("scalar", lambda e, x1i=x1i, qi=qi: e.activation(out=xsq[qi][:], in_=x1[x1i][:], func=AF.Square), waits=[sigX1] + XS.pw(), inc=XS)
                    X1.release(sigXS)

                    def stat_mm(m=m, qi=qi, XS=XS, sigXS=sigXS):
                        for th in range(2):
                            sig = P.op("tensor", lambda e, th=th: e.matmul(psb[6 + th][:], lhsT=ones[:], rhs=xsq[qi][:, th * 512:(th + 1) * 512], start=(m == 0), stop=(m == 31)),
                                       waits=([sigXS] + (STAT.pw() if m == 0 else [])) if th == 0 else [], inc=(pep, 1) if th == 1 else None)
                        XS.release(sig)
                        return sig
                    deferred = stat_mm
                sigSTAT = deferred()
                RT = p4["rt"]
                for th in range(2):
                    sigRT = P.op("scalar", lambda e, th=th: e.activation(out=rt2[:, th * 512:(th + 1) * 512], in_=psb[6 + th][:], func=AF.Ln, scale=1.0 / D, bias=EPS),
                                 waits=([sigSTAT] + RT.pw()) if th == 0 else [], inc=RT if th == 1 else None)
                STAT.release(sigRT)
                sigRR = P.op("scalar", lambda e: e.activation(out=rt2[:], in_=rt2[:], func=AF.Exp, scale=-0.5), waits=[sigRT], inc=(acp, 1))
                xq = [load_xin(0, xres.ap(), xres_st[0]), load_xin(1, xres.ap(), xres_st[1])]
                sigB2 = None
                for m in range(32):
                    xi, sigXI = xq.pop(0)
                    if m + 2 < 32:
                        xq.append(load_xin(m + 2, xres.ap(), xres_st[m + 2]))
                    XI = p4["xin"][xi]
                    sigB2 = P.op("vector", lambda e, m=m, xi=xi, l=l: e.scalar_tensor_tensor(out=actb[:, m, :], in0=xin[xi][:], scalar=g2s[l][:, m:m + 1], in1=rt2[:], op0=ALU.mult, op1=ALU.mult),
                                 waits=[sigXI] + (([sigRR] + Bs.pw()) if m == 0 else []), inc=(dvp, 1))
                    XI.release(sigB2)
                RT.release(sigB2)
                if l + 1 < L:
                    P.op("gpsimd", None, waits=[sigB2])
                    for piece in range(3):
                        cast_win(l + 1, piece)
                    issue_wag(l + 1, 14)
                if debug and l == 0:
                    v = P.op("gpsimd", lambda e: e.dma_start(out=dbg["x1"], in_=xres.ap()), waits=xres_st[28:32], inc=(s_dbg, 16))
                    P.op("gpsimd", None, waits=[v])
                if stop("p4"):
                    P.op("sync", None, waits=xres_st[28:32])
                    P.op("vector", None, waits=[sigB2])
                    P.build()
                    break

                PG, PU = p4["pg"], p4["pu"]
                sigPU = None
                sigPM = None
                for gi, (f0, nf) in enumerate(FGROUPS):
                    lastg = (gi == len(FGROUPS) - 1)
                    wq = [(load_w(wga[f0], ("wg", f0 // 4)), load_w(wua[f0], ("wu", f0 // 4)))]
                    sigACT = None
                    for fi in range(nf):
                        f = f0 + fi
                        (wgi, sigWg), (wui, sigWu) = wq.pop(0)
                        if fi + 1 < nf:
                            wq.append((load_w(wga[f + 1], ("wg", (f + 1) // 4)), load_w(wua[f + 1], ("wu", (f + 1) // 4))))
                        sigs = {}
                        for (wi, sigW_, PSL, base) in ((wgi, sigWg, PG, 0), (wui, sigWu, PU, 2)):
                            for kc in range(32):
                                for th in range(2):
                                    w_ = []
                                    if kc == 0 and th == 0:
                                        w_ = PSL.pw() + [sigW_] + ([sigB2] if (gi == 0 and fi == 0 and base == 0) else [])
                                    sig = P.op("tensor", lambda e, wi=wi, kc=kc, th=th, base=base: e.matmul(psb[base + th][:], lhsT=wr[wi][:, kc, :], rhs=actb[:, kc, th * 512:(th + 1) * 512], start=(kc == 0), stop=(kc == 31)),
                                               waits=w_, inc=PSL if (kc == 31 and th == 1) else None)
                            p4["w"][wi].release(sig)
                            sigs[base] = sig
                        sigPG, sigPU = sigs[0], sigs[2]
                        si = c4["sg"] % 2
                        c4["sg"] += 1
                        SGS = p4["sg"][si]
                        for th in range(2):
                            sigSG = P.op("scalar", lambda e, th=th, si=si: e.activation(out=sgt[si][:, th * 512:(th + 1) * 512], in_=psb[th][:], func=AF.Silu),
                                         waits=([sigPG] + SGS.pw()) if th == 0 else [], inc=SGS if th == 1 else None)
                        PG.release(sigSG)
                        for th in range(2):
                            sigACT = P.op("vector", lambda e, th=th, si=si, fi=fi: e.tensor_tensor(out=acta[:, fi, th * 512:(th + 1) * 512], in0=psb[2 + th][:], in1=sgt[si][:, th * 512:(th + 1) * 512], op=ALU.mult),
                                          waits=([sigPU, sigSG] + (ACT_.pw() + A.pw() if fi == 0 else [])) if th == 0 else [], inc=(dvp, 1) if th == 1 else None)
                        PU.release(sigACT)
                        SGS.release(sigACT)

                    def load_wd(m, gi=gi, nf=nf):
                        return load_w(wda[gi][m][:, 0:nf * 128], ("wd", gi * 8 + m // 4), ncols=nf * 128)
                    wq = [load_wd(0), load_wd(1), load_wd(2)]
                    xq = [load_xin(0, xres.ap(), xres_st[0]), load_xin(1, xres.ap(), xres_st[1])]
                    for m in range(32):
                        wi, sigW = wq.pop(0)
                        if m + 3 < 32:
                            wq.append(load_wd(m + 3))
                        xi, sigXI = xq.pop(0)
                        if m + 2 < 32:
                            xq.append(load_xin(m + 2, xres.ap(), xres_st[m + 2]))
                        WSl, XI = p4["w"][wi], p4["xin"][xi]
                        pi_ = c4["pm"] % 2
                        c4["pm"] += 1
                        PM = p4["pm"][pi_]
                        for fi in range(nf):
                            for th in range(2):
                                w_ = []
                                if fi == 0 and th == 0:
                                    w_ = PM.pw() + [sigW] + ([sigACT] if m == 0 else [])
                                sigPM = P.op("tensor", lambda e, wi=wi, fi=fi, th=th, pi_=pi_, nf=nf: e.matmul(psb[4 + 2 * pi_ + th][:], lhsT=wr[wi][:, fi, :], rhs=acta[:, fi, th * 512:(th + 1) * 512], start=(fi == 0), stop=(fi == nf - 1)),
                                              waits=w_, inc=PM if (fi == nf - 1 and th == 1) else None)
                        WSl.release(sigPM)
                        x1i = c4["x1"] % 4
                        c4["x1"] += 1
                        X1 = p4["x1"][x1i]
                        for th in range(2):
                            sigX1 = P.op("vector", lambda e, th=th, pi_=pi_, xi=xi, x1i=x1i: e.tensor_tensor(out=x1[x1i][:, th * 512:(th + 1) * 512], in0=psb[4 + 2 * pi_ + th][:], in1=xin[xi][:, th * 512:(th + 1) * 512], op=ALU.add),
                                          waits=([sigPM, sigXI] + X1.pw()) if th == 0 else [], inc=X1 if th == 1 else None)
                        PM.release(sigX1)
                        XI.release(sigX1)
                        dst = xdst if lastg else xres.ap()
                        sigSt = P.op("sync", lambda e, x1i=x1i, m=m, dst=dst: e.dma_start(out=dst[m], in_=x1[x1i][:]), waits=[sigX1], inc=(X1.s, 16))
                        xres_st[m] = sigSt
                        X1.release(sigSt)
                    ACT_.release(sigPM)
                    if l + 1 < L and gi < 3:
                        P.op("gpsimd", None, waits=[sigPM])
                        issue_wag(l + 1, 12)
                A.release(sigPM)
                Bs.release(sigPU)
                P.op("sync", None, waits=xres_st[28:32])
                P.build()
    return nc


def _alibi(n):
    return np.exp2(-8.0 * np.arange(1, n + 1, dtype=np.float64) / n)


def _bias_A(g):
    slope = _alibi(4)[g]
    p = np.arange(128)[:, None]
    c = np.arange(8064)[None, :]
    return (-slope * np.abs(c - 3968 - p)).astype(np.float32)


def _bias_C(g):
    out = np.empty((128, 3, 2944), np.float32)
    p = np.arange(128)[:, None]
    c = np.arange(2944)[None, :]
    d = c - 1408 - p
    ad = np.abs(d)
    cnt = (ad <= 64).astype(np.int64) + ((d % 4 == 0) & (ad <= 256)) + ((d % 16 == 0) & (ad <= 1024))
    with np.errstate(divide="ignore"):
        lc = np.log(cnt.astype(np.float64))
    for i in range(3):
        slope = _alibi(12)[3 * g + i]
        v = -slope * ad + lc
        out[:, i, :] = np.where(cnt > 0, v, NEG).astype(np.float32)
    return out.reshape(128, 3 * 2944)


def _bias_B(rpb_l, g):
    out = np.full((3, 128, 20, 512), NEG, np.float32)
    pidx = np.arange(128)
    krl, kc = pidx // 64, pidx % 64
    fidx = np.arange(512)
    qrl, qc = fidx // 64, fidx % 64
    cstart = np.clip(qc - 8, 0, 48)
    colok = (kc[:, None] >= cstart[None, :]) & (kc[:, None] < cstart[None, :] + 16)
    dc = np.clip(kc[:, None] - qc[None, :] + 15, 0, 30)
    cases = [(1, 4 * 1 - 2 + r, r) for r in range(8)] + [(0, kb, 8 + kb) for kb in range(6)] + [(7, kb, 14 + kb - 26) for kb in range(26, 32)]
    for (qt, kb, ti) in cases:
        kr = 2 * kb + krl
        qr = 8 * qt + qrl
        rstart = np.clip(qr - 4, 0, 56)
        rowok = (kr[:, None] >= rstart[None, :]) & (kr[:, None] < rstart[None, :] + 8)
        dr = np.clip(kr[:, None] - qr[None, :] + 7, 0, 14)
        ok = rowok & colok
        for i in range(3):
            vals = rpb_l[3 * g + i][dr, dc]
            out[i, :, ti, :] = np.where(ok, vals, NEG)
    return out.reshape(3, 128, 20 * 512)


def _block_w(Wm, kcn, nb):
    return np.ascontiguousarray(Wm.reshape(kcn, 128, nb, 128).transpose(2, 1, 0, 3))


def _qk_cols(g):
    cols = []
    for m in range(2):
        cols.append(np.arange(g * 256 + m * 128, g * 256 + (m + 1) * 128))
    for m in range(2):
        cols.append(1024 + np.arange(g * 256 + m * 128, g * 256 + (m + 1) * 128))
    for i in range(3):
        cols.append(3072 + (3 * g + i) * 128 + np.arange(128))
    for i in range(3):
        cols.append(4608 + (3 * g + i) * 128 + np.arange(128))
    for i in range(3):
        cols.append(7680 + (3 * g + i) * 128 + np.arange(128))
    for i in range(3):
        cols.append(9216 + (3 * g + i) * 128 + np.arange(128))
    return cols


def _v_cols(g):
    c = [2048 + g * 256 + np.arange(256)]
    for i in range(3):
        c.append(6144 + (3 * g + i) * 128 + np.arange(128))
    for i in range(3):
        c.append(10752 + (3 * g + i) * 128 + np.arange(128))
    return np.concatenate(c)


def _wout_perm():
    rows = []
    for c in range(8):
        for r in range(4):
            if c < 2:
                rows.append(r * 256 + c * 128 + np.arange(128))
            elif c < 5:
                rows.append(1024 + (3 * r + (c - 2)) * 128 + np.arange(128))
            else:
                rows.append(2560 + (3 * r + (c - 5)) * 128 + np.arange(128))
    return np.concatenate(rows)


def _col128(v):
    return np.ascontiguousarray(v.reshape(-1, 128).T.astype(np.float32))


def prepare_inputs(inp, layers):
    f32 = np.float32
    maps = [dict() for _ in range(NCORES)]
    x = np.asarray(inp["x"], f32)
    for c in range(NCORES):
        b, g = c // 4, c % 4
        maps[c]["xT"] = np.ascontiguousarray(x[b, g * 1024:(g + 1) * 1024, :].T).reshape(32, 128, 1024)
        maps[c]["biasA"] = _bias_A(g)
        maps[c]["biasC"] = _bias_C(g)
        oh = np.zeros((128, 4), f32)
        oh[:, g] = 1.0
        maps[c]["onehot"] = oh
    perm = _wout_perm()
    for li, l in enumerate(layers):
        w_in = np.asarray(inp["w_in"][l], f32)
        def shard_blocks(blk, npad):
            nb, _, X = blk.shape
            out = np.zeros((4, npad // 4, 128, X), f32)
            for r in range(4):
                sel = blk[r::4]
                out[r, :sel.shape[0]] = sel
            return out.reshape(4, npad // 4 * 128, X)
        wout_blk = shard_blocks(_block_w(np.asarray(inp["w_out"][l], f32)[perm, :], 32, 32).reshape(32, 128, 4096), 32)
        wg_blk = shard_blocks(_block_w(np.asarray(inp["w_gate"][l], f32), 32, NF).reshape(NF, 128, 4096), 88)
        wu_blk = shard_blocks(_block_w(np.asarray(inp["w_up"][l], f32), 32, NF).reshape(NF, 128, 4096), 88)
        wd_full = _block_w(np.asarray(inp["w_down"][l], f32), NF, 32).reshape(32, 128, NF, 128)
        wd_units = np.zeros((4, 32, 128, 2816), f32)
        for gi, (f0, nf) in enumerate(FGROUPS):
            wd_units[gi, :, :, :nf * 128] = wd_full[:, :, f0:f0 + nf, :].reshape(32, 128, nf * 128)
        wd_blk = np.stack([wd_units[:, r::4].reshape(32, 128, 2816) for r in range(4)]).reshape(4, 32 * 128, 2816)
        lamv = np.stack([inp["lambda_q1"][l], inp["lambda_k1"][l], inp["lambda_q2"][l], inp["lambda_k2"][l]]).astype(f32)
        lamv = np.ascontiguousarray(np.broadcast_to(lamv[None], (128, 4, 128)))
        g1 = _col128(np.asarray(inp["norm1_g"][l]))
        g2 = _col128(np.asarray(inp["norm2_g"][l]))
        qkg = np.stack([inp["a_q_g"][l]] * 2 + [inp["a_k_g"][l]] * 2 + [inp["b_q_g"][l]] * 3 + [inp["b_k_g"][l]] * 3
                       + [inp["c_q_g"][l]] * 3 + [inp["c_k_g"][l]] * 3, axis=1).astype(f32)
        og = np.stack([inp["a_out_g"][l][0:128], inp["a_out_g"][l][128:256]] + [inp["b_out_g"][l]] * 3 + [inp["c_out_g"][l]] * 3, axis=1).astype(f32)
        for g in range(4):
            cols = _qk_cols(g)
            wqk = np.stack([w_in[:, cc].reshape(32, 128, 128).transpose(1, 0, 2) for cc in cols]).reshape(16 * 128, 4096)
            wv = np.ascontiguousarray(w_in[:, _v_cols(g)].reshape(32, 128, 1024).transpose(1, 0, 2)).reshape(128, 32768)
            bB = _bias_B(np.asarray(inp["b_rpb"][l], f32), g)
            for b in range(2):
                m = maps[b * 4 + g]
                m[f"wqk_{li}"] = wqk
                m[f"wv_{li}"] = wv
                m[f"biasB_{li}"] = bB
        for c in range(NCORES):
            m = maps[c]
            m[f"g1_{li}"] = g1
            m[f"g2_{li}"] = g2
            m[f"qkg_{li}"] = np.ascontiguousarray(qkg)
            m[f"og_{li}"] = np.ascontiguousarray(og)
            m[f"lamv_{li}"] = lamv
            m[f"wout_{li}"] = wout_blk[c % 4]
            m[f"wg_{li}"] = wg_blk[c % 4]
            m[f"wu_{li}"] = wu_blk[c % 4]
            m[f"wd_{li}"] = wd_blk[c % 4]
    return maps


_NC_CACHE = {}


def _get_nc(n_layers, first, debug=False):
    key = (n_layers, first, debug)
    if key not in _NC_CACHE:
        _NC_CACHE[key] = build_program(n_layers, first, debug)
    return _NC_CACHE[key]


def assemble_output(res):
    out = np.empty((NB, S, D), np.float32)
    for c in range(NCORES):
        b, g = c // 4, c % 4
        out[b, g * 1024:(g + 1) * 1024, :] = res[c]["yT"].reshape(D, 1024).T
    return out


def kernel(**inputs):
    nc = _get_nc(DEPTH, 0)
    maps = prepare_inputs(inputs, list(range(DEPTH)))
    res = run_bass_kernel_spmd(nc, maps, core_ids=list(range(NCORES)))
    return assemble_output(res.results)
```

```python
import math
from contextlib import ExitStack

import numpy as np
import concourse.bass as bass
import concourse.mybir as mybir
from concourse.bass_utils import run_bass_kernel_spmd

F32 = mybir.dt.float32
BF16 = mybir.dt.bfloat16
AF = mybir.ActivationFunctionType
ALU = mybir.AluOpType

D = 4096
S = 4096
NB = 2
DEPTH = 2
DFF = 11008
NF = DFF // 128
EPS = 1e-6
NEG = -30000.0
SCALE = 128.0 ** -0.5
FGROUPS = [(0, 22), (22, 22), (44, 21), (65, 21)]
NCORES = 8
ENGS = ["sync", "scalar", "vector", "gpsimd", "tensor"]
GRP4 = [[0, 1, 2, 3], [4, 5, 6, 7]]
GRP8 = [list(range(8))]


def lam_init(l):
    return 0.8 - 0.6 * math.exp(-0.3 * l)


class Prog:
    def __init__(self, nc, es):
        self.nc = nc
        self.es = es
        self.semcnt = {}
        self.semh = {}
        self.ops = {e: [] for e in ENGS}

    def sem(self, name):
        assert name not in self.semcnt, name
        self.semcnt[name] = 0
        self.semh[name] = self.es.enter_context(self.nc.semaphore(name))
        return name

    def op(self, eng, fn, waits=(), inc=None):
        sig = None
        if inc is not None:
            if isinstance(inc, Slot):
                inc = (inc.f, 16 if eng in ("sync", "gpsimd_dma") else 1)
            s, a = inc
            self.semcnt[s] += a
            sig = (s, self.semcnt[s])
        if eng == "gpsimd_dma":
            eng = "gpsimd"
        wm = {}
        for w in waits:
            if w is None:
                continue
            s_, v_ = w
            if v_ > wm.get(s_, 0):
                wm[s_] = v_
        self.ops[eng].append((tuple(wm.items()), fn, inc))
        return sig

    def build(self):
        h = self.semh
        ops = self.ops
        with self.nc.Block() as block:
            def mk(engname):
                def body(eng):
                    for waits, fn, inc in ops[engname]:
                        for (s, v) in waits:
                            eng.wait_ge(h[s], v)
                        if fn is None:
                            continue
                        ins = fn(eng)
                        if inc is not None:
                            ins.then_inc(h[inc[0]], inc[1])
                return body
            for e in ENGS:
                if ops[e]:
                    getattr(block, e)(mk(e))
        self.ops = {e: [] for e in ENGS}
        self.nc.all_engine_barrier()


class Slot:
    def __init__(self, P, name, own=True, store=False):
        self.P = P
        self.f = P.sem(name) if own else None
        self.s = P.sem(name + "S") if store else None
        self.rel = []

    def pw(self):
        w = self.rel
        self.rel = []
        return list(w)

    def release(self, sig):
        assert sig is not None
        self.rel.append(sig)


def build_program(n_layers, first_layer_index=0, debug=False, stop_after=None):
    nc = bass.Bass("TRN2", target_bir_lowering=False)
    L = n_layers

    def din(name, shape, dt=F32):
        return nc.dram_tensor(name, list(shape), dt, kind="ExternalInput").ap()

    def dscr(name, shape, dt):
        return nc.dram_tensor(name, list(shape), dt)

    xT = din("xT", [32, 128, 1024])
    yT = nc.dram_tensor("yT", [32, 128, 1024], F32, kind="ExternalOutput").ap()
    biasA = din("biasA", [128, 8064])
    biasC = din("biasC", [128, 3 * 2944])
    onehot = din("onehot", [128, 4])
    lw = []
    for l in range(L):
        lw.append(dict(
            g1=din(f"g1_{l}", [128, 32]), g2=din(f"g2_{l}", [128, 32]),
            wqk=din(f"wqk_{l}", [16 * 128, 4096]), wv=din(f"wv_{l}", [128, 32768]),
            wout=din(f"wout_{l}", [8 * 128, 4096]), wg=din(f"wg_{l}", [22 * 128, 4096]),
            wu=din(f"wu_{l}", [22 * 128, 4096]), wd=din(f"wd_{l}", [32 * 128, 2816]),
            wqk_b=dscr(f"wqkb_{l}", [16 * 128, 4096], BF16), wv_b=dscr(f"wvb_{l}", [128, 32768], BF16),
            qkg=din(f"qkg_{l}", [128, 16]), og=din(f"og_{l}", [128, 8]),
            lamv=din(f"lamv_{l}", [128, 4, 128]), biasB=din(f"biasB_{l}", [3, 128, 20 * 512]),
            wout_b=dscr(f"woutb_{l}", [8 * 128, 4096], BF16), wout_a=dscr(f"wouta_{l}", [32 * 128, 4096], BF16),
            wg_b=dscr(f"wgb_{l}", [22 * 128, 4096], BF16), wg_a=dscr(f"wga_{l}", [88 * 128, 4096], BF16),
            wu_b=dscr(f"wub_{l}", [22 * 128, 4096], BF16), wu_a=dscr(f"wua_{l}", [88 * 128, 4096], BF16),
            wd_b=dscr(f"wdb_{l}", [32 * 128, 2816], BF16), wd_a=dscr(f"wda_{l}", [128 * 128, 2816], BF16),
        ))
    hb = dscr("hb", [8 * 128 * 2 * 4, 512], BF16)
    hall = dscr("hall", [8 * 4 * 128 * 2 * 4, 512], BF16)
    qk = dscr("qk", [16, 128, 4096], BF16)
    vv = dscr("vv", [4096, 1024], BF16)
    mixo = dscr("mixo", [1024, 4096], BF16)
    mixall = dscr("mixall", [8 * 4 * 128, 4096], BF16)
    xres = dscr("xres", [32, 128, 1024], F32)
    dbg = {}
    if debug:
        dbg["hb"] = nc.dram_tensor("d_hb", [8192, 512], BF16, kind="ExternalOutput").ap()
        dbg["qk"] = nc.dram_tensor("d_qk", [16, 128, 4096], BF16, kind="ExternalOutput").ap()
        dbg["vv"] = nc.dram_tensor("d_vv", [4096, 1024], BF16, kind="ExternalOutput").ap()
        dbg["mixo"] = nc.dram_tensor("d_mixo", [1024, 4096], BF16, kind="ExternalOutput").ap()
        dbg["x1"] = nc.dram_tensor("d_x1", [32, 128, 1024], F32, kind="ExternalOutput").ap()

    def stop(name):
        return stop_after is not None and stop_after == name

    with ExitStack() as es:
        P = Prog(nc, es)
        ones = es.enter_context(nc.sbuf_tensor("ones", [128, 128], BF16))
        g1s = [es.enter_context(nc.sbuf_tensor(f"g1s{l}", [128, 32], F32)) for l in range(L)]
        g2s = [es.enter_context(nc.sbuf_tensor(f"g2s{l}", [128, 32], F32)) for l in range(L)]
        qkgs = [es.enter_context(nc.sbuf_tensor(f"qkgs{l}", [128, 16], F32)) for l in range(L)]
        ogs = [es.enter_context(nc.sbuf_tensor(f"ogs{l}", [128, 8], F32)) for l in range(L)]
        nlam = [es.enter_context(nc.sbuf_tensor(f"nlam{l}", [128, 1], F32)) for l in range(L)]
        ohs = es.enter_context(nc.sbuf_tensor("ohs", [128, 4], F32))
        psb = [es.enter_context(nc.psum_tensor(f"psb{i}", [128, 512], F32)) for i in range(8)]

        s_c = P.sem("constld")
        s_cv = P.sem("constv")
        s_wc = [P.sem(f"wcast{l}") for l in range(L)]
        s_wag = [P.sem(f"wag{l}") for l in range(L)]
        s_wcc = P.sem("wcc")
        s_ag = P.sem("ag")
        s_dbg = P.sem("dbg") if debug else None
        dvp = P.sem("dvp")
        pep = P.sem("pep")
        acp = P.sem("acp")

        def cast_win(l, piece):
            if piece < 2:
                src = lw[l]["wqk"][piece * 1024:(piece + 1) * 1024, :]
                dst = lw[l]["wqk_b"].ap()[piece * 1024:(piece + 1) * 1024, :]
            else:
                src = lw[l]["wv"]
                dst = lw[l]["wv_b"].ap()
            v = P.op("gpsimd", lambda e, src=src, dst=dst: e.dma_start(out=dst, in_=src, max_dma_last_dim=8192), inc=(s_wc[l], 16))
            P.op("gpsimd", None, waits=[v])

        wag_calls = []
        wag_idx = []
        wag_issued = [0] * L
        for l in range(L):
            calls = [("wout", j) for j in range(8)]
            done = set()
            for gi, (f0, nf) in enumerate(FGROUPS):
                for j in range(f0 // 4, (f0 + nf - 1) // 4 + 1):
                    if j not in done:
                        done.add(j)
                        calls += [("wg", j), ("wu", j)]
                calls += [("wd", gi * 8 + j) for j in range(8)]
            wag_calls.append(calls)
            wag_idx.append({c: i for i, c in enumerate(calls)})

        def issue_wag(l, n):
            for _ in range(n):
                i = wag_issued[l]
                if i >= len(wag_calls[l]):
                    return
                k, j = wag_calls[l][i]
                wag_issued[l] += 1
                v = P.op("gpsimd", lambda e, l=l, k=k, j=j: e.dma_start(out=lw[l][k + "_b"].ap()[j * 128:(j + 1) * 128, :], in_=lw[l][k][j * 128:(j + 1) * 128, :], max_dma_last_dim=8192), inc=(s_wcc, 16))
                P.op("gpsimd", None, waits=[v])
                P.op("gpsimd", lambda e, l=l, k=k, j=j: e.collective_compute("AllGather", ALU.bypass, replica_groups=GRP4,
                     ins=[lw[l][k + "_b"].ap()[j * 128:(j + 1) * 128, :]], outs=[lw[l][k + "_a"].ap()[j * 512:(j + 1) * 512, :]]), inc=(s_wag[l], 1))

        def wag_sig(l, call):
            return (s_wag[l], wag_idx[l][call] + 1)

        with ExitStack() as ps:
            lamt = ps.enter_context(nc.sbuf_tensor("lamt", [128, 4, 128], F32))
            lamp = ps.enter_context(nc.sbuf_tensor("lamp", [128, 2, 128], F32))
            lams = ps.enter_context(nc.sbuf_tensor("lams", [128, 2], F32))
            lame = ps.enter_context(nc.sbuf_tensor("lame", [128, 2], F32))
            P.op("vector", lambda e: e.memset(ones[:], 1.0), inc=(s_cv, 1))
            for l in range(L):
                for (dst, src) in ((g1s[l], lw[l]["g1"]), (g2s[l], lw[l]["g2"]), (qkgs[l], lw[l]["qkg"]), (ogs[l], lw[l]["og"])):
                    P.op("sync", lambda e, dst=dst, src=src: e.dma_start(out=dst[:], in_=src), inc=(s_c, 16))
            P.op("sync", lambda e: e.dma_start(out=ohs[:], in_=onehot), inc=(s_c, 16))
            cast_win(0, 0)
            prev = None
            for l in range(L):
                v = P.op("sync", lambda e, l=l: e.dma_start(out=lamt[:], in_=lw[l]["lamv"]), waits=[prev], inc=(s_c, 16))
                li = lam_init(l + first_layer_index)
                v1 = P.op("vector", lambda e: e.tensor_tensor(out=lamp[:, 0, :], in0=lamt[:, 0, :], in1=lamt[:, 1, :], op=ALU.mult), waits=[v], inc=(s_cv, 1))
                v2 = P.op("vector", lambda e: e.tensor_tensor(out=lamp[:, 1, :], in0=lamt[:, 2, :], in1=lamt[:, 3, :], op=ALU.mult), inc=(s_cv, 1))
                v3 = P.op("vector", lambda e: e.tensor_reduce(out=lams[:], in_=lamp[:], axis=mybir.AxisListType.X, op=ALU.add), waits=[v2], inc=(s_cv, 1))
                v4 = P.op("scalar", lambda e: e.activation(out=lame[:], in_=lams[:], func=AF.Exp), waits=[v3], inc=(s_cv, 1))
                prev = P.op("vector", lambda e, l=l, li=li: e.scalar_tensor_tensor(out=nlam[l][:], in0=lame[:, 1:2], scalar=-li, in1=lame[:, 0:1], op0=ALU.add, op1=ALU.subtract), waits=[v4], inc=(s_cv, 1))
            P.op("vector", None, waits=[prev])
            P.op("sync", None, waits=[(s_c, P.semcnt[s_c]), prev])
            P.build()

        st = dict(xres_st=[None] * 32)

        for l in range(L):
            W = lw[l]
            labs = l + first_layer_index
            last = (l == L - 1)
            xsrc = xT if l == 0 else xres.ap()
            xdst = yT if last else xres.ap()

            with ExitStack() as ps:
                xs = [ps.enter_context(nc.sbuf_tensor(f"p1x{i}_L{l}", [128, 32, 512], F32)) for i in range(2)]
                hs = [ps.enter_context(nc.sbuf_tensor(f"p1h{i}_L{l}", [128, 32, 512], BF16)) for i in range(2)]
                sq = [ps.enter_context(nc.sbuf_tensor(f"p1sq{i}_L{l}", [128, 4, 512], BF16)) for i in range(2)]
                rt = [ps.enter_context(nc.sbuf_tensor(f"p1rt{i}_L{l}", [128, 512], F32)) for i in range(2)]
                if l == 0:
                    P.p1 = dict(x=[Slot(P, f"p1x{i}") for i in range(2)], sq=[Slot(P, f"p1sq{i}") for i in range(2)],
                                ps=[Slot(P, f"p1ps{i}") for i in range(2)], rt=[Slot(P, f"p1rt{i}") for i in range(2)],
                                h=[Slot(P, f"p1h{i}", store=True) for i in range(2)])
                p1 = P.p1
                hbv = hb.ap().rearrange("(j p h a) t -> j p h a t", j=8, p=128, h=2, a=4)
                stores = []
                for th in range(2):
                    b = th
                    X, SQ, PS_, RT, H = p1["x"][b], p1["sq"], p1["ps"][b], p1["rt"][b], p1["h"][b]
                    sigX = P.op("sync", lambda e, b=b, th=th: e.dma_start(out=xs[b][:], in_=xsrc[:, :, th * 512:(th + 1) * 512].rearrange("fc p t -> p fc t")), waits=X.pw(), inc=X)
                    sigPS = None
                    for j in range(8):
                        sb = j % 2
                        sigSQ = P.op("scalar", lambda e, b=b, j=j, sb=sb: e.activation(out=sq[sb][:], in_=xs[b][:, 4 * j:4 * j + 4, :], func=AF.Square), waits=[sigX] + SQ[sb].pw(), inc=SQ[sb])
                        for i in range(4):
                            fc = 4 * j + i
                            sig = P.op("tensor", lambda e, b=b, sb=sb, i=i, fc=fc: e.matmul(psb[b][:], lhsT=ones[:], rhs=sq[sb][:, i, :], start=(fc == 0), stop=(fc == 31)),
                                       waits=([sigSQ] if i == 0 else []) + (PS_.pw() if fc == 0 else []), inc=PS_ if i == 3 else None)
                        SQ[sb].release(sig)
                        sigPS = sig
                    sigRT = P.op("scalar", lambda e, b=b: e.activation(out=rt[b][:], in_=psb[b][:], func=AF.Ln, scale=1.0 / D, bias=EPS), waits=[sigPS] + RT.pw(), inc=RT)
                    PS_.release(sigRT)
                    sigRR = P.op("scalar", lambda e, b=b: e.activation(out=rt[b][:], in_=rt[b][:], func=AF.Exp, scale=-0.5), waits=[sigRT], inc=(acp, 1))
                    for fc in range(32):
                        sigH = P.op("vector", lambda e, b=b, fc=fc, l=l: e.scalar_tensor_tensor(out=hs[b][:, fc, :], in0=xs[b][:, fc, :], scalar=g1s[l][:, fc:fc + 1], in1=rt[b][:], op0=ALU.mult, op1=ALU.mult),
                                    waits=([sigRR] + H.pw()) if fc == 0 else [], inc=H if fc == 31 else None)
                    RT.release(sigH)
                    X.release(sigH)
                    for j in range(8):
                        sigSt = P.op("sync", lambda e, b=b, th=th, j=j: e.dma_start(out=hbv[j][:, th, :, :], in_=hs[b][:, 4 * j:4 * j + 4, :]), waits=[sigH] if j == 0 else [], inc=(H.s, 16))
                    H.release(sigSt)
                    stores.append(sigSt)
                P.op("sync", None, waits=stores)
                for j in range(8):
                    vag = P.op("gpsimd", lambda e, j=j: e.collective_compute("AllGather", ALU.bypass, replica_groups=GRP4, ins=[hb.ap()[j * 1024:(j + 1) * 1024, :]], outs=[hall.ap()[j * 4096:(j + 1) * 4096, :]]), waits=stores if j == 0 else [], inc=(s_ag, 1))
                P.op("gpsimd", None, waits=[vag])
                if debug and l == 0:
                    v = P.op("gpsimd", lambda e: e.dma_start(out=dbg["hb"], in_=hb.ap()), inc=(s_dbg, 16))
                    P.op("gpsimd", None, waits=[v])
                P.build()
            if stop("p1"):
                break

            with ExitStack() as ps:
                wres = ps.enter_context(nc.sbuf_tensor(f"p2w_L{l}", [128, 32768], BF16))
                ht = [ps.enter_context(nc.sbuf_tensor(f"p2h{i}_L{l}", [128, 32, 512], BF16)) for i in range(3)]
                sqb = [ps.enter_context(nc.sbuf_tensor(f"p2sq{i}_L{l}", [128, 512], BF16)) for i in range(2)]
                rtb = [ps.enter_context(nc.sbuf_tensor(f"p2rt{i}_L{l}", [128, 512], F32)) for i in range(2)]
                qo = [ps.enter_context(nc.sbuf_tensor(f"p2qo{i}_L{l}", [128, 512], BF16)) for i in range(3)]
                if l == 0:
                    P.p2 = dict(w=Slot(P, "p2w"), h=[Slot(P, f"p2h{i}") for i in range(3)],
                                pa=[Slot(P, f"p2pa{i}") for i in range(3)], sq=[Slot(P, f"p2sq{i}") for i in range(2)],
                                pb=[Slot(P, f"p2pb{i}") for i in range(2)], rt=[Slot(P, f"p2rt{i}") for i in range(2)],
                                qo=[Slot(P, f"p2qo{i}", store=True) for i in range(3)],
                                cnt=dict(h=0, pa=0, sq=0, qo=0))
                p2 = P.p2
                c2 = p2["cnt"]
                WS = p2["w"]
                hallv = hall.ap().rearrange("(j r p h a) t -> j r p h a t", j=8, r=4, p=128, h=2, a=4)
                if l == 0:
                    cast_win(0, 1)
                    cast_win(0, 2)
                wag_pace = [15, 15, 15] if l == 0 else [7, 7, 6]
                vvv = vv.ap()
                p2stores = []

                def load_h(t):
                    i = c2["h"] % 3
                    c2["h"] += 1
                    r, half = t // 2, t % 2
                    sig = None
                    for j in range(8):
                        sig = P.op("sync", lambda e, i=i, r=r, half=half, j=j: e.dma_start(out=ht[i][:, 4 * j:4 * j + 4, :], in_=hallv[j][r][:, half, :, :]), waits=p2["h"][i].pw() if j == 0 else [], inc=p2["h"][i])
                    return i, sig

                def load_wreg(pss_, i, waits):
                    if pss_ < 2:
                        src = W["wqk_b"].ap()[(pss_ * 8 + i) * 128:(pss_ * 8 + i + 1) * 128, :]
                    else:
                        src = W["wv_b"].ap()[:, i * 4096:(i + 1) * 4096]
                    return P.op("sync", lambda e, i=i, src=src: e.dma_start(out=wres[:, i * 4096:(i + 1) * 4096], in_=src),
                                waits=waits + [(s_wc[l], 16 * (pss_ + 1))], inc=WS)

                sigW_next = None
                for pss in range(3):
                    if pss == 0:
                        sigW = None
                        for i in range(8):
                            sigW = load_wreg(0, i, WS.pw() if i == 0 else [])
                    else:
                        sigW = sigW_next
                    issue_wag(l, wag_pace[pss])
                    hq = [load_h(0), load_h(1)]
                    sigPA_last = None
                    for t in range(8):
                        hi, sigH = hq.pop(0)
                        H = p2["h"][hi]
                        if t + 2 < 8:
                            hq.append(load_h(t + 2))
                        if pss < 2:
                            deferred = None
                            for s in range(8):
                                sg_ = pss * 8 + s
                                ia = c2["pa"] % 3
                                c2["pa"] += 1
                                PA = p2["pa"][ia]
                                for kc in range(32):
                                    w_ = []
                                    if kc == 0:
                                        w_ = PA.pw() + ([sigH] if s == 0 else []) + ([sigW] if (s == 0 and t == 0) else [])
                                    sigPA = P.op("tensor", lambda e, ia=ia, s=s, kc=kc, hi=hi: e.matmul(psb[ia][:], lhsT=wres[:, (s * 32 + kc) * 128:(s * 32 + kc + 1) * 128], rhs=ht[hi][:, kc, :], start=(kc == 0), stop=(kc == 31)),
                                                  waits=w_, inc=PA if kc == 31 else None)
                                sigPA_last = sigPA
                                if t == 7:
                                    sigW_next = load_wreg(pss + 1, s, [sigPA])
                                isq = c2["sq"] % 2
                                c2["sq"] += 1
                                SQ, PB, RT = p2["sq"][isq], p2["pb"][isq], p2["rt"][isq]
                                sigSQ = P.op("scalar", lambda e, ia=ia, isq=isq: e.activation(out=sqb[isq][:], in_=psb[ia][:], func=AF.Square), waits=[sigPA] + SQ.pw(), inc=SQ)

                                def post(ia=ia, isq=isq, sg_=sg_, t=t, SQ=SQ, PB=PB, RT=RT, PA=PA, sigSQ=sigSQ):
                                    sigPB = P.op("tensor", lambda e: e.matmul(psb[4 + isq][:], lhsT=ones[:], rhs=sqb[isq][:], start=True, stop=True), waits=[sigSQ] + PB.pw(), inc=PB)
                                    SQ.release(sigPB)
                                    sigRT = P.op("scalar", lambda e: e.activation(out=rtb[isq][:], in_=psb[4 + isq][:], func=AF.Ln, scale=1.0 / 128, bias=EPS), waits=[sigPB] + RT.pw(), inc=RT)
                                    PB.release(sigRT)
                                    sigRR = P.op("scalar", lambda e: e.activation(out=rtb[isq][:], in_=rtb[isq][:], func=AF.Exp, scale=-0.5), waits=[sigRT], inc=(acp, 1))
                                    iq = c2["qo"] % 3
                                    c2["qo"] += 1
                                    QO = p2["qo"][iq]
                                    sigQO = P.op("vector", lambda e: e.scalar_tensor_tensor(out=qo[iq][:], in0=psb[ia][:], scalar=qkgs[l][:, sg_:sg_ + 1], in1=rtb[isq][:], op0=ALU.mult, op1=ALU.mult),
                                                 waits=[sigRR] + QO.pw(), inc=QO)
                                    PA.release(sigQO)
                                    RT.release(sigQO)
                                    sigSt = P.op("sync", lambda e: e.dma_start(out=qk.ap()[sg_][:, t * 512:(t + 1) * 512], in_=qo[iq][:]), waits=[sigQO], inc=(QO.s, 16))
                                    QO.release(sigSt)
                                    p2stores.append(sigSt)

                                if deferred is not None:
                                    deferred()
                                deferred = post
                            deferred()
                            H.release(sigPA_last)
                        else:
                            for tb4 in range(4):
                                for ch in range(2):
                                    ia = c2["pa"] % 3
                                    c2["pa"] += 1
                                    PA = p2["pa"][ia]
                                    for kc in range(32):
                                        w_ = []
                                        if kc == 0:
                                            first = (tb4 == 0 and ch == 0)
                                            w_ = PA.pw() + ([sigH] if first else []) + ([sigW] if (first and t == 0) else [])
                                        sigPA = P.op("tensor", lambda e, ia=ia, kc=kc, hi=hi, tb4=tb4, ch=ch: e.matmul(psb[ia][:], lhsT=ht[hi][:, kc, tb4 * 128:(tb4 + 1) * 128], rhs=wres[:, kc * 1024 + ch * 512:kc * 1024 + (ch + 1) * 512], start=(kc == 0), stop=(kc == 31)),
                                                      waits=w_, inc=PA if kc == 31 else None)
                                    sigPA_last = sigPA
                                    iq = c2["qo"] % 3
                                    c2["qo"] += 1
                                    QO = p2["qo"][iq]
                                    sigQO = P.op("scalar", lambda e, ia=ia, iq=iq: e.activation(out=qo[iq][:], in_=psb[ia][:], func=AF.Copy), waits=[sigPA] + QO.pw(), inc=QO)
                                    PA.release(sigQO)
                                    sigSt = P.op("sync", lambda e, iq=iq, t=t, tb4=tb4, ch=ch: e.dma_start(out=vvv[t * 512 + tb4 * 128:t * 512 + (tb4 + 1) * 128, ch * 512:(ch + 1) * 512], in_=qo[iq][:]), waits=[sigQO], inc=(QO.s, 16))
                                    QO.release(sigSt)
                                    p2stores.append(sigSt)
                            H.release(sigPA_last)
                    WS.release(sigPA_last)
                    P.op("gpsimd", None, waits=[sigPA_last])
                P.op("sync", None, waits=p2stores[-3:])
                P.op("gpsimd", None, waits=p2stores[-3:])
                if debug and l == 0:
                    v = P.op("gpsimd", lambda e: e.dma_start(out=dbg["qk"], in_=qk.ap()), inc=(s_dbg, 16))
                    v = P.op("gpsimd", lambda e: e.dma_start(out=dbg["vv"], in_=vv.ap()), inc=(s_dbg, 16))
                    P.op("gpsimd", None, waits=[v])
                P.build()
            if stop("p2"):
                break

            with ExitStack() as ps:
                qT = [ps.enter_context(nc.sbuf_tensor(f"p3q{i}_L{l}", [128, 2, 4096], BF16)) for i in range(2)]
                kT = [ps.enter_context(nc.sbuf_tensor(f"p3k{i}_L{l}", [128, 2, 4096], BF16)) for i in range(2)]
                vt = [ps.enter_context(nc.sbuf_tensor(f"p3v{i}_L{l}", [128, 32, 256], BF16)) for i in range(2)]
                bias = ps.enter_context(nc.sbuf_tensor(f"p3bias_L{l}", [128, 10240], F32))
                stt = [ps.enter_context(nc.sbuf_tensor(f"p3st{i}_L{l}", [128, 512], F32)) for i in range(4)]
                et = [ps.enter_context(nc.sbuf_tensor(f"p3e{i}_L{l}", [128, 512], BF16)) for i in range(4)]
                rz = ps.enter_context(nc.sbuf_tensor(f"p3rz_L{l}", [128, 512], F32))
                t0 = ps.enter_context(nc.sbuf_tensor(f"p3t0_L{l}", [128, 2, 512], F32))
                ob = ps.enter_context(nc.sbuf_tensor(f"p3o_L{l}", [128, 2, 512], F32))
                osq = ps.enter_context(nc.sbuf_tensor(f"p3osq_L{l}", [128, 2, 512], BF16))
                ort = ps.enter_context(nc.sbuf_tensor(f"p3ort_L{l}", [128, 512], F32))
                mo = [ps.enter_context(nc.sbuf_tensor(f"p3mo{i}_L{l}", [128, 2, 512], BF16)) for i in range(2)]
                if l == 0:
                    P.p3 = dict(hd=[Slot(P, f"p3hd{i}") for i in range(2)], bias=Slot(P, "p3bias"),
                                sp=[Slot(P, f"p3sp{i}") for i in range(4)], st=[Slot(P, f"p3st{i}") for i in range(4)],
                                e=[Slot(P, f"p3e{i}") for i in range(4)], acc=[Slot(P, f"p3acc{i}", own=False) for i in range(2)],
                                z=Slot(P, "p3z", own=False), rz=Slot(P, "p3rz"), t0=Slot(P, "p3t0"), o=Slot(P, "p3o"), osq=Slot(P, "p3osq"),
                                ss=Slot(P, "p3ss"), ort=Slot(P, "p3ort"), mo=[Slot(P, f"p3mo{i}", store=True) for i in range(2)],
                                pv=P.sem("p3pv"), cnt=dict(hd=0, sp=0, e=0, acc=0, mo=0))
                p3 = P.p3
                c3 = p3["cnt"]
                pv = p3["pv"]
                units = [("A", 0)] + [("B", i) for i in range(3)] + [("C", i) for i in range(3)]
                qkv = qk.ap()
                vvh = vv.ap().rearrange("(tb p) c -> p tb c", p=128)
                mixv = mixo.ap().rearrange("(c p) t -> p c t", p=128)
                p3stores = []
                mixag = []

                def load_unit(u):
                    kind, hi_ = units[u]
                    i = c3["hd"] % 2
                    c3["hd"] += 1
                    HD = p3["hd"][i]
                    if kind == "A":
                        srcs = [(qT[i][:, 0, :], qkv[0]), (qT[i][:, 1, :], qkv[1]), (kT[i][:, 0, :], qkv[2]), (kT[i][:, 1, :], qkv[3]),
                                (vt[i][:, :, :], vvh[:, :, 0:256])]
                    elif kind == "B":
                        srcs = [(qT[i][:, 0, :], qkv[4 + hi_]), (kT[i][:, 0, :], qkv[7 + hi_]), (vt[i][:, :, 0:128], vvh[:, :, 256 + 128 * hi_:256 + 128 * (hi_ + 1)])]
                    else:
                        srcs = [(qT[i][:, 0, :], qkv[10 + hi_]), (kT[i][:, 0, :], qkv[13 + hi_]), (vt[i][:, :, 0:128], vvh[:, :, 640 + 128 * hi_:640 + 128 * (hi_ + 1)])]
                    sig = None
                    for j, (dst, src) in enumerate(srcs):
                        sig = P.op("sync", lambda e, dst=dst, src=src: e.dma_start(out=dst, in_=src), waits=HD.pw() if j == 0 else [], inc=HD)
                    return i, sig

                def load_bias(u):
                    kind, hi_ = units[u]
                    B = p3["bias"]
                    if kind == "A":
                        return P.op("sync", lambda e: e.dma_start(out=bias[:, 0:8064], in_=biasA), waits=B.pw(), inc=B)
                    if kind == "B":
                        return P.op("sync", lambda e, hi_=hi_: e.dma_start(out=bias[:, :], in_=W["biasB"][hi_]), waits=B.pw(), inc=B)
                    return P.op("sync", lambda e: e.dma_start(out=bias[:, 0:3 * 2944], in_=biasC), waits=B.pw(), inc=B)

                def blocks_for(kind, qt):
                    if kind == "A":
                        return [(kb, qt * 512 - kb * 128 + 3968) for kb in range(32)]
                    if kind == "C":
                        return [(kb, qt * 512 - kb * 128 + 1408) for kb in range(max(0, 4 * qt - 8), min(31, 4 * qt + 11) + 1)]
                    if qt == 0:
                        return [(kb, (8 + kb) * 512) for kb in range(6)]
                    if qt == 7:
                        return [(kb, (14 + kb - 26) * 512) for kb in range(26, 32)]
                    return [(4 * qt - 2 + r, r * 512) for r in range(8)]

                nxt = load_unit(0)
                sigB = load_bias(0)
                mix_chunk = 0
                tail = []
                for u, (kind, hi_) in enumerate(units):
                    hs_, sigHD = nxt
                    HD = p3["hd"][hs_]
                    if u + 1 < len(units):
                        nxt = load_unit(u + 1)
                    nmaps = 2 if kind == "A" else 1
                    ndv = 2 if kind == "A" else 1
                    cbase = (hi_ * 2944) if kind == "C" else 0
                    first_of_unit = True
                    sigSTlast = None
                    sigPVlast = None
                    sigT0 = None
                    for qt in range(8):
                        blks = blocks_for(kind, qt)
                        for m in range(nmaps):
                            if kind == "A":
                                ai = 0
                                obank = [4, 5]
                            else:
                                ai = c3["acc"] % 2
                                c3["acc"] += 1
                                obank = [4 + ai]
                            ACC, Z = p3["acc"][ai], p3["z"]
                            nb_ = len(blks)
                            pend = []

                            def emit_pv(item, isfirst, islast, obank=obank, hs_=hs_, ACC=ACC, Z=Z):
                                (kb, ei, E, sigE) = item
                                for dvc in range(ndv):
                                    w_ = [sigE] if dvc == 0 else []
                                    if isfirst and dvc == 0:
                                        w_ = w_ + ACC.pw()
                                    P.op("tensor", lambda e, ei=ei, kb=kb, dvc=dvc: e.matmul(psb[obank[dvc]][:], lhsT=vt[hs_][:, kb, dvc * 128:(dvc + 1) * 128], rhs=et[ei][:], start=isfirst, stop=islast), waits=w_)
                                sig = P.op("tensor", lambda e, ei=ei: e.matmul(psb[6][:], lhsT=ones[:], rhs=et[ei][:], start=isfirst, stop=islast),
                                           waits=Z.pw() if isfirst else [], inc=(pv, 1))
                                E.release(sig)
                                return sig

                            npv = 0
                            for bi, (kb, boff) in enumerate(blks):
                                si = c3["sp"] % 4
                                c3["sp"] += 1
                                SP, ST = p3["sp"][si], p3["st"][si]
                                sigSP = P.op("tensor", lambda e, si=si, kb=kb, qt=qt, m=m, hs_=hs_: e.matmul(psb[si][:], lhsT=kT[hs_][:, m, kb * 128:(kb + 1) * 128], rhs=qT[hs_][:, m, qt * 512:(qt + 1) * 512], start=True, stop=True),
                                              waits=SP.pw() + ([sigHD] if first_of_unit else []), inc=SP)
                                sigST = P.op("vector", lambda e, si=si, boff=boff, cbase=cbase: e.scalar_tensor_tensor(out=stt[si][:], in0=psb[si][:], scalar=SCALE, in1=bias[:, cbase + boff:cbase + boff + 512], op0=ALU.mult, op1=ALU.add),
                                              waits=[sigSP] + ST.pw() + ([sigB] if first_of_unit else []), inc=ST)
                                first_of_unit = False
                                SP.release(sigST)
                                sigSTlast = sigST
                                ei = c3["e"] % 4
                                c3["e"] += 1
                                E = p3["e"][ei]
                                sigE = P.op("scalar", lambda e, si=si, ei=ei: e.activation(out=et[ei][:], in_=stt[si][:], func=AF.Exp), waits=[sigST] + E.pw(), inc=E)
                                ST.release(sigE)
                                pend.append((kb, ei, E, sigE))
                                if len(pend) > 3:
                                    emit_pv(pend.pop(0), npv == 0, False)
                                    npv += 1
                                if bi == 2 and tail:
                                    for fn_ in tail:
                                        fn_()
                                    tail = []
                            while pend:
                                sigPVlast = emit_pv(pend.pop(0), npv == 0, len(pend) == 0)
                                npv += 1
                            RZ, T0, O, OSQ, SS, ORT = p3["rz"], p3["t0"], p3["o"], p3["osq"], p3["ss"], p3["ort"]
                            sigLZ = P.op("scalar", lambda e: e.activation(out=rz[:], in_=psb[6][:], func=AF.Ln), waits=[sigPVlast] + RZ.pw(), inc=(acp, 1))
                            Z.release(sigLZ)
                            sigRZ = P.op("scalar", lambda e: e.activation(out=rz[:], in_=rz[:], func=AF.Exp, scale=-1.0), waits=[sigLZ], inc=RZ)
                            if kind == "A" and m == 0:
                                for dvc in range(2):
                                    sigT0 = P.op("vector", lambda e, dvc=dvc, ai=ai: e.tensor_tensor(out=t0[:, dvc, :], in0=psb[4 + dvc][:], in1=rz[:], op=ALU.mult),
                                                 waits=([sigRZ] + T0.pw()) if dvc == 0 else [], inc=T0 if dvc == 1 else None)
                                ACC.release(sigT0)
                                RZ.release(sigT0)
                                continue
                            if kind == "A":
                                for dvc in range(2):
                                    sigX_ = P.op("vector", lambda e, dvc=dvc, ai=ai: e.tensor_tensor(out=ob[:, dvc, :], in0=psb[4 + dvc][:], in1=rz[:], op=ALU.mult),
                                                 waits=([sigRZ] + O.pw()) if dvc == 0 else [], inc=(dvp, 1) if dvc == 1 else None)
                                ACC.release(sigX_)
                                RZ.release(sigX_)
                                for dvc in range(2):
                                    sigO = P.op("vector", lambda e, dvc=dvc, l=l: e.scalar_tensor_tensor(out=ob[:, dvc, :], in0=ob[:, dvc, :], scalar=nlam[l][:, 0:1], in1=t0[:, dvc, :], op0=ALU.mult, op1=ALU.add),
                                                waits=[sigX_, sigT0] if dvc == 0 else [], inc=O if dvc == 1 else None)
                                T0.release(sigO)
                                nfeat = 256
                            else:
                                sigO = P.op("vector", lambda e, ai=ai: e.tensor_tensor(out=ob[:, 0, :], in0=psb[4 + ai][:], in1=rz[:], op=ALU.mult), waits=[sigRZ] + O.pw(), inc=O)
                                ACC.release(sigO)
                                RZ.release(sigO)
                                nfeat = 128
                            sigOSQ = P.op("scalar", lambda e, ndv=ndv: e.activation(out=osq[:, 0:ndv, :], in_=ob[:, 0:ndv, :], func=AF.Square), waits=[sigO] + OSQ.pw(), inc=OSQ)
                            cm = (1.0 - lam_init(labs)) if kind == "A" else 1.0

                            def fin_tail(ndv=ndv, nfeat=nfeat, cm=cm, mc=mix_chunk, qt=qt, sigOSQ=sigOSQ, O=O, OSQ=OSQ, SS=SS, ORT=ORT, u=u):
                                for dvc in range(ndv):
                                    sigSS = P.op("tensor", lambda e, dvc=dvc: e.matmul(psb[7][:], lhsT=ones[:], rhs=osq[:, dvc, :], start=(dvc == 0), stop=(dvc == ndv - 1)),
                                                 waits=([sigOSQ] + SS.pw()) if dvc == 0 else [], inc=SS if dvc == ndv - 1 else None)
                                OSQ.release(sigSS)
                                sigORT = P.op("scalar", lambda e: e.activation(out=ort[:], in_=psb[7][:], func=AF.Ln, scale=1.0 / nfeat, bias=EPS), waits=[sigSS] + ORT.pw(), inc=ORT)
                                SS.release(sigORT)
                                sigORR = P.op("scalar", lambda e: e.activation(out=ort[:], in_=ort[:], func=AF.Exp, scale=-0.5, bias=math.log(cm)), waits=[sigORT], inc=(acp, 1))
                                mi = c3["mo"] % 2
                                c3["mo"] += 1
                                MO = p3["mo"][mi]
                                for dvc in range(ndv):
                                    sigMO = P.op("vector", lambda e, dvc=dvc: e.scalar_tensor_tensor(out=mo[mi][:, dvc, :], in0=ob[:, dvc, :], scalar=ogs[l][:, mc + dvc:mc + dvc + 1], in1=ort[:], op0=ALU.mult, op1=ALU.mult),
                                                  waits=([sigORR] + MO.pw()) if dvc == 0 else [], inc=MO if dvc == ndv - 1 else None)
                                O.release(sigMO)
                                ORT.release(sigMO)
                                sigSt = P.op("sync", lambda e: e.dma_start(out=mixv[:, mc:mc + ndv, qt * 512:(qt + 1) * 512], in_=mo[mi][:, 0:ndv, :]), waits=[sigMO], inc=(MO.s, 16))
                                MO.release(sigSt)
                                p3stores.append(sigSt)
                                if qt == 7:
                                    for cc in range(mc, mc + ndv):
                                        vag_ = P.op("gpsimd", lambda e, cc=cc: e.collective_compute("AllGather", ALU.bypass, replica_groups=GRP4, ins=[mixo.ap()[cc * 128:(cc + 1) * 128, :]], outs=[mixall.ap()[cc * 512:(cc + 1) * 512, :]]),
                                                     waits=p3stores[-2:], inc=(s_ag, 1))
                                        mixag.append(vag_)
                                    issue_wag(l, 2 if u < 5 else 0)

                            tail.append(fin_tail)
                    mix_chunk += ndv
                    HD.release(sigPVlast)
                    nxtu = units[u + 1] if u + 1 < len(units) else None
                    if nxtu is not None and (nxtu[0] != "C" or nxtu[1] == 0):
                        p3["bias"].release(sigSTlast)
                        sigB = load_bias(u + 1)
                    elif nxtu is None:
                        p3["bias"].release(sigSTlast)
                for fn_ in tail:
                    fn_()
                tail = []
                vst = p3stores[-2:]
                P.op("sync", None, waits=vst)
                st["mixag"] = list(mixag)
                if debug and l == 0:
                    v = P.op("gpsimd", lambda e: e.dma_start(out=dbg["mixo"], in_=mixo.ap()), inc=(s_dbg, 16))
                    P.op("gpsimd", None, waits=[v])
                P.build()
            if stop("p3"):
                break

            with ExitStack() as ps:
                acta = ps.enter_context(nc.sbuf_tensor(f"p4a_L{l}", [128, 32, 1024], BF16))
                actb = ps.enter_context(nc.sbuf_tensor(f"p4b_L{l}", [128, 32, 1024], BF16))
                wr = [ps.enter_context(nc.sbuf_tensor(f"p4w{i}_L{l}", [128, 32, 128], BF16)) for i in range(4)]
                stage = [wr[i][:].rearrange("p a b -> p (a b)").rearrange("p (q t) -> p q t", q=4) for i in range(4)]
                xin = [ps.enter_context(nc.sbuf_tensor(f"p4xin{i}_L{l}", [128, 1024], F32)) for i in range(3)]
                x1 = [ps.enter_context(nc.sbuf_tensor(f"p4x1{i}_L{l}", [128, 1024], F32)) for i in range(4)]
                xsq = [ps.enter_context(nc.sbuf_tensor(f"p4xsq{i}_L{l}", [128, 1024], BF16)) for i in range(2)]
                rt2 = ps.enter_context(nc.sbuf_tensor(f"p4rt2_L{l}", [128, 1024], F32))
                sgt = [ps.enter_context(nc.sbuf_tensor(f"p5sg{i}_L{l}", [128, 1024], F32)) for i in range(2)]
                if l == 0:
                    P.p4 = dict(stg=[Slot(P, f"p4stg{i}") for i in range(4)], a=Slot(P, "p4a", own=False),
                                w=[Slot(P, f"p4w{i}") for i in range(4)], xin=[Slot(P, f"p4xin{i}") for i in range(3)],
                                pm=[Slot(P, f"p4pm{i}") for i in range(2)], x1=[Slot(P, f"p4x1{i}", store=True) for i in range(4)],
                                xsq=[Slot(P, f"p4xsq{i}") for i in range(2)], stat=Slot(P, "p4stat", own=False), rt=Slot(P, "p4rt"),
                                b=Slot(P, "p4b", own=False), pg=Slot(P, "p5pg"), pu=Slot(P, "p5pu"), sg=[Slot(P, f"p5sg{i}") for i in range(2)],
                                act=Slot(P, "p5act", own=False), cnt=dict(w=0, xin=0, pm=0, x1=0, xsq=0, stg=0, sg=0))
                p4 = P.p4
                c4 = p4["cnt"]
                xres_st = st["xres_st"]
                wouta = W["wout_a"].ap().rearrange("(m p) c -> m p c", p=128)
                wga = W["wg_a"].ap().rearrange("(f p) c -> f p c", p=128)
                wua = W["wu_a"].ap().rearrange("(f p) c -> f p c", p=128)
                wda = W["wd_a"].ap().rearrange("(g m p) c -> g m p c", g=4, p=128)
                issue_wag(l, 1000)
                mixallv = mixall.ap().rearrange("(kc p) (q t) -> p kc q t", p=128, q=4)
                A, Bs, STAT, ACT_ = p4["a"], p4["b"], p4["stat"], p4["act"]

                sigA = None
                for kc in range(32):
                    si = c4["stg"] % 4
                    c4["stg"] += 1
                    SG = p4["stg"][si]
                    sigSG = P.op("sync", lambda e, si=si, kc=kc: e.dma_start(out=stage[si], in_=mixallv[:, kc, :, :]), waits=SG.pw() + [st["mixag"][kc // 4]], inc=SG)
                    v = P.op("vector", lambda e, si=si, kc=kc: e.tensor_scalar(out=acta[:, kc, :], in0=stage[si][:, 0, :], scalar1=ohs[:, 0:1], scalar2=0.0, op0=ALU.mult, op1=ALU.add),
                             waits=[sigSG] + (A.pw() if kc == 0 else []), inc=(dvp, 1))
                    for q in range(1, 4):
                        v = P.op("vector", lambda e, si=si, kc=kc, q=q: e.scalar_tensor_tensor(out=acta[:, kc, :], in0=stage[si][:, q, :], scalar=ohs[:, q:q + 1], in1=acta[:, kc, :], op0=ALU.mult, op1=ALU.add),
                                 waits=[v], inc=(dvp, 1))
                    SG.release(v)
                    sigA = v
                for i_ in range(4):
                    p4["w"][i_].release(sigA)

                def load_w(src_ap, call, ncols=4096):
                    i = c4["w"] % 4
                    c4["w"] += 1
                    WSl = p4["w"][i]
                    sig = P.op("sync", lambda e, i=i, src_ap=src_ap, ncols=ncols: e.dma_start(out=wr[i][:].rearrange("p a b -> p (a b)")[:, 0:ncols], in_=src_ap), waits=WSl.pw() + [wag_sig(l, call)], inc=WSl)
                    return i, sig

                def load_xin(m, src, wait_store):
                    i = c4["xin"] % 3
                    c4["xin"] += 1
                    XI = p4["xin"][i]
                    sig = P.op("sync", lambda e, i=i, m=m, src=src: e.dma_start(out=xin[i][:], in_=src[m]), waits=XI.pw() + [wait_store], inc=XI)
                    return i, sig

                wq = [load_w(wouta[0], ("wout", 0)), load_w(wouta[1], ("wout", 0)), load_w(wouta[2], ("wout", 0))]
                xq = [load_xin(0, xsrc, xres_st[0] if l > 0 else None), load_xin(1, xsrc, xres_st[1] if l > 0 else None)]
                deferred = None
                sigSTAT = None
                for m in range(32):
                    wi, sigW = wq.pop(0)
                    if m + 3 < 32:
                        wq.append(load_w(wouta[m + 3], ("wout", (m + 3) // 4)))
                    xi, sigXI = xq.pop(0)
                    if m + 2 < 32:
                        xq.append(load_xin(m + 2, xsrc, xres_st[m + 2] if l > 0 else None))
                    WSl, XI = p4["w"][wi], p4["xin"][xi]
                    pi_ = c4["pm"] % 2
                    c4["pm"] += 1
                    PM = p4["pm"][pi_]
                    for kc in range(32):
                        for th in range(2):
                            w_ = []
                            if kc == 0 and th == 0:
                                w_ = PM.pw() + [sigW] + ([sigA] if m == 0 else [])
                            sigPM = P.op("tensor", lambda e, wi=wi, kc=kc, th=th, pi_=pi_: e.matmul(psb[2 * pi_ + th][:], lhsT=wr[wi][:, kc, :], rhs=acta[:, kc, th * 512:(th + 1) * 512], start=(kc == 0), stop=(kc == 31)),
                                          waits=w_, inc=PM if (kc == 31 and th == 1) else None)
                    WSl.release(sigPM)
                    if deferred is not None:
                        deferred()
                        deferred = None
                    x1i = c4["x1"] % 4
                    c4["x1"] += 1
                    X1 = p4["x1"][x1i]
                    for th in range(2):
                        sigX1 = P.op("vector", lambda e, th=th, pi_=pi_, xi=xi, x1i=x1i: e.tensor_tensor(out=x1[x1i][:, th * 512:(th + 1) * 512], in0=psb[2 * pi_ + th][:], in1=xin[xi][:, th * 512:(th + 1) * 512], op=ALU.add),
                                      waits=([sigPM, sigXI] + X1.pw()) if th == 0 else [], inc=X1 if th == 1 else None)
                    PM.release(sigX1)
                    XI.release(sigX1)
                    sigSt = P.op("sync", lambda e, x1i=x1i, m=m: e.dma_start(out=xres.ap()[m], in_=x1[x1i][:]), waits=[sigX1], inc=(X1.s, 16))
                    xres_st[m] = sigSt
                    X1.release(sigSt)
                    qi = c4["xsq"] % 2
                    c4["xsq"] += 1
                    XS = p4["xsq"][qi]
                    sigXS = P.op("scalar", lambda e, x1i=x1i, qi=qi: e.activation(out=xsq[qi][:], in_=x1[x1i][:], func=AF.Square), waits=[sigX1] + XS.pw(), inc=XS)
                    X1.release(sigXS)

                    def stat_mm(m=m, qi=qi, XS=XS, sigXS=sigXS):
                        for th in range(2):
                            sig = P.op("tensor", lambda e, th=th: e.matmul(psb[6 + th][:], lhsT=ones[:], rhs=xsq[qi][:, th * 512:(th + 1) * 512], start=(m == 0), stop=(m == 31)),
                                       waits=([sigXS] + (STAT.pw() if m == 0 else [])) if th == 0 else [], inc=(pep, 1) if th == 1 else None)
                        XS.release(sig)
                        return sig
                    deferred = stat_mm
                sigSTAT = deferred()
                RT = p4["rt"]
                for th in range(2):
                    sigRT = P.op("scalar", lambda e, th=th: e.activation(out=rt2[:, th * 512:(th + 1) * 512], in_=psb[6 + th][:], func=AF.Ln, scale=1.0 / D, bias=EPS),
                                 waits=([sigSTAT] + RT.pw()) if th == 0 else [], inc=RT if th == 1 else None)
                STAT.release(sigRT)
                sigRR = P.op("scalar", lambda e: e.activation(out=rt2[:], in_=rt2[:], func=AF.Exp, scale=-0.5), waits=[sigRT], inc=(acp, 1))
                xq = [load_xin(0, xres.ap(), xres_st[0]), load_xin(1, xres.ap(), xres_st[1])]
                sigB2 = None
                for m in range(32):
                    xi, sigXI = xq.pop(0)
                    if m + 2 < 32:
                        xq.append(load_xin(m + 2, xres.ap(), xres_st[m + 2]))
                    XI = p4["xin"][xi]
                    sigB2 = P.op("vector", lambda e, m=m, xi=xi, l=l: e.scalar_tensor_tensor(out=actb[:, m, :], in0=xin[xi][:], scalar=g2s[l][:, m:m + 1], in1=rt2[:], op0=ALU.mult, op1=ALU.mult),
                                 waits=[sigXI] + (([sigRR] + Bs.pw()) if m == 0 else []), inc=(dvp, 1))
                    XI.release(sigB2)
                RT.release(sigB2)
                if l + 1 < L:
                    P.op("gpsimd", None, waits=[sigB2])
                    for piece in range(3):
                        cast_win(l + 1, piece)
                    issue_wag(l + 1, 14)
                if debug and l == 0:
                    v = P.op("gpsimd", lambda e: e.dma_start(out=dbg["x1"], in_=xres.ap()), waits=xres_st[28:32], inc=(s_dbg, 16))
                    P.op("gpsimd", None, waits=[v])
                if stop("p4"):
                    P.op("sync", None, waits=xres_st[28:32])
                    P.op("vector", None, waits=[sigB2])
                    P.build()
                    break

                PG, PU = p4["pg"], p4["pu"]
                sigPU = None
                sigPM = None
                for gi, (f0, nf) in enumerate(FGROUPS):
                    lastg = (gi == len(FGROUPS) - 1)
                    wq = [(load_w(wga[f0], ("wg", f0 // 4)), load_w(wua[f0], ("wu", f0 // 4)))]
                    sigACT = None
                    for fi in range(nf):
                        f = f0 + fi
                        (wgi, sigWg), (wui, sigWu) = wq.pop(0)
                        if fi + 1 < nf:
                            wq.append((load_w(wga[f + 1], ("wg", (f + 1) // 4)), load_w(wua[f + 1], ("wu", (f + 1) // 4))))
                        sigs = {}
                        for (wi, sigW_, PSL, base) in ((wgi, sigWg, PG, 0), (wui, sigWu, PU, 2)):
                            for kc in range(32):
                                for th in range(2):
                                    w_ = []
                                    if kc == 0 and th == 0:
                                        w_ = PSL.pw() + [sigW_] + ([sigB2] if (gi == 0 and fi == 0 and base == 0) else [])
                                    sig = P.op("tensor", lambda e, wi=wi, kc=kc, th=th, base=base: e.matmul(psb[base + th][:], lhsT=wr[wi][:, kc, :], rhs=actb[:, kc, th * 512:(th + 1) * 512], start=(kc == 0), stop=(kc == 31)),
                                               waits=w_, inc=PSL if (kc == 31 and th == 1) else None)
                            p4["w"][wi].release(sig)
                            sigs[base] = sig
                        sigPG, sigPU = sigs[0], sigs[2]
                        si = c4["sg"] % 2
                        c4["sg"] += 1
                        SGS = p4["sg"][si]
                        for th in range(2):
                            sigSG = P.op("scalar", lambda e, th=th, si=si: e.activation(out=sgt[si][:, th * 512:(th + 1) * 512], in_=psb[th][:], func=AF.Silu),
                                         waits=([sigPG] + SGS.pw()) if th == 0 else [], inc=SGS if th == 1 else None)
                        PG.release(sigSG)
                        for th in range(2):
                            sigACT = P.op("vector", lambda e, th=th, si=si, fi=fi: e.tensor_tensor(out=acta[:, fi, th * 512:(th + 1) * 512], in0=psb[2 + th][:], in1=sgt[si][:, th * 512:(th + 1) * 512], op=ALU.mult),
                                          waits=([sigPU, sigSG] + (ACT_.pw() + A.pw() if fi == 0 else [])) if th == 0 else [], inc=(dvp, 1) if th == 1 else None)
                        PU.release(sigACT)
                        SGS.release(sigACT)

                    def load_wd(m, gi=gi, nf=nf):
                        return load_w(wda[gi][m][:, 0:nf * 128], ("wd", gi * 8 + m // 4), ncols=nf * 128)
                    wq = [load_wd(0), load_wd(1), load_wd(2)]
                    xq = [load_xin(0, xres.ap(), xres_st[0]), load_xin(1, xres.ap(), xres_st[1])]
                    for m in range(32):
                        wi, sigW = wq.pop(0)
                        if m + 3 < 32:
                            wq.append(load_wd(m + 3))
                        xi, sigXI = xq.pop(0)
                        if m + 2 < 32:
                            xq.append(load_xin(m + 2, xres.ap(), xres_st[m + 2]))
                        WSl, XI = p4["w"][wi], p4["xin"][xi]
                        pi_ = c4["pm"] % 2
                        c4["pm"] += 1
                        PM = p4["pm"][pi_]
                        for fi in range(nf):
                            for th in range(2):
                                w_ = []
                                if fi == 0 and th == 0:
                                    w_ = PM.pw() + [sigW] + ([sigACT] if m == 0 else [])
                                sigPM = P.op("tensor", lambda e, wi=wi, fi=fi, th=th, pi_=pi_, nf=nf: e.matmul(psb[4 + 2 * pi_ + th][:], lhsT=wr[wi][:, fi, :], rhs=acta[:, fi, th * 512:(th + 1) * 512], start=(fi == 0), stop=(fi == nf - 1)),
                                              waits=w_, inc=PM if (fi == nf - 1 and th == 1) else None)
                        WSl.release(sigPM)
                        x1i = c4["x1"] % 4
                        c4["x1"] += 1
                        X1 = p4["x1"][x1i]
                        for th in range(2):
                            sigX1 = P.op("vector", lambda e, th=th, pi_=pi_, xi=xi, x1i=x1i: e.tensor_tensor(out=x1[x1i][:, th * 512:(th + 1) * 512], in0=psb[4 + 2 * pi_ + th][:], in1=xin[xi][:, th * 512:(th + 1) * 512], op=ALU.add),
                                          waits=([sigPM, sigXI] + X1.pw()) if th == 0 else [], inc=X1 if th == 1 else None)
                        PM.release(sigX1)
                        XI.release(sigX1)
                        dst = xdst if lastg else xres.ap()
                        sigSt = P.op("sync", lambda e, x1i=x1i, m=m, dst=dst: e.dma_start(out=dst[m], in_=x1[x1i][:]), waits=[sigX1], inc=(X1.s, 16))
                        xres_st[m] = sigSt
                        X1.release(sigSt)
                    ACT_.release(sigPM)
                    if l + 1 < L and gi < 3:
                        P.op("gpsimd", None, waits=[sigPM])
                        issue_wag(l + 1, 12)
                A.release(sigPM)
                Bs.release(sigPU)
                P.op("sync", None, waits=xres_st[28:32])
                P.build()
    return nc


def _alibi(n):
    return np.exp2(-8.0 * np.arange(1, n + 1, dtype=np.float64) / n)


def _bias_A(g):
    slope = _alibi(4)[g]
    p = np.arange(128)[:, None]
    c = np.arange(8064)[None, :]
    return (-slope * np.abs(c - 3968 - p)).astype(np.float32)


def _bias_C(g):
    out = np.empty((128, 3, 2944), np.float32)
    p = np.arange(128)[:, None]
    c = np.arange(2944)[None, :]
    d = c - 1408 - p
    ad = np.abs(d)
    cnt = (ad <= 64).astype(np.int64) + ((d % 4 == 0) & (ad <= 256)) + ((d % 16 == 0) & (ad <= 1024))
    with np.errstate(divide="ignore"):
        lc = np.log(cnt.astype(np.float64))
    for i in range(3):
        slope = _alibi(12)[3 * g + i]
        v = -slope * ad + lc
        out[:, i, :] = np.where(cnt > 0, v, NEG).astype(np.float32)
    return out.reshape(128, 3 * 2944)


def _bias_B(rpb_l, g):
    out = np.full((3, 128, 20, 512), NEG, np.float32)
    pidx = np.arange(128)
    krl, kc = pidx // 64, pidx % 64
    fidx = np.arange(512)
    qrl, qc = fidx // 64, fidx % 64
    cstart = np.clip(qc - 8, 0, 48)
    colok = (kc[:, None] >= cstart[None, :]) & (kc[:, None] < cstart[None, :] + 16)
    dc = np.clip(kc[:, None] - qc[None, :] + 15, 0, 30)
    cases = [(1, 4 * 1 - 2 + r, r) for r in range(8)] + [(0, kb, 8 + kb) for kb in range(6)] + [(7, kb, 14 + kb - 26) for kb in range(26, 32)]
    for (qt, kb, ti) in cases:
        kr = 2 * kb + krl
        qr = 8 * qt + qrl
        rstart = np.clip(qr - 4, 0, 56)
        rowok = (kr[:, None] >= rstart[None, :]) & (kr[:, None] < rstart[None, :] + 8)
        dr = np.clip(kr[:, None] - qr[None, :] + 7, 0, 14)
        ok = rowok & colok
        for i in range(3):
            vals = rpb_l[3 * g + i][dr, dc]
            out[i, :, ti, :] = np.where(ok, vals, NEG)
    return out.reshape(3, 128, 20 * 512)


def _block_w(Wm, kcn, nb):
    return np.ascontiguousarray(Wm.reshape(kcn, 128, nb, 128).transpose(2, 1, 0, 3))


def _qk_cols(g):
    cols = []
    for m in range(2):
        cols.append(np.arange(g * 256 + m * 128, g * 256 + (m + 1) * 128))
    for m in range(2):
        cols.append(1024 + np.arange(g * 256 + m * 128, g * 256 + (m + 1) * 128))
    for i in range(3):
        cols.append(3072 + (3 * g + i) * 128 + np.arange(128))
    for i in range(3):
        cols.append(4608 + (3 * g + i) * 128 + np.arange(128))
    for i in range(3):
        cols.append(7680 + (3 * g + i) * 128 + np.arange(128))
    for i in range(3):
        cols.append(9216 + (3 * g + i) * 128 + np.arange(128))
    return cols


def _v_cols(g):
    c = [2048 + g * 256 + np.arange(256)]
    for i in range(3):
        c.append(6144 + (3 * g + i) * 128 + np.arange(128))
    for i in range(3):
        c.append(10752 + (3 * g + i) * 128 + np.arange(128))
    return np.concatenate(c)


def _wout_perm():
    rows = []
    for c in range(8):
        for r in range(4):
            if c < 2:
                rows.append(r * 256 + c * 128 + np.arange(128))
            elif c < 5:
                rows.append(1024 + (3 * r + (c - 2)) * 128 + np.arange(128))
            else:
                rows.append(2560 + (3 * r + (c - 5)) * 128 + np.arange(128))
    return np.concatenate(rows)


def _col128(v):
    return np.ascontiguousarray(v.reshape(-1, 128).T.astype(np.float32))


def prepare_inputs(inp, layers):
    f32 = np.float32
    maps = [dict() for _ in range(NCORES)]
    x = np.asarray(inp["x"], f32)
    for c in range(NCORES):
        b, g = c // 4, c % 4
        maps[c]["xT"] = np.ascontiguousarray(x[b, g * 1024:(g + 1) * 1024, :].T).reshape(32, 128, 1024)
        maps[c]["biasA"] = _bias_A(g)
        maps[c]["biasC"] = _bias_C(g)
        oh = np.zeros((128, 4), f32)
        oh[:, g] = 1.0
        maps[c]["onehot"] = oh
    perm = _wout_perm()
    for li, l in enumerate(layers):
        w_in = np.asarray(inp["w_in"][l], f32)
        def shard_blocks(blk, npad):
            nb, _, X = blk.shape
            out = np.zeros((4, npad // 4, 128, X), f32)
            for r in range(4):
                sel = blk[r::4]
                out[r, :sel.shape[0]] = sel
            return out.reshape(4, npad // 4 * 128, X)
        wout_blk = shard_blocks(_block_w(np.asarray(inp["w_out"][l], f32)[perm, :], 32, 32).reshape(32, 128, 4096), 32)
        wg_blk = shard_blocks(_block_w(np.asarray(inp["w_gate"][l], f32), 32, NF).reshape(NF, 128, 4096), 88)
        wu_blk = shard_blocks(_block_w(np.asarray(inp["w_up"][l], f32), 32, NF).reshape(NF, 128, 4096), 88)
        wd_full = _block_w(np.asarray(inp["w_down"][l], f32), NF, 32).reshape(32, 128, NF, 128)
        wd_units = np.zeros((4, 32, 128, 2816), f32)
        for gi, (f0, nf) in enumerate(FGROUPS):
            wd_units[gi, :, :, :nf * 128] = wd_full[:, :, f0:f0 + nf, :].reshape(32, 128, nf * 128)
        wd_blk = np.stack([wd_units[:, r::4].reshape(32, 128, 2816) for r in range(4)]).reshape(4, 32 * 128, 2816)
        lamv = np.stack([inp["lambda_q1"][l], inp["lambda_k1"][l], inp["lambda_q2"][l], inp["lambda_k2"][l]]).astype(f32)
        lamv = np.ascontiguousarray(np.broadcast_to(lamv[None], (128, 4, 128)))
        g1 = _col128(np.asarray(inp["norm1_g"][l]))
        g2 = _col128(np.asarray(inp["norm2_g"][l]))
        qkg = np.stack([inp["a_q_g"][l]] * 2 + [inp["a_k_g"][l]] * 2 + [inp["b_q_g"][l]] * 3 + [inp["b_k_g"][l]] * 3
                       + [inp["c_q_g"][l]] * 3 + [inp["c_k_g"][l]] * 3, axis=1).astype(f32)
        og = np.stack([inp["a_out_g"][l][0:128], inp["a_out_g"][l][128:256]] + [inp["b_out_g"][l]] * 3 + [inp["c_out_g"][l]] * 3, axis=1).astype(f32)
        for g in range(4):
            cols = _qk_cols(g)
            wqk = np.stack([w_in[:, cc].reshape(32, 128, 128).transpose(1, 0, 2) for cc in cols]).reshape(16 * 128, 4096)
            wv = np.ascontiguousarray(w_in[:, _v_cols(g)].reshape(32, 128, 1024).transpose(1, 0, 2)).reshape(128, 32768)
            bB = _bias_B(np.asarray(inp["b_rpb"][l], f32), g)
            for b in range(2):
                m = maps[b * 4 + g]
                m[f"wqk_{li}"] = wqk
                m[f"wv_{li}"] = wv
                m[f"biasB_{li}"] = bB
        for c in range(NCORES):
            m = maps[c]
            m[f"g1_{li}"] = g1
            m[f"g2_{li}"] = g2
            m[f"qkg_{li}"] = np.ascontiguousarray(qkg)
            m[f"og_{li}"] = np.ascontiguousarray(og)
            m[f"lamv_{li}"] = lamv
            m[f"wout_{li}"] = wout_blk[c % 4]
            m[f"wg_{li}"] = wg_blk[c % 4]
            m[f"wu_{li}"] = wu_blk[c % 4]
            m[f"wd_{li}"] = wd_blk[c % 4]
    return maps


_NC_CACHE = {}


def _get_nc(n_layers, first, debug=False):
    key = (n_layers, first, debug)
    if key not in _NC_CACHE:
        _NC_CACHE[key] = build_program(n_layers, first, debug)
    return _NC_CACHE[key]


def assemble_output(res):
    out = np.empty((NB, S, D), np.float32)
    for c in range(NCORES):
        b, g = c // 4, c % 4
        out[b, g * 1024:(g + 1) * 1024, :] = res[c]["yT"].reshape(D, 1024).T
    return out


def kernel(**inputs):
    nc = _get_nc(DEPTH, 0)
    maps = prepare_inputs(inputs, list(range(DEPTH)))
    res = run_bass_kernel_spmd(nc, maps, core_ids=list(range(NCORES)))
    return assemble_output(res.results)
```

```python
import math
from contextlib import ExitStack

import numpy as np
import concourse.bass as bass
import concourse.mybir as mybir
from concourse.bass_utils import run_bass_kernel_spmd

F32 = mybir.dt.float32
BF16 = mybir.dt.bfloat16
AF = mybir.ActivationFunctionType
ALU = mybir.AluOpType

D = 4096
S = 4096
NB = 2
DEPTH = 2
DFF = 11008
NF = DFF // 128
EPS = 1e-6
NEG = -30000.0
SCALE = 128.0 ** -0.5
FGROUPS = [(0, 22), (22, 22), (44, 21), (65, 21)]
NCORES = 8
ENGS = ["sync", "scalar", "vector", "gpsimd", "tensor"]
GRP4 = [[0, 1, 2, 3], [4, 5, 6, 7]]
GRP8 = [list(range(8))]


def lam_init(l):
    return 0.8 - 0.6 * math.exp(-0.3 * l)


class Prog:
    def __init__(self, nc, es):
        self.nc = nc
        self.es = es
        self.semcnt = {}
        self.semh = {}
        self.ops = {e: [] for e in ENGS}

    def sem(self, name):
        assert name not in self.semcnt, name
        self.semcnt[name] = 0
        self.semh[name] = self.es.enter_context(self.nc.semaphore(name))
        return name

    def op(self, eng, fn, waits=(), inc=None):
        sig = None
        if inc is not None:
            if isinstance(inc, Slot):
                inc = (inc.f, 16 if eng in ("sync", "gpsimd_dma") else 1)
            s, a = inc
            self.semcnt[s] += a
            sig = (s, self.semcnt[s])
        if eng == "gpsimd_dma":
            eng = "gpsimd"
        wm = {}
        for w in waits:
            if w is None:
                continue
            s_, v_ = w
            if v_ > wm.get(s_, 0):
                wm[s_] = v_
        self.ops[eng].append((tuple(wm.items()), fn, inc))
        return sig

    def build(self):
        h = self.semh
        ops = self.ops
        with self.nc.Block() as block:
            def mk(engname):
                def body(eng):
                    for waits, fn, inc in ops[engname]:
                        for (s, v) in waits:
                            eng.wait_ge(h[s], v)
                        if fn is None:
                            continue
                        ins = fn(eng)
                        if inc is not None:
                            ins.then_inc(h[inc[0]], inc[1])
                return body
            for e in ENGS:
                if ops[e]:
                    getattr(block, e)(mk(e))
        self.ops = {e: [] for e in ENGS}
        self.nc.all_engine_barrier()


class Slot:
    def __init__(self, P, name, own=True, store=False):
        self.P = P
        self.f = P.sem(name) if own else None
        self.s = P.sem(name + "S") if store else None
        self.rel = []

    def pw(self):
        w = self.rel
        self.rel = []
        return list(w)

    def release(self, sig):
        assert sig is not None
        self.rel.append(sig)


def build_program(n_layers, first_layer_index=0, debug=False, stop_after=None):
    nc = bass.Bass("TRN2", target_bir_lowering=False)
    L = n_layers

    def din(name, shape, dt=F32):
        return nc.dram_tensor(name, list(shape), dt, kind="ExternalInput").ap()

    def dscr(name, shape, dt):
        return nc.dram_tensor(name, list(shape), dt)

    xT = din("xT", [32, 128, 1024])
    yT = nc.dram_tensor("yT", [32, 128, 1024], F32, kind="ExternalOutput").ap()
    biasA = din("biasA", [128, 8064])
    biasC = din("biasC", [128, 3 * 2944])
    onehot = din("onehot", [128, 4])
    lw = []
    for l in range(L):
        lw.append(dict(
            g1=din(f"g1_{l}", [128, 32]), g2=din(f"g2_{l}", [128, 32]),
            wqk=din(f"wqk_{l}", [16 * 128, 4096]), wv=din(f"wv_{l}", [128, 32768]),
            wout=din(f"wout_{l}", [8 * 128, 4096]), wg=din(f"wg_{l}", [22 * 128, 4096]),
            wu=din(f"wu_{l}", [22 * 128, 4096]), wd=din(f"wd_{l}", [32 * 128, 2816]),
            wqk_b=dscr(f"wqkb_{l}", [16 * 128, 4096], BF16), wv_b=dscr(f"wvb_{l}", [128, 32768], BF16),
            qkg=din(f"qkg_{l}", [128, 16]), og=din(f"og_{l}", [128, 8]),
            lamv=din(f"lamv_{l}", [128, 4, 128]), biasB=din(f"biasB_{l}", [3, 128, 20 * 512]),
            wout_b=dscr(f"woutb_{l}", [8 * 128, 4096], BF16), wout_a=dscr(f"wouta_{l}", [32 * 128, 4096], BF16),
            wg_b=dscr(f"wgb_{l}", [22 * 128, 4096], BF16), wg_a=dscr(f"wga_{l}", [88 * 128, 4096], BF16),
            wu_b=dscr(f"wub_{l}", [22 * 128, 4096], BF16), wu_a=dscr(f"wua_{l}", [88 * 128, 4096], BF16),
            wd_b=dscr(f"wdb_{l}", [32 * 128, 2816], BF16), wd_a=dscr(f"wda_{l}", [128 * 128, 2816], BF16),
        ))
    hb = dscr("hb", [8 * 128 * 2 * 4, 512], BF16)
    hall = dscr("hall", [8 * 4 * 128 * 2 * 4, 512], BF16)
    qk = dscr("qk", [16, 128, 4096], BF16)
    vv = dscr("vv", [4096, 1024], BF16)
    mixo = dscr("mixo", [1024, 4096], BF16)
    mixall = dscr("mixall", [8 * 4 * 128, 4096], BF16)
    xres = dscr("xres", [32, 128, 1024], F32)
    dbg = {}
    if debug:
        dbg["hb"] = nc.dram_tensor("d_hb", [8192, 512], BF16, kind="ExternalOutput").ap()
        dbg["qk"] = nc.dram_tensor("d_qk", [16, 128, 4096], BF16, kind="ExternalOutput").ap()
        dbg["vv"] = nc.dram_tensor("d_vv", [4096, 1024], BF16, kind="ExternalOutput").ap()
        dbg["mixo"] = nc.dram_tensor("d_mixo", [1024, 4096], BF16, kind="ExternalOutput").ap()
        dbg["x1"] = nc.dram_tensor("d_x1", [32, 128, 1024], F32, kind="ExternalOutput").ap()

    def stop(name):
        return stop_after is not None and stop_after == name

    with ExitStack() as es:
        P = Prog(nc, es)
        ones = es.enter_context(nc.sbuf_tensor("ones", [128, 128], BF16))
        g1s = [es.enter_context(nc.sbuf_tensor(f"g1s{l}", [128, 32], F32)) for l in range(L)]
        g2s = [es.enter_context(nc.sbuf_tensor(f"g2s{l}", [128, 32], F32)) for l in range(L)]
        qkgs = [es.enter_context(nc.sbuf_tensor(f"qkgs{l}", [128, 16], F32)) for l in range(L)]
        ogs = [es.enter_context(nc.sbuf_tensor(f"ogs{l}", [128, 8], F32)) for l in range(L)]
        nlam = [es.enter_context(nc.sbuf_tensor(f"nlam{l}", [128, 1], F32)) for l in range(L)]
        ohs = es.enter_context(nc.sbuf_tensor("ohs", [128, 4], F32))
        psb = [es.enter_context(nc.psum_tensor(f"psb{i}", [128, 512], F32)) for i in range(8)]

        s_c = P.sem("constld")
        s_cv = P.sem("constv")
        s_wc = [P.sem(f"wcast{l}") for l in range(L)]
        s_wag = [P.sem(f"wag{l}") for l in range(L)]
        s_wcc = P.sem("wcc")
        s_ag = P.sem("ag")
        s_dbg = P.sem("dbg") if debug else None
        dvp = P.sem("dvp")
        pep = P.sem("pep")
        acp = P.sem("acp")

        def cast_win(l, piece):
            if piece < 2:
                src = lw[l]["wqk"][piece * 1024:(piece + 1) * 1024, :]
                dst = lw[l]["wqk_b"].ap()[piece * 1024:(piece + 1) * 1024, :]
            else:
                src = lw[l]["wv"]
                dst = lw[l]["wv_b"].ap()
            v = P.op("gpsimd", lambda e, src=src, dst=dst: e.dma_start(out=dst, in_=src, max_dma_last_dim=8192), inc=(s_wc[l], 16))
            P.op("gpsimd", None, waits=[v])

        wag_calls = []
        wag_idx = []
        wag_issued = [0] * L
        for l in range(L):
            calls = [("wout", j) for j in range(8)]
            done = set()
            for gi, (f0, nf) in enumerate(FGROUPS):
                for j in range(f0 // 4, (f0 + nf - 1) // 4 + 1):
                    if j not in done:
                        done.add(j)
                        calls += [("wg", j), ("wu", j)]
                calls += [("wd", gi * 8 + j) for j in range(8)]
            wag_calls.append(calls)
            wag_idx.append({c: i for i, c in enumerate(calls)})

        def issue_wag(l, n):
            for _ in range(n):
                i = wag_issued[l]
                if i >= len(wag_calls[l]):
                    return
                k, j = wag_calls[l][i]
                wag_issued[l] += 1
                v = P.op("gpsimd", lambda e, l=l, k=k, j=j: e.dma_start(out=lw[l][k + "_b"].ap()[j * 128:(j + 1) * 128, :], in_=lw[l][k][j * 128:(j + 1) * 128, :], max_dma_last_dim=8192), inc=(s_wcc, 16))
                P.op("gpsimd", None, waits=[v])
                P.op("gpsimd", lambda e, l=l, k=k, j=j: e.collective_compute("AllGather", ALU.bypass, replica_groups=GRP4,
                     ins=[lw[l][k + "_b"].ap()[j * 128:(j + 1) * 128, :]], outs=[lw[l][k + "_a"].ap()[j * 512:(j + 1) * 512, :]]), inc=(s_wag[l], 1))

        def wag_sig(l, call):
            return (s_wag[l], wag_idx[l][call] + 1)

        with ExitStack() as ps:
            lamt = ps.enter_context(nc.sbuf_tensor("lamt", [128, 4, 128], F32))
            lamp = ps.enter_context(nc.sbuf_tensor("lamp", [128, 2, 128], F32))
            lams = ps.enter_context(nc.sbuf_tensor("lams", [128, 2], F32))
            lame = ps.enter_context(nc.sbuf_tensor("lame", [128, 2], F32))
            P.op("vector", lambda e: e.memset(ones[:], 1.0), inc=(s_cv, 1))
            for l in range(L):
                for (dst, src) in ((g1s[l], lw[l]["g1"]), (g2s[l], lw[l]["g2"]), (qkgs[l], lw[l]["qkg"]), (ogs[l], lw[l]["og"])):
                    P.op("sync", lambda e, dst=dst, src=src: e.dma_start(out=dst[:], in_=src), inc=(s_c, 16))
            P.op("sync", lambda e: e.dma_start(out=ohs[:], in_=onehot), inc=(s_c, 16))
            cast_win(0, 0)
            prev = None
            for l in range(L):
                v = P.op("sync", lambda e, l=l: e.dma_start(out=lamt[:], in_=lw[l]["lamv"]), waits=[prev], inc=(s_c, 16))
                li = lam_init(l + first_layer_index)
                v1 = P.op("vector", lambda e: e.tensor_tensor(out=lamp[:, 0, :], in0=lamt[:, 0, :], in1=lamt[:, 1, :], op=ALU.mult), waits=[v], inc=(s_cv, 1))
                v2 = P.op("vector", lambda e: e.tensor_tensor(out=lamp[:, 1, :], in0=lamt[:, 2, :], in1=lamt[:, 3, :], op=ALU.mult), inc=(s_cv, 1))
                v3 = P.op("vector", lambda e: e.tensor_reduce(out=lams[:], in_=lamp[:], axis=mybir.AxisListType.X, op=ALU.add), waits=[v2], inc=(s_cv, 1))
                v4 = P.op("scalar", lambda e: e.activation(out=lame[:], in_=lams[:], func=AF.Exp), waits=[v3], inc=(s_cv, 1))
                prev = P.op("vector", lambda e, l=l, li=li: e.scalar_tensor_tensor(out=nlam[l][:], in0=lame[:, 1:2], scalar=-li, in1=lame[:, 0:1], op0=ALU.add, op1=ALU.subtract), waits=[v4], inc=(s_cv, 1))
            P.op("vector", None, waits=[prev])
            P.op("sync", None, waits=[(s_c, P.semcnt[s_c]), prev])
            P.build()

        st = dict(xres_st=[None] * 32)

        for l in range(L):
            W = lw[l]
            labs = l + first_layer_index
            last = (l == L - 1)
            xsrc = xT if l == 0 else xres.ap()
            xdst = yT if last else xres.ap()

            with ExitStack() as ps:
                xs = [ps.enter_context(nc.sbuf_tensor(f"p1x{i}_L{l}", [128, 32, 512], F32)) for i in range(2)]
                hs = [ps.enter_context(nc.sbuf_tensor(f"p1h{i}_L{l}", [128, 32, 512], BF16)) for i in range(2)]
                sq = [ps.enter_context(nc.sbuf_tensor(f"p1sq{i}_L{l}", [128, 4, 512], BF16)) for i in range(2)]
                rt = [ps.enter_context(nc.sbuf_tensor(f"p1rt{i}_L{l}", [128, 512], F32)) for i in range(2)]
                if l == 0:
                    P.p1 = dict(x=[Slot(P, f"p1x{i}") for i in range(2)], sq=[Slot(P, f"p1sq{i}") for i in range(2)],
                                ps=[Slot(P, f"p1ps{i}") for i in range(2)], rt=[Slot(P, f"p1rt{i}") for i in range(2)],
                                h=[Slot(P, f"p1h{i}", store=True) for i in range(2)])
                p1 = P.p1
                hbv = hb.ap().rearrange("(j p h a) t -> j p h a t", j=8, p=128, h=2, a=4)
                stores = []
                for th in range(2):
                    b = th
                    X, SQ, PS_, RT, H = p1["x"][b], p1["sq"], p1["ps"][b], p1["rt"][b], p1["h"][b]
                    sigX = P.op("sync", lambda e, b=b, th=th: e.dma_start(out=xs[b][:], in_=xsrc[:, :, th * 512:(th + 1) * 512].rearrange("fc p t -> p fc t")), waits=X.pw(), inc=X)
                    sigPS = None
                    for j in range(8):
                        sb = j % 2
                        sigSQ = P.op("scalar", lambda e, b=b, j=j, sb=sb: e.activation(out=sq[sb][:], in_=xs[b][:, 4 * j:4 * j + 4, :], func=AF.Square), waits=[sigX] + SQ[sb].pw(), inc=SQ[sb])
                        for i in range(4):
                            fc = 4 * j + i
                            sig = P.op("tensor", lambda e, b=b, sb=sb, i=i, fc=fc: e.matmul(psb[b][:], lhsT=ones[:], rhs=sq[sb][:, i, :], start=(fc == 0), stop=(fc == 31)),
                                       waits=([sigSQ] if i == 0 else []) + (PS_.pw() if fc == 0 else []), inc=PS_ if i == 3 else None)
                        SQ[sb].release(sig)
                        sigPS = sig
                    sigRT = P.op("scalar", lambda e, b=b: e.activation(out=rt[b][:], in_=psb[b][:], func=AF.Ln, scale=1.0 / D, bias=EPS), waits=[sigPS] + RT.pw(), inc=RT)
                    PS_.release(sigRT)
                    sigRR = P.op("scalar", lambda e, b=b: e.activation(out=rt[b][:], in_=rt[b][:], func=AF.Exp, scale=-0.5), waits=[sigRT], inc=(acp, 1))
                    for fc in range(32):
                        sigH = P.op("vector", lambda e, b=b, fc=fc, l=l: e.scalar_tensor_tensor(out=hs[b][:, fc, :], in0=xs[b][:, fc, :], scalar=g1s[l][:, fc:fc + 1], in1=rt[b][:], op0=ALU.mult, op1=ALU.mult),
                                    waits=([sigRR] + H.pw()) if fc == 0 else [], inc=H if fc == 31 else None)
                    RT.release(sigH)
                    X.release(sigH)
                    for j in range(8):
                        sigSt = P.op("sync", lambda e, b=b, th=th, j=j: e.dma_start(out=hbv[j][:, th, :, :], in_=hs[b][:, 4 * j:4 * j + 4, :]), waits=[sigH] if j == 0 else [], inc=(H.s, 16))
                    H.release(sigSt)
                    stores.append(sigSt)
                P.op("sync", None, waits=stores)
                for j in range(8):
                    vag = P.op("gpsimd", lambda e, j=j: e.collective_compute("AllGather", ALU.bypass, replica_groups=GRP4, ins=[hb.ap()[j * 1024:(j + 1) * 1024, :]], outs=[hall.ap()[j * 4096:(j + 1) * 4096, :]]), waits=stores if j == 0 else [], inc=(s_ag, 1))
                P.op("gpsimd", None, waits=[vag])
                if debug and l == 0:
                    v = P.op("gpsimd", lambda e: e.dma_start(out=dbg["hb"], in_=hb.ap()), inc=(s_dbg, 16))
                    P.op("gpsimd", None, waits=[v])
                P.build()
            if stop("p1"):
                break

            with ExitStack() as ps:
                wres = ps.enter_context(nc.sbuf_tensor(f"p2w_L{l}", [128, 32768], BF16))
                ht = [ps.enter_context(nc.sbuf_tensor(f"p2h{i}_L{l}", [128, 32, 512], BF16)) for i in range(3)]
                sqb = [ps.enter_context(nc.sbuf_tensor(f"p2sq{i}_L{l}", [128, 512], BF16)) for i in range(2)]
                rtb = [ps.enter_context(nc.sbuf_tensor(f"p2rt{i}_L{l}", [128, 512], F32)) for i in range(2)]
                qo = [ps.enter_context(nc.sbuf_tensor(f"p2qo{i}_L{l}", [128, 512], BF16)) for i in range(3)]
                if l == 0:
                    P.p2 = dict(w=Slot(P, "p2w"), h=[Slot(P, f"p2h{i}") for i in range(3)],
                                pa=[Slot(P, f"p2pa{i}") for i in range(3)], sq=[Slot(P, f"p2sq{i}") for i in range(2)],
                                pb=[Slot(P, f"p2pb{i}") for i in range(2)], rt=[Slot(P, f"p2rt{i}") for i in range(2)],
                                qo=[Slot(P, f"p2qo{i}", store=True) for i in range(3)],
                                cnt=dict(h=0, pa=0, sq=0, qo=0))
                p2 = P.p2
                c2 = p2["cnt"]
                WS = p2["w"]
                hallv = hall.ap().rearrange("(j r p h a) t -> j r p h a t", j=8, r=4, p=128, h=2, a=4)
                if l == 0:
                    cast_win(0, 1)
                    cast_win(0, 2)
                wag_pace = [12, 8, 8] if l == 0 else [7, 7, 6]
                vvv = vv.ap()
                p2stores = []

                def load_h(t):
                    i = c2["h"] % 3
                    c2["h"] += 1
                    r, half = t // 2, t % 2
                    sig = None
                    for j in range(8):
                        sig = P.op("sync", lambda e, i=i, r=r, half=half, j=j: e.dma_start(out=ht[i][:, 4 * j:4 * j + 4, :], in_=hallv[j][r][:, half, :, :]), waits=p2["h"][i].pw() if j == 0 else [], inc=p2["h"][i])
                    return i, sig

                def load_wreg(pss_, i, waits):
                    if pss_ < 2:
                        src = W["wqk_b"].ap()[(pss_ * 8 + i) * 128:(pss_ * 8 + i + 1) * 128, :]
                    else:
                        src = W["wv_b"].ap()[:, i * 4096:(i + 1) * 4096]
                    return P.op("sync", lambda e, i=i, src=src: e.dma_start(out=wres[:, i * 4096:(i + 1) * 4096], in_=src),
                                waits=waits + [(s_wc[l], 16 * (pss_ + 1))], inc=WS)

                sigW_next = None
                for pss in range(3):
                    if pss == 0:
                        sigW = None
                        for i in range(8):
                            sigW = load_wreg(0, i, WS.pw() if i == 0 else [])
                    else:
                        sigW = sigW_next
                    issue_wag(l, wag_pace[pss])
                    hq = [load_h(0), load_h(1)]
                    sigPA_last = None
                    for t in range(8):
                        hi, sigH = hq.pop(0)
                        H = p2["h"][hi]
                        if t + 2 < 8:
                            hq.append(load_h(t + 2))
                        if pss < 2:
                            deferred = None
                            for s in range(8):
                                sg_ = pss * 8 + s
                                ia = c2["pa"] % 3
                                c2["pa"] += 1
                                PA = p2["pa"][ia]
                                for kc in range(32):
                                    w_ = []
                                    if kc == 0:
                                        w_ = PA.pw() + ([sigH] if s == 0 else []) + ([sigW] if (s == 0 and t == 0) else [])
                                    sigPA = P.op("tensor", lambda e, ia=ia, s=s, kc=kc, hi=hi: e.matmul(psb[ia][:], lhsT=wres[:, (s * 32 + kc) * 128:(s * 32 + kc + 1) * 128], rhs=ht[hi][:, kc, :], start=(kc == 0), stop=(kc == 31)),
                                                  waits=w_, inc=PA if kc == 31 else None)
                                sigPA_last = sigPA
                                if t == 7:
                                    sigW_next = load_wreg(pss + 1, s, [sigPA])
                                isq = c2["sq"] % 2
                                c2["sq"] += 1
                                SQ, PB, RT = p2["sq"][isq], p2["pb"][isq], p2["rt"][isq]
                                sigSQ = P.op("scalar", lambda e, ia=ia, isq=isq: e.activation(out=sqb[isq][:], in_=psb[ia][:], func=AF.Square), waits=[sigPA] + SQ.pw(), inc=SQ)

                                def post(ia=ia, isq=isq, sg_=sg_, t=t, SQ=SQ, PB=PB, RT=RT, PA=PA, sigSQ=sigSQ):
                                    sigPB = P.op("tensor", lambda e: e.matmul(psb[4 + isq][:], lhsT=ones[:], rhs=sqb[isq][:], start=True, stop=True), waits=[sigSQ] + PB.pw(), inc=PB)
                                    SQ.release(sigPB)
                                    sigRT = P.op("scalar", lambda e: e.activation(out=rtb[isq][:], in_=psb[4 + isq][:], func=AF.Ln, scale=1.0 / 128, bias=EPS), waits=[sigPB] + RT.pw(), inc=RT)
                                    PB.release(sigRT)
                                    sigRR = P.op("scalar", lambda e: e.activation(out=rtb[isq][:], in_=rtb[isq][:], func=AF.Exp, scale=-0.5), waits=[sigRT], inc=(acp, 1))
                                    iq = c2["qo"] % 3
                                    c2["qo"] += 1
                                    QO = p2["qo"][iq]
                                    sigQO = P.op("vector", lambda e: e.scalar_tensor_tensor(out=qo[iq][:], in0=psb[ia][:], scalar=qkgs[l][:, sg_:sg_ + 1], in1=rtb[isq][:], op0=ALU.mult, op1=ALU.mult),
                                                 waits=[sigRR] + QO.pw(), inc=QO)
                                    PA.release(sigQO)
                                    RT.release(sigQO)
                                    sigSt = P.op("sync", lambda e: e.dma_start(out=qk.ap()[sg_][:, t * 512:(t + 1) * 512], in_=qo[iq][:]), waits=[sigQO], inc=(QO.s, 16))
                                    QO.release(sigSt)
                                    p2stores.append(sigSt)

                                if deferred is not None:
                                    deferred()
                                deferred = post
                            deferred()
                            H.release(sigPA_last)
                        else:
                            for tb4 in range(4):
                                for ch in range(2):
                                    ia = c2["pa"] % 3
                                    c2["pa"] += 1
                                    PA = p2["pa"][ia]
                                    for kc in range(32):
                                        w_ = []
                                        if kc == 0:
                                            first = (tb4 == 0 and ch == 0)
                                            w_ = PA.pw() + ([sigH] if first else []) + ([sigW] if (first and t == 0) else [])
                                        sigPA = P.op("tensor", lambda e, ia=ia, kc=kc, hi=hi, tb4=tb4, ch=ch: e.matmul(psb[ia][:], lhsT=ht[hi][:, kc, tb4 * 128:(tb4 + 1) * 128], rhs=wres[:, kc * 1024 + ch * 512:kc * 1024 + (ch + 1) * 512], start=(kc == 0), stop=(kc == 31)),
                                                      waits=w_, inc=PA if kc == 31 else None)
                                    sigPA_last = sigPA
                                    iq = c2["qo"] % 3
                                    c2["qo"] += 1
                                    QO = p2["qo"][iq]
                                    sigQO = P.op("scalar", lambda e, ia=ia, iq=iq: e.activation(out=qo[iq][:], in_=psb[ia][:], func=AF.Copy), waits=[sigPA] + QO.pw(), inc=QO)
                                    PA.release(sigQO)
                                    sigSt = P.op("sync", lambda e, iq=iq, t=t, tb4=tb4, ch=ch: e.dma_start(out=vvv[t * 512 + tb4 * 128:t * 512 + (tb4 + 1) * 128, ch * 512:(ch + 1) * 512], in_=qo[iq][:]), waits=[sigQO], inc=(QO.s, 16))
                                    QO.release(sigSt)
                                    p2stores.append(sigSt)
                            H.release(sigPA_last)
                    WS.release(sigPA_last)
                    P.op("gpsimd", None, waits=[sigPA_last])
                P.op("sync", None, waits=p2stores[-3:])
                P.op("gpsimd", None, waits=p2stores[-3:])
                if debug and l == 0:
                    v = P.op("gpsimd", lambda e: e.dma_start(out=dbg["qk"], in_=qk.ap()), inc=(s_dbg, 16))
                    v = P.op("gpsimd", lambda e: e.dma_start(out=dbg["vv"], in_=vv.ap()), inc=(s_dbg, 16))
                    P.op("gpsimd", None, waits=[v])
                P.build()
            if stop("p2"):
                break

            with ExitStack() as ps:
                qT = [ps.enter_context(nc.sbuf_tensor(f"p3q{i}_L{l}", [128, 2, 4096], BF16)) for i in range(2)]
                kT = [ps.enter_context(nc.sbuf_tensor(f"p3k{i}_L{l}", [128, 2, 4096], BF16)) for i in range(2)]
                vt = [ps.enter_context(nc.sbuf_tensor(f"p3v{i}_L{l}", [128, 32, 256], BF16)) for i in range(2)]
                bias = ps.enter_context(nc.sbuf_tensor(f"p3bias_L{l}", [128, 10240], F32))
                stt = [ps.enter_context(nc.sbuf_tensor(f"p3st{i}_L{l}", [128, 512], F32)) for i in range(4)]
                et = [ps.enter_context(nc.sbuf_tensor(f"p3e{i}_L{l}", [128, 512], BF16)) for i in range(4)]
                rz = ps.enter_context(nc.sbuf_tensor(f"p3rz_L{l}", [128, 512], F32))
                t0 = ps.enter_context(nc.sbuf_tensor(f"p3t0_L{l}", [128, 2, 512], F32))
                ob = ps.enter_context(nc.sbuf_tensor(f"p3o_L{l}", [128, 2, 512], F32))
                osq = ps.enter_context(nc.sbuf_tensor(f"p3osq_L{l}", [128, 2, 512], BF16))
                ort = ps.enter_context(nc.sbuf_tensor(f"p3ort_L{l}", [128, 512], F32))
                mo = [ps.enter_context(nc.sbuf_tensor(f"p3mo{i}_L{l}", [128, 2, 512], BF16)) for i in range(2)]
                if l == 0:
                    P.p3 = dict(hd=[Slot(P, f"p3hd{i}") for i in range(2)], bias=Slot(P, "p3bias"),
                                sp=[Slot(P, f"p3sp{i}") for i in range(4)], st=[Slot(P, f"p3st{i}") for i in range(4)],
                                e=[Slot(P, f"p3e{i}") for i in range(4)], acc=[Slot(P, f"p3acc{i}", own=False) for i in range(2)],
                                z=Slot(P, "p3z", own=False), rz=Slot(P, "p3rz"), t0=Slot(P, "p3t0"), o=Slot(P, "p3o"), osq=Slot(P, "p3osq"),
                                ss=Slot(P, "p3ss"), ort=Slot(P, "p3ort"), mo=[Slot(P, f"p3mo{i}", store=True) for i in range(2)],
                                pv=P.sem("p3pv"), cnt=dict(hd=0, sp=0, e=0, acc=0, mo=0))
                p3 = P.p3
                c3 = p3["cnt"]
                pv = p3["pv"]
                units = [("A", 0)] + [("B", i) for i in range(3)] + [("C", i) for i in range(3)]
                qkv = qk.ap()
                vvh = vv.ap().rearrange("(tb p) c -> p tb c", p=128)
                mixv = mixo.ap().rearrange("(c p) t -> p c t", p=128)
                p3stores = []
                mixag = []

                def load_unit(u):
                    kind, hi_ = units[u]
                    i = c3["hd"] % 2
                    c3["hd"] += 1
                    HD = p3["hd"][i]
                    if kind == "A":
                        srcs = [(qT[i][:, 0, :], qkv[0]), (qT[i][:, 1, :], qkv[1]), (kT[i][:, 0, :], qkv[2]), (kT[i][:, 1, :], qkv[3]),
                                (vt[i][:, :, :], vvh[:, :, 0:256])]
                    elif kind == "B":
                        srcs = [(qT[i][:, 0, :], qkv[4 + hi_]), (kT[i][:, 0, :], qkv[7 + hi_]), (vt[i][:, :, 0:128], vvh[:, :, 256 + 128 * hi_:256 + 128 * (hi_ + 1)])]
                    else:
                        srcs = [(qT[i][:, 0, :], qkv[10 + hi_]), (kT[i][:, 0, :], qkv[13 + hi_]), (vt[i][:, :, 0:128], vvh[:, :, 640 + 128 * hi_:640 + 128 * (hi_ + 1)])]
                    sig = None
                    for j, (dst, src) in enumerate(srcs):
                        sig = P.op("sync", lambda e, dst=dst, src=src: e.dma_start(out=dst, in_=src), waits=HD.pw() if j == 0 else [], inc=HD)
                    return i, sig

                def load_bias(u):
                    kind, hi_ = units[u]
                    B = p3["bias"]
                    if kind == "A":
                        return P.op("sync", lambda e: e.dma_start(out=bias[:, 0:8064], in_=biasA), waits=B.pw(), inc=B)
                    if kind == "B":
                        return P.op("sync", lambda e, hi_=hi_: e.dma_start(out=bias[:, :], in_=W["biasB"][hi_]), waits=B.pw(), inc=B)
                    return P.op("sync", lambda e: e.dma_start(out=bias[:, 0:3 * 2944], in_=biasC), waits=B.pw(), inc=B)

                def blocks_for(kind, qt):
                    if kind == "A":
                        return [(kb, qt * 512 - kb * 128 + 3968) for kb in range(32)]
                    if kind == "C":
                        return [(kb, qt * 512 - kb * 128 + 1408) for kb in range(max(0, 4 * qt - 8), min(31, 4 * qt + 11) + 1)]
                    if qt == 0:
                        return [(kb, (8 + kb) * 512) for kb in range(6)]
                    if qt == 7:
                        return [(kb, (14 + kb - 26) * 512) for kb in range(26, 32)]
                    return [(4 * qt - 2 + r, r * 512) for r in range(8)]

                nxt = load_unit(0)
                sigB = load_bias(0)
                mix_chunk = 0
                tail = []
                for u, (kind, hi_) in enumerate(units):
                    hs_, sigHD = nxt
                    HD = p3["hd"][hs_]
                    if u + 1 < len(units):
                        nxt = load_unit(u + 1)
                    nmaps = 2 if kind == "A" else 1
                    ndv = 2 if kind == "A" else 1
                    cbase = (hi_ * 2944) if kind == "C" else 0
                    first_of_unit = True
                    sigSTlast = None
                    sigPVlast = None
                    sigT0 = None
                    for qt in range(8):
                        blks = blocks_for(kind, qt)
                        for m in range(nmaps):
                            if kind == "A":
                                ai = 0
                                obank = [4, 5]
                            else:
                                ai = c3["acc"] % 2
                                c3["acc"] += 1
                                obank = [4 + ai]
                            ACC, Z = p3["acc"][ai], p3["z"]
                            nb_ = len(blks)
                            pend = []

                            def emit_pv(item, isfirst, islast, obank=obank, hs_=hs_, ACC=ACC, Z=Z):
                                (kb, ei, E, sigE) = item
                                for dvc in range(ndv):
                                    w_ = [sigE] if dvc == 0 else []
                                    if isfirst and dvc == 0:
                                        w_ = w_ + ACC.pw()
                                    P.op("tensor", lambda e, ei=ei, kb=kb, dvc=dvc: e.matmul(psb[obank[dvc]][:], lhsT=vt[hs_][:, kb, dvc * 128:(dvc + 1) * 128], rhs=et[ei][:], start=isfirst, stop=islast), waits=w_)
                                sig = P.op("tensor", lambda e, ei=ei: e.matmul(psb[6][:], lhsT=ones[:], rhs=et[ei][:], start=isfirst, stop=islast),
                                           waits=Z.pw() if isfirst else [], inc=(pv, 1))
                                E.release(sig)
                                return sig

                            npv = 0
                            for bi, (kb, boff) in enumerate(blks):
                                si = c3["sp"] % 4
                                c3["sp"] += 1
                                SP, ST = p3["sp"][si], p3["st"][si]
                                sigSP = P.op("tensor", lambda e, si=si, kb=kb, qt=qt, m=m, hs_=hs_: e.matmul(psb[si][:], lhsT=kT[hs_][:, m, kb * 128:(kb + 1) * 128], rhs=qT[hs_][:, m, qt * 512:(qt + 1) * 512], start=True, stop=True),
                                              waits=SP.pw() + ([sigHD] if first_of_unit else []), inc=SP)
                                sigST = P.op("vector", lambda e, si=si, boff=boff, cbase=cbase: e.scalar_tensor_tensor(out=stt[si][:], in0=psb[si][:], scalar=SCALE, in1=bias[:, cbase + boff:cbase + boff + 512], op0=ALU.mult, op1=ALU.add),
                                              waits=[sigSP] + ST.pw() + ([sigB] if first_of_unit else []), inc=ST)
                                first_of_unit = False
                                SP.release(sigST)
                                sigSTlast = sigST
                                ei = c3["e"] % 4
                                c3["e"] += 1
                                E = p3["e"][ei]
                                sigE = P.op("scalar", lambda e, si=si, ei=ei: e.activation(out=et[ei][:], in_=stt[si][:], func=AF.Exp), waits=[sigST] + E.pw(), inc=E)
                                ST.release(sigE)
                                pend.append((kb, ei, E, sigE))
                                if len(pend) > 3:
                                    emit_pv(pend.pop(0), npv == 0, False)
                                    npv += 1
                                if bi == 2 and tail:
                                    for fn_ in tail:
                                        fn_()
                                    tail = []
                            while pend:
                                sigPVlast = emit_pv(pend.pop(0), npv == 0, len(pend) == 0)
                                npv += 1
                            RZ, T0, O, OSQ, SS, ORT = p3["rz"], p3["t0"], p3["o"], p3["osq"], p3["ss"], p3["ort"]
                            sigLZ = P.op("scalar", lambda e: e.activation(out=rz[:], in_=psb[6][:], func=AF.Ln), waits=[sigPVlast] + RZ.pw(), inc=(acp, 1))
                            Z.release(sigLZ)
                            sigRZ = P.op("scalar", lambda e: e.activation(out=rz[:], in_=rz[:], func=AF.Exp, scale=-1.0), waits=[sigLZ], inc=RZ)
                            if kind == "A" and m == 0:
                                for dvc in range(2):
                                    sigT0 = P.op("vector", lambda e, dvc=dvc, ai=ai: e.tensor_tensor(out=t0[:, dvc, :], in0=psb[4 + dvc][:], in1=rz[:], op=ALU.mult),
                                                 waits=([sigRZ] + T0.pw()) if dvc == 0 else [], inc=T0 if dvc == 1 else None)
                                ACC.release(sigT0)
                                RZ.release(sigT0)
                                continue
                            if kind == "A":
                                for dvc in range(2):
                                    sigX_ = P.op("vector", lambda e, dvc=dvc, ai=ai: e.tensor_tensor(out=ob[:, dvc, :], in0=psb[4 + dvc][:], in1=rz[:], op=ALU.mult),
                                                 waits=([sigRZ] + O.pw()) if dvc == 0 else [], inc=(dvp, 1) if dvc == 1 else None)
                                ACC.release(sigX_)
                                RZ.release(sigX_)
                                for dvc in range(2):
                                    sigO = P.op("vector", lambda e, dvc=dvc, l=l: e.scalar_tensor_tensor(out=ob[:, dvc, :], in0=ob[:, dvc, :], scalar=nlam[l][:, 0:1], in1=t0[:, dvc, :], op0=ALU.mult, op1=ALU.add),
                                                waits=[sigX_, sigT0] if dvc == 0 else [], inc=O if dvc == 1 else None)
                                T0.release(sigO)
                                nfeat = 256
                            else:
                                sigO = P.op("vector", lambda e, ai=ai: e.tensor_tensor(out=ob[:, 0, :], in0=psb[4 + ai][:], in1=rz[:], op=ALU.mult), waits=[sigRZ] + O.pw(), inc=O)
                                ACC.release(sigO)
                                RZ.release(sigO)
                                nfeat = 128
                            sigOSQ = P.op("scalar", lambda e, ndv=ndv: e.activation(out=osq[:, 0:ndv, :], in_=ob[:, 0:ndv, :], func=AF.Square), waits=[sigO] + OSQ.pw(), inc=OSQ)
                            cm = (1.0 - lam_init(labs)) if kind == "A" else 1.0

                            def fin_tail(ndv=ndv, nfeat=nfeat, cm=cm, mc=mix_chunk, qt=qt, sigOSQ=sigOSQ, O=O, OSQ=OSQ, SS=SS, ORT=ORT, u=u):
                                for dvc in range(ndv):
                                    sigSS = P.op("tensor", lambda e, dvc=dvc: e.matmul(psb[7][:], lhsT=ones[:], rhs=osq[:, dvc, :], start=(dvc == 0), stop=(dvc == ndv - 1)),
                                                 waits=([sigOSQ] + SS.pw()) if dvc == 0 else [], inc=SS if dvc == ndv - 1 else None)
                                OSQ.release(sigSS)
                                sigORT = P.op("scalar", lambda e: e.activation(out=ort[:], in_=psb[7][:], func=AF.Ln, scale=1.0 / nfeat, bias=EPS), waits=[sigSS] + ORT.pw(), inc=ORT)
                                SS.release(sigORT)
                                sigORR = P.op("scalar", lambda e: e.activation(out=ort[:], in_=ort[:], func=AF.Exp, scale=-0.5, bias=math.log(cm)), waits=[sigORT], inc=(acp, 1))
                                mi = c3["mo"] % 2
                                c3["mo"] += 1
                                MO = p3["mo"][mi]
                                for dvc in range(ndv):
                                    sigMO = P.op("vector", lambda e, dvc=dvc: e.scalar_tensor_tensor(out=mo[mi][:, dvc, :], in0=ob[:, dvc, :], scalar=ogs[l][:, mc + dvc:mc + dvc + 1], in1=ort[:], op0=ALU.mult, op1=ALU.mult),
                                                  waits=([sigORR] + MO.pw()) if dvc == 0 else [], inc=MO if dvc == ndv - 1 else None)
                                O.release(sigMO)
                                ORT.release(sigMO)
                                sigSt = P.op("sync", lambda e: e.dma_start(out=mixv[:, mc:mc + ndv, qt * 512:(qt + 1) * 512], in_=mo[mi][:, 0:ndv, :]), waits=[sigMO], inc=(MO.s, 16))
                                MO.release(sigSt)
                                p3stores.append(sigSt)
                                if qt == 7:
                                    for cc in range(mc, mc + ndv):
                                        vag_ = P.op("gpsimd", lambda e, cc=cc: e.collective_compute("AllGather", ALU.bypass, replica_groups=GRP4, ins=[mixo.ap()[cc * 128:(cc + 1) * 128, :]], outs=[mixall.ap()[cc * 512:(cc + 1) * 512, :]]),
                                                     waits=p3stores[-2:], inc=(s_ag, 1))
                                        mixag.append(vag_)
                                    issue_wag(l, 2 if u < 5 else 0)

                            tail.append(fin_tail)
                    mix_chunk += ndv
                    HD.release(sigPVlast)
                    nxtu = units[u + 1] if u + 1 < len(units) else None
                    if nxtu is not None and (nxtu[0] != "C" or nxtu[1] == 0):
                        p3["bias"].release(sigSTlast)
                        sigB = load_bias(u + 1)
                    elif nxtu is None:
                        p3["bias"].release(sigSTlast)
                for fn_ in tail:
                    fn_()
                tail = []
                vst = p3stores[-2:]
                P.op("sync", None, waits=vst)
                st["mixag"] = list(mixag)
                if debug and l == 0:
                    v = P.op("gpsimd", lambda e: e.dma_start(out=dbg["mixo"], in_=mixo.ap()), inc=(s_dbg, 16))
                    P.op("gpsimd", None, waits=[v])
                P.build()
            if stop("p3"):
                break

            with ExitStack() as ps:
                acta = ps.enter_context(nc.sbuf_tensor(f"p4a_L{l}", [128, 32, 1024], BF16))
                actb = ps.enter_context(nc.sbuf_tensor(f"p4b_L{l}", [128, 32, 1024], BF16))
                wr = [ps.enter_context(nc.sbuf_tensor(f"p4w{i}_L{l}", [128, 32, 128], BF16)) for i in range(4)]
                stage = [wr[i][:].rearrange("p a b -> p (a b)").rearrange("p (q t) -> p q t", q=4) for i in range(4)]
                xin = [ps.enter_context(nc.sbuf_tensor(f"p4xin{i}_L{l}", [128, 1024], F32)) for i in range(3)]
                x1 = [ps.enter_context(nc.sbuf_tensor(f"p4x1{i}_L{l}", [128, 1024], F32)) for i in range(4)]
                xsq = [ps.enter_context(nc.sbuf_tensor(f"p4xsq{i}_L{l}", [128, 1024], BF16)) for i in range(2)]
                rt2 = ps.enter_context(nc.sbuf_tensor(f"p4rt2_L{l}", [128, 1024], F32))
                sgt = [ps.enter_context(nc.sbuf_tensor(f"p5sg{i}_L{l}", [128, 1024], F32)) for i in range(2)]
                if l == 0:
                    P.p4 = dict(stg=[Slot(P, f"p4stg{i}") for i in range(4)], a=Slot(P, "p4a", own=False),
                                w=[Slot(P, f"p4w{i}") for i in range(4)], xin=[Slot(P, f"p4xin{i}") for i in range(3)],
                                pm=[Slot(P, f"p4pm{i}") for i in range(2)], x1=[Slot(P, f"p4x1{i}", store=True) for i in range(4)],
                                xsq=[Slot(P, f"p4xsq{i}") for i in range(2)], stat=Slot(P, "p4stat", own=False), rt=Slot(P, "p4rt"),
                                b=Slot(P, "p4b", own=False), pg=Slot(P, "p5pg"), pu=Slot(P, "p5pu"), sg=[Slot(P, f"p5sg{i}") for i in range(2)],
                                act=Slot(P, "p5act", own=False), cnt=dict(w=0, xin=0, pm=0, x1=0, xsq=0, stg=0, sg=0))
                p4 = P.p4
                c4 = p4["cnt"]
                xres_st = st["xres_st"]
                wouta = W["wout_a"].ap().rearrange("(m p) c -> m p c", p=128)
                wga = W["wg_a"].ap().rearrange("(f p) c -> f p c", p=128)
                wua = W["wu_a"].ap().rearrange("(f p) c -> f p c", p=128)
                wda = W["wd_a"].ap().rearrange("(g m p) c -> g m p c", g=4, p=128)
                issue_wag(l, 1000)
                mixallv = mixall.ap().rearrange("(kc p) (q t) -> p kc q t", p=128, q=4)
                A, Bs, STAT, ACT_ = p4["a"], p4["b"], p4["stat"], p4["act"]

                sigA = None
                for kc in range(32):
                    si = c4["stg"] % 4
                    c4["stg"] += 1
                    SG = p4["stg"][si]
                    sigSG = P.op("sync", lambda e, si=si, kc=kc: e.dma_start(out=stage[si], in_=mixallv[:, kc, :, :]), waits=SG.pw() + [st["mixag"][kc // 4]], inc=SG)
                    v = P.op("vector", lambda e, si=si, kc=kc: e.tensor_scalar(out=acta[:, kc, :], in0=stage[si][:, 0, :], scalar1=ohs[:, 0:1], scalar2=0.0, op0=ALU.mult, op1=ALU.add),
                             waits=[sigSG] + (A.pw() if kc == 0 else []), inc=(dvp, 1))
                    for q in range(1, 4):
                        v = P.op("vector", lambda e, si=si, kc=kc, q=q: e.scalar_tensor_tensor(out=acta[:, kc, :], in0=stage[si][:, q, :], scalar=ohs[:, q:q + 1], in1=acta[:, kc, :], op0=ALU.mult, op1=ALU.add),
                                 waits=[v], inc=(dvp, 1))
                    SG.release(v)
                    sigA = v
                for i_ in range(4):
                    p4["w"][i_].release(sigA)

                def load_w(src_ap, call, ncols=4096):
                    i = c4["w"] % 4
                    c4["w"] += 1
                    WSl = p4["w"][i]
                    sig = P.op("sync", lambda e, i=i, src_ap=src_ap, ncols=ncols: e.dma_start(out=wr[i][:].rearrange("p a b -> p (a b)")[:, 0:ncols], in_=src_ap), waits=WSl.pw() + [wag_sig(l, call)], inc=WSl)
                    return i, sig

                def load_xin(m, src, wait_store):
                    i = c4["xin"] % 3
                    c4["xin"] += 1
                    XI = p4["xin"][i]
                    sig = P.op("sync", lambda e, i=i, m=m, src=src: e.dma_start(out=xin[i][:], in_=src[m]), waits=XI.pw() + [wait_store], inc=XI)
                    return i, sig

                wq = [load_w(wouta[0], ("wout", 0)), load_w(wouta[1], ("wout", 0)), load_w(wouta[2], ("wout", 0))]
                xq = [load_xin(0, xsrc, xres_st[0] if l > 0 else None), load_xin(1, xsrc, xres_st[1] if l > 0 else None)]
                deferred = None
                sigSTAT = None
                for m in range(32):
                    wi, sigW = wq.pop(0)
                    if m + 3 < 32:
                        wq.append(load_w(wouta[m + 3], ("wout", (m + 3) // 4)))
                    xi, sigXI = xq.pop(0)
                    if m + 2 < 32:
                        xq.append(load_xin(m + 2, xsrc, xres_st[m + 2] if l > 0 else None))
                    WSl, XI = p4["w"][wi], p4["xin"][xi]
                    pi_ = c4["pm"] % 2
                    c4["pm"] += 1
                    PM = p4["pm"][pi_]
                    for kc in range(32):
                        for th in range(2):
                            w_ = []
                            if kc == 0 and th == 0:
                                w_ = PM.pw() + [sigW] + ([sigA] if m == 0 else [])
                            sigPM = P.op("tensor", lambda e, wi=wi, kc=kc, th=th, pi_=pi_: e.matmul(psb[2 * pi_ + th][:], lhsT=wr[wi][:, kc, :], rhs=acta[:, kc, th * 512:(th + 1) * 512], start=(kc == 0), stop=(kc == 31)),
                                          waits=w_, inc=PM if (kc == 31 and th == 1) else None)
                    WSl.release(sigPM)
                    if deferred is not None:
                        deferred()
                        deferred = None
                    x1i = c4["x1"] % 4
                    c4["x1"] += 1
                    X1 = p4["x1"][x1i]
                    for th in range(2):
                        sigX1 = P.op("vector", lambda e, th=th, pi_=pi_, xi=xi, x1i=x1i: e.tensor_tensor(out=x1[x1i][:, th * 512:(th + 1) * 512], in0=psb[2 * pi_ + th][:], in1=xin[xi][:, th * 512:(th + 1) * 512], op=ALU.add),
                                      waits=([sigPM, sigXI] + X1.pw()) if th == 0 else [], inc=X1 if th == 1 else None)
                    PM.release(sigX1)
                    XI.release(sigX1)
                    sigSt = P.op("sync", lambda e, x1i=x1i, m=m: e.dma_start(out=xres.ap()[m], in_=x1[x1i][:]), waits=[sigX1], inc=(X1.s, 16))
                    xres_st[m] = sigSt
                    X1.release(sigSt)
                    qi = c4["xsq"] % 2
                    c4["xsq"] += 1
                    XS = p4["xsq"][qi]
                    sigXS = P.op("scalar", lambda e, x1i=x1i, qi=qi: e.activation(out=xsq[qi][:], in_=x1[x1i][:], func=AF.Square), waits=[sigX1] + XS.pw(), inc=XS)
                    X1.release(sigXS)

                    def stat_mm(m=m, qi=qi, XS=XS, sigXS=sigXS):
                        for th in range(2):
                            sig = P.op("tensor", lambda e, th=th: e.matmul(psb[6 + th][:], lhsT=ones[:], rhs=xsq[qi][:, th * 512:(th + 1) * 512], start=(m == 0), stop=(m == 31)),
                                       waits=([sigXS] + (STAT.pw() if m == 0 else [])) if th == 0 else [], inc=(pep, 1) if th == 1 else None)
                        XS.release(sig)
                        return sig
                    deferred = stat_mm
                sigSTAT = deferred()
                RT = p4["rt"]
                for th in range(2):
                    sigRT = P.op("scalar", lambda e, th=th: e.activation(out=rt2[:, th * 512:(th + 1) * 512], in_=psb[6 + th][:], func=AF.Ln, scale=1.0 / D, bias=EPS),
                                 waits=([sigSTAT] + RT.pw()) if th == 0 else [], inc=RT if th == 1 else None)
                STAT.release(sigRT)
                sigRR = P.op("scalar", lambda e: e.activation(out=rt2[:], in_=rt2[:], func=AF.Exp, scale=-0.5), waits=[sigRT], inc=(acp, 1))
                xq = [load_xin(0, xres.ap(), xres_st[0]), load_xin(1, xres.ap(), xres_st[1])]
                sigB2 = None
                for m in range(32):
                    xi, sigXI = xq.pop(0)
                    if m + 2 < 32:
                        xq.append(load_xin(m + 2, xres.ap(), xres_st[m + 2]))
                    XI = p4["xin"][xi]
                    sigB2 = P.op("vector", lambda e, m=m, xi=xi, l=l: e.scalar_tensor_tensor(out=actb[:, m, :], in0=xin[xi][:], scalar=g2s[l][:, m:m + 1], in1=rt2[:], op0=ALU.mult, op1=ALU.mult),
                                 waits=[sigXI] + (([sigRR] + Bs.pw()) if m == 0 else []), inc=(dvp, 1))
                    XI.release(sigB2)
                RT.release(sigB2)
                if l + 1 < L:
                    P.op("gpsimd", None, waits=[sigB2])
                    for piece in range(3):
                        cast_win(l + 1, piece)
                    issue_wag(l + 1, 14)
                if debug and l == 0:
                    v = P.op("gpsimd", lambda e: e.dma_start(out=dbg["x1"], in_=xres.ap()), waits=xres_st[28:32], inc=(s_dbg, 16))
                    P.op("gpsimd", None, waits=[v])
                if stop("p4"):
                    P.op("sync", None, waits=xres_st[28:32])
                    P.op("vector", None, waits=[sigB2])
                    P.build()
                    break

                PG, PU = p4["pg"], p4["pu"]
                sigPU = None
                sigPM = None
                for gi, (f0, nf) in enumerate(FGROUPS):
                    lastg = (gi == len(FGROUPS) - 1)
                    wq = [(load_w(wga[f0], ("wg", f0 // 4)), load_w(wua[f0], ("wu", f0 // 4)))]
                    sigACT = None
                    for fi in range(nf):
                        f = f0 + fi
                        (wgi, sigWg), (wui, sigWu) = wq.pop(0)
                        if fi + 1 < nf:
                            wq.append((load_w(wga[f + 1], ("wg", (f + 1) // 4)), load_w(wua[f + 1], ("wu", (f + 1) // 4))))
                        sigs = {}
                        for (wi, sigW_, PSL, base) in ((wgi, sigWg, PG, 0), (wui, sigWu, PU, 2)):
                            for kc in range(32):
                                for th in range(2):
                                    w_ = []
                                    if kc == 0 and th == 0:
                                        w_ = PSL.pw() + [sigW_] + ([sigB2] if (gi == 0 and fi == 0 and base == 0) else [])
                                    sig = P.op("tensor", lambda e, wi=wi, kc=kc, th=th, base=base: e.matmul(psb[base + th][:], lhsT=wr[wi][:, kc, :], rhs=actb[:, kc, th * 512:(th + 1) * 512], start=(kc == 0), stop=(kc == 31)),
                                               waits=w_, inc=PSL if (kc == 31 and th == 1) else None)
                            p4["w"][wi].release(sig)
                            sigs[base] = sig
                        sigPG, sigPU = sigs[0], sigs[2]
                        si = c4["sg"] % 2
                        c4["sg"] += 1
                        SGS = p4["sg"][si]
                        for th in range(2):
                            sigSG = P.op("scalar", lambda e, th=th, si=si: e.activation(out=sgt[si][:, th * 512:(th + 1) * 512], in_=psb[th][:], func=AF.Silu),
                                         waits=([sigPG] + SGS.pw()) if th == 0 else [], inc=SGS if th == 1 else None)
                        PG.release(sigSG)
                        for th in range(2):
                            sigACT = P.op("vector", lambda e, th=th, si=si, fi=fi: e.tensor_tensor(out=acta[:, fi, th * 512:(th + 1) * 512], in0=psb[2 + th][:], in1=sgt[si][:, th * 512:(th + 1) * 512], op=ALU.mult),
                                          waits=([sigPU, sigSG] + (ACT_.pw() + A.pw() if fi == 0 else [])) if th == 0 else [], inc=(dvp, 1) if th == 1 else None)
                        PU.release(sigACT)
                        SGS.release(sigACT)

                    def load_wd(m, gi=gi, nf=nf):
                        return load_w(wda[gi][m][:, 0:nf * 128], ("wd", gi * 8 + m // 4), ncols=nf * 128)
                    wq = [load_wd(0), load_wd(1), load_wd(2)]
                    xq = [load_xin(0, xres.ap(), xres_st[0]), load_xin(1, xres.ap(), xres_st[1])]
                    for m in range(32):
                        wi, sigW = wq.pop(0)
                        if m + 3 < 32:
                            wq.append(load_wd(m + 3))
                        xi, sigXI = xq.pop(0)
                        if m + 2 < 32:
                            xq.append(load_xin(m + 2, xres.ap(), xres_st[m + 2]))
                        WSl, XI = p4["w"][wi], p4["xin"][xi]
                        pi_ = c4["pm"] % 2
                        c4["pm"] += 1
                        PM = p4["pm"][pi_]
                        for fi in range(nf):
                            for th in range(2):
                                w_ = []
                                if fi == 0 and th == 0:
                                    w_ = PM.pw() + [sigW] + ([sigACT] if m == 0 else [])
                                sigPM = P.op("tensor", lambda e, wi=wi, fi=fi, th=th, pi_=pi_, nf=nf: e.matmul(psb[4 + 2 * pi_ + th][:], lhsT=wr[wi][:, fi, :], rhs=acta[:, fi, th * 512:(th + 1) * 512], start=(fi == 0), stop=(fi == nf - 1)),
                                              waits=w_, inc=PM if (fi == nf - 1 and th == 1) else None)
                        WSl.release(sigPM)
                        x1i = c4["x1"] % 4
                        c4["x1"] += 1
                        X1 = p4["x1"][x1i]
                        for th in range(2):
                            sigX1 = P.op("vector", lambda e, th=th, pi_=pi_, xi=xi, x1i=x1i: e.tensor_tensor(out=x1[x1i][:, th * 512:(th + 1) * 512], in0=psb[4 + 2 * pi_ + th][:], in1=xin[xi][:, th * 512:(th + 1) * 512], op=ALU.add),
                                          waits=([sigPM, sigXI] + X1.pw()) if th == 0 else [], inc=X1 if th == 1 else None)
                        PM.release(sigX1)
                        XI.release(sigX1)
                        dst = xdst if lastg else xres.ap()
                        sigSt = P.op("sync", lambda e, x1i=x1i, m=m, dst=dst: e.dma_start(out=dst[m], in_=x1[x1i][:]), waits=[sigX1], inc=(X1.s, 16))
                        xres_st[m] = sigSt
                        X1.release(sigSt)
                    ACT_.release(sigPM)
                    if l + 1 < L and gi < 3:
                        P.op("gpsimd", None, waits=[sigPM])
                        issue_wag(l + 1, 12)
                A.release(sigPM)
                Bs.release(sigPU)
                P.op("sync", None, waits=xres_st[28:32])
                P.build()
    return nc


def _alibi(n):
    return np.exp2(-8.0 * np.arange(1, n + 1, dtype=np.float64) / n)


def _bias_A(g):
    slope = _alibi(4)[g]
    p = np.arange(128)[:, None]
    c = np.arange(8064)[None, :]
    return (-slope * np.abs(c - 3968 - p)).astype(np.float32)


def _bias_C(g):
    out = np.empty((128, 3, 2944), np.float32)
    p = np.arange(128)[:, None]
    c = np.arange(2944)[None, :]
    d = c - 1408 - p
    ad = np.abs(d)
    cnt = (ad <= 64).astype(np.int64) + ((d % 4 == 0) & (ad <= 256)) + ((d % 16 == 0) & (ad <= 1024))
    with np.errstate(divide="ignore"):
        lc = np.log(cnt.astype(np.float64))
    for i in range(3):
        slope = _alibi(12)[3 * g + i]
        v = -slope * ad + lc
        out[:, i, :] = np.where(cnt > 0, v, NEG).astype(np.float32)
    return out.reshape(128, 3 * 2944)


def _bias_B(rpb_l, g):
    out = np.full((3, 128, 20, 512), NEG, np.float32)
    pidx = np.arange(128)
    krl, kc = pidx // 64, pidx % 64
    fidx = np.arange(512)
    qrl, qc = fidx // 64, fidx % 64
    cstart = np.clip(qc - 8, 0, 48)
    colok = (kc[:, None] >= cstart[None, :]) & (kc[:, None] < cstart[None, :] + 16)
    dc = np.clip(kc[:, None] - qc[None, :] + 15, 0, 30)
    cases = [(1, 4 * 1 - 2 + r, r) for r in range(8)] + [(0, kb, 8 + kb) for kb in range(6)] + [(7, kb, 14 + kb - 26) for kb in range(26, 32)]
    for (qt, kb, ti) in cases:
        kr = 2 * kb + krl
        qr = 8 * qt + qrl
        rstart = np.clip(qr - 4, 0, 56)
        rowok = (kr[:, None] >= rstart[None, :]) & (kr[:, None] < rstart[None, :] + 8)
        dr = np.clip(kr[:, None] - qr[None, :] + 7, 0, 14)
        ok = rowok & colok
        for i in range(3):
            vals = rpb_l[3 * g + i][dr, dc]
            out[i, :, ti, :] = np.where(ok, vals, NEG)
    return out.reshape(3, 128, 20 * 512)


def _block_w(Wm, kcn, nb):
    return np.ascontiguousarray(Wm.reshape(kcn, 128, nb, 128).transpose(2, 1, 0, 3))


def _qk_cols(g):
    cols = []
    for m in range(2):
        cols.append(np.arange(g * 256 + m * 128, g * 256 + (m + 1) * 128))
    for m in range(2):
        cols.append(1024 + np.arange(g * 256 + m * 128, g * 256 + (m + 1) * 128))
    for i in range(3):
        cols.append(3072 + (3 * g + i) * 128 + np.arange(128))
    for i in range(3):
        cols.append(4608 + (3 * g + i) * 128 + np.arange(128))
    for i in range(3):
        cols.append(7680 + (3 * g + i) * 128 + np.arange(128))
    for i in range(3):
        cols.append(9216 + (3 * g + i) * 128 + np.arange(128))
    return cols


def _v_cols(g):
    c = [2048 + g * 256 + np.arange(256)]
    for i in range(3):
        c.append(6144 + (3 * g + i) * 128 + np.arange(128))
    for i in range(3):
        c.append(10752 + (3 * g + i) * 128 + np.arange(128))
    return np.concatenate(c)


def _wout_perm():
    rows = []
    for c in range(8):
        for r in range(4):
            if c < 2:
                rows.append(r * 256 + c * 128 + np.arange(128))
            elif c < 5:
                rows.append(1024 + (3 * r + (c - 2)) * 128 + np.arange(128))
            else:
                rows.append(2560 + (3 * r + (c - 5)) * 128 + np.arange(128))
    return np.concatenate(rows)


def _col128(v):
    return np.ascontiguousarray(v.reshape(-1, 128).T.astype(np.float32))


def prepare_inputs(inp, layers):
    f32 = np.float32
    maps = [dict() for _ in range(NCORES)]
    x = np.asarray(inp["x"], f32)
    for c in range(NCORES):
        b, g = c // 4, c % 4
        maps[c]["xT"] = np.ascontiguousarray(x[b, g * 1024:(g + 1) * 1024, :].T).reshape(32, 128, 1024)
        maps[c]["biasA"] = _bias_A(g)
        maps[c]["biasC"] = _bias_C(g)
        oh = np.zeros((128, 4), f32)
        oh[:, g] = 1.0
        maps[c]["onehot"] = oh
    perm = _wout_perm()
    for li, l in enumerate(layers):
        w_in = np.asarray(inp["w_in"][l], f32)
        def shard_blocks(blk, npad):
            nb, _, X = blk.shape
            out = np.zeros((4, npad // 4, 128, X), f32)
            for r in range(4):
                sel = blk[r::4]
                out[r, :sel.shape[0]] = sel
            return out.reshape(4, npad // 4 * 128, X)
        wout_blk = shard_blocks(_block_w(np.asarray(inp["w_out"][l], f32)[perm, :], 32, 32).reshape(32, 128, 4096), 32)
        wg_blk = shard_blocks(_block_w(np.asarray(inp["w_gate"][l], f32), 32, NF).reshape(NF, 128, 4096), 88)
        wu_blk = shard_blocks(_block_w(np.asarray(inp["w_up"][l], f32), 32, NF).reshape(NF, 128, 4096), 88)
        wd_full = _block_w(np.asarray(inp["w_down"][l], f32), NF, 32).reshape(32, 128, NF, 128)
        wd_units = np.zeros((4, 32, 128, 2816), f32)
        for gi, (f0, nf) in enumerate(FGROUPS):
            wd_units[gi, :, :, :nf * 128] = wd_full[:, :, f0:f0 + nf, :].reshape(32, 128, nf * 128)
        wd_blk = np.stack([wd_units[:, r::4].reshape(32, 128, 2816) for r in range(4)]).reshape(4, 32 * 128, 2816)
        lamv = np.stack([inp["lambda_q1"][l], inp["lambda_k1"][l], inp["lambda_q2"][l], inp["lambda_k2"][l]]).astype(f32)
        lamv = np.ascontiguousarray(np.broadcast_to(lamv[None], (128, 4, 128)))
        g1 = _col128(np.asarray(inp["norm1_g"][l]))
        g2 = _col128(np.asarray(inp["norm2_g"][l]))
        qkg = np.stack([inp["a_q_g"][l]] * 2 + [inp["a_k_g"][l]] * 2 + [inp["b_q_g"][l]] * 3 + [inp["b_k_g"][l]] * 3
                       + [inp["c_q_g"][l]] * 3 + [inp["c_k_g"][l]] * 3, axis=1).astype(f32)
        og = np.stack([inp["a_out_g"][l][0:128], inp["a_out_g"][l][128:256]] + [inp["b_out_g"][l]] * 3 + [inp["c_out_g"][l]] * 3, axis=1).astype(f32)
        for g in range(4):
            cols = _qk_cols(g)
            wqk = np.stack([w_in[:, cc].reshape(32, 128, 128).transpose(1, 0, 2) for cc in cols]).reshape(16 * 128, 4096)
            wv = np.ascontiguousarray(w_in[:, _v_cols(g)].reshape(32, 128, 1024).transpose(1, 0, 2)).reshape(128, 32768)
            bB = _bias_B(np.asarray(inp["b_rpb"][l], f32), g)
            for b in range(2):
                m = maps[b * 4 + g]
                m[f"wqk_{li}"] = wqk
                m[f"wv_{li}"] = wv
                m[f"biasB_{li}"] = bB
        for c in range(NCORES):
            m = maps[c]
            m[f"g1_{li}"] = g1
            m[f"g2_{li}"] = g2
            m[f"qkg_{li}"] = np.ascontiguousarray(qkg)
            m[f"og_{li}"] = np.ascontiguousarray(og)
            m[f"lamv_{li}"] = lamv
            m[f"wout_{li}"] = wout_blk[c % 4]
            m[f"wg_{li}"] = wg_blk[c % 4]
            m[f"wu_{li}"] = wu_blk[c % 4]
            m[f"wd_{li}"] = wd_blk[c % 4]
    return maps


_NC_CACHE = {}


def _get_nc(n_layers, first, debug=False):
    key = (n_layers, first, debug)
    if key not in _NC_CACHE:
        _NC_CACHE[key] = build_program(n_layers, first, debug)
    return _NC_CACHE[key]


def assemble_output(res):
    out = np.empty((NB, S, D), np.float32)
    for c in range(NCORES):
        b, g = c // 4, c % 4
        out[b, g * 1024:(g + 1) * 1024, :] = res[c]["yT"].reshape(D, 1024).T
    return out


def kernel(**inputs):
    nc = _get_nc(DEPTH, 0)
    maps = prepare_inputs(inputs, list(range(DEPTH)))
    res = run_bass_kernel_spmd(nc, maps, core_ids=list(range(NCORES)))
    return assemble_output(res.results)
```
